# Optimizing a Trainium2 kernel written in Bass

```python
import functools
import jax, jax.numpy as jnp
from jax import lax
import numpy as np

D_MODEL = 1024
BATCH = 32
SEQ = 256
DEPTH = 4
DEC_BATCH = 2
DEC_SEQ = 1024
PAST_LEN = 256

GRID_W = 64
HEAD_DIM = 64
RWKV_HEADS = 4
NAT_HEADS = 4
HGRN_HEADS = 4
SWA_HEADS = 4
SWA_KV_HEADS = 2
RWKV_W = RWKV_HEADS * HEAD_DIM
NAT_W = NAT_HEADS * HEAD_DIM
HGRN_W = HGRN_HEADS * HEAD_DIM
SWA_W = SWA_HEADS * HEAD_DIM
SWA_KV_W = SWA_KV_HEADS * HEAD_DIM
D_MIX = RWKV_W + NAT_W + HGRN_W + SWA_W
RWKV_LORA_RANK = 64
RWKV_GATE_RANK = 128
RWKV_DECAY_SCALE = 0.6065306597126334
RWKV_LN_EPS = 64e-5
NAT_KH = 8
NAT_KW = 16
NAT_QCOLS = 16
NAT_KCOLS = 32
HGRN_CHUNK = 64
SWA_WINDOW = 128
SWA_BLOCK = 128
CTX_QBLOCK = 128
ROPE_BASE = 10000.0
FFN_HIDDEN = 4 * D_MODEL
NORM_EPS = 1e-6
MASK_VALUE = -1e30
N_MOD = 6
ATTN_SCALE = HEAD_DIM ** -0.5
IN_SPLITS = (RWKV_W, RWKV_W, RWKV_W, RWKV_GATE_RANK, RWKV_LORA_RANK, RWKV_LORA_RANK, RWKV_LORA_RANK, RWKV_LORA_RANK,
             NAT_W, NAT_W, NAT_W,
             HGRN_W, HGRN_W, HGRN_W, HGRN_W, HGRN_W,
             SWA_W, SWA_KV_W, SWA_KV_W)
D_IN = 3 * RWKV_W + RWKV_GATE_RANK + 4 * RWKV_LORA_RANK + 3 * NAT_W + 5 * HGRN_W + SWA_W + 2 * SWA_KV_W

kernel_name = 'hybrid_rwkv7_nat_hgrn2_swa_dit_step'


def _rms(x, gain):
    xf = x.astype(jnp.float32)
    y = xf * lax.rsqrt(jnp.mean(xf * xf, axis=-1, keepdims=True) + NORM_EPS)
    return y.astype(x.dtype) * gain


def _modulation(cond, w, b):
    m = jax.nn.silu(cond) @ w + b
    return m.reshape(cond.shape[0], N_MOD, D_MODEL)


def _split_cols(p):
    cuts, acc = [], 0
    for s in IN_SPLITS[:-1]:
        acc += s
        cuts.append(acc)
    return jnp.split(p, cuts, axis=-1)


def _heads(x, n):
    return x.reshape(x.shape[0], x.shape[1], n, HEAD_DIM)


def _shift_lerp(x, mu):
    prev = jnp.pad(x[:, :-1], ((0, 0), (1, 0), (0, 0)))
    return x + (prev - x) * mu


def _axial_rope(x):
    T = x.shape[1]
    t = jnp.arange(T)
    row = (t // GRID_W).astype(jnp.float32)
    col = (t % GRID_W).astype(jnp.float32)
    half = HEAD_DIM // 2
    nf = half // 2
    inv = 1.0 / (ROPE_BASE ** (jnp.arange(nf, dtype=jnp.float32) / nf))

    def rot(xa, pos):
        ang = pos[:, None] * inv[None, :]
        cs = jnp.cos(ang)[None, :, None, :].astype(x.dtype)
        sn = jnp.sin(ang)[None, :, None, :].astype(x.dtype)
        x1, x2 = xa[..., :nf], xa[..., nf:]
        return jnp.concatenate([x1 * cs - x2 * sn, x2 * cs + x1 * sn], axis=-1)

    return jnp.concatenate([rot(x[..., :half], row), rot(x[..., half:], col)], axis=-1)


def _ctx_attention(q, k, v, sink):
    B, L, H, dh = q.shape
    KV = k.shape[2]
    G = H // KV
    nb = L // CTX_QBLOCK
    qb = jnp.moveaxis((q * ATTN_SCALE).reshape(B, nb, CTX_QBLOCK, KV, G, dh), 1, 0)

    def one_block(qi):
        s = jnp.einsum('bqkgd,blkd->bkgql', qi, k).astype(jnp.float32)
        if sink is not None:
            col = jnp.broadcast_to(sink.reshape(KV, G)[None, :, :, None, None].astype(jnp.float32), s.shape[:-1] + (1,))
            s = jnp.concatenate([s, col], axis=-1)
        p = jax.nn.softmax(s, axis=-1)[..., :L]
        return jnp.einsum('bkgql,blkd->bqkgd', p.astype(v.dtype), v)

    o = lax.map(one_block, qb)
    return jnp.moveaxis(o, 0, 1).reshape(B, L, H * dh)


def _nat_latent(q, k, v, ck, cv, rpb):
    B, T, H, dh = q.shape
    dt = q.dtype
    rows = T // GRID_W
    kh = min(NAT_KH, rows)
    ncb = GRID_W // NAT_QCOLS
    row_start = jnp.clip(jnp.arange(rows) - kh // 2, 0, rows - kh)
    row_idx = row_start[:, None] + jnp.arange(kh)[None, :]
    qcol = jnp.arange(GRID_W).reshape(ncb, NAT_QCOLS)
    win_start = jnp.clip(qcol - NAT_KW // 2, 0, GRID_W - NAT_KW)
    kcol_start = jnp.clip(jnp.arange(ncb) * NAT_QCOLS - NAT_KW // 2, 0, GRID_W - NAT_KCOLS)
    col_idx = kcol_start[:, None] + jnp.arange(NAT_KCOLS)[None, :]
    kc = col_idx[:, None, :]
    col_ok = (kc >= win_start[..., None]) & (kc < win_start[..., None] + NAT_KW)
    roff = row_idx - jnp.arange(rows)[:, None] + NAT_KH - 1
    coff = jnp.clip(kc - qcol[..., None], -(NAT_KW - 1), NAT_KW - 1) + NAT_KW - 1
    bias = rpb[:, roff[:, None, None, :, None], coff[None, :, :, None, :]].astype(jnp.float32)
    kg = k.reshape(B, rows, GRID_W, H, dh)
    vg = v.reshape(B, rows, GRID_W, H, dh)
    ri = row_idx[:, :, None, None]
    ci = col_idx[None, None, :, :]
    kb = kg[:, ri, ci]
    vb = vg[:, ri, ci]
    qg = (q * ATTN_SCALE).reshape(B, rows, ncb, NAT_QCOLS, H, dh)
    s_loc = jnp.einsum('brcqhd,brackhd->bhrcqak', qg, kb).astype(jnp.float32) + bias
    s_loc = jnp.where(col_ok[None, :, :, None, :], s_loc, MASK_VALUE)
    nloc = kh * NAT_KCOLS
    s_loc = s_loc.reshape(B, H, rows, ncb, NAT_QCOLS, nloc)
    s_ctx = jnp.einsum('brcqhd,blhd->bhrcql', qg, ck).astype(jnp.float32)
    p = jax.nn.softmax(jnp.concatenate([s_loc, s_ctx], axis=-1), axis=-1)
    p_loc = p[..., :nloc].reshape(B, H, rows, ncb, NAT_QCOLS, kh, NAT_KCOLS).astype(dt)
    p_ctx = p[..., nloc:].astype(dt)
    o = jnp.einsum('bhrcqak,brackhd->brcqhd', p_loc, vb) + jnp.einsum('bhrcql,blhd->brcqhd', p_ctx, cv)
    return o.reshape(B, T, H * dh)


def _swa_latent(q, k, v, ck, cv, sink):
    B, T, H, dh = q.shape
    dt = q.dtype
    KV = k.shape[2]
    G = H // KV
    L = ck.shape[1]
    nb = T // SWA_BLOCK
    pad = ((0, 0), (SWA_BLOCK, SWA_BLOCK), (0, 0), (0, 0))
    kp, vp = jnp.pad(k, pad), jnp.pad(v, pad)
    idx = jnp.arange(nb)[:, None] * SWA_BLOCK + jnp.arange(3 * SWA_BLOCK)[None, :]
    kb, vb = kp[:, idx], vp[:, idx]
    qb = (q * ATTN_SCALE).reshape(B, nb, SWA_BLOCK, KV, G, dh)
    qpos = jnp.arange(nb)[:, None] * SWA_BLOCK + jnp.arange(SWA_BLOCK)[None, :]
    kpos = (idx - SWA_BLOCK)[:, None, :]
    ok = (jnp.abs(kpos - qpos[:, :, None]) <= SWA_WINDOW) & (kpos >= 0) & (kpos < T)
    s_loc = jnp.einsum('bnqkgd,bnskd->bnkgqs', qb, kb).astype(jnp.float32)
    s_loc = jnp.where(ok[None, :, None, None], s_loc, MASK_VALUE)
    s_ctx = jnp.einsum('bnqkgd,blkd->bnkgql', qb, ck).astype(jnp.float32)
    s_sink = jnp.broadcast_to(sink.reshape(KV, G)[None, None, :, :, None, None].astype(jnp.float32), s_loc.shape[:-1] + (1,))
    p = jax.nn.softmax(jnp.concatenate([s_loc, s_ctx, s_sink], axis=-1), axis=-1)
    nloc = 3 * SWA_BLOCK
    o = (jnp.einsum('bnkgqs,bnskd->bnqkgd', p[..., :nloc].astype(dt), vb)
         + jnp.einsum('bnkgql,blkd->bnqkgd', p[..., nloc:nloc + L].astype(dt), cv))
    return o.reshape(B, T, H * dh)


def _rwkv_direction(r, k, v, wh, ah, mu_rkv, mu_lora, w0, w2, a0, a2, k_k, k_a, r_k, s0):
    B, T, _ = r.shape
    dt = r.dtype
    r = _shift_lerp(r, mu_rkv[0])
    k = _shift_lerp(k, mu_rkv[1])
    v = _shift_lerp(v, mu_rkv[2])
    wh = _shift_lerp(wh, mu_lora[0])
    ah = _shift_lerp(ah, mu_lora[1])
    log_w = -RWKV_DECAY_SCALE * jax.nn.sigmoid((w0 + jnp.tanh(wh) @ w2).astype(jnp.float32))
    a = jax.nn.sigmoid((a0 + ah @ a2).astype(jnp.float32))
    hd = lambda t: t.reshape(B, T, RWKV_HEADS, HEAD_DIM).astype(jnp.float32)
    r, k, v, w, a = hd(r), hd(k), hd(v), hd(jnp.exp(log_w)), hd(a)
    kk = k * k_k.reshape(RWKV_HEADS, HEAD_DIM)
    kk = kk / jnp.maximum(jnp.sqrt(jnp.sum(kk * kk, axis=-1, keepdims=True)), 1e-12)
    k = k * (1.0 + (a - 1.0) * k_a.reshape(RWKV_HEADS, HEAD_DIM))

    def step(S, inp):
        r_t, w_t, k_t, v_t, kk_t, a_t = inp
        sa = jnp.einsum('bhvk,bhk->bhv', S, -kk_t)
        S = S * w_t[:, :, None, :] + sa[..., None] * (kk_t * a_t)[:, :, None, :] + v_t[..., None] * k_t[:, :, None, :]
        return S, jnp.einsum('bhvk,bhk->bhv', S, r_t)

    tm = lambda t: jnp.moveaxis(t, 1, 0)
    s_fin, y = lax.scan(step, s0.astype(jnp.float32), (tm(r), tm(w), tm(k), tm(v), tm(kk), tm(a)))
    y = jnp.moveaxis(y, 0, 1)
    bonus = jnp.sum(r * k * r_k, axis=-1, keepdims=True) * v
    return y, bonus, s_fin.astype(dt)


def _rwkv_mixer(r, k, v, gh, whf, ahf, whb, ahb, lw, s_f, s_b):
    B, T, _ = r.shape
    dt = r.dtype

    def direction(d, r_, k_, v_, wh_, ah_, s0):
        return _rwkv_direction(r_, k_, v_, wh_, ah_, lw['rw_mu_rkv'][d], lw['rw_mu_lora'][d], lw['rw_w0'][d],
                               lw['rw_w2'][d], lw['rw_a0'][d], lw['rw_a2'][d], lw['rw_kk'], lw['rw_ka'],
                               lw['rw_rk'], s0)

    rev = lambda t: t[:, ::-1]
    yf, bf, sf = direction(0, r, k, v, whf, ahf, s_f)
    yb, bb, sb = direction(1, rev(r), rev(k), rev(v), rev(whb), rev(ahb), s_b)
    y = yf + rev(yb)
    mu = jnp.mean(y, axis=-1, keepdims=True)
    var = jnp.mean(jnp.square(y - mu), axis=-1, keepdims=True)
    y = ((y - mu) * lax.rsqrt(var + RWKV_LN_EPS)).reshape(B, T, RWKV_W) * lw['rw_lnx_w'] + lw['rw_lnx_b']
    y = y + (bf + rev(bb)).reshape(B, T, RWKV_W)
    g = jax.nn.sigmoid(gh) @ lw['rw_g2']
    return y.astype(dt) * g, sf, sb


def _hgrn_direction(q, v, f_raw, lb, s0):
    B, T, H, N = q.shape
    dt = q.dtype
    C = HGRN_CHUNK
    nc = T // C
    f_raw = f_raw.astype(jnp.float32)
    f = lb + (1.0 - lb) * jax.nn.sigmoid(f_raw)
    log_f = jnp.log(f)
    k = (1.0 - lb) * jax.nn.sigmoid(-f_raw)
    chunks = lambda t: jnp.moveaxis(t.astype(jnp.float32).reshape(B, nc, C, H, N), 1, 0)
    causal = jnp.tril(jnp.ones((C, C), dtype=bool))[None, :, :, None, None]

    def step(S, inp):
        qc, kc, vc, lfc = inp
        b = jnp.cumsum(lfc, axis=1)
        o_inter = jnp.einsum('bthk,bhkv->bthv', qc * jnp.exp(b), S)
        diff = b[:, :, None] - b[:, None, :]
        dec = jnp.where(causal, jnp.exp(jnp.where(causal, diff, 0.0)), 0.0)
        att = jnp.einsum('bthk,bshk,btshk->bhts', qc, kc, dec)
        o_intra = jnp.einsum('bhts,bshv->bthv', att, vc)
        b_end = b[:, -1]
        S = S * jnp.exp(b_end)[..., None] + jnp.einsum('bshk,bshv->bhkv', kc * jnp.exp(b_end[:, None] - b), vc)
        return S, o_inter + o_intra

    s_fin, o = lax.scan(step, s0.astype(jnp.float32), (chunks(q), chunks(k), chunks(v), chunks(log_f)))
    o = jnp.moveaxis(o, 0, 1).reshape(B, T, H, N)
    return o.astype(dt), s_fin.astype(dt)


def _hgrn_mixer(q, i, g, f_f, f_b, lw, s_f, s_b):
    B, T, _ = q.shape
    dt = q.dtype
    q4 = _heads(jax.nn.silu(q), HGRN_HEADS)
    v4 = _heads(i, HGRN_HEADS)
    lb = lw['hg_lb'].reshape(2, HGRN_HEADS, HEAD_DIM)
    rev = lambda t: t[:, ::-1]
    of, sf = _hgrn_direction(q4, v4, _heads(f_f, HGRN_HEADS), lb[0], s_f)
    ob, sb = _hgrn_direction(rev(q4), rev(v4), rev(_heads(f_b, HGRN_HEADS)), lb[1], s_b)
    o = (of + rev(ob)).astype(jnp.float32)
    o = o * lax.rsqrt(jnp.mean(o * o, axis=-1, keepdims=True) + NORM_EPS)
    o = o.reshape(B, T, HGRN_W).astype(dt) * lw['hg_norm'] * jax.nn.sigmoid(g)
    return o, sf, sb


def _mix_context(h, lw):
    (r, k, v, gh, whf, ahf, whb, ahb, nq, nk, nv, hq, hi, hg, hff, hfb, sq, sk, sv) = _split_cols(h @ lw['w_in'])
    B = h.shape[0]
    z = jnp.zeros((B, RWKV_HEADS, HEAD_DIM, HEAD_DIM), h.dtype)
    ya, rs_f, rs_b = _rwkv_mixer(r, k, v, gh, whf, ahf, whb, ahb, lw, z, z)
    nk4, nv4 = _heads(nk, NAT_HEADS), _heads(nv, NAT_HEADS)
    yb = _ctx_attention(_heads(nq, NAT_HEADS), nk4, nv4, None)
    zh = jnp.zeros((B, HGRN_HEADS, HEAD_DIM, HEAD_DIM), h.dtype)
    yc, hs_f, hs_b = _hgrn_mixer(hq, hi, hg, hff, hfb, lw, zh, zh)
    sk4, sv4 = _heads(sk, SWA_KV_HEADS), _heads(sv, SWA_KV_HEADS)
    yd = _ctx_attention(_heads(sq, SWA_HEADS), sk4, sv4, lw['swa_sink'])
    y = jnp.concatenate([ya, yb, yc, yd], axis=-1)
    new = (jnp.stack([nk4, nv4], axis=1), jnp.stack([sk4, sv4], axis=1),
           jnp.stack([rs_f, rs_b], axis=1), jnp.stack([hs_f, hs_b], axis=1))
    return y, new


def _mix_latent(h, lw, nat_kv, swa_kv, rw_s, hg_s):
    (r, k, v, gh, whf, ahf, whb, ahb, nq, nk, nv, hq, hi, hg, hff, hfb, sq, sk, sv) = _split_cols(h @ lw['w_in'])
    ya, _, _ = _rwkv_mixer(r, k, v, gh, whf, ahf, whb, ahb, lw, rw_s[:, 0], rw_s[:, 1])
    yb = _nat_latent(_heads(nq, NAT_HEADS), _heads(nk, NAT_HEADS), _heads(nv, NAT_HEADS),
                     nat_kv[:, 0], nat_kv[:, 1], lw['nat_rpb'])
    yc, _, _ = _hgrn_mixer(hq, hi, hg, hff, hfb, lw, hg_s[:, 0], hg_s[:, 1])
    yd = _swa_latent(_axial_rope(_heads(sq, SWA_HEADS)), _axial_rope(_heads(sk, SWA_KV_HEADS)),
                     _heads(sv, SWA_KV_HEADS), swa_kv[:, 0], swa_kv[:, 1], lw['swa_sink'])
    return jnp.concatenate([ya, yb, yc, yd], axis=-1), None


def _block(x, mod, lw, mix):
    g = lw['norm_g']
    h = _rms(x, g[0]) * (1 + mod[:, None, 1]) + mod[:, None, 0]
    y, aux = mix(h)
    x = x + mod[:, None, 2] * _rms(y @ lw['w_out'], g[1])
    h = _rms(x, g[2]) * (1 + mod[:, None, 4]) + mod[:, None, 3]
    f = jnp.square(jax.nn.relu(h @ lw['ffn_w1'])) @ lw['ffn_w2']
    x = x + mod[:, None, 5] * _rms(f, g[3])
    return x, aux


def setup_inputs(seed: int = 0) -> dict:
    key = jax.random.key(seed)
    ks = jax.random.split(key, 32)
    nrm = lambda i, shape, s: s * jax.random.normal(ks[i], shape, jnp.float32)
    uni = lambda i, shape: jax.random.uniform(ks[i], shape, jnp.float32)
    R = RWKV_LORA_RANK
    return {
        'x_prompt': nrm(0, (BATCH, SEQ, D_MODEL), 1.0),
        'x_sample': nrm(1, (DEC_BATCH, DEC_SEQ, D_MODEL), 1.0),
        'cache_nat_kv': nrm(2, (DEC_BATCH, DEPTH, 2, PAST_LEN, NAT_HEADS, HEAD_DIM), 1.0),
        'cache_swa_kv': nrm(3, (DEC_BATCH, DEPTH, 2, PAST_LEN, SWA_KV_HEADS, HEAD_DIM), 1.0),
        'state_rwkv': nrm(4, (DEC_BATCH, DEPTH, 2, RWKV_HEADS, HEAD_DIM, HEAD_DIM), 0.3),
        'state_hgrn': nrm(5, (DEC_BATCH, DEPTH, 2, HGRN_HEADS, HEAD_DIM, HEAD_DIM), 0.3),
        'c': nrm(6, (DEC_BATCH, D_MODEL), 1.0),
        'c_ctx': nrm(7, (D_MODEL,), 1.0),
        'norm_g': 1.0 + nrm(8, (DEPTH, 4, D_MODEL), 0.05),
        'mod_w': nrm(9, (DEPTH, D_MODEL, N_MOD * D_MODEL), 0.5 * D_MODEL ** -0.5),
        'mod_b': nrm(10, (DEPTH, N_MOD * D_MODEL), 0.02),
        'w_in': nrm(11, (DEPTH, D_MODEL, D_IN), D_MODEL ** -0.5),
        'w_out': nrm(12, (DEPTH, D_MIX, D_MODEL), D_MIX ** -0.5),
        'rw_mu_rkv': uni(13, (DEPTH, 2, 3, RWKV_W)),
        'rw_mu_lora': uni(14, (DEPTH, 2, 2, R)),
        'rw_w0': nrm(15, (DEPTH, 2, RWKV_W), 0.5),
        'rw_w2': nrm(16, (DEPTH, 2, R, RWKV_W), 0.1),
        'rw_a0': nrm(17, (DEPTH, 2, RWKV_W), 0.1),
        'rw_a2': nrm(18, (DEPTH, 2, R, RWKV_W), 0.5 * R ** -0.5),
        'rw_g2': nrm(19, (DEPTH, RWKV_GATE_RANK, RWKV_W), RWKV_GATE_RANK ** -0.5),
        'rw_kk': 0.85 + nrm(20, (DEPTH, RWKV_W), 0.05),
        'rw_ka': 1.0 + nrm(21, (DEPTH, RWKV_W), 0.05),
        'rw_rk': nrm(22, (DEPTH, RWKV_HEADS, HEAD_DIM), 0.1),
        'rw_lnx_w': 1.0 + nrm(23, (DEPTH, RWKV_W), 0.05),
        'rw_lnx_b': nrm(24, (DEPTH, RWKV_W), 0.02),
        'nat_rpb': nrm(25, (DEPTH, NAT_HEADS, 2 * NAT_KH - 1, 2 * NAT_KW - 1), 0.2),
        'hg_lb_logits': nrm(26, (2, DEPTH, HGRN_W), 0.1),
        'hg_norm': 1.0 + nrm(27, (DEPTH, HGRN_W), 0.05),
        'swa_sink': nrm(28, (DEPTH, SWA_HEADS), 0.5),
        'ffn_w1': nrm(29, (DEPTH, D_MODEL, FFN_HIDDEN), D_MODEL ** -0.5),
        'ffn_w2': nrm(30, (DEPTH, FFN_HIDDEN, D_MODEL), FFN_HIDDEN ** -0.5),
    }


def reference(x_prompt, x_sample, cache_nat_kv, cache_swa_kv, state_rwkv, state_hgrn, c, c_ctx,
              norm_g, mod_w, mod_b, w_in, w_out, rw_mu_rkv, rw_mu_lora, rw_w0, rw_w2, rw_a0, rw_a2,
              rw_g2, rw_kk, rw_ka, rw_rk, rw_lnx_w, rw_lnx_b, nat_rpb, hg_lb_logits, hg_norm,
              swa_sink, ffn_w1, ffn_w2):
    lb_sm = jax.nn.softmax(hg_lb_logits.astype(jnp.float32), axis=1)
    hg_lb = jnp.cumsum(lb_sm, axis=1) - lb_sm[:, :1]
    xp, xs = x_prompt, x_sample
    nat_out, swa_out, rw_out, hg_out = [], [], [], []
    for l in range(DEPTH):
        lw = {
            'norm_g': norm_g[l], 'w_in': w_in[l], 'w_out': w_out[l], 'ffn_w1': ffn_w1[l], 'ffn_w2': ffn_w2[l],
            'rw_mu_rkv': rw_mu_rkv[l], 'rw_mu_lora': rw_mu_lora[l], 'rw_w0': rw_w0[l], 'rw_w2': rw_w2[l],
            'rw_a0': rw_a0[l], 'rw_a2': rw_a2[l], 'rw_g2': rw_g2[l], 'rw_kk': rw_kk[l], 'rw_ka': rw_ka[l],
            'rw_rk': rw_rk[l], 'rw_lnx_w': rw_lnx_w[l], 'rw_lnx_b': rw_lnx_b[l], 'nat_rpb': nat_rpb[l],
            'hg_lb': hg_lb[:, l], 'hg_norm': hg_norm[l], 'swa_sink': swa_sink[l],
        }
        mod_p = _modulation(c_ctx[None, :], mod_w[l], mod_b[l])
        xp, (nkv, skv, rws, hgs) = _block(xp, mod_p, lw, functools.partial(_mix_context, lw=lw))
        nat_out.append(nkv)
        swa_out.append(skv)
        rw_out.append(rws)
        hg_out.append(hgs)
        mod_s = _modulation(c, mod_w[l], mod_b[l])
        xs, _ = _block(xs, mod_s, lw, functools.partial(
            _mix_latent, lw=lw, nat_kv=cache_nat_kv[:, l], swa_kv=cache_swa_kv[:, l],
            rw_s=state_rwkv[:, l], hg_s=state_hgrn[:, l]))
    new_cache_nat_kv = jnp.stack(nat_out, axis=1)
    new_cache_swa_kv = jnp.stack(swa_out, axis=1)
    new_state_rwkv = jnp.stack(rw_out, axis=1)
    new_state_hgrn = jnp.stack(hg_out, axis=1)
    return (xp, xs, new_cache_nat_kv, new_cache_swa_kv, new_state_rwkv, new_state_hgrn)
```

```python
import numpy as np
import concourse.bass as bass
import concourse.mybir as mybir
from concourse.bass_utils import run_bass_kernel_spmd

F32 = mybir.dt.float32
BF16 = mybir.dt.bfloat16
AF = mybir.ActivationFunctionType
ALU = mybir.AluOpType

DEPTH = 4
D = 1024
T = 1024
NT = 8
D_IN = 3712
DECAY = 0.6065306597126334
EPS = 1e-6
LN_EPS = 64e-5


class Buf:
    __slots__ = ("name", "t", "lw", "rd", "wsem", "wcnt", "rsem", "rcnt")

    def __init__(self, name, t):
        self.name = name
        self.t = t
        self.lw = None
        self.rd = []
        self.wsem = None
        self.wcnt = 0
        self.rsem = None
        self.rcnt = 0

    def __getitem__(self, idx):
        return self.t[idx]


class FW:
    ENG = ("pe", "dve", "act", "pool", "sp")

    def __init__(self, nc):
        self.nc = nc
        self.eng = {"pe": nc.tensor, "dve": nc.vector, "act": nc.scalar, "pool": nc.gpsimd, "sp": nc.sync}
        self.sem = {}
        self.cnt = {}
        for e in ("pe", "dve", "act", "pool"):
            self.sem[e] = nc.alloc_semaphore(name="s_" + e)
            self.cnt[e] = 0
        self.seen = {e: {} for e in self.ENG}
        self.ninst = 0
        self.out_dma = []
        self.dma_keys = {}
        self.rr = {}
        self.banks = None
        self.bank_i = 0
        self.dcnt = {}
        self.free_dsems = []

    def sb(self, name, shape, dtype=F32):
        return Buf(name, self.nc.alloc_sbuf_tensor(name, list(shape), dtype))

    def ps(self, name, shape, dtype=F32):
        return Buf(name, self.nc.alloc_psum_tensor(name, list(shape), dtype))

    def dram(self, name, shape, dtype=F32, kind="Internal"):
        return Buf(name, self.nc.dram_tensor(name, list(shape), dtype, kind=kind))

    def bank(self):
        if self.banks is None:
            self.banks = [self.ps("bank%d" % i, [128, 512]) for i in range(8)]
        b = self.banks[self.bank_i % 8]
        self.bank_i += 1
        return b

    def pick(self, key, engs):
        i = self.rr.get(key, 0)
        self.rr[key] = i + 1
        return engs[i % len(engs)]

    def _need(self, e, deps):
        eng = self.eng[e]
        seen = self.seen[e]
        best = {}
        for d in deps:
            if d is None:
                continue
            k, v = d
            if best.get(k, 0) < v:
                best[k] = v
        for k, v in best.items():
            if seen.get(k, 0) >= v:
                continue
            if e == "pe" and k == "pe":
                continue
            seen[k] = v
            eng.wait_ge(self.sem[k], v)

    @staticmethod
    def _deps(reads, writes, skip_dma_waw=False):
        deps = []
        for b in reads:
            deps.append(b.lw)
        for b in writes:
            if not (skip_dma_waw and b.lw is not None and isinstance(b.lw[0], tuple)):
                deps.append(b.lw)
            deps.extend(b.rd)
        return deps

    def op(self, e, fn, reads=(), writes=()):
        self._need(e, self._deps(reads, writes))
        ins = fn(self.eng[e])
        self.cnt[e] += 1
        v = self.cnt[e]
        ins.then_inc(self.sem[e], 1)
        for b in writes:
            b.lw = (e, v)
            b.rd = []
        for b in reads:
            if b not in writes:
                b.rd.append((e, v))
        self.ninst += 1
        return ins

    def dma(self, q, out, in_, reads=(), writes=(), part=False, is_output=False):
        self._need(q, self._deps(reads, writes, skip_dma_waw=part))
        ins = self.eng[q].dma_start(out=out, in_=in_)
        if writes:
            b = writes[0]
            if b.wsem is None:
                b.wsem = self._dsem()
            key = b.wsem
        else:
            b = reads[0]
            if b.rsem is None:
                b.rsem = self._dsem()
            key = b.rsem
        self.dcnt[key] += 16
        v = self.dcnt[key]
        ins.then_inc(self.sem[key], 16)
        self.dma_keys[key] = v
        for w in writes:
            w.lw = (key, v)
            w.rd = []
        for r in reads:
            r.rd.append((key, v))
        if is_output:
            self.out_dma.append((key, v))
        self.ninst += 1
        return ins

    def _dsem(self):
        if self.free_dsems:
            return self.free_dsems.pop()
        key = ("d", len(self.dcnt))
        self.sem[key] = self.nc.alloc_semaphore(name="dsem%d" % len(self.dcnt))
        self.dcnt[key] = 0
        return key

    def barrier(self):
        deps = [(e, self.cnt[e]) for e in ("pe", "dve", "act", "pool") if self.cnt[e]]
        deps += list(self.dma_keys.items())
        for e in self.ENG:
            self._need(e, deps)

    def finish(self):
        self.barrier()


class Arena:
    def __init__(self, fw, nwords):
        self.fw = fw
        self.nwords = nwords
        self.t32 = fw.nc.alloc_sbuf_tensor("arena", [128, nwords], F32)
        self.t16 = self.t32.bitcast(BF16)
        self.off = 0
        self.peak = 0
        self.made = []

    def reset(self):
        self.fw.barrier()
        self.off = 0
        for b in self.made:
            for k in (b.wsem, b.rsem):
                if k is not None:
                    self.fw.free_dsems.append(k)
            b.wsem = b.rsem = None
        self.made = []

    def get(self, name, shape, dtype=F32):
        n = int(np.prod(shape[1:]))
        words = n if dtype == F32 else (n + 1) // 2
        assert self.off + words <= self.nwords, (name, self.off, words, self.nwords)
        if dtype == F32:
            ap = self.t32[:, self.off:self.off + n]
        else:
            ap = self.t16[:, 2 * self.off:2 * self.off + n]
        if len(shape) > 2:
            names = "abcdefg"[:len(shape) - 1]
            kw = {names[i]: shape[1 + i] for i in range(len(shape) - 2)}
            ap = ap.rearrange("p (%s) -> p %s" % (" ".join(names), " ".join(names)), **kw)
        if shape[0] < 128:
            ap = ap[0:shape[0]]
        self.off += words
        self.peak = max(self.peak, self.off)
        b = Buf(name, ap)
        self.made.append(b)
        return b

    def pool(self, name, n, shape, dtype=F32):
        return Rot([self.get("%s%d" % (name, i), shape, dtype) for i in range(n)])


class Rot:
    def __init__(self, bufs):
        self.bufs = bufs
        self.i = 0

    def next(self):
        b = self.bufs[self.i % len(self.bufs)]
        self.i += 1
        return b


class ColTable:
    def __init__(self):
        self.idx = {}
        self.cols = []
        self.n = 0

    def add(self, name, per_layer_vecs):
        k = len(per_layer_vecs[0]) // 128
        self.idx[name] = self.n
        self.cols.append(np.stack([np.asarray(v, np.float32).reshape(k, 128) for v in per_layer_vecs]))
        self.n += k

    def build(self):
        a = np.concatenate(self.cols, axis=1)
        return np.ascontiguousarray(a.transpose(2, 0, 1))


def _col_names():
    ct = ColTable()
    z = lambda n: [np.zeros(n, np.float32)] * DEPTH
    for nm, n in COL_SPEC:
        ct.add(nm, z(n))
    return ct.idx, ct.n


COL_SPEC = [("norm_g", 4096), ("mod_b", 6144),
            ("mu_rkv0", 768), ("mu_rkv1", 768), ("mu_lora0", 128), ("mu_lora1", 128),
            ("w0_0", 256), ("w0_1", 256), ("a0_0", 256), ("a0_1", 256),
            ("kk", 256), ("ka", 256), ("rk", 256), ("lnx_w", 256), ("lnx_b", 256),
            ("lbl0", 1024), ("lbl1", 1024), ("hg_norm", 256), ("sink", 512)]
COLI, NCOL = _col_names()


def _build_cols(inp):
    ct = ColTable()
    L = range(DEPTH)
    g = lambda k: np.asarray(inp[k], np.float32)
    vals = {
        "norm_g": [g("norm_g")[l].reshape(-1) for l in L],
        "mod_b": [g("mod_b")[l] for l in L],
        "mu_rkv0": [g("rw_mu_rkv")[l, 0].reshape(-1) for l in L],
        "mu_rkv1": [g("rw_mu_rkv")[l, 1].reshape(-1) for l in L],
        "mu_lora0": [g("rw_mu_lora")[l, 0].reshape(-1) for l in L],
        "mu_lora1": [g("rw_mu_lora")[l, 1].reshape(-1) for l in L],
        "w0_0": [g("rw_w0")[l, 0] for l in L], "w0_1": [g("rw_w0")[l, 1] for l in L],
        "a0_0": [g("rw_a0")[l, 0] for l in L], "a0_1": [g("rw_a0")[l, 1] for l in L],
        "kk": [g("rw_kk")[l] for l in L], "ka": [g("rw_ka")[l] for l in L],
        "rk": [g("rw_rk")[l].reshape(-1) for l in L],
        "lnx_w": [g("rw_lnx_w")[l] for l in L], "lnx_b": [g("rw_lnx_b")[l] for l in L],
        "lbl0": [g("hg_lb_logits")[0].reshape(-1) for l in L],
        "lbl1": [g("hg_lb_logits")[1].reshape(-1) for l in L],
        "hg_norm": [g("hg_norm")[l] for l in L],
        "sink": [np.repeat(g("swa_sink")[l], 128) for l in L],
    }
    for nm, n in COL_SPEC:
        assert len(vals[nm][0]) == n, nm
        ct.add(nm, vals[nm])
    return ct.build()


CST_SPEC = [("ident", 128), ("ones", 128), ("blk", 128),
            ("g4_0", 512), ("g4_1", 512), ("mb_0", 128), ("mb_1", 128),
            ("reset", 256), ("natmask", 64),
            ("mprev", 128), ("mnext", 128), ("flip", 64)]
CSTI = {}
_o = 0
for _n, _w in CST_SPEC:
    CSTI[_n] = (_o, _w)
    _o += _w
NCST = _o


def _build_consts():
    c = np.zeros((128, NCST), np.float32)

    def put(nm, a):
        o, w = CSTI[nm]
        c[:a.shape[0], o:o + w] = a
    i = np.arange(128)
    put("ident", np.eye(128))
    put("ones", np.ones((128, 128)))
    put("blk", ((i[:, None] // 64) == (i[None, :] // 64)).astype(np.float32))
    for d in (0, 1):
        if d == 0:
            strictT = (i[:, None] < i[None, :]).astype(np.float32)
            inclT = (i[:, None] <= i[None, :]).astype(np.float32)
        else:
            strictT = (i[:, None] > i[None, :]).astype(np.float32)
            inclT = (i[:, None] >= i[None, :]).astype(np.float32)
        put("g4_%d" % d, np.concatenate([-strictT, inclT, strictT, inclT], axis=1))
        put("mb_%d" % d, -strictT.T)
    r = np.ones((128, 256), np.float32)
    r[:, 0] = 0
    r[:, 128] = 0
    put("reset", r)
    t = np.arange(1024)
    row = (t // 64).astype(np.float32)
    col = (t % 64).astype(np.float32)
    inv = (1.0 / (10000.0 ** (np.arange(16, dtype=np.float32) / 16))).astype(np.float32)
    C = np.zeros((64, 1024), np.float32)
    S = np.zeros((64, 1024), np.float32)
    for base, pos in ((0, row), (32, col)):
        ang = (pos[None, :] * inv[:, None]).astype(np.float32)
        C[base:base + 16] = np.cos(ang)
        C[base + 16:base + 32] = np.cos(ang)
        S[base:base + 16] = -np.sin(ang)
        S[base + 16:base + 32] = np.sin(ang)
    rope = np.stack([np.concatenate([C, C], axis=0), np.concatenate([S, S], axis=0)], axis=1)
    kc = np.arange(64)[:, None]
    qc = np.arange(64)[None, :]
    ws = np.clip(qc - 8, 0, 48)
    put("natmask", ((kc >= ws) & (kc < ws + 16)).astype(np.float32))
    put("mprev", (i[:, None] >= i[None, :]).astype(np.float32))
    put("mnext", (i[:, None] <= i[None, :]).astype(np.float32))
    put("flip", np.eye(64)[::-1].copy())
    return c, np.ascontiguousarray(rope.astype(np.float32))


ROPE_PERM = np.concatenate([np.arange(16, 32), np.arange(0, 16), np.arange(48, 64), np.arange(32, 48)])


class Prog:
    def __init__(self, debug=(), layers=DEPTH, groups=("P", "S")):
        self.debug = set(debug)
        self.layers = layers
        self.groups = groups
        self.nc = bass.Bass("TRN2", target_bir_lowering=False)
        self.fw = FW(self.nc)
        self.dbg_outs = {}
        self.build()

    def mm(self, out, lhsT, rhs, start, stop, reads, writes):
        return self.fw.op("pe", lambda e: e.matmul(out, lhsT, rhs, start=start, stop=stop), reads, writes)

    def tp(self, out, in_, ident, reads, writes):
        return self.fw.op("pe", lambda e: e.transpose(out, in_, ident), reads, writes)

    def act(self, out, in_, func, reads, writes, bias=None, scale=1.0):
        if bias is None:
            return self.fw.op("act", lambda e: e.activation(out, in_, func, scale=scale), reads, writes)
        return self.fw.op("act", lambda e: e.activation(out, in_, func, bias=bias, scale=scale), reads, writes)

    def tt(self, eng, out, in0, in1, op, reads, writes):
        return self.fw.op(eng, lambda e: e.tensor_tensor(out, in0, in1, op), reads, writes)

    def ts(self, eng, out, in0, s1, s2, op0, op1, reads, writes):
        if s2 is None:
            return self.fw.op(eng, lambda e: e.tensor_scalar(out, in0, s1, None, op0), reads, writes)
        return self.fw.op(eng, lambda e: e.tensor_scalar(out, in0, s1, s2, op0, op1), reads, writes)

    def stt(self, eng, out, in0, scalar, in1, op0, op1, reads, writes):
        eng = "dve"
        return self.fw.op(eng, lambda e: e.scalar_tensor_tensor(out, in0, scalar, in1, op0, op1), reads, writes)

    def rsqrt(self, out, in_, addc, reads, wbuf):
        self.act(out, in_, AF.Ln, reads + [self.epsb], [wbuf], bias=self.epsc(addc))
        self.act(out, out, AF.Exp, [wbuf], [wbuf], scale=-0.5)

    def epsc(self, v):
        i = self.eps_vals.index(v)
        return self.epsb[:, i:i + 1]

    def cp(self, eng, out, in_, reads, writes):
        if eng == "act":
            return self.fw.op("act", lambda e: e.copy(out, in_), reads, writes)
        return self.fw.op(eng, lambda e: e.tensor_copy(out, in_), reads, writes)

    def ev(self):
        return self.fw.pick("ev", ("dve", "act"))

    def el(self):
        return self.fw.pick("el", ("dve", "pool"))

    def dump(self, name, buf, ap, shape, dtype=F32):
        if name not in self.debug:
            return
        o = self.fw.dram("dbg_" + name, list(shape), dtype, kind="ExternalOutput")
        self.dbg_outs[name] = o
        self.fw.dma("sp", o[:], ap, reads=[buf], is_output=True)

    def col(self, l, name, j=0, n=1):
        i = COLI[name] + j
        return self.cols[:, l, i:i + n]

    def cst(self, name, lo=0, hi=None, rows=128):
        o, w = CSTI[name]
        hi = w if hi is None else hi
        return self.cstb[0:rows, o + lo:o + hi]

    def load_block(self, src_ap, a, b, cast=True):
        fw = self.fw
        n = a * b
        st = self.WST.next()
        sview = st[:, 0:n].rearrange("p (a b) -> p a b", a=a)
        fw.dma("sp", sview, src_ap, writes=[st])
        self.mod_tick()
        if not cast:
            return st, sview
        wb = self.WBF.next()
        wview = wb[:, 0:n].rearrange("p (a b) -> p a b", a=a)
        self.cp(self.fw.pick("cast", ("pool", "dve", "pool")), wb[:, 0:n], st[:, 0:n], [st], [wb])
        return wb, wview

    def win_block(self, l, c0, ncol, src=None):
        src = self.w_in if src is None else src
        ap = src[l, :, c0:c0 + ncol].rearrange("(k p) n -> p k n", p=128)
        return self.load_block(ap, 8, ncol)

    def mod_tick(self, force=False):
        if not self.mod_pending:
            return
        if self.in_mod:
            return
        self.in_mod = True
        l, kc, third = self.mod_pending.pop(0)
        fw = self.fw
        st = self.WST.next()
        fw.dma("sp", st[:, 0:2048], self.mod_w[l, kc * 128:(kc + 1) * 128, third * 2048:(third + 1) * 2048], writes=[st])
        bk = fw.bank()
        for j in range(16):
            self.mm(bk[:, 2 * j:2 * j + 2], st[:, j * 128:(j + 1) * 128], self.silu[:, kc, :], True, True, [st, self.silu], [bk])
        dst = self.modacc[:, l, third * 16:(third + 1) * 16, :]
        src = bk[:, 0:32].rearrange("p (a b) -> p a b", b=2)
        if kc == 0:
            self.cp("dve", dst, src, [bk], [self.modacc])
        else:
            self.tt("dve", dst, dst, src, ALU.add, [bk, self.modacc], [self.modacc])
        self.in_mod = False

    def mod_flush(self, l):
        while self.mod_pending and self.mod_pending[0][0] <= l:
            self.mod_tick()

    def mod_finalize(self, l, g):
        fw = self.fw
        mc = self.modc
        m = lambda i: self.modacc[:, l, i * 8:(i + 1) * 8, g]
        mb = lambda i: self.col(l, "mod_b", i * 8, 8)
        ng = lambda i: self.col(l, "norm_g", i * 8, 8)
        R = [self.modacc, self.cols, mc]
        for i in range(6):
            self.tt("dve", mc[:, i, :], m(i), mb(i), ALU.add, R, [mc])
        self.ts("dve", mc[:, 6, :], mc[:, 1, :], 1.0, 32.0, ALU.add, ALU.mult, R, [mc])
        self.tt("dve", mc[:, 6, :], mc[:, 6, :], ng(0), ALU.mult, R, [mc])
        self.ts("dve", mc[:, 7, :], mc[:, 2, :], 32.0, None, ALU.mult, None, R, [mc])
        self.tt("dve", mc[:, 7, :], mc[:, 7, :], ng(1), ALU.mult, R, [mc])
        self.ts("dve", mc[:, 8, :], mc[:, 4, :], 1.0, 32.0, ALU.add, ALU.mult, R, [mc])
        self.tt("dve", mc[:, 8, :], mc[:, 8, :], ng(2), ALU.mult, R, [mc])
        self.ts("dve", mc[:, 9, :], mc[:, 5, :], 32.0, None, ALU.mult, None, R, [mc])
        self.tt("dve", mc[:, 9, :], mc[:, 9, :], ng(3), ALU.mult, R, [mc])
        return dict(shift1=mc[:, 0, :], coefA1=mc[:, 6, :], gco1=mc[:, 7, :], shift2=mc[:, 3, :], coefA2=mc[:, 8, :], gco2=mc[:, 9, :])

    def rstd_of(self, src_bufs, src_ap, sq, rstd):
        fw = self.fw
        bks = [fw.bank(), fw.bank()]
        for c in range(8):
            s = sq.next()
            self.act(s[:, :], src_ap(c), AF.Square, [src_bufs[c]], [s])
            for th in range(2):
                self.mm(bks[th][:, :], self.ones_bf[:, :], s[:, th * 512:(th + 1) * 512], c == 0, c == 7, [self.ones_bf, s], [bks[th]])
        for th in range(2):
            self.rsqrt(rstd[:, th * 512:(th + 1) * 512], bks[th][:, :], D * EPS, [bks[th]], rstd)
        return rstd

    def build(self):
        fw = self.fw
        nc = self.nc
        di = lambda n, s: fw.dram(n, s, F32, kind="ExternalInput")
        do = lambda n, s: fw.dram(n, s, F32, kind="ExternalOutput")
        self.xT = {"P": di("xT_p", [128, 8, T]), "S": di("xT_s", [128, 8, T])}
        self.yT = {"P": do("yT_p", [128, 8, T]), "S": do("yT_s", [128, 8, T])}
        condT = di("condT", [128, 8, 2])
        cols_d = di("cols", [128, DEPTH, NCOL])
        cst_d = di("cst", [128, NCST])
        NL = self.layers
        self.mod_w = di("mod_w", [NL, D, 6 * D])
        self.w_in = di("w_in", [NL, D, D_IN])
        self.w_in_sw = di("w_in_sw", [NL, D, 384])
        self.w_out = di("w_out", [NL, D, D])
        self.ffn_w1 = di("ffn_w1", [NL, D, 4 * D])
        self.ffn_w2 = di("ffn_w2", [NL, 4 * D, D])
        self.lora2 = di("lora2", [DEPTH, 128, 2, 256])
        self.rw_g2 = di("rw_g2", [DEPTH, 128, 256])
        self.rpbpad = di("rpbpad", [DEPTH, 4, 15, 127])
        self.rope = di("rope", [128, 2, T])
        self.cnk = di("cnk", [DEPTH, 2, 128, 256])
        self.cnv = di("cnv", [DEPTH, 256, 256])
        self.csk = di("csk", [DEPTH, 2, 128, 256])
        self.csv = di("csv", [DEPTH, 256, 128])
        self.srw = di("srw", [DEPTH, 2, 128, 2, 64])
        self.shg = di("shg", [DEPTH, 2, 128, 2, 64])
        self.natk = do("natk", [DEPTH, 2, 128, T])
        self.natv = do("natv", [DEPTH, NT, 128, 256])
        self.swak = do("swak", [DEPTH, 128, T])
        self.swav = do("swav", [DEPTH, NT, 128, 128])
        self.rwst = do("rwst", [DEPTH, 2, 4, 128, 2, 64])
        self.hgst = do("hgst", [DEPTH, 2, 4, 128, 2, 64])

        self.cstb = fw.sb("cstb", [128, NCST])
        self.cols = fw.sb("colsb", [128, DEPTH, NCOL])
        self.ident_bf = fw.sb("ident_bf", [128, 128], BF16)
        self.ones_bf = fw.sb("ones_bf", [128, 128], BF16)
        self.blk_bf = fw.sb("blk_bf", [128, 128], BF16)
        self.silu = fw.sb("silu", [128, 8, 2])
        self.modacc = fw.sb("modacc", [128, DEPTH, 48, 2])
        self.modc = fw.sb("modc", [128, 10, 8])
        self.lb = fw.sb("lb", [128, 2, DEPTH, 2])
        self.oml = fw.sb("oml", [128, 2, DEPTH, 2])
        self.X = [fw.sb("X%d" % c, [128, T]) for c in range(8)]
        self.H = [fw.sb("H%d" % c, [128, T], BF16) for c in range(8)]
        self.MIX = [fw.sb("MIX%d" % c, [128, T], BF16) for c in range(8)]
        self.WST = Rot([fw.sb("wst%d" % i, [128, 2048]) for i in range(2)])
        self.WBF = Rot([fw.sb("wbf%d" % i, [128, 2048], BF16) for i in range(2)])
        self.rstd = fw.sb("rstd", [128, T])
        self.sq = Rot([fw.sb("sq%d" % i, [128, T], BF16) for i in range(2)])
        self.arena = Arena(fw, (nc.sbuf_bytes_remaining - 2048) // 4)

        self.eps_vals = [D * EPS, 1e-24, 64 * LN_EPS, 64 * EPS]
        self.epsb = fw.sb("epsb", [128, 4])
        for i, v in enumerate(self.eps_vals):
            fw.op("pool", lambda e: e.memset(self.epsb[:, i:i + 1], v), [], [self.epsb])
        fw.dma("sp", self.cstb[:, :], cst_d[:, :], writes=[self.cstb])
        fw.dma("sp", self.cols[:, :, :], cols_d[:, :, :], writes=[self.cols])
        fw.dma("sp", self.silu[:, :, :], condT[:, :, :], writes=[self.silu])
        self.cp("dve", self.ident_bf[:, :], self.cst("ident"), [self.cstb], [self.ident_bf])
        self.cp("dve", self.ones_bf[:, :], self.cst("ones"), [self.cstb], [self.ones_bf])
        self.cp("dve", self.blk_bf[:, :], self.cst("blk"), [self.cstb], [self.blk_bf])
        self.act(self.silu[:, :, :], self.silu[:, :, :], AF.Silu, [self.silu], [self.silu])
        self.setup_lb()

        self.mod_pending = [(l, kc, th) for l in range(self.layers) for kc in range(8) for th in range(3)]
        self.in_mod = False

        for G in self.groups:
            for c in range(8):
                fw.dma("sp", self.X[c][:, :], self.xT[G][:, c, :], writes=[self.X[c]])
            for l in range(self.layers):
                self.block(G, l)
            for c in range(8):
                fw.dma("sp", self.yT[G][:, c, :], self.X[c][:, :], reads=[self.X[c]], is_output=True)
        fw.finish()

    def setup_lb(self):
        E = self.fw.sb("lbE", [128, 2, DEPTH, 2])
        S = self.fw.sb("lbS", [128, 2, 2])
        R = [self.cols, E, S, self.lb, self.oml]
        for d in range(2):
            i = COLI["lbl%d" % d]
            self.act(E[:, d, :, :], self.cols[:, 0, i:i + 8].rearrange("p (a b) -> p a b", a=DEPTH), AF.Exp, R, [E])
            self.tt("dve", S[:, d, :], E[:, d, 0, :], E[:, d, 1, :], ALU.add, R, [S])
            self.tt("dve", S[:, d, :], S[:, d, :], E[:, d, 2, :], ALU.add, R, [S])
            self.tt("dve", S[:, d, :], S[:, d, :], E[:, d, 3, :], ALU.add, R, [S])
            self.fw.op("dve", lambda e: e.reciprocal(S[:, d, :], S[:, d, :]), R, [S])
            for l in range(DEPTH):
                self.tt("dve", E[:, d, l, :], E[:, d, l, :], S[:, d, :], ALU.mult, R, [E])
            self.fw.op("dve", lambda e: e.memset(self.lb[:, d, 0, :], 0.0), R, [self.lb])
            for l in range(1, DEPTH):
                self.tt("dve", self.lb[:, d, l, :], self.lb[:, d, l - 1, :], E[:, d, l, :], ALU.add, R, [self.lb])
        self.ts("dve", self.oml[:, :, :, :], self.lb[:, :, :, :], -1.0, 1.0, ALU.mult, ALU.add, R, [self.oml])

    def block(self, G, l):
        fw = self.fw
        g = 0 if G == "P" else 1
        self.G, self.l = G, l
        self.seqs = [(s * 2, 2) for s in range(4)] if G == "P" else [(0, 8)]
        self.mod_flush(l)
        mc = self.mod_finalize(l, g)
        self.mc = mc
        X, H = self.X, self.H
        rstd = self.rstd_of(X, lambda c: X[c][:, :], self.sq, self.rstd)
        for c in range(8):
            tmp = self.sq.next()
            e = self.el()
            self.stt(e, tmp[:, :], X[c][:, :], mc["coefA1"][:, c:c + 1], rstd[:, :], ALU.mult, ALU.mult, [X[c], rstd, self.modc], [tmp])
            self.act(H[c][:, :], tmp[:, :], AF.Identity, [tmp, self.modc], [H[c]], bias=mc["shift1"][:, c:c + 1])
        self.dump("h_%s%d" % (G, l), H[0], H[0][:, :], [128, T], BF16)
        self.mixer_A()
        self.mixer_B()
        self.mixer_C()
        self.mixer_D()
        for c in range(8):
            self.dump("mix%d_%s%d" % (c, G, l), self.MIX[c], self.MIX[c][:, :], [128, T], BF16)
        self.out_and_ffn()

    def proj_fm(self, wb, wv, nchunks, evac):
        for jj in range(nchunks):
            for th in range(2):
                bk = self.fw.bank()
                for kc in range(8):
                    self.mm(bk[:, :], wv[:, kc, jj * 128:(jj + 1) * 128], self.H[kc][:, th * 512:(th + 1) * 512],
                            kc == 0, kc == 7, [wb, self.H[kc]], [bk])
                evac(jj, th, bk)

    def proj_tm(self, wb, wv, ncol, evac, tile_tokens=128):
        nt = T // tile_tokens
        for ti in range(nt):
            bk = self.fw.bank()
            for kc in range(8):
                self.mm(bk[0:tile_tokens, 0:ncol], self.H[kc][:, ti * tile_tokens:(ti + 1) * tile_tokens], wv[:, kc, 0:ncol],
                        kc == 0, kc == 7, [wb, self.H[kc]], [bk])
            evac(ti, bk)

    def ctx_attn(self, QT, KT, ksel, Vt, vsel, mix0, PTp, RDp, esink=None):
        for s in range(4):
            t0 = s * 256
            for h in range(4):
                j, hp = h // 2, h % 2
                P = slice(hp * 64, hp * 64 + 64)
                PT = PTp.next()
                for kc in range(2):
                    bk = self.fw.bank()
                    self.mm(bk[:, 0:256], KT[P, ksel(h), t0 + kc * 128:t0 + (kc + 1) * 128], QT[P, j, t0:t0 + 256], True, True, [KT, QT], [bk])
                    self.act(PT[:, kc, :], bk[:, 0:256], AF.Exp, [bk], [PT], scale=0.125)
                bo = self.fw.bank()
                for kc in range(2):
                    self.mm(bo[:, 0:256], Vt[:, s * 2 + kc, vsel(h), :], PT[:, kc, :], kc == 0, kc == 1, [Vt, PT], [bo])
                rd = RDp.next()
                self.cp("dve", rd[0:64, :], bo[64:128, 0:256], [bo], [rd])
                if esink is not None:
                    self.ts("dve", rd[0:64, :], rd[0:64, :], esink[0:64, h:h + 1], None, ALU.add, None, [rd, esink], [rd])
                self.fw.op("dve", lambda e: e.reciprocal(rd[0:64, :], rd[0:64, :]), [rd], [rd])
                self.tt("dve", self.MIX[mix0 + j][P, t0:t0 + 256], bo[0:64, 0:256], rd[0:64, :], ALU.mult, [bo, rd], [self.MIX[mix0 + j]])

    def mixer_B(self):
        fw, l, G = self.fw, self.l, self.G
        ar = self.arena
        ar.reset()
        QT = ar.get("QT", [128, 2, T], BF16)
        KT = ar.get("KT", [128, 2, T], BF16)
        if G == "P":
            KTf = ar.get("KTf", [128, 2, T], F32)
            Vf = ar.get("Vf", [128, NT, 256], F32)
            Vt = ar.get("Vt", [128, NT, 4, 128], BF16)
            PTp = ar.pool("PT", 2, [128, 2, 256], BF16)
            RDp = ar.pool("RD", 2, [128, 256], F32)
            fw.op("pool", lambda e: e.memset(Vt[:, :, :, 64:128], 1.0), [], [Vt])
            wb, wv = self.win_block(l, 1152, 256)
            self.proj_fm(wb, wv, 2, lambda jj, th, bk: self.cp(self.ev(), QT[:, jj, th * 512:(th + 1) * 512], bk[:, :], [bk], [QT]))
            wb, wv = self.win_block(l, 1408, 256)

            def evk(jj, th, bk):
                self.cp(self.ev(), KTf[:, jj, th * 512:(th + 1) * 512], bk[:, :], [bk], [KTf])
                self.cp("pool", KT[:, jj, th * 512:(th + 1) * 512], KTf[:, jj, th * 512:(th + 1) * 512], [KTf], [KT])
            self.proj_fm(wb, wv, 2, evk)
            fw.dma("sp", self.natk[l].rearrange("j p t -> p j t"), KTf[:, :, :], reads=[KTf], is_output=True)
            wb, wv = self.win_block(l, 1664, 256)

            def evv(ti, bk):
                self.cp(self.ev(), Vf[:, ti, :], bk[:, 0:256], [bk], [Vf])
                self.cp("pool", Vt[:, ti, :, 0:64], Vf[:, ti, :].rearrange("p (h d) -> p h d", h=4), [Vf], [Vt])
            self.proj_tm(wb, wv, 256, evv)
            fw.dma("sp", self.natv[l].rearrange("n p f -> p n f"), Vf[:, :, :], reads=[Vf], is_output=True)
            self.ctx_attn(QT, KT, lambda h: h // 2, Vt, lambda h: h, 2, PTp, RDp)
        else:
            self.nat_latent(QT, KT)

    def mixer_D(self):
        fw, l, G = self.fw, self.l, self.G
        ar = self.arena
        ar.reset()
        QT = ar.get("QT", [128, 2, T], BF16)
        KT = ar.get("KTd", [128, 2, T], BF16)
        esink = ar.get("esink", [128, 4], F32)
        self.act(esink[:, :], self.col(l, "sink", 0, 4), AF.Exp, [self.cols], [esink])
        if G == "P":
            KTf = ar.get("KTf", [128, T], F32)
            Vf = ar.get("Vf", [128, NT, 128], F32)
            Vt = ar.get("Vt", [128, NT, 2, 128], BF16)
            PTp = ar.pool("PT", 2, [128, 2, 256], BF16)
            RDp = ar.pool("RD", 2, [128, 256], F32)
            fw.op("pool", lambda e: e.memset(Vt[:, :, :, 64:128], 1.0), [], [Vt])
            wb, wv = self.win_block(l, 3200, 256)
            self.proj_fm(wb, wv, 2, lambda jj, th, bk: self.cp(self.ev(), QT[:, jj, th * 512:(th + 1) * 512], bk[:, :], [bk], [QT]))
            wb, wv = self.win_block(l, 3456, 256)

            def evk(jj, th, bk):
                sl = slice(th * 512, (th + 1) * 512)
                self.cp(self.ev(), KTf[:, sl], bk[:, :], [bk], [KTf])
                for gg in range(2):
                    for hp in range(2):
                        self.cp(self.ev(), KT[hp * 64:hp * 64 + 64, gg, sl], KTf[gg * 64:gg * 64 + 64, sl], [KTf], [KT])
            self.proj_fm(wb, wv, 1, evk)
            fw.dma("sp", self.swak[l], KTf[:, :], reads=[KTf], is_output=True)

            def evv(ti, bk):
                self.cp(self.ev(), Vf[:, ti, :], bk[:, 0:128], [bk], [Vf])
                self.cp("pool", Vt[:, ti, :, 0:64], Vf[:, ti, :].rearrange("p (h d) -> p h d", h=2), [Vf], [Vt])
            self.proj_tm(wb, wv[:, :, 128:256], 128, evv)
            fw.dma("sp", self.swav[l].rearrange("n p f -> p n f"), Vf[:, :, :], reads=[Vf], is_output=True)
            self.ctx_attn(QT, KT, lambda h: h // 2, Vt, lambda h: h // 2, 6, PTp, RDp, esink=esink)
        else:
            self.swa_latent(QT, KT, esink)

    def lerp(self, d, out_ap, src_buf, src_ap, mu_ap, DIFF, out_buf):
        ns = len(self.seqs)
        L = T // ns
        s3 = src_ap.rearrange("p (s t) -> p s t", s=ns)
        d3 = DIFF[:, :].rearrange("p (s t) -> p s t", s=ns)
        e = self.el()
        if d == 0:
            self.tt(e, d3[:, :, 1:L], s3[:, :, 0:L - 1], s3[:, :, 1:L], ALU.subtract, [src_buf], [DIFF])
            self.ts(e, d3[:, :, 0:1], s3[:, :, 0:1], -1.0, None, ALU.mult, None, [src_buf], [DIFF])
        else:
            self.tt(e, d3[:, :, 0:L - 1], s3[:, :, 1:L], s3[:, :, 0:L - 1], ALU.subtract, [src_buf], [DIFF])
            self.ts(e, d3[:, :, L - 1:L], s3[:, :, L - 1:L], -1.0, None, ALU.mult, None, [src_buf], [DIFF])
        self.stt(e, out_ap, DIFF[:, :], mu_ap, src_ap, ALU.mult, ALU.add, [DIFF, src_buf, self.cols], [out_buf])

    def tile_order(self, d):
        out = []
        for si, (t0, n) in enumerate(self.seqs):
            tl = list(range(t0, t0 + n))
            if d == 1:
                tl = tl[::-1]
            for i, ti in enumerate(tl):
                out.append((si, ti, i == n - 1))
        return out

    def scan_tile(self, d, gs_ap, src_ap, reads, wbuf, C=128):
        rs = self.cst("reset", 0, C)
        if d == 0:
            self.fw.op("dve", lambda e: e.tensor_tensor_scan(gs_ap, rs, src_ap, 0.0, ALU.mult, ALU.add), reads + [self.cstb], [wbuf])
        else:
            self.fw.op("dve", lambda e: e.tensor_tensor_scan(gs_ap[:, ::-1], rs, src_ap[:, ::-1], 0.0, ALU.mult, ALU.add), reads + [self.cstb], [wbuf])

    def lin_head(self, d, j, hp, t0, first_dir, W, dplr, C=128):
        fw = self.fw
        P = slice(hp * 64, hp * 64 + 64)
        h = 2 * j + hp
        RK, KB, TOK, YACC = W["RK"], W["KB"], W["TOK"], W["YACC"]
        hst = W["hst"]
        HST = W["HST"]
        rt = RK[P, j, 1, 0:C]
        kb = KB[P, j, 0:C]
        Vh = TOK[0:C, 0, h * 64:(h + 1) * 64]
        Kbh = TOK[0:C, 1, h * 64:(h + 1) * 64]
        G4 = W["G4p"].next()
        g4c = lambda a, b: self.cst("g4_%d" % d, a, b)
        if dplr:
            PB = W["PB"]
            kt = RK[P, j, 0, :]
            rk2 = RK[P, j, :, :]
            pb = PB[P, j, :]
            Pbh = TOK[:, 2, h * 64:(h + 1) * 64]
            bk = fw.bank()
            self.mm(bk[:, 0:256], pb, rk2, True, True, [PB, RK], [bk])
            self.mm(bk[:, 256:512], kb, rk2, True, True, [KB, RK], [bk])
            self.tt("dve", G4[:, :], bk[:, :], g4c(0, 512), ALU.mult, [bk, self.cstb], [G4])
            bk2 = fw.bank()
            self.mm(bk2[:, 0:128], kt, pb, True, True, [RK, PB], [bk2])
            Bm = W["Bp"].next()
            self.tt("dve", Bm[:, :], bk2[:, 0:128], self.cst("mb_%d" % d), ALU.mult, [bk2, self.cstb], [Bm])
            CN = W["CNp"].next()
            self.tt("pool", CN[:, 128:256], G4[:, 0:128], self.cst("ident"), ALU.add, [G4, self.cstb], [CN])
            bk = fw.bank()
            self.mm(bk[:, 0:128], Bm[:, :], G4[:, 0:128], True, True, [Bm, G4], [bk])
            self.cp("act", CN[:, 0:128], bk[:, 0:128], [bk], [CN])
            Cprev_buf, Cprev = G4, G4[:, 0:128]
            Nfin = None
            for lev in range(1, 7):
                bk = fw.bank()
                self.mm(bk[:, 0:128], Cprev, Bm[:, :], True, True, [Cprev_buf, Bm], [bk])
                Bn = W["Bp"].next()
                self.cp("act", Bn[:, :], bk[:, 0:128], [bk], [Bn])
                Bm = Bn
                bk = fw.bank()
                if lev < 6:
                    self.mm(bk[:, 0:256], Bm[:, :], CN[:, 0:256], True, True, [Bm, CN], [bk])
                    CN2 = W["CNp"].next()
                    self.cp("act", CN2[:, 0:128], bk[:, 0:128], [bk], [CN2])
                    self.tt("dve", CN2[:, 128:256], bk[:, 128:256], CN[:, 128:256], ALU.add, [bk, CN], [CN2])
                    Cprev_buf, Cprev = CN, CN[:, 0:128]
                    CN = CN2
                else:
                    self.mm(bk[:, 0:128], Bm[:, :], CN[:, 128:256], True, True, [Bm, CN], [bk])
                    Nfin = W["Bp"].next()
                    self.tt("dve", Nfin[:, :], bk[:, 0:128], CN[:, 128:256], ALU.add, [bk, CN], [Nfin])
            mqk = G4[:, 384:512]
        else:
            bk = fw.bank()
            self.mm(bk[0:C, 0:C], kb, rt, True, True, [KB, RK], [bk])
            self.tt("dve", G4[0:C, 384:384 + C], bk[0:C, 0:C], self.cst("g4_%d" % d, 384, 384 + C, rows=C), ALU.mult, [bk, self.cstb], [G4])
            mqk = G4[0:C, 384:384 + C]
        Hp = W["Hpp"].next()
        Hb = W["Hbp"].next()
        self.ts("dve", Hp[P, :], hst, W["expc"][P, j:j + 1], None, ALU.mult, None, [HST, W["expcb"]], [Hp])
        self.cp("act", Hb[P, :], Hp[P, :], [Hp], [Hb])
        if dplr:
            bx = fw.bank()
            self.mm(bx[:, 0:64], G4[:, 256:384], Vh, True, False, [G4, TOK], [bx])
            self.mm(bx[:, 0:64], kt, Hb[P, :], False, True, [RK, Hb], [bx])
            Xs = W["Xp"].next()
            self.cp("act", Xs[:, :], bx[:, 0:64], [bx], [Xs])
            bu = fw.bank()
            self.mm(bu[:, 0:64], Nfin[:, :], Xs[:, :], True, True, [Nfin, Xs], [bu])
            Un = W["Xp"].next()
            self.ts("dve", Un[:, :], bu[:, 0:64], -1.0, None, ALU.mult, None, [bu], [Un])
        by = fw.bank()
        self.mm(by[P, 0:C], Hb[P, :], rt, True, False, [Hb, RK], [by])
        self.mm(by[P, 0:C], Vh, mqk, False, not dplr, [TOK, G4], [by])
        if dplr:
            self.mm(by[P, 0:128], Un[:, :], G4[:, 128:256], False, True, [Un, G4], [by])
        ydst = YACC[P, j, t0:t0 + C]
        if first_dir:
            self.cp("act", ydst, by[P, 0:C], [by], [YACC])
        else:
            self.tt("dve", ydst, by[P, 0:C], ydst, ALU.add, [by, YACC], [YACC])
        bz = fw.bank()
        self.mm(bz[P, 0:64], Kbh, Vh, True, not dplr, [TOK], [bz])
        if dplr:
            self.mm(bz[P, 0:64], Pbh, Un[:, :], False, True, [TOK, Un], [bz])
        self.tt("dve", Hp[P, :], bz[P, 0:64], Hp[P, :], ALU.add, [bz, Hp], [Hp])
        self.ts("dve", hst, Hp[P, :], W["eplast"][P, j:j + 1], None, ALU.mult, None, [Hp, W["eplastb"]], [HST])

    def lin_common_bufs(self, ar, dplr):
        W = {}
        W["G4p"] = ar.pool("G4", 3, [128, 512], BF16)
        W["Hpp"] = ar.pool("Hp", 3, [128, 64], F32)
        W["Hbp"] = ar.pool("Hb", 3, [128, 64], BF16)
        if dplr:
            W["Bp"] = ar.pool("Bm", 5, [128, 128], BF16)
            W["CNp"] = ar.pool("CN", 4, [128, 256], BF16)
            W["Xp"] = ar.pool("Xs", 4, [128, 64], BF16)
        return W

    def lin_state_io(self, HST, src_dram, dst_dram, load):
        l = self.l
        if load:
            self.fw.dma("sp", HST[:, :, 0, :, :], src_dram[l].rearrange("d p j v -> p d j v"), writes=[HST])
        else:
            self.fw.dma("sp", dst_dram[l].rearrange("d s p j v -> p d s j v"), HST[:, :, :, :, :], reads=[HST], is_output=True)

    def mixer_A(self):
        fw, l, G = self.fw, self.l, self.G
        ar = self.arena
        ar.reset()
        ns = len(self.seqs)
        RAW = ar.get("RAW", [128, 6, T], BF16)
        WA = ar.get("WA", [128, 2, T], F32)
        SG = ar.get("SG", [128, T], BF16)
        YACC = ar.get("YACC", [128, 2, T], F32)
        BACC = ar.get("BACC", [128, 2, T], BF16)
        HST = ar.get("HST", [128, 2, ns, 2, 64], F32)
        L2 = ar.get("L2", [128, 2, 256], F32)
        G2f = ar.get("G2f", [128, 256], F32)
        G2 = ar.get("G2", [128, 256], BF16)
        LR = ar.get("LR", [128, 6, T], BF16)
        LW = ar.get("LW", [128, T], F32)
        DIFF = ar.get("DIFF", [128, T], F32)
        cc = ar.get("cc", [128, 8], F32)
        W = self.lin_common_bufs(ar, True)
        W["YACC"], W["HST"] = YACC, HST
        wk = lambda nm, shape, dt=F32: ar.pool(nm, 2, shape, dt)
        THp, SIGp, Ap = wk("th", [128, 128]), wk("sig", [128, 2, 128]), wk("a", [128, 2, 128])
        T1p, T2p, T3p = wk("t1", [128, 128]), wk("t2", [128, 128]), wk("t3", [128, 128])
        wk1 = lambda nm, shape, dt=F32: ar.pool(nm, 1, shape, dt)
        KAPp, KMp, PPp = wk1("kap", [128, 2, 128]), wk1("km", [128, 2, 128]), wk1("pp", [128, 2, 128])
        SQp = wk("sqb", [128, 128], BF16)
        GSp, EPp, EMp, EPPp = wk1("gs", [128, 2, 128]), wk1("ep", [128, 2, 128]), wk1("em", [128, 2, 128]), wk("epp", [128, 2, 128])
        EXCp = wk("exc", [128, 2])
        RKp, KBp, PBp = wk("RK", [128, 2, 2, 128], BF16), wk("KB", [128, 2, 128], BF16), wk("PB", [128, 2, 128], BF16)
        TOKp = wk("TOK", [128, 3, 256], BF16)

        fw.dma("sp", L2[:, :, :], self.lora2[l], writes=[L2])
        fw.dma("sp", G2f[:, :], self.rw_g2[l], writes=[G2f])
        self.cp("pool", G2[:, :], G2f[:, :], [G2f], [G2])
        self.ts("dve", cc[:, 0:2], self.col(l, "ka", 0, 2), -1.0, 1.0, ALU.mult, ALU.add, [self.cols], [cc])
        self.ts("dve", cc[:, 2:4], self.col(l, "lnx_w", 0, 2), 8.0, None, ALU.mult, None, [self.cols], [cc])
        if G == "P":
            fw.op("pool", lambda e: e.memset(HST[:, :, :, :, :], 0.0), [], [HST])
        else:
            self.lin_state_io(HST, self.srw, None, True)

        def evA(c0):
            def f(jj, th, bk):
                ci = c0 + jj
                sl = slice(th * 512, (th + 1) * 512)
                if ci < 6:
                    self.cp(self.ev(), RAW[:, ci, sl], bk[:, :], [bk], [RAW])
                elif ci == 6:
                    self.act(SG[:, sl], bk[:, :], AF.Sigmoid, [bk], [SG])
                else:
                    self.cp(self.ev(), WA[:, ci - 7, sl], bk[:, :], [bk], [WA])
            return f
        for b0 in range(0, 9, 2):
            nch = min(2, 9 - b0)
            wb, wv = self.win_block(l, b0 * 128, nch * 128)
            self.proj_fm(wb, wv, nch, evA(b0))

        for d in range(2):
            for a in range(3):
                for jj in range(2):
                    ci = a * 2 + jj
                    self.lerp(d, LR[:, ci, :], RAW, RAW[:, ci, :], self.col(l, "mu_rkv%d" % d, ci, 1), DIFF, LR)
            self.lerp(d, LW[:, :], WA, WA[:, d, :], self.col(l, "mu_lora%d" % d, 0, 1), DIFF, LW)
            for (si, ti, last) in self.tile_order(d):
                t0 = ti * 128
                ts_ = slice(t0, t0 + 128)
                th = THp.next()
                self.act(th[0:64, :], LW[0:64, ts_], AF.Tanh, [LW], [th])
                sig, av = SIGp.next(), Ap.next()
                for jj in range(2):
                    bk = fw.bank()
                    self.mm(bk[:, 0:128], L2[0:64, d, jj * 128:(jj + 1) * 128], th[0:64, :], True, True, [L2, th], [bk])
                    self.act(sig[:, jj, :], bk[:, 0:128], AF.Sigmoid, [bk, self.cols], [sig], bias=self.col(l, "w0_%d" % d, jj, 1))
                    bk = fw.bank()
                    self.mm(bk[:, 0:128], L2[64:128, d, jj * 128:(jj + 1) * 128], LW[64:128, ts_], True, True, [L2, LW], [bk])
                    self.act(av[:, jj, :], bk[:, 0:128], AF.Sigmoid, [bk, self.cols], [av], bias=self.col(l, "a0_%d" % d, jj, 1))
                kap, km, pp = KAPp.next(), KMp.next(), PPp.next()
                gs, ep, em, epp, exc = GSp.next(), EPp.next(), EMp.next(), EPPp.next(), EXCp.next()
                RK, KB, PB, TOK = RKp.next(), KBp.next(), PBp.next(), TOKp.next()
                mid = 63 if d == 0 else 64
                lastc = 127 if d == 0 else 0
                for jj in range(2):
                    rl, kl, vl = LR[:, jj, ts_], LR[:, 2 + jj, ts_], LR[:, 4 + jj, ts_]
                    e = self.el()
                    t1, t2 = T1p.next(), T2p.next()
                    self.ts(e, t1[:, :], kl, self.col(l, "kk", jj, 1), None, ALU.mult, None, [LR, self.cols], [t1])
                    sqb = SQp.next()
                    self.tt(e, sqb[:, :], t1[:, :], t1[:, :], ALU.mult, [t1], [sqb])
                    bk = fw.bank()
                    self.mm(bk[:, 0:128], self.blk_bf[:, :], sqb[:, :], True, True, [self.blk_bf, sqb], [bk])
                    self.rsqrt(t2[:, :], bk[:, 0:128], 1e-24, [bk], t2)
                    self.tt(e, kap[:, jj, :], t1[:, :], t2[:, :], ALU.mult, [t1, t2], [kap])
                    t3 = T3p.next()
                    self.ts(e, t3[:, :], av[:, jj, :], self.col(l, "ka", jj, 1), cc[:, jj:jj + 1], ALU.mult, ALU.add, [av, self.cols, cc], [t3])
                    self.tt(e, km[:, jj, :], kl, t3[:, :], ALU.mult, [LR, t3], [km])
                    self.tt(e, pp[:, jj, :], kap[:, jj, :], av[:, jj, :], ALU.mult, [kap, av], [pp])
                    t1b = T1p.next()
                    self.stt(e, t1b[:, :], rl, self.col(l, "rk", jj, 1), km[:, jj, :], ALU.mult, ALU.mult, [LR, self.cols, km], [t1b])
                    sqb2 = SQp.next()
                    self.cp(e, sqb2[:, :], t1b[:, :], [t1b], [sqb2])
                    bk = fw.bank()
                    self.mm(bk[:, 0:128], self.blk_bf[:, :], sqb2[:, :], True, True, [self.blk_bf, sqb2], [bk])
                    if d == 0:
                        self.tt("dve", BACC[:, jj, ts_], bk[:, 0:128], vl, ALU.mult, [bk, LR], [BACC])
                    else:
                        t2b = T2p.next()
                        self.tt("dve", t2b[:, :], bk[:, 0:128], vl, ALU.mult, [bk, LR], [t2b])
                        self.tt(e, BACC[:, jj, ts_], BACC[:, jj, ts_], t2b[:, :], ALU.add, [t2b, BACC], [BACC])
                    self.scan_tile(d, gs[:, jj, :], sig[:, jj, :], [sig], gs)
                self.cp("dve", exc[:, 0:2], gs[:, :, mid], [gs], [exc])
                gc = EPPp.next()
                for jj in range(2):
                    self.ts("dve", gc[:, jj, :], gs[:, jj, :], exc[:, jj:jj + 1], None, ALU.subtract, None, [gs, exc], [gc])
                self.act(ep[:, :, :], gc[:, :, :], AF.Exp, [gc], [ep], scale=-DECAY)
                self.act(em[:, :, :], gc[:, :, :], AF.Exp, [gc], [em], scale=DECAY)
                self.tt(self.el(), gs[:, :, :], gc[:, :, :], sig[:, :, :], ALU.subtract, [gc, sig], [gs])
                self.act(epp[:, :, :], gs[:, :, :], AF.Exp, [gs], [epp], scale=-DECAY)
                self.act(exc[:, 0:2], exc[:, 0:2], AF.Exp, [exc], [exc], scale=-DECAY)
                for jj in range(2):
                    e = self.el()
                    self.tt(e, RK[:, jj, 0, :], kap[:, jj, :], epp[:, jj, :], ALU.mult, [kap, epp], [RK])
                    self.tt(e, RK[:, jj, 1, :], LR[:, jj, ts_], ep[:, jj, :], ALU.mult, [LR, ep], [RK])
                    self.tt(e, KB[:, jj, :], km[:, jj, :], em[:, jj, :], ALU.mult, [km, em], [KB])
                    self.tt(e, PB[:, jj, :], pp[:, jj, :], em[:, jj, :], ALU.mult, [pp, em], [PB])
                bt = fw.bank()
                btv = bt.t.bitcast(BF16)
                for jj in range(2):
                    self.tp(btv[:, (0 * 2 + jj) * 128:(0 * 2 + jj + 1) * 128], LR[:, 4 + jj, ts_], self.ident_bf[:, :], [LR, self.ident_bf], [bt])
                    self.tp(btv[:, (1 * 2 + jj) * 128:(1 * 2 + jj + 1) * 128], KB[:, jj, :], self.ident_bf[:, :], [KB, self.ident_bf], [bt])
                    self.tp(btv[:, (2 * 2 + jj) * 128:(2 * 2 + jj + 1) * 128], PB[:, jj, :], self.ident_bf[:, :], [PB, self.ident_bf], [bt])
                self.cp("act", TOK[:, :, :], btv[:, 0:768].rearrange("p (a b) -> p a b", a=3), [bt], [TOK])
                W.update(RK=RK, KB=KB, PB=PB, TOK=TOK, expc=exc, expcb=exc, eplast=ep[:, :, lastc], eplastb=ep)
                for jj in range(2):
                    for hp in range(2):
                        W["hst"] = HST[hp * 64:hp * 64 + 64, d, si, jj, :]
                        self.lin_head(d, jj, hp, t0, d == 0, W, True)
        if G == "P":
            self.lin_state_io(HST, None, self.rwst, False)
        self.dump("yacc_%s%d" % (G, l), YACC, YACC[:, :, :], [128, 2, T])
        self.dump("bacc_%s%d" % (G, l), BACC, BACC[:, :, :], [128, 2, T])
        blk = self.cst("blk")
        for jj in range(2):
            for th in range(2):
                sl = slice(th * 512, (th + 1) * 512)
                bk = fw.bank()
                self.mm(bk[:, :], blk, YACC[:, jj, sl], True, True, [self.cstb, YACC], [bk])
                yc = DIFF
                self.stt("dve", yc[:, sl], bk[:, :], -1.0 / 64, YACC[:, jj, sl], ALU.mult, ALU.add, [bk, YACC], [yc])
                self.tt("pool", LW[:, sl], yc[:, sl], yc[:, sl], ALU.mult, [yc], [LW])
                bk2 = fw.bank()
                self.mm(bk2[:, :], blk, LW[:, sl], True, True, [self.cstb, LW], [bk2])
                self.rsqrt(LW[:, sl], bk2[:, :], 64 * LN_EPS, [bk2], LW)
                self.tt("pool", yc[:, sl], yc[:, sl], LW[:, sl], ALU.mult, [yc, LW], [yc])
                self.ts("dve", yc[:, sl], yc[:, sl], cc[:, 2 + jj:3 + jj], self.col(l, "lnx_b", jj, 1), ALU.mult, ALU.add, [yc, cc, self.cols], [yc])
                self.tt("pool", yc[:, sl], yc[:, sl], BACC[:, jj, sl], ALU.add, [yc, BACC], [yc])
                bk3 = fw.bank()
                self.mm(bk3[:, :], G2[:, jj * 128:(jj + 1) * 128], SG[:, sl], True, True, [G2, SG], [bk3])
                self.tt("dve", self.MIX[jj][:, sl], yc[:, sl], bk3[:, :], ALU.mult, [yc, bk3], [self.MIX[jj]])

    def mixer_C(self):
        fw, l, G = self.fw, self.l, self.G
        ar = self.arena
        ar.reset()
        ns = len(self.seqs)
        Q = ar.get("Q", [128, 2, T], BF16)
        V = ar.get("V", [128, 2, T], BF16)
        GATE = ar.get("GATE", [128, 2, T], BF16)
        FR = ar.get("FR", [128, 2, 2, T], F32)
        YACC = ar.get("YACC", [128, 2, T], F32)
        HST = ar.get("HST", [128, 2, ns, 2, 64], F32)
        LF = ar.get("LF", [128, 2, T], F32)
        KK = ar.get("KK", [128, 2, T], BF16)
        W = self.lin_common_bufs(ar, False)
        W["YACC"], W["HST"] = YACC, HST
        hn8 = ar.get("hn8", [128, 2], F32)
        self.ts("dve", hn8[:, :], self.col(l, "hg_norm", 0, 2), 8.0, None, ALU.mult, None, [self.cols], [hn8])
        wk = lambda nm, shape, dt=F32: ar.pool(nm, 2, shape, dt)
        GSp, GCp, EPp, EMp, EXCp = wk("gs", [128, 2, 128]), wk("gc", [128, 2, 128]), wk("ep", [128, 2, 128]), wk("em", [128, 2, 128]), wk("exc", [128, 2])
        RKp, KBp, TOKp = wk("RK", [128, 2, 2, 128], BF16), wk("KB", [128, 2, 128], BF16), wk("TOK", [128, 2, 256], BF16)
        if G == "P":
            fw.op("pool", lambda e: e.memset(HST[:, :, :, :, :], 0.0), [], [HST])
        else:
            self.lin_state_io(HST, self.shg, None, True)

        def evC(c0):
            def f(jj, th, bk):
                ci = c0 + jj
                sl = slice(th * 512, (th + 1) * 512)
                if ci < 2:
                    self.act(Q[:, ci, sl], bk[:, :], AF.Silu, [bk], [Q])
                elif ci < 4:
                    self.cp(self.ev(), V[:, ci - 2, sl], bk[:, :], [bk], [V])
                elif ci < 6:
                    self.act(GATE[:, ci - 4, sl], bk[:, :], AF.Sigmoid, [bk], [GATE])
                else:
                    self.cp(self.ev(), FR[:, (ci - 6) // 2, (ci - 6) % 2, sl], bk[:, :], [bk], [FR])
            return f
        for b0 in range(0, 10, 2):
            wb, wv = self.win_block(l, 1920 + b0 * 128, 256)
            self.proj_fm(wb, wv, 2, evC(b0))

        for d in range(2):
            for jj in range(2):
                self.act(LF[:, jj, :], FR[:, d, jj, :], AF.Sigmoid, [FR], [LF])
                self.ts("dve", LF[:, jj, :], LF[:, jj, :], self.oml[:, d, l, jj:jj + 1], self.lb[:, d, l, jj:jj + 1], ALU.mult, ALU.add, [LF, self.oml, self.lb], [LF])
                self.ts("pool", KK[:, jj, :], LF[:, jj, :], -1.0, 1.0, ALU.mult, ALU.add, [LF], [KK])
                self.act(LF[:, jj, :], LF[:, jj, :], AF.Ln, [LF], [LF])
            C = 32
            for (si, ti, last) in self.tile_order(d):
              for sub in (range(128 // C) if d == 0 else range(128 // C - 1, -1, -1)):
                t0 = ti * 128 + sub * C
                ts_ = slice(t0, t0 + C)
                gs, gc, ep, em, exc = GSp.next(), GCp.next(), EPp.next(), EMp.next(), EXCp.next()
                RK, KB, TOK = RKp.next(), KBp.next(), TOKp.next()
                mid = C // 2 - 1 if d == 0 else C // 2
                lastc = C - 1 if d == 0 else 0
                for jj in range(2):
                    self.scan_tile(d, gs[:, jj, 0:C], LF[:, jj, ts_], [LF], gs, C)
                self.cp("dve", exc[:, 0:2], gs[:, :, mid], [gs], [exc])
                for jj in range(2):
                    self.ts("dve", gc[:, jj, 0:C], gs[:, jj, 0:C], exc[:, jj:jj + 1], None, ALU.subtract, None, [gs, exc], [gc])
                self.act(ep[:, :, 0:C], gc[:, :, 0:C], AF.Exp, [gc], [ep])
                self.act(em[:, :, 0:C], gc[:, :, 0:C], AF.Exp, [gc], [em], scale=-1.0)
                self.act(exc[:, 0:2], exc[:, 0:2], AF.Exp, [exc], [exc])
                for jj in range(2):
                    e = self.el()
                    self.tt(e, RK[:, jj, 1, 0:C], Q[:, jj, ts_], ep[:, jj, 0:C], ALU.mult, [Q, ep], [RK])
                    self.tt(e, KB[:, jj, 0:C], KK[:, jj, ts_], em[:, jj, 0:C], ALU.mult, [KK, em], [KB])
                bt = fw.bank()
                btv = bt.t.bitcast(BF16)
                for jj in range(2):
                    self.tp(btv[0:C, jj * 128:(jj + 1) * 128], V[:, jj, ts_], self.ident_bf[:, :], [V, self.ident_bf], [bt])
                    self.tp(btv[0:C, (2 + jj) * 128:(3 + jj) * 128], KB[:, jj, 0:C], self.ident_bf[:, :], [KB, self.ident_bf], [bt])
                self.cp("act", TOK[0:C, :, :], btv[0:C, 0:512].rearrange("p (a b) -> p a b", a=2), [bt], [TOK])
                W.update(RK=RK, KB=KB, TOK=TOK, expc=exc, expcb=exc, eplast=ep[:, :, lastc], eplastb=ep)
                for jj in range(2):
                    for hp in range(2):
                        W["hst"] = HST[hp * 64:hp * 64 + 64, d, si, jj, :]
                        self.lin_head(d, jj, hp, t0, d == 0, W, False, C)
        if G == "P":
            self.lin_state_io(HST, None, self.hgst, False)
        self.dump("yaccC_%s%d" % (G, l), YACC, YACC[:, :, :], [128, 2, T])
        self.dump("lfC_%s%d" % (G, l), LF, LF[:, :, :], [128, 2, T])
        blk = self.cst("blk")
        for jj in range(2):
            for th in range(2):
                sl = slice(th * 512, (th + 1) * 512)
                self.tt("pool", LF[:, 0, sl], YACC[:, jj, sl], YACC[:, jj, sl], ALU.mult, [YACC], [LF])
                bk = fw.bank()
                self.mm(bk[:, :], blk, LF[:, 0, sl], True, True, [self.cstb, LF], [bk])
                self.rsqrt(LF[:, 0, sl], bk[:, :], 64 * EPS, [bk], LF)
                self.tt("pool", LF[:, 0, sl], LF[:, 0, sl], YACC[:, jj, sl], ALU.mult, [LF, YACC], [LF])
                self.stt("dve", self.MIX[4 + jj][:, sl], LF[:, 0, sl], hn8[:, jj:jj + 1], GATE[:, jj, sl], ALU.mult, ALU.mult, [LF, hn8, GATE], [self.MIX[4 + jj]])

    def out_and_ffn(self):
        fw, l, G = self.fw, self.l, self.G
        ar = self.arena
        ar.reset()
        mc = self.mc
        X, H, MIX = self.X, self.H, self.MIX
        O = [ar.get("O%d" % c, [128, T], F32) for c in range(8)]
        WBIG = ar.get("WBIG", [128, 8, 1024], BF16)
        HID = [ar.get("HID%d" % c, [128, T], BF16) for c in range(8)]
        RL = ar.get("RL", [128, T], BF16)

        def add_residual(gco):
            rstd = self.rstd_of(O, lambda c: O[c][:, :], self.sq, self.rstd)
            for c in range(8):
                e = self.el()
                self.stt(e, O[c][:, :], O[c][:, :], gco[:, c:c + 1], rstd[:, :], ALU.mult, ALU.mult, [O[c], rstd, self.modc], [O[c]])
                self.tt(e, X[c][:, :], X[c][:, :], O[c][:, :], ALU.add, [O[c], X[c]], [X[c]])

        for q in range(4):
            st_ap = self.w_out[l, q * 256:(q + 1) * 256, :].rearrange("(k p) n -> p k n", p=128)
            stb, sv = self.load_block(st_ap, 2, 1024, cast=False)
            self.cp(self.fw.pick("cast", ("pool", "dve", "pool")), WBIG[:, 2 * q:2 * q + 2, :], sv, [stb], [WBIG])
        for oc in range(8):
            for th in range(2):
                bk = fw.bank()
                for kc in range(8):
                    self.mm(bk[:, :], WBIG[:, kc, oc * 128:(oc + 1) * 128], MIX[kc][:, th * 512:(th + 1) * 512], kc == 0, kc == 7, [WBIG, MIX[kc]], [bk])
                self.cp(self.ev(), O[oc][:, th * 512:(th + 1) * 512], bk[:, :], [bk], [O[oc]])
        add_residual(mc["gco1"])
        rstd = self.rstd_of(X, lambda c: X[c][:, :], self.sq, self.rstd)
        for c in range(8):
            tmp = self.sq.next()
            self.stt(self.el(), tmp[:, :], X[c][:, :], mc["coefA2"][:, c:c + 1], rstd[:, :], ALU.mult, ALU.mult, [X[c], rstd, self.modc], [tmp])
            self.act(H[c][:, :], tmp[:, :], AF.Identity, [tmp, self.modc], [H[c]], bias=mc["shift2"][:, c:c + 1])
        for q in range(4):
            for fb in range(4):
                wb, wv = self.load_block(self.ffn_w1[l, :, (q * 8 + fb * 2) * 128:(q * 8 + fb * 2 + 2) * 128].rearrange("(k p) n -> p k n", p=128), 8, 256)

                def evh(jj, th, bk, fb=fb):
                    sl = slice(th * 512, (th + 1) * 512)
                    self.act(RL[:, sl], bk[:, :], AF.Relu, [bk], [RL])
                    self.tt(self.el(), HID[fb * 2 + jj][:, sl], RL[:, sl], RL[:, sl], ALU.mult, [RL], [HID[fb * 2 + jj]])
                self.proj_fm(wb, wv, 2, evh)
            for hb in range(4):
                st_ap = self.ffn_w2[l, (q * 8 + hb * 2) * 128:(q * 8 + hb * 2 + 2) * 128, :].rearrange("(k p) n -> p k n", p=128)
                stb, sv = self.load_block(st_ap, 2, 1024, cast=False)
                self.cp(self.fw.pick("cast", ("pool", "dve", "pool")), WBIG[:, 2 * hb:2 * hb + 2, :], sv, [stb], [WBIG])
            for oc in range(8):
                for th in range(2):
                    sl = slice(th * 512, (th + 1) * 512)
                    bk = fw.bank()
                    for kc in range(8):
                        self.mm(bk[:, :], WBIG[:, kc, oc * 128:(oc + 1) * 128], HID[kc][:, sl], kc == 0, kc == 7, [WBIG, HID[kc]], [bk])
                    if q == 0:
                        self.cp(self.ev(), O[oc][:, sl], bk[:, :], [bk], [O[oc]])
                    else:
                        self.tt("dve", O[oc][:, sl], bk[:, :], O[oc][:, sl], ALU.add, [bk, O[oc]], [O[oc]])
        add_residual(mc["gco2"])

    def nat_latent(self, QT, KT):
        fw, l = self.fw, self.l
        ar = self.arena
        Vr = ar.get("Vr", [128, 16, 4, 128], BF16)
        CKf = ar.get("CKf", [128, 2, 256], F32)
        CK = ar.get("CK", [128, 2, 256], BF16)
        CVf = ar.get("CVf", [128, 2, 256], F32)
        CV = ar.get("CV", [128, 2, 4, 128], BF16)
        ET = ar.get("ET", [128, 4, 15, 64], F32)
        PTp = ar.pool("PT", 2, [128, 8, 64], BF16)
        PCp = ar.pool("PC", 2, [128, 2, 64], BF16)
        RDp = ar.pool("RD", 2, [128, 64], F32)
        fw.op("pool", lambda e: e.memset(Vr[0:64, :, :, 64:128], 1.0), [], [Vr])
        fw.op("pool", lambda e: e.memset(CV[:, :, :, 64:128], 1.0), [], [CV])
        fw.dma("sp", CKf[:, :, :], self.cnk[l].rearrange("j p t -> p j t"), writes=[CKf])
        self.cp("pool", CK[:, :, :], CKf[:, :, :], [CKf], [CK])
        fw.dma("sp", CVf[:, :, :], self.cnv[l].rearrange("(c p) f -> p c f", p=128), writes=[CVf])
        self.cp("pool", CV[:, :, :, 0:64], CVf[:, :, :].rearrange("p c (h d) -> p c h d", h=4), [CVf], [CV])
        ETr = ar.get("ETr", [128, 4, 15, 64], F32)
        src = bass.AP(tensor=self.rpbpad.t, offset=l * 4 * 15 * 127, ap=[[1, 64], [15 * 127, 4], [127, 15], [1, 64]])
        fw.dma("sp", ETr[0:64, :, :, :], src, writes=[ETr])
        flip = self.cst("flip", rows=64)
        etr2 = ETr[0:64, :, :, :].rearrange("p h r q -> p (h r q)")
        et2 = ET[0:64, :, :, :].rearrange("p h r q -> p (h r q)")
        for cb in range(8):
            bk = fw.bank()
            self.mm(bk[0:64, 0:480], flip, etr2[:, cb * 480:(cb + 1) * 480], True, True, [self.cstb, ETr], [bk])
            self.act(et2[:, cb * 480:(cb + 1) * 480], bk[0:64, 0:480], AF.Exp, [bk], [ET])
        nm = self.cst("natmask", rows=64)
        for h in range(4):
            self.tt("dve", ET[0:64, h, :, :], ET[0:64, h, :, :], nm.unsqueeze(1).to_broadcast([64, 15, 64]), ALU.mult, [ET, self.cstb], [ET])
        wb, wv = self.win_block(l, 1152, 256)
        self.proj_fm(wb, wv, 2, lambda jj, th, bk: self.cp(self.ev(), QT[:, jj, th * 512:(th + 1) * 512], bk[:, :], [bk], [QT]))
        wb, wv = self.win_block(l, 1408, 256)
        self.proj_fm(wb, wv, 2, lambda jj, th, bk: self.cp(self.ev(), KT[:, jj, th * 512:(th + 1) * 512], bk[:, :], [bk], [KT]))
        wb, wv = self.win_block(l, 1664, 256)
        self.proj_tm(wb, wv, 256, lambda ti, bk: self.cp(self.ev(), Vr[0:64, ti, :, 0:64], bk[0:64, 0:256].rearrange("p (h d) -> p h d", h=4), [bk], [Vr]), tile_tokens=64)
        for r in range(16):
            rs = min(max(r - 4, 0), 8)
            roff0 = rs - r + 7
            qs = slice(r * 64, (r + 1) * 64)
            for h in range(4):
                j, hp = h // 2, h % 2
                P = slice(hp * 64, hp * 64 + 64)
                bl = fw.bank()
                for a in range(8):
                    kr = rs + a
                    self.mm(bl[0:64, a * 64:(a + 1) * 64], KT[P, j, kr * 64:(kr + 1) * 64], QT[P, j, qs], True, True, [KT, QT], [bl])
                PT = PTp.next()
                self.act(PT[0:64, :, :], bl[0:64, :].rearrange("p (a q) -> p a q", a=8), AF.Exp, [bl], [PT], scale=0.125)
                self.tt("dve", PT[0:64, :, :], PT[0:64, :, :], ET[0:64, h, roff0:roff0 + 8, :], ALU.mult, [PT, ET], [PT])
                bc = fw.bank()
                for c in range(2):
                    self.mm(bc[:, c * 64:(c + 1) * 64], CK[P, j, c * 128:(c + 1) * 128], QT[P, j, qs], True, True, [CK, QT], [bc])
                PC = PCp.next()
                self.act(PC[:, :, :], bc[:, 0:128].rearrange("p (c q) -> p c q", c=2), AF.Exp, [bc], [PC], scale=0.125)
                bo = fw.bank()
                for a in range(8):
                    self.mm(bo[:, 0:64], Vr[0:64, rs + a, h, :], PT[0:64, a, :], a == 0, False, [Vr, PT], [bo])
                for c in range(2):
                    self.mm(bo[:, 0:64], CV[:, c, h, :], PC[:, c, :], False, c == 1, [CV, PC], [bo])
                rd = RDp.next()
                self.fw.op("dve", lambda e: e.reciprocal(rd[0:64, :], bo[64:128, 0:64]), [bo], [rd])
                self.tt("dve", self.MIX[2 + j][P, qs], bo[0:64, 0:64], rd[0:64, :], ALU.mult, [bo, rd], [self.MIX[2 + j]])

    def swa_latent(self, QT, KT, esink):
        fw, l = self.fw, self.l
        ar = self.arena
        ROPE = ar.get("ROPE", [128, 2, T], F32)
        TQ = ar.get("TQ", [128, 2, T], F32)
        KTf = ar.get("KTf", [128, T], F32)
        Vt = ar.get("Vt", [128, NT, 2, 128], BF16)
        CKf = ar.get("CKf", [128, 2, 256], F32)
        CK = ar.get("CK", [128, 2, 256], BF16)
        CVf = ar.get("CVf", [128, 2, 128], F32)
        CV = ar.get("CV", [128, 2, 2, 128], BF16)
        TMPp = ar.pool("TMP", 2, [128, 512], F32)
        PTp = ar.pool("PT", 2, [128, 3, 128], BF16)
        PCp = ar.pool("PC", 2, [128, 2, 128], BF16)
        RDp = ar.pool("RD", 2, [128, 128], F32)
        fw.dma("sp", ROPE[:, :, :], self.rope[:, :, :], writes=[ROPE])
        fw.op("pool", lambda e: e.memset(Vt[:, :, :, 64:128], 1.0), [], [Vt])
        fw.op("pool", lambda e: e.memset(CV[:, :, :, 64:128], 1.0), [], [CV])
        fw.dma("sp", CKf[:, :, :], self.csk[l].rearrange("g p t -> p g t"), writes=[CKf])
        self.cp("pool", CK[:, :, :], CKf[:, :, :], [CKf], [CK])
        fw.dma("sp", CVf[:, :, :], self.csv[l].rearrange("(c p) f -> p c f", p=128), writes=[CVf])
        self.cp("pool", CV[:, :, :, 0:64], CVf[:, :, :].rearrange("p c (h d) -> p c h d", h=2), [CVf], [CV])
        wb, wv = self.win_block(l, 3200, 256)
        self.proj_fm(wb, wv, 2, lambda jj, th, bk: self.tt("dve", TQ[:, jj, th * 512:(th + 1) * 512], bk[:, :], ROPE[:, 0, th * 512:(th + 1) * 512], ALU.mult, [bk, ROPE], [TQ]))
        wb, wv = self.win_block(l, 0, 256, src=self.w_in_sw)

        def evq(jj, th, bk):
            sl = slice(th * 512, (th + 1) * 512)
            tmp = TMPp.next()
            self.tt("dve", tmp[:, :], bk[:, :], ROPE[:, 1, sl], ALU.mult, [bk, ROPE], [tmp])
            self.tt("pool", QT[:, jj, sl], tmp[:, :], TQ[:, jj, sl], ALU.add, [tmp, TQ], [QT])
        self.proj_fm(wb, wv, 2, evq)
        wb, wv = self.win_block(l, 3456, 256)
        self.proj_fm(wb, wv, 1, lambda jj, th, bk: self.tt("dve", KTf[:, th * 512:(th + 1) * 512], bk[:, :], ROPE[:, 0, th * 512:(th + 1) * 512], ALU.mult, [bk, ROPE], [KTf]))
        self.proj_tm(wb, wv[:, :, 128:256], 128, lambda ti, bk: self.cp(self.ev(), Vt[:, ti, :, 0:64], bk[:, 0:128].rearrange("p (h d) -> p h d", h=2), [bk], [Vt]))
        wb, wv = self.win_block(l, 256, 128, src=self.w_in_sw)

        def evk(jj, th, bk):
            sl = slice(th * 512, (th + 1) * 512)
            tmp = TMPp.next()
            self.tt("dve", tmp[:, :], bk[:, :], ROPE[:, 1, sl], ALU.mult, [bk, ROPE], [tmp])
            self.tt("pool", KTf[:, sl], tmp[:, :], KTf[:, sl], ALU.add, [tmp, KTf], [KTf])
            for gg in range(2):
                for hp in range(2):
                    self.cp(self.ev(), KT[hp * 64:hp * 64 + 64, gg, sl], KTf[gg * 64:gg * 64 + 64, sl], [KTf], [KT])
        self.proj_fm(wb, wv, 1, evk)
        for n in range(NT):
            qs = slice(n * 128, (n + 1) * 128)
            kbl = [kb for kb in (n - 1, n, n + 1) if 0 <= kb < NT]
            for h in range(4):
                j, hp = h // 2, h % 2
                g = j
                P = slice(hp * 64, hp * 64 + 64)
                bl = fw.bank()
                for i, kb in enumerate(kbl):
                    self.mm(bl[:, i * 128:(i + 1) * 128], KT[P, g, kb * 128:(kb + 1) * 128], QT[P, j, qs], True, True, [KT, QT], [bl])
                PT = PTp.next()
                nk = len(kbl)
                self.act(PT[:, 0:nk, :], bl[:, 0:nk * 128].rearrange("p (a q) -> p a q", a=nk), AF.Exp, [bl], [PT], scale=0.125)
                for i, kb in enumerate(kbl):
                    if kb == n - 1:
                        self.tt("dve", PT[:, i, :], PT[:, i, :], self.cst("mprev"), ALU.mult, [PT, self.cstb], [PT])
                    elif kb == n + 1:
                        self.tt("pool", PT[:, i, :], PT[:, i, :], self.cst("mnext"), ALU.mult, [PT, self.cstb], [PT])
                bc = fw.bank()
                for c in range(2):
                    self.mm(bc[:, c * 128:(c + 1) * 128], CK[P, g, c * 128:(c + 1) * 128], QT[P, j, qs], True, True, [CK, QT], [bc])
                PC = PCp.next()
                self.act(PC[:, :, :], bc[:, 0:256].rearrange("p (c q) -> p c q", c=2), AF.Exp, [bc], [PC], scale=0.125)
                bo = fw.bank()
                for i, kb in enumerate(kbl):
                    self.mm(bo[:, 0:128], Vt[:, kb, g, :], PT[:, i, :], i == 0, False, [Vt, PT], [bo])
                for c in range(2):
                    self.mm(bo[:, 0:128], CV[:, c, g, :], PC[:, c, :], False, c == 1, [CV, PC], [bo])
                rd = RDp.next()
                self.cp("dve", rd[0:64, :], bo[64:128, 0:128], [bo], [rd])
                self.ts("dve", rd[0:64, :], rd[0:64, :], esink[0:64, h:h + 1], None, ALU.add, None, [rd, esink], [rd])
                self.fw.op("dve", lambda e: e.reciprocal(rd[0:64, :], rd[0:64, :]), [rd], [rd])
                self.tt("dve", self.MIX[6 + j][P, qs], bo[0:64, 0:128], rd[0:64, :], ALU.mult, [bo, rd], [self.MIX[6 + j]])


_PROG_CACHE = {}


def _get_prog(debug=(), layers=DEPTH, groups=("P", "S")):
    key = (tuple(sorted(debug)), layers, tuple(groups))
    if key not in _PROG_CACHE:
        _PROG_CACHE[key] = Prog(debug, layers, groups)
    return _PROG_CACHE[key]


def _prep_inputs(inp, layers=DEPTH):
    f = lambda k: np.ascontiguousarray(np.asarray(inp[k], dtype=np.float32))
    cst, rope = _build_consts()
    cols = _build_cols(inp)
    w_in = f("w_in")
    idx = np.concatenate([3200 + h * 64 + ROPE_PERM for h in range(4)] + [3456 + h * 64 + ROPE_PERM for h in range(2)])
    w_in_sw = np.ascontiguousarray(w_in[:, :, idx])
    lora2 = np.ascontiguousarray(np.concatenate([f("rw_w2").transpose(0, 2, 1, 3), f("rw_a2").transpose(0, 2, 1, 3)], axis=1))
    rpb = f("nat_rpb")
    rpbpad = np.zeros((DEPTH, 4, 15, 127), np.float32)
    rpbpad[..., 48:79] = rpb[..., ::-1]
    NL = layers
    shared = dict(cols=cols, cst=cst, rope=rope, mod_w=f("mod_w")[:NL], w_in=w_in[:NL], w_in_sw=w_in_sw[:NL], w_out=f("w_out")[:NL],
                  ffn_w1=f("ffn_w1")[:NL], ffn_w2=f("ffn_w2")[:NL], lora2=lora2, rw_g2=f("rw_g2"), rpbpad=rpbpad)
    xp, xs = f("x_prompt"), f("x_sample")
    c, c_ctx = f("c"), f("c_ctx")
    cn, cs = f("cache_nat_kv"), f("cache_swa_kv")
    srw, shg = f("state_rwkv"), f("state_hgrn")
    maps = []
    for core in range(8):
        s = core % 2
        m = dict(shared)
        xpc = xp[core * 4:(core + 1) * 4].reshape(T, D)
        m["xT_p"] = np.ascontiguousarray(xpc.T.reshape(8, 128, T).transpose(1, 0, 2))
        m["xT_s"] = np.ascontiguousarray(xs[s].T.reshape(8, 128, T).transpose(1, 0, 2))
        cond = np.stack([c_ctx, c[s]], axis=1)
        m["condT"] = np.ascontiguousarray(cond.reshape(8, 128, 2).transpose(1, 0, 2))
        nk = cn[s, :, 0].reshape(DEPTH, 256, 256)
        m["cnk"] = np.ascontiguousarray(nk.transpose(0, 2, 1).reshape(DEPTH, 2, 128, 256))
        m["cnv"] = np.ascontiguousarray(cn[s, :, 1].reshape(DEPTH, 256, 256))
        sk = cs[s, :, 0].reshape(DEPTH, 256, 2, 64).transpose(0, 2, 3, 1)
        m["csk"] = np.ascontiguousarray(np.concatenate([sk, sk], axis=2))
        m["csv"] = np.ascontiguousarray(cs[s, :, 1].reshape(DEPTH, 256, 128))
        st = srw[s].transpose(0, 1, 2, 4, 3)
        m["srw"] = np.ascontiguousarray(st.reshape(DEPTH, 2, 2, 2, 64, 64).transpose(0, 1, 3, 4, 2, 5).reshape(DEPTH, 2, 128, 2, 64))
        sh = shg[s]
        m["shg"] = np.ascontiguousarray(sh.reshape(DEPTH, 2, 2, 2, 64, 64).transpose(0, 1, 3, 4, 2, 5).reshape(DEPTH, 2, 128, 2, 64))
        maps.append(m)
    return maps


def _assemble(results):
    B = 32
    y_p = np.zeros((B, 256, D), np.float32)
    y_s = np.zeros((2, 1024, D), np.float32)
    natkv = np.zeros((B, DEPTH, 2, 256, 4, 64), np.float32)
    swakv = np.zeros((B, DEPTH, 2, 256, 2, 64), np.float32)
    rwst = np.zeros((B, DEPTH, 2, 4, 64, 64), np.float32)
    hgst = np.zeros((B, DEPTH, 2, 4, 64, 64), np.float32)
    for core in range(8):
        r = results[core]
        bs = slice(core * 4, core * 4 + 4)
        yT = r["yT_p"].transpose(1, 0, 2).reshape(D, T)
        y_p[bs] = yT.T.reshape(4, 256, D)
        if core < 2:
            y_s[core] = r["yT_s"].transpose(1, 0, 2).reshape(D, T).T
        nk = r["natk"].reshape(DEPTH, 256, 4, 256)
        natkv[bs, :, 0] = nk.transpose(2, 0, 3, 1).reshape(4, DEPTH, 256, 4, 64)
        nv = r["natv"].reshape(DEPTH, 4, 256, 256)
        natkv[bs, :, 1] = nv.transpose(1, 0, 2, 3).reshape(4, DEPTH, 256, 4, 64)
        sk = r["swak"].reshape(DEPTH, 128, 4, 256)
        swakv[bs, :, 0] = sk.transpose(2, 0, 3, 1).reshape(4, DEPTH, 256, 2, 64)
        sv = r["swav"].reshape(DEPTH, 4, 256, 128)
        swakv[bs, :, 1] = sv.transpose(1, 0, 2, 3).reshape(4, DEPTH, 256, 2, 64)
        rs = r["rwst"].reshape(DEPTH, 2, 4, 2, 64, 2, 64)
        rwst[bs] = rs.transpose(2, 0, 1, 5, 3, 6, 4).reshape(4, DEPTH, 2, 4, 64, 64)
        hs = r["hgst"].reshape(DEPTH, 2, 4, 2, 64, 2, 64)
        hgst[bs] = hs.transpose(2, 0, 1, 5, 3, 4, 6).reshape(4, DEPTH, 2, 4, 64, 64)
    return (y_p, y_s, natkv, swakv, rwst, hgst)


def kernel(**inputs):
    prog = _get_prog()
    maps = _prep_inputs(inputs)
    res = run_bass_kernel_spmd(prog.nc, maps, core_ids=list(range(8)))
    return _assemble(res.results)
```

```python
import numpy as np
import concourse.bass as bass
import concourse.mybir as mybir
from concourse.bass_utils import run_bass_kernel_spmd

F32 = mybir.dt.float32
BF16 = mybir.dt.bfloat16
AF = mybir.ActivationFunctionType
ALU = mybir.AluOpType

DEPTH = 4
D = 1024
T = 1024
NT = 8
D_IN = 3712
DECAY = 0.6065306597126334
EPS = 1e-6
LN_EPS = 64e-5


class Buf:
    __slots__ = ("name", "t", "lw", "rd", "wsem", "wcnt", "rsem", "rcnt")

    def __init__(self, name, t):
        self.name = name
        self.t = t
        self.lw = None
        self.rd = []
        self.wsem = None
        self.wcnt = 0
        self.rsem = None
        self.rcnt = 0

    def __getitem__(self, idx):
        return self.t[idx]


class FW:
    ENG = ("pe", "dve", "act", "pool", "sp")

    def __init__(self, nc):
        self.nc = nc
        self.eng = {"pe": nc.tensor, "dve": nc.vector, "act": nc.scalar, "pool": nc.gpsimd, "sp": nc.sync}
        self.sem = {}
        self.cnt = {}
        for e in ("pe", "dve", "act", "pool"):
            self.sem[e] = nc.alloc_semaphore(name="s_" + e)
            self.cnt[e] = 0
        self.seen = {e: {} for e in self.ENG}
        self.ninst = 0
        self.out_dma = []
        self.dma_keys = {}
        self.rr = {}
        self.banks = None
        self.bank_i = 0
        self.dcnt = {}
        self.free_dsems = []

    def sb(self, name, shape, dtype=F32):
        return Buf(name, self.nc.alloc_sbuf_tensor(name, list(shape), dtype))

    def ps(self, name, shape, dtype=F32):
        return Buf(name, self.nc.alloc_psum_tensor(name, list(shape), dtype))

    def dram(self, name, shape, dtype=F32, kind="Internal"):
        return Buf(name, self.nc.dram_tensor(name, list(shape), dtype, kind=kind))

    def bank(self):
        if self.banks is None:
            self.banks = [self.ps("bank%d" % i, [128, 512]) for i in range(8)]
        b = self.banks[self.bank_i % 8]
        self.bank_i += 1
        return b

    def pick(self, key, engs):
        i = self.rr.get(key, 0)
        self.rr[key] = i + 1
        return engs[i % len(engs)]

    def _need(self, e, deps):
        eng = self.eng[e]
        seen = self.seen[e]
        best = {}
        for d in deps:
            if d is None:
                continue
            k, v = d
            if best.get(k, 0) < v:
                best[k] = v
        for k, v in best.items():
            if seen.get(k, 0) >= v:
                continue
            if e == "pe" and k == "pe":
                continue
            seen[k] = v
            eng.wait_ge(self.sem[k], v)

    @staticmethod
    def _deps(reads, writes, skip_dma_waw=False):
        deps = []
        for b in reads:
            deps.append(b.lw)
        for b in writes:
            if not (skip_dma_waw and b.lw is not None and isinstance(b.lw[0], tuple)):
                deps.append(b.lw)
            deps.extend(b.rd)
        return deps

    def op(self, e, fn, reads=(), writes=()):
        self._need(e, self._deps(reads, writes))
        ins = fn(self.eng[e])
        self.cnt[e] += 1
        v = self.cnt[e]
        ins.then_inc(self.sem[e], 1)
        for b in writes:
            b.lw = (e, v)
            b.rd = []
        for b in reads:
            if b not in writes:
                b.rd.append((e, v))
        self.ninst += 1
        return ins

    def dma(self, q, out, in_, reads=(), writes=(), part=False, is_output=False):
        self._need(q, self._deps(reads, writes, skip_dma_waw=part))
        ins = self.eng[q].dma_start(out=out, in_=in_)
        if writes:
            b = writes[0]
            if b.wsem is None:
                b.wsem = self._dsem()
            key = b.wsem
        else:
            b = reads[0]
            if b.rsem is None:
                b.rsem = self._dsem()
            key = b.rsem
        self.dcnt[key] += 16
        v = self.dcnt[key]
        ins.then_inc(self.sem[key], 16)
        self.dma_keys[key] = v
        for w in writes:
            w.lw = (key, v)
            w.rd = []
        for r in reads:
            r.rd.append((key, v))
        if is_output:
            self.out_dma.append((key, v))
        self.ninst += 1
        return ins

    def _dsem(self):
        if self.free_dsems:
            return self.free_dsems.pop()
        key = ("d", len(self.dcnt))
        self.sem[key] = self.nc.alloc_semaphore(name="dsem%d" % len(self.dcnt))
        self.dcnt[key] = 0
        return key

    def barrier(self):
        deps = [(e, self.cnt[e]) for e in ("pe", "dve", "act", "pool") if self.cnt[e]]
        deps += list(self.dma_keys.items())
        for e in self.ENG:
            self._need(e, deps)

    def finish(self):
        self.barrier()


class Arena:
    def __init__(self, fw, nwords):
        self.fw = fw
        self.nwords = nwords
        self.t32 = fw.nc.alloc_sbuf_tensor("arena", [128, nwords], F32)
        self.t16 = self.t32.bitcast(BF16)
        self.off = 0
        self.peak = 0
        self.made = []

    def reset(self):
        self.fw.barrier()
        self.off = 0
        for b in self.made:
            for k in (b.wsem, b.rsem):
                if k is not None:
                    self.fw.free_dsems.append(k)
            b.wsem = b.rsem = None
        self.made = []

    def get(self, name, shape, dtype=F32):
        n = int(np.prod(shape[1:]))
        words = n if dtype == F32 else (n + 1) // 2
        assert self.off + words <= self.nwords, (name, self.off, words, self.nwords)
        if dtype == F32:
            ap = self.t32[:, self.off:self.off + n]
        else:
            ap = self.t16[:, 2 * self.off:2 * self.off + n]
        if len(shape) > 2:
            names = "abcdefg"[:len(shape) - 1]
            kw = {names[i]: shape[1 + i] for i in range(len(shape) - 2)}
            ap = ap.rearrange("p (%s) -> p %s" % (" ".join(names), " ".join(names)), **kw)
        if shape[0] < 128:
            ap = ap[0:shape[0]]
        self.off += words
        self.peak = max(self.peak, self.off)
        b = Buf(name, ap)
        self.made.append(b)
        return b

    def pool(self, name, n, shape, dtype=F32):
        return Rot([self.get("%s%d" % (name, i), shape, dtype) for i in range(n)])


class Rot:
    def __init__(self, bufs):
        self.bufs = bufs
        self.i = 0

    def next(self):
        b = self.bufs[self.i % len(self.bufs)]
        self.i += 1
        return b


class ColTable:
    def __init__(self):
        self.idx = {}
        self.cols = []
        self.n = 0

    def add(self, name, per_layer_vecs):
        k = len(per_layer_vecs[0]) // 128
        self.idx[name] = self.n
        self.cols.append(np.stack([np.asarray(v, np.float32).reshape(k, 128) for v in per_layer_vecs]))
        self.n += k

    def build(self):
        a = np.concatenate(self.cols, axis=1)
        return np.ascontiguousarray(a.transpose(2, 0, 1))


def _col_names():
    ct = ColTable()
    z = lambda n: [np.zeros(n, np.float32)] * DEPTH
    for nm, n in COL_SPEC:
        ct.add(nm, z(n))
    return ct.idx, ct.n


COL_SPEC = [("norm_g", 4096), ("mod_b", 6144),
            ("mu_rkv0", 768), ("mu_rkv1", 768), ("mu_lora0", 128), ("mu_lora1", 128),
            ("w0_0", 256), ("w0_1", 256), ("a0_0", 256), ("a0_1", 256),
            ("kk", 256), ("ka", 256), ("rk", 256), ("lnx_w", 256), ("lnx_b", 256),
            ("lbl0", 1024), ("lbl1", 1024), ("hg_norm", 256), ("sink", 512)]
COLI, NCOL = _col_names()


def _build_cols(inp):
    ct = ColTable()
    L = range(DEPTH)
    g = lambda k: np.asarray(inp[k], np.float32)
    vals = {
        "norm_g": [g("norm_g")[l].reshape(-1) for l in L],
        "mod_b": [g("mod_b")[l] for l in L],
        "mu_rkv0": [g("rw_mu_rkv")[l, 0].reshape(-1) for l in L],
        "mu_rkv1": [g("rw_mu_rkv")[l, 1].reshape(-1) for l in L],
        "mu_lora0": [g("rw_mu_lora")[l, 0].reshape(-1) for l in L],
        "mu_lora1": [g("rw_mu_lora")[l, 1].reshape(-1) for l in L],
        "w0_0": [g("rw_w0")[l, 0] for l in L], "w0_1": [g("rw_w0")[l, 1] for l in L],
        "a0_0": [g("rw_a0")[l, 0] for l in L], "a0_1": [g("rw_a0")[l, 1] for l in L],
        "kk": [g("rw_kk")[l] for l in L], "ka": [g("rw_ka")[l] for l in L],
        "rk": [g("rw_rk")[l].reshape(-1) for l in L],
        "lnx_w": [g("rw_lnx_w")[l] for l in L], "lnx_b": [g("rw_lnx_b")[l] for l in L],
        "lbl0": [g("hg_lb_logits")[0].reshape(-1) for l in L],
        "lbl1": [g("hg_lb_logits")[1].reshape(-1) for l in L],
        "hg_norm": [g("hg_norm")[l] for l in L],
        "sink": [np.repeat(g("swa_sink")[l], 128) for l in L],
    }
    for nm, n in COL_SPEC:
        assert len(vals[nm][0]) == n, nm
        ct.add(nm, vals[nm])
    return ct.build()


CST_SPEC = [("ident", 128), ("ones", 128), ("blk", 128),
            ("g4_0", 512), ("g4_1", 512), ("mb_0", 128), ("mb_1", 128),
            ("reset", 256), ("natmask", 64),
            ("mprev", 128), ("mnext", 128), ("flip", 64)]
CSTI = {}
_o = 0
for _n, _w in CST_SPEC:
    CSTI[_n] = (_o, _w)
    _o += _w
NCST = _o


def _build_consts():
    c = np.zeros((128, NCST), np.float32)

    def put(nm, a):
        o, w = CSTI[nm]
        c[:a.shape[0], o:o + w] = a
    i = np.arange(128)
    put("ident", np.eye(128))
    put("ones", np.ones((128, 128)))
    put("blk", ((i[:, None] // 64) == (i[None, :] // 64)).astype(np.float32))
    for d in (0, 1):
        if d == 0:
            strictT = (i[:, None] < i[None, :]).astype(np.float32)
            inclT = (i[:, None] <= i[None, :]).astype(np.float32)
        else:
            strictT = (i[:, None] > i[None, :]).astype(np.float32)
            inclT = (i[:, None] >= i[None, :]).astype(np.float32)
        put("g4_%d" % d, np.concatenate([-strictT, inclT, strictT, inclT], axis=1))
        put("mb_%d" % d, -strictT.T)
    r = np.ones((128, 256), np.float32)
    r[:, 0] = 0
    r[:, 128] = 0
    put("reset", r)
    t = np.arange(1024)
    row = (t // 64).astype(np.float32)
    col = (t % 64).astype(np.float32)
    inv = (1.0 / (10000.0 ** (np.arange(16, dtype=np.float32) / 16))).astype(np.float32)
    C = np.zeros((64, 1024), np.float32)
    S = np.zeros((64, 1024), np.float32)
    for base, pos in ((0, row), (32, col)):
        ang = (pos[None, :] * inv[:, None]).astype(np.float32)
        C[base:base + 16] = np.cos(ang)
        C[base + 16:base + 32] = np.cos(ang)
        S[base:base + 16] = -np.sin(ang)
        S[base + 16:base + 32] = np.sin(ang)
    rope = np.stack([np.concatenate([C, C], axis=0), np.concatenate([S, S], axis=0)], axis=1)
    kc = np.arange(64)[:, None]
    qc = np.arange(64)[None, :]
    ws = np.clip(qc - 8, 0, 48)
    put("natmask", ((kc >= ws) & (kc < ws + 16)).astype(np.float32))
    put("mprev", (i[:, None] >= i[None, :]).astype(np.float32))
    put("mnext", (i[:, None] <= i[None, :]).astype(np.float32))
    put("flip", np.eye(64)[::-1].copy())
    return c, np.ascontiguousarray(rope.astype(np.float32))


ROPE_PERM = np.concatenate([np.arange(16, 32), np.arange(0, 16), np.arange(48, 64), np.arange(32, 48)])


class Prog:
    def __init__(self, debug=(), layers=DEPTH, groups=("P", "S")):
        self.debug = set(debug)
        self.layers = layers
        self.groups = groups
        self.nc = bass.Bass("TRN2", target_bir_lowering=False)
        self.fw = FW(self.nc)
        self.dbg_outs = {}
        self.build()

    def mm(self, out, lhsT, rhs, start, stop, reads, writes):
        return self.fw.op("pe", lambda e: e.matmul(out, lhsT, rhs, start=start, stop=stop), reads, writes)

    def tp(self, out, in_, ident, reads, writes):
        return self.fw.op("pe", lambda e: e.transpose(out, in_, ident), reads, writes)

    def act(self, out, in_, func, reads, writes, bias=None, scale=1.0):
        if bias is None:
            return self.fw.op("act", lambda e: e.activation(out, in_, func, scale=scale), reads, writes)
        return self.fw.op("act", lambda e: e.activation(out, in_, func, bias=bias, scale=scale), reads, writes)

    def tt(self, eng, out, in0, in1, op, reads, writes):
        return self.fw.op(eng, lambda e: e.tensor_tensor(out, in0, in1, op), reads, writes)

    def ts(self, eng, out, in0, s1, s2, op0, op1, reads, writes):
        if s2 is None:
            return self.fw.op(eng, lambda e: e.tensor_scalar(out, in0, s1, None, op0), reads, writes)
        return self.fw.op(eng, lambda e: e.tensor_scalar(out, in0, s1, s2, op0, op1), reads, writes)

    def stt(self, eng, out, in0, scalar, in1, op0, op1, reads, writes):
        eng = "dve"
        return self.fw.op(eng, lambda e: e.scalar_tensor_tensor(out, in0, scalar, in1, op0, op1), reads, writes)

    def rsqrt(self, out, in_, addc, reads, wbuf):
        self.act(out, in_, AF.Ln, reads + [self.epsb], [wbuf], bias=self.epsc(addc))
        self.act(out, out, AF.Exp, [wbuf], [wbuf], scale=-0.5)

    def epsc(self, v):
        i = self.eps_vals.index(v)
        return self.epsb[:, i:i + 1]

    def cp(self, eng, out, in_, reads, writes):
        if eng == "act":
            return self.fw.op("act", lambda e: e.copy(out, in_), reads, writes)
        return self.fw.op(eng, lambda e: e.tensor_copy(out, in_), reads, writes)

    def ev(self):
        return self.fw.pick("ev", ("dve", "act"))

    def el(self):
        return self.fw.pick("el", ("dve", "pool"))

    def dump(self, name, buf, ap, shape, dtype=F32):
        if name not in self.debug:
            return
        o = self.fw.dram("dbg_" + name, list(shape), dtype, kind="ExternalOutput")
        self.dbg_outs[name] = o
        self.fw.dma("sp", o[:], ap, reads=[buf], is_output=True)

    def col(self, l, name, j=0, n=1):
        i = COLI[name] + j
        return self.cols[:, l, i:i + n]

    def cst(self, name, lo=0, hi=None, rows=128):
        o, w = CSTI[name]
        hi = w if hi is None else hi
        return self.cstb[0:rows, o + lo:o + hi]

    def load_block(self, src_ap, a, b, cast=True):
        fw = self.fw
        n = a * b
        st = self.WST.next()
        sview = st[:, 0:n].rearrange("p (a b) -> p a b", a=a)
        fw.dma("sp", sview, src_ap, writes=[st])
        self.mod_tick()
        if not cast:
            return st, sview
        wb = self.WBF.next()
        wview = wb[:, 0:n].rearrange("p (a b) -> p a b", a=a)
        self.cp(self.fw.pick("cast", ("pool", "dve", "pool")), wb[:, 0:n], st[:, 0:n], [st], [wb])
        return wb, wview

    def win_block(self, l, c0, ncol, src=None):
        src = self.w_in if src is None else src
        ap = src[l, :, c0:c0 + ncol].rearrange("(k p) n -> p k n", p=128)
        return self.load_block(ap, 8, ncol)

    def mod_tick(self, force=False):
        if not self.mod_pending:
            return
        if self.in_mod:
            return
        self.in_mod = True
        l, kc, third = self.mod_pending.pop(0)
        fw = self.fw
        st = self.WST.next()
        fw.dma("sp", st[:, 0:2048], self.mod_w[l, kc * 128:(kc + 1) * 128, third * 2048:(third + 1) * 2048], writes=[st])
        bk = fw.bank()
        for j in range(16):
            self.mm(bk[:, 2 * j:2 * j + 2], st[:, j * 128:(j + 1) * 128], self.silu[:, kc, :], True, True, [st, self.silu], [bk])
        dst = self.modacc[:, l, third * 16:(third + 1) * 16, :]
        src = bk[:, 0:32].rearrange("p (a b) -> p a b", b=2)
        if kc == 0:
            self.cp("dve", dst, src, [bk], [self.modacc])
        else:
            self.tt("dve", dst, dst, src, ALU.add, [bk, self.modacc], [self.modacc])
        self.in_mod = False

    def mod_flush(self, l):
        while self.mod_pending and self.mod_pending[0][0] <= l:
            self.mod_tick()

    def mod_finalize(self, l, g):
        fw = self.fw
        mc = self.modc
        m = lambda i: self.modacc[:, l, i * 8:(i + 1) * 8, g]
        mb = lambda i: self.col(l, "mod_b", i * 8, 8)
        ng = lambda i: self.col(l, "norm_g", i * 8, 8)
        R = [self.modacc, self.cols, mc]
        for i in range(6):
            self.tt("dve", mc[:, i, :], m(i), mb(i), ALU.add, R, [mc])
        self.ts("dve", mc[:, 6, :], mc[:, 1, :], 1.0, 32.0, ALU.add, ALU.mult, R, [mc])
        self.tt("dve", mc[:, 6, :], mc[:, 6, :], ng(0), ALU.mult, R, [mc])
        self.ts("dve", mc[:, 7, :], mc[:, 2, :], 32.0, None, ALU.mult, None, R, [mc])
        self.tt("dve", mc[:, 7, :], mc[:, 7, :], ng(1), ALU.mult, R, [mc])
        self.ts("dve", mc[:, 8, :], mc[:, 4, :], 1.0, 32.0, ALU.add, ALU.mult, R, [mc])
        self.tt("dve", mc[:, 8, :], mc[:, 8, :], ng(2), ALU.mult, R, [mc])
        self.ts("dve", mc[:, 9, :], mc[:, 5, :], 32.0, None, ALU.mult, None, R, [mc])
        self.tt("dve", mc[:, 9, :], mc[:, 9, :], ng(3), ALU.mult, R, [mc])
        return dict(shift1=mc[:, 0, :], coefA1=mc[:, 6, :], gco1=mc[:, 7, :], shift2=mc[:, 3, :], coefA2=mc[:, 8, :], gco2=mc[:, 9, :])

    def rstd_of(self, src_bufs, src_ap, sq, rstd):
        fw = self.fw
        bks = [fw.bank(), fw.bank()]
        for c in range(8):
            s = sq.next()
            self.act(s[:, :], src_ap(c), AF.Square, [src_bufs[c]], [s])
            for th in range(2):
                self.mm(bks[th][:, :], self.ones_bf[:, :], s[:, th * 512:(th + 1) * 512], c == 0, c == 7, [self.ones_bf, s], [bks[th]])
        for th in range(2):
            self.rsqrt(rstd[:, th * 512:(th + 1) * 512], bks[th][:, :], D * EPS, [bks[th]], rstd)
        return rstd

    def build(self):
        fw = self.fw
        nc = self.nc
        di = lambda n, s: fw.dram(n, s, F32, kind="ExternalInput")
        do = lambda n, s: fw.dram(n, s, F32, kind="ExternalOutput")
        self.xT = {"P": di("xT_p", [128, 8, T]), "S": di("xT_s", [128, 8, T])}
        self.yT = {"P": do("yT_p", [128, 8, T]), "S": do("yT_s", [128, 8, T])}
        condT = di("condT", [128, 8, 2])
        cols_d = di("cols", [128, DEPTH, NCOL])
        cst_d = di("cst", [128, NCST])
        NL = self.layers
        self.mod_w = di("mod_w", [NL, D, 6 * D])
        self.w_in = di("w_in", [NL, D, D_IN])
        self.w_in_sw = di("w_in_sw", [NL, D, 384])
        self.w_out = di("w_out", [NL, D, D])
        self.ffn_w1 = di("ffn_w1", [NL, D, 4 * D])
        self.ffn_w2 = di("ffn_w2", [NL, 4 * D, D])
        self.lora2 = di("lora2", [DEPTH, 128, 2, 256])
        self.rw_g2 = di("rw_g2", [DEPTH, 128, 256])
        self.rpbpad = di("rpbpad", [DEPTH, 4, 15, 127])
        self.rope = di("rope", [128, 2, T])
        self.cnk = di("cnk", [DEPTH, 2, 128, 256])
        self.cnv = di("cnv", [DEPTH, 256, 256])
        self.csk = di("csk", [DEPTH, 2, 128, 256])
        self.csv = di("csv", [DEPTH, 256, 128])
        self.srw = di("srw", [DEPTH, 2, 128, 2, 64])
        self.shg = di("shg", [DEPTH, 2, 128, 2, 64])
        self.natk = do("natk", [DEPTH, 2, 128, T])
        self.natv = do("natv", [DEPTH, NT, 128, 256])
        self.swak = do("swak", [DEPTH, 128, T])
        self.swav = do("swav", [DEPTH, NT, 128, 128])
        self.rwst = do("rwst", [DEPTH, 2, 4, 128, 2, 64])
        self.hgst = do("hgst", [DEPTH, 2, 4, 128, 2, 64])

        self.cstb = fw.sb("cstb", [128, NCST])
        self.cols = fw.sb("colsb", [128, DEPTH, NCOL])
        self.ident_bf = fw.sb("ident_bf", [128, 128], BF16)
        self.ones_bf = fw.sb("ones_bf", [128, 128], BF16)
        self.blk_bf = fw.sb("blk_bf", [128, 128], BF16)
        self.silu = fw.sb("silu", [128, 8, 2])
        self.modacc = fw.sb("modacc", [128, DEPTH, 48, 2])
        self.modc = fw.sb("modc", [128, 10, 8])
        self.lb = fw.sb("lb", [128, 2, DEPTH, 2])
        self.oml = fw.sb("oml", [128, 2, DEPTH, 2])
        self.X = [fw.sb("X%d" % c, [128, T]) for c in range(8)]
        self.H = [fw.sb("H%d" % c, [128, T], BF16) for c in range(8)]
        self.MIX = [fw.sb("MIX%d" % c, [128, T], BF16) for c in range(8)]
        self.WST = Rot([fw.sb("wst%d" % i, [128, 2048]) for i in range(2)])
        self.WBF = Rot([fw.sb("wbf%d" % i, [128, 2048], BF16) for i in range(2)])
        self.rstd = fw.sb("rstd", [128, T])
        self.sq = Rot([fw.sb("sq%d" % i, [128, T], BF16) for i in range(2)])
        self.arena = Arena(fw, (nc.sbuf_bytes_remaining - 2048) // 4)

        self.eps_vals = [D * EPS, 1e-24, 64 * LN_EPS, 64 * EPS]
        self.epsb = fw.sb("epsb", [128, 4])
        for i, v in enumerate(self.eps_vals):
            fw.op("pool", lambda e: e.memset(self.epsb[:, i:i + 1], v), [], [self.epsb])
        fw.dma("sp", self.cstb[:, :], cst_d[:, :], writes=[self.cstb])
        fw.dma("sp", self.cols[:, :, :], cols_d[:, :, :], writes=[self.cols])
        fw.dma("sp", self.silu[:, :, :], condT[:, :, :], writes=[self.silu])
        self.cp("dve", self.ident_bf[:, :], self.cst("ident"), [self.cstb], [self.ident_bf])
        self.cp("dve", self.ones_bf[:, :], self.cst("ones"), [self.cstb], [self.ones_bf])
        self.cp("dve", self.blk_bf[:, :], self.cst("blk"), [self.cstb], [self.blk_bf])
        self.act(self.silu[:, :, :], self.silu[:, :, :], AF.Silu, [self.silu], [self.silu])
        self.setup_lb()

        self.mod_pending = [(l, kc, th) for l in range(self.layers) for kc in range(8) for th in range(3)]
        self.in_mod = False

        for G in self.groups:
            for c in range(8):
                fw.dma("sp", self.X[c][:, :], self.xT[G][:, c, :], writes=[self.X[c]])
            for l in range(self.layers):
                self.block(G, l)
            for c in range(8):
                fw.dma("sp", self.yT[G][:, c, :], self.X[c][:, :], reads=[self.X[c]], is_output=True)
        fw.finish()

    def setup_lb(self):
        E = self.fw.sb("lbE", [128, 2, DEPTH, 2])
        S = self.fw.sb("lbS", [128, 2, 2])
        R = [self.cols, E, S, self.lb, self.oml]
        for d in range(2):
            i = COLI["lbl%d" % d]
            self.act(E[:, d, :, :], self.cols[:, 0, i:i + 8].rearrange("p (a b) -> p a b", a=DEPTH), AF.Exp, R, [E])
            self.tt("dve", S[:, d, :], E[:, d, 0, :], E[:, d, 1, :], ALU.add, R, [S])
            self.tt("dve", S[:, d, :], S[:, d, :], E[:, d, 2, :], ALU.add, R, [S])
            self.tt("dve", S[:, d, :], S[:, d, :], E[:, d, 3, :], ALU.add, R, [S])
            self.fw.op("dve", lambda e: e.reciprocal(S[:, d, :], S[:, d, :]), R, [S])
            for l in range(DEPTH):
                self.tt("dve", E[:, d, l, :], E[:, d, l, :], S[:, d, :], ALU.mult, R, [E])
            self.fw.op("dve", lambda e: e.memset(self.lb[:, d, 0, :], 0.0), R, [self.lb])
            for l in range(1, DEPTH):
                self.tt("dve", self.lb[:, d, l, :], self.lb[:, d, l - 1, :], E[:, d, l, :], ALU.add, R, [self.lb])
        self.ts("dve", self.oml[:, :, :, :], self.lb[:, :, :, :], -1.0, 1.0, ALU.mult, ALU.add, R, [self.oml])

    def block(self, G, l):
        fw = self.fw
        g = 0 if G == "P" else 1
        self.G, self.l = G, l
        self.seqs = [(s * 2, 2) for s in range(4)] if G == "P" else [(0, 8)]
        self.mod_flush(l)
        mc = self.mod_finalize(l, g)
        self.mc = mc
        X, H = self.X, self.H
        rstd = self.rstd_of(X, lambda c: X[c][:, :], self.sq, self.rstd)
        for c in range(8):
            tmp = self.sq.next()
            e = self.el()
            self.stt(e, tmp[:, :], X[c][:, :], mc["coefA1"][:, c:c + 1], rstd[:, :], ALU.mult, ALU.mult, [X[c], rstd, self.modc], [tmp])
            self.act(H[c][:, :], tmp[:, :], AF.Identity, [tmp, self.modc], [H[c]], bias=mc["shift1"][:, c:c + 1])
        self.dump("h_%s%d" % (G, l), H[0], H[0][:, :], [128, T], BF16)
        self.mixer_A()
        self.mixer_B()
        self.mixer_C()
        self.mixer_D()
        for c in range(8):
            self.dump("mix%d_%s%d" % (c, G, l), self.MIX[c], self.MIX[c][:, :], [128, T], BF16)
        self.out_and_ffn()

    def proj_fm(self, wb, wv, nchunks, evac):
        for jj in range(nchunks):
            for th in range(2):
                bk = self.fw.bank()
                for kc in range(8):
                    self.mm(bk[:, :], wv[:, kc, jj * 128:(jj + 1) * 128], self.H[kc][:, th * 512:(th + 1) * 512],
                            kc == 0, kc == 7, [wb, self.H[kc]], [bk])
                evac(jj, th, bk)

    def proj_tm(self, wb, wv, ncol, evac, tile_tokens=128):
        nt = T // tile_tokens
        for ti in range(nt):
            bk = self.fw.bank()
            for kc in range(8):
                self.mm(bk[0:tile_tokens, 0:ncol], self.H[kc][:, ti * tile_tokens:(ti + 1) * tile_tokens], wv[:, kc, 0:ncol],
                        kc == 0, kc == 7, [wb, self.H[kc]], [bk])
            evac(ti, bk)

    def ctx_attn(self, QT, KT, ksel, Vt, vsel, mix0, PTp, RDp, esink=None):
        fw = self.fw

        def unit(slot, s, h):
            t0 = s * 256
            j, hp = h // 2, h % 2
            P = slice(hp * 64, hp * 64 + 64)
            PT = PTp.bufs[slot]
            rd = RDp.bufs[slot]
            bks = [fw.banks[slot * 4 + kc] for kc in range(2)]
            for kc in range(2):
                self.mm(bks[kc][:, 0:256], KT[P, ksel(h), t0 + kc * 128:t0 + (kc + 1) * 128], QT[P, j, t0:t0 + 256], True, True, [KT, QT], [bks[kc]])
            yield
            for kc in range(2):
                self.act(PT[:, kc, :], bks[kc][:, 0:256], AF.Exp, [bks[kc]], [PT], scale=0.125)
            yield
            bo = fw.banks[slot * 4 + 2]
            for kc in range(2):
                self.mm(bo[:, 0:256], Vt[:, s * 2 + kc, vsel(h), :], PT[:, kc, :], kc == 0, kc == 1, [Vt, PT], [bo])
            yield
            self.cp("dve", rd[0:64, :], bo[64:128, 0:256], [bo], [rd])
            if esink is not None:
                self.ts("dve", rd[0:64, :], rd[0:64, :], esink[0:64, h:h + 1], None, ALU.add, None, [rd, esink], [rd])
            self.fw.op("dve", lambda e: e.reciprocal(rd[0:64, :], rd[0:64, :]), [rd], [rd])
            self.tt("dve", self.MIX[mix0 + j][P, t0:t0 + 256], bo[0:64, 0:256], rd[0:64, :], ALU.mult, [bo, rd], [self.MIX[mix0 + j]])
        self.run_slots([(s_, h_) for s_ in range(4) for h_ in range(4)], unit, 2)

    def run_slots(self, items, unit, width):
        pend = list(items)
        active = {}
        while pend or active:
            for slot in range(width):
                if slot not in active and pend:
                    active[slot] = unit(slot, *pend.pop(0))
            for slot in list(active.keys()):
                try:
                    next(active[slot])
                except StopIteration:
                    del active[slot]

    def mixer_B(self):
        fw, l, G = self.fw, self.l, self.G
        ar = self.arena
        ar.reset()
        QT = ar.get("QT", [128, 2, T], BF16)
        KT = ar.get("KT", [128, 2, T], BF16)
        if G == "P":
            KTf = ar.get("KTf", [128, 2, T], F32)
            Vf = ar.get("Vf", [128, NT, 256], F32)
            Vt = ar.get("Vt", [128, NT, 4, 128], BF16)
            PTp = ar.pool("PT", 2, [128, 2, 256], BF16)
            RDp = ar.pool("RD", 2, [128, 256], F32)
            fw.op("pool", lambda e: e.memset(Vt[:, :, :, 64:128], 1.0), [], [Vt])
            wb, wv = self.win_block(l, 1152, 256)
            self.proj_fm(wb, wv, 2, lambda jj, th, bk: self.cp(self.ev(), QT[:, jj, th * 512:(th + 1) * 512], bk[:, :], [bk], [QT]))
            wb, wv = self.win_block(l, 1408, 256)

            def evk(jj, th, bk):
                self.cp(self.ev(), KTf[:, jj, th * 512:(th + 1) * 512], bk[:, :], [bk], [KTf])
                self.cp("pool", KT[:, jj, th * 512:(th + 1) * 512], KTf[:, jj, th * 512:(th + 1) * 512], [KTf], [KT])
            self.proj_fm(wb, wv, 2, evk)
            fw.dma("sp", self.natk[l].rearrange("j p t -> p j t"), KTf[:, :, :], reads=[KTf], is_output=True)
            wb, wv = self.win_block(l, 1664, 256)

            def evv(ti, bk):
                self.cp(self.ev(), Vf[:, ti, :], bk[:, 0:256], [bk], [Vf])
                self.cp("pool", Vt[:, ti, :, 0:64], Vf[:, ti, :].rearrange("p (h d) -> p h d", h=4), [Vf], [Vt])
            self.proj_tm(wb, wv, 256, evv)
            fw.dma("sp", self.natv[l].rearrange("n p f -> p n f"), Vf[:, :, :], reads=[Vf], is_output=True)
            self.ctx_attn(QT, KT, lambda h: h // 2, Vt, lambda h: h, 2, PTp, RDp)
        else:
            self.nat_latent(QT, KT)

    def mixer_D(self):
        fw, l, G = self.fw, self.l, self.G
        ar = self.arena
        ar.reset()
        QT = ar.get("QT", [128, 2, T], BF16)
        KT = ar.get("KTd", [128, 2, T], BF16)
        esink = ar.get("esink", [128, 4], F32)
        self.act(esink[:, :], self.col(l, "sink", 0, 4), AF.Exp, [self.cols], [esink])
        if G == "P":
            KTf = ar.get("KTf", [128, T], F32)
            Vf = ar.get("Vf", [128, NT, 128], F32)
            Vt = ar.get("Vt", [128, NT, 2, 128], BF16)
            PTp = ar.pool("PT", 2, [128, 2, 256], BF16)
            RDp = ar.pool("RD", 2, [128, 256], F32)
            fw.op("pool", lambda e: e.memset(Vt[:, :, :, 64:128], 1.0), [], [Vt])
            wb, wv = self.win_block(l, 3200, 256)
            self.proj_fm(wb, wv, 2, lambda jj, th, bk: self.cp(self.ev(), QT[:, jj, th * 512:(th + 1) * 512], bk[:, :], [bk], [QT]))
            wb, wv = self.win_block(l, 3456, 256)

            def evk(jj, th, bk):
                sl = slice(th * 512, (th + 1) * 512)
                self.cp(self.ev(), KTf[:, sl], bk[:, :], [bk], [KTf])
                for gg in range(2):
                    for hp in range(2):
                        self.cp(self.ev(), KT[hp * 64:hp * 64 + 64, gg, sl], KTf[gg * 64:gg * 64 + 64, sl], [KTf], [KT])
            self.proj_fm(wb, wv, 1, evk)
            fw.dma("sp", self.swak[l], KTf[:, :], reads=[KTf], is_output=True)

            def evv(ti, bk):
                self.cp(self.ev(), Vf[:, ti, :], bk[:, 0:128], [bk], [Vf])
                self.cp("pool", Vt[:, ti, :, 0:64], Vf[:, ti, :].rearrange("p (h d) -> p h d", h=2), [Vf], [Vt])
            self.proj_tm(wb, wv[:, :, 128:256], 128, evv)
            fw.dma("sp", self.swav[l].rearrange("n p f -> p n f"), Vf[:, :, :], reads=[Vf], is_output=True)
            self.ctx_attn(QT, KT, lambda h: h // 2, Vt, lambda h: h // 2, 6, PTp, RDp, esink=esink)
        else:
            self.swa_latent(QT, KT, esink)

    def lerp(self, d, out_ap, src_buf, src_ap, mu_ap, DIFF, out_buf):
        ns = len(self.seqs)
        L = T // ns
        s3 = src_ap.rearrange("p (s t) -> p s t", s=ns)
        d3 = DIFF[:, :].rearrange("p (s t) -> p s t", s=ns)
        e = self.el()
        if d == 0:
            self.tt(e, d3[:, :, 1:L], s3[:, :, 0:L - 1], s3[:, :, 1:L], ALU.subtract, [src_buf], [DIFF])
            self.ts(e, d3[:, :, 0:1], s3[:, :, 0:1], -1.0, None, ALU.mult, None, [src_buf], [DIFF])
        else:
            self.tt(e, d3[:, :, 0:L - 1], s3[:, :, 1:L], s3[:, :, 0:L - 1], ALU.subtract, [src_buf], [DIFF])
            self.ts(e, d3[:, :, L - 1:L], s3[:, :, L - 1:L], -1.0, None, ALU.mult, None, [src_buf], [DIFF])
        self.stt(e, out_ap, DIFF[:, :], mu_ap, src_ap, ALU.mult, ALU.add, [DIFF, src_buf, self.cols], [out_buf])

    def tile_order(self, d):
        out = []
        for si, (t0, n) in enumerate(self.seqs):
            tl = list(range(t0, t0 + n))
            if d == 1:
                tl = tl[::-1]
            for i, ti in enumerate(tl):
                out.append((si, ti, i == n - 1))
        return out

    def scan_tile(self, d, gs_ap, src_ap, reads, wbuf, C=128):
        rs = self.cst("reset", 0, C)
        if d == 0:
            self.fw.op("dve", lambda e: e.tensor_tensor_scan(gs_ap, rs, src_ap, 0.0, ALU.mult, ALU.add), reads + [self.cstb], [wbuf])
        else:
            self.fw.op("dve", lambda e: e.tensor_tensor_scan(gs_ap[:, ::-1], rs, src_ap[:, ::-1], 0.0, ALU.mult, ALU.add), reads + [self.cstb], [wbuf])

    def run_gens(self, gens, width=None):
        pend = list(gens)
        width = width or len(pend)
        active = []
        while pend or active:
            while pend and len(active) < width:
                active.append(pend.pop(0))
            nxt = []
            for g in active:
                try:
                    next(g)
                    nxt.append(g)
                except StopIteration:
                    pass
            active = nxt

    def lin_head(self, d, j, hp, t0, W, dplr, banks, C=128):
        fw = self.fw
        bi = [0]

        def nb():
            b = banks[bi[0] % len(banks)]
            bi[0] += 1
            return b
        P = slice(hp * 64, hp * 64 + 64)
        h = 2 * j + hp
        RK, KB, TOK, YACC = W["RK"], W["KB"], W["TOK"], W["YACC"]
        hst = W["hst"]
        HST = W["HST"]
        rt = RK[P, j, 1, 0:C]
        kb = KB[P, j, 0:C]
        Vh = TOK[0:C, 0, h * 64:(h + 1) * 64]
        Kbh = TOK[0:C, 1, h * 64:(h + 1) * 64]
        G4 = W["G4p"].next()
        g4c = lambda a, b: self.cst("g4_%d" % d, a, b)
        if dplr:
            PB = W["PB"]
            kt = RK[P, j, 0, :]
            rk2 = RK[P, j, :, :]
            pb = PB[P, j, :]
            Pbh = TOK[:, 2, h * 64:(h + 1) * 64]
            bk = nb()
            self.mm(bk[:, 0:256], pb, rk2, True, True, [PB, RK], [bk])
            self.mm(bk[:, 256:512], kb, rk2, True, True, [KB, RK], [bk])
            bk2 = nb()
            self.mm(bk2[:, 0:128], kt, pb, True, True, [RK, PB], [bk2])
            yield
            self.tt("dve", G4[:, :], bk[:, :], g4c(0, 512), ALU.mult, [bk, self.cstb], [G4])
            Bm = W["Bp"].next()
            self.tt("dve", Bm[:, :], bk2[:, 0:128], self.cst("mb_%d" % d), ALU.mult, [bk2, self.cstb], [Bm])
            CN = W["CNp"].next()
            self.tt("pool", CN[:, 128:256], G4[:, 0:128], self.cst("ident"), ALU.add, [G4, self.cstb], [CN])
            yield
            bk = nb()
            self.mm(bk[:, 0:128], Bm[:, :], G4[:, 0:128], True, True, [Bm, G4], [bk])
            yield
            self.cp("act", CN[:, 0:128], bk[:, 0:128], [bk], [CN])
            Cprev_buf, Cprev = G4, G4[:, 0:128]
            Nfin = None
            for lev in range(1, 7):
                bk = nb()
                self.mm(bk[:, 0:128], Cprev, Bm[:, :], True, True, [Cprev_buf, Bm], [bk])
                yield
                Bn = W["Bp"].next()
                self.cp("act", Bn[:, :], bk[:, 0:128], [bk], [Bn])
                Bm = Bn
                yield
                bk = nb()
                if lev < 6:
                    self.mm(bk[:, 0:256], Bm[:, :], CN[:, 0:256], True, True, [Bm, CN], [bk])
                    yield
                    CN2 = W["CNp"].next()
                    self.cp("act", CN2[:, 0:128], bk[:, 0:128], [bk], [CN2])
                    self.tt("dve", CN2[:, 128:256], bk[:, 128:256], CN[:, 128:256], ALU.add, [bk, CN], [CN2])
                    Cprev_buf, Cprev = CN, CN[:, 0:128]
                    CN = CN2
                    yield
                else:
                    self.mm(bk[:, 0:128], Bm[:, :], CN[:, 128:256], True, True, [Bm, CN], [bk])
                    yield
                    Nfin = W["Bp"].next()
                    self.tt("dve", Nfin[:, :], bk[:, 0:128], CN[:, 128:256], ALU.add, [bk, CN], [Nfin])
                    yield
            mqk = G4[:, 384:512]
        else:
            bk = nb()
            self.mm(bk[0:C, 0:C], kb, rt, True, True, [KB, RK], [bk])
            yield
            self.tt("dve", G4[0:C, 0:C], bk[0:C, 0:C], self.cst("g4_%d" % d, 384, 384 + C, rows=C), ALU.mult, [bk, self.cstb], [G4])
            mqk = G4[0:C, 0:C]
        Hp = W["Hpp"].next()
        Hb = W["Hbp"].next()
        self.ts("dve", Hp[P, :], hst, W["expc"][P, j:j + 1], None, ALU.mult, None, [HST, W["expcb"]], [Hp])
        yield
        self.cp("act", Hb[P, :], Hp[P, :], [Hp], [Hb])
        yield
        if dplr:
            bx = nb()
            self.mm(bx[:, 0:64], G4[:, 256:384], Vh, True, False, [G4, TOK], [bx])
            self.mm(bx[:, 0:64], kt, Hb[P, :], False, True, [RK, Hb], [bx])
            yield
            Xs = W["Xp"].next()
            self.cp("act", Xs[:, :], bx[:, 0:64], [bx], [Xs])
            yield
            bu = nb()
            self.mm(bu[:, 0:64], Nfin[:, :], Xs[:, :], True, True, [Nfin, Xs], [bu])
            yield
            Un = W["Xp"].next()
            self.ts("dve", Un[:, :], bu[:, 0:64], -1.0, None, ALU.mult, None, [bu], [Un])
            yield
        by = nb()
        self.mm(by[P, 0:C], Hb[P, :], rt, True, False, [Hb, RK], [by])
        self.mm(by[P, 0:C], Vh, mqk, False, not dplr, [TOK, G4], [by])
        if dplr:
            self.mm(by[P, 0:C], Un[:, :], G4[:, 128:256], False, True, [Un, G4], [by])
        self.mm(by[P, 256:320], Kbh, Vh, True, not dplr, [TOK], [by])
        if dplr:
            self.mm(by[P, 256:320], Pbh, Un[:, :], False, True, [TOK, Un], [by])
        yield
        ydst = YACC[P, j, t0:t0 + C]
        self.tt("dve", ydst, by[P, 0:C], ydst, ALU.add, [by, YACC], [YACC])
        self.tt("dve", Hp[P, :], by[P, 256:320], Hp[P, :], ALU.add, [by, Hp], [Hp])
        yield
        self.ts("dve", hst, Hp[P, :], W["eplast"][P, j:j + 1], None, ALU.mult, None, [Hp, W["eplastb"]], [HST])

    def lin_common_bufs(self, ar, dplr, njobs):
        Ws = []
        for k in range(njobs):
            W = {}
            if dplr:
                W["G4p"] = ar.pool("G4_%d" % k, 1, [128, 512], BF16)
                W["Bp"] = ar.pool("Bm_%d" % k, 3, [128, 128], BF16)
                W["CNp"] = ar.pool("CN_%d" % k, 3, [128, 256], BF16)
                W["Xp"] = ar.pool("Xs_%d" % k, 2, [128, 64], BF16)
            else:
                W["G4p"] = ar.pool("G4_%d" % k, 1, [128, 32], BF16)
            W["Hpp"] = ar.pool("Hp_%d" % k, 1, [128, 64], F32)
            W["Hbp"] = ar.pool("Hb_%d" % k, 1, [128, 64], BF16)
            Ws.append(W)
        return Ws

    def lin_state_io(self, HST, src_dram, dst_dram, load):
        l = self.l
        if load:
            self.fw.dma("sp", HST[:, :, 0, :, :], src_dram[l].rearrange("d p j v -> p d j v"), writes=[HST])
        else:
            self.fw.dma("sp", dst_dram[l].rearrange("d s p j v -> p d s j v"), HST[:, :, :, :, :], reads=[HST], is_output=True)

    def mixer_A(self):
        fw, l, G = self.fw, self.l, self.G
        ar = self.arena
        ar.reset()
        ns = len(self.seqs)
        RAW = ar.get("RAW", [128, 6, T], BF16)
        WA = ar.get("WA", [128, 2, T], F32)
        SG = ar.get("SG", [128, T], BF16)
        YACC = ar.get("YACC", [128, 2, T], F32)
        BACC = ar.get("BACC", [128, 2, T], BF16)
        HST = ar.get("HST", [128, 2, ns, 2, 64], F32)
        L2 = ar.get("L2", [128, 2, 256], F32)
        G2 = ar.get("G2", [128, 256], BF16)
        LR = ar.get("LR", [128, 6, T], BF16)
        DIFF = ar.get("DIFF", [128, T], F32)
        cc = ar.get("cc", [128, 8], F32)
        W4 = self.lin_common_bufs(ar, True, 4)
        fw.op("pool", lambda e: e.memset(YACC[:, :, :], 0.0), [], [YACC])
        wk = lambda nm, shape, dt=F32: ar.pool(nm, 2, shape, dt)
        THp, SIGp, Ap = wk("th", [128, 128]), wk("sig", [128, 2, 128]), wk("a", [128, 2, 128])
        T1p, T2p, T3p = wk("t1", [128, 128]), wk("t2", [128, 128]), ar.pool("t3", 1, [128, 128], F32)
        wk1 = lambda nm, shape, dt=F32: ar.pool(nm, 1, shape, dt)
        KAPp, KMp, PPp = wk1("kap", [128, 2, 128]), wk1("km", [128, 2, 128]), wk1("pp", [128, 2, 128])
        SQp = wk("sqb", [128, 128], BF16)
        GSp, EPp, EMp, EPPp = wk1("gs", [128, 2, 128]), wk1("ep", [128, 2, 128]), wk1("em", [128, 2, 128]), wk("epp", [128, 2, 128])
        EXCp = wk("exc", [128, 2])
        RKp, KBp, PBp = wk("RK", [128, 2, 2, 128], BF16), wk("KB", [128, 2, 128], BF16), wk("PB", [128, 2, 128], BF16)
        TOKp = wk("TOK", [128, 3, 256], BF16)

        fw.dma("sp", L2[:, :, :], self.lora2[l], writes=[L2])
        g2st = self.WST.next()
        fw.dma("sp", g2st[:, 0:256], self.rw_g2[l], writes=[g2st])
        self.cp("pool", G2[:, :], g2st[:, 0:256], [g2st], [G2])
        self.ts("dve", cc[:, 0:2], self.col(l, "ka", 0, 2), -1.0, 1.0, ALU.mult, ALU.add, [self.cols], [cc])
        self.ts("dve", cc[:, 2:4], self.col(l, "lnx_w", 0, 2), 8.0, None, ALU.mult, None, [self.cols], [cc])
        if G == "P":
            fw.op("pool", lambda e: e.memset(HST[:, :, :, :, :], 0.0), [], [HST])
        else:
            self.lin_state_io(HST, self.srw, None, True)

        def evA(c0):
            def f(jj, th, bk):
                ci = c0 + jj
                sl = slice(th * 512, (th + 1) * 512)
                if ci < 6:
                    self.cp(self.ev(), RAW[:, ci, sl], bk[:, :], [bk], [RAW])
                elif ci == 6:
                    self.act(SG[:, sl], bk[:, :], AF.Sigmoid, [bk], [SG])
                else:
                    self.cp(self.ev(), WA[:, ci - 7, sl], bk[:, :], [bk], [WA])
            return f
        for b0 in range(0, 9, 2):
            nch = min(2, 9 - b0)
            wb, wv = self.win_block(l, b0 * 128, nch * 128)
            self.proj_fm(wb, wv, nch, evA(b0))

        for d in range(2):
            for a in range(3):
                for jj in range(2):
                    ci = a * 2 + jj
                    self.lerp(d, LR[:, ci, :], RAW, RAW[:, ci, :], self.col(l, "mu_rkv%d" % d, ci, 1), DIFF, LR)
            self.lerp(d, WA[:, d, :], WA, WA[:, d, :], self.col(l, "mu_lora%d" % d, 0, 1), DIFF, WA)
            LW = WA
            for (si, ti, last) in self.tile_order(d):
                t0 = ti * 128
                ts_ = slice(t0, t0 + 128)
                th = THp.next()
                self.act(th[0:64, :], WA[0:64, d, ts_], AF.Tanh, [WA], [th])
                sig, av = SIGp.next(), Ap.next()
                for jj in range(2):
                    bk = fw.bank()
                    self.mm(bk[:, 0:128], L2[0:64, d, jj * 128:(jj + 1) * 128], th[0:64, :], True, True, [L2, th], [bk])
                    self.act(sig[:, jj, :], bk[:, 0:128], AF.Sigmoid, [bk, self.cols], [sig], bias=self.col(l, "w0_%d" % d, jj, 1))
                    bk = fw.bank()
                    self.mm(bk[:, 0:128], L2[64:128, d, jj * 128:(jj + 1) * 128], WA[64:128, d, ts_], True, True, [L2, WA], [bk])
                    self.act(av[:, jj, :], bk[:, 0:128], AF.Sigmoid, [bk, self.cols], [av], bias=self.col(l, "a0_%d" % d, jj, 1))
                kap, km, pp = KAPp.next(), KMp.next(), PPp.next()
                gs, ep, em, epp, exc = GSp.next(), EPp.next(), EMp.next(), EPPp.next(), EXCp.next()
                RK, KB, PB, TOK = RKp.next(), KBp.next(), PBp.next(), TOKp.next()
                mid = 63 if d == 0 else 64
                lastc = 127 if d == 0 else 0
                for jj in range(2):
                    rl, kl, vl = LR[:, jj, ts_], LR[:, 2 + jj, ts_], LR[:, 4 + jj, ts_]
                    e = self.el()
                    t1, t2 = T1p.next(), T2p.next()
                    self.ts(e, t1[:, :], kl, self.col(l, "kk", jj, 1), None, ALU.mult, None, [LR, self.cols], [t1])
                    sqb = SQp.next()
                    self.tt(e, sqb[:, :], t1[:, :], t1[:, :], ALU.mult, [t1], [sqb])
                    bk = fw.bank()
                    self.mm(bk[:, 0:128], self.blk_bf[:, :], sqb[:, :], True, True, [self.blk_bf, sqb], [bk])
                    self.rsqrt(t2[:, :], bk[:, 0:128], 1e-24, [bk], t2)
                    self.tt(e, kap[:, jj, :], t1[:, :], t2[:, :], ALU.mult, [t1, t2], [kap])
                    t3 = T3p.next()
                    self.ts(e, t3[:, :], av[:, jj, :], self.col(l, "ka", jj, 1), cc[:, jj:jj + 1], ALU.mult, ALU.add, [av, self.cols, cc], [t3])
                    self.tt(e, km[:, jj, :], kl, t3[:, :], ALU.mult, [LR, t3], [km])
                    self.tt(e, pp[:, jj, :], kap[:, jj, :], av[:, jj, :], ALU.mult, [kap, av], [pp])
                    t1b = T1p.next()
                    self.stt(e, t1b[:, :], rl, self.col(l, "rk", jj, 1), km[:, jj, :], ALU.mult, ALU.mult, [LR, self.cols, km], [t1b])
                    sqb2 = SQp.next()
                    self.cp(e, sqb2[:, :], t1b[:, :], [t1b], [sqb2])
                    bk = fw.bank()
                    self.mm(bk[:, 0:128], self.blk_bf[:, :], sqb2[:, :], True, True, [self.blk_bf, sqb2], [bk])
                    if d == 0:
                        self.tt("dve", BACC[:, jj, ts_], bk[:, 0:128], vl, ALU.mult, [bk, LR], [BACC])
                    else:
                        t2b = T2p.next()
                        self.tt("dve", t2b[:, :], bk[:, 0:128], vl, ALU.mult, [bk, LR], [t2b])
                        self.tt(e, BACC[:, jj, ts_], BACC[:, jj, ts_], t2b[:, :], ALU.add, [t2b, BACC], [BACC])
                    self.scan_tile(d, gs[:, jj, :], sig[:, jj, :], [sig], gs)
                self.cp("dve", exc[:, 0:2], gs[:, :, mid], [gs], [exc])
                gc = EPPp.next()
                for jj in range(2):
                    self.ts("dve", gc[:, jj, :], gs[:, jj, :], exc[:, jj:jj + 1], None, ALU.subtract, None, [gs, exc], [gc])
                self.act(ep[:, :, :], gc[:, :, :], AF.Exp, [gc], [ep], scale=-DECAY)
                self.act(em[:, :, :], gc[:, :, :], AF.Exp, [gc], [em], scale=DECAY)
                self.tt(self.el(), gs[:, :, :], gc[:, :, :], sig[:, :, :], ALU.subtract, [gc, sig], [gs])
                self.act(epp[:, :, :], gs[:, :, :], AF.Exp, [gs], [epp], scale=-DECAY)
                self.act(exc[:, 0:2], exc[:, 0:2], AF.Exp, [exc], [exc], scale=-DECAY)
                for jj in range(2):
                    e = self.el()
                    self.tt(e, RK[:, jj, 0, :], kap[:, jj, :], epp[:, jj, :], ALU.mult, [kap, epp], [RK])
                    self.tt(e, RK[:, jj, 1, :], LR[:, jj, ts_], ep[:, jj, :], ALU.mult, [LR, ep], [RK])
                    self.tt(e, KB[:, jj, :], km[:, jj, :], em[:, jj, :], ALU.mult, [km, em], [KB])
                    self.tt(e, PB[:, jj, :], pp[:, jj, :], em[:, jj, :], ALU.mult, [pp, em], [PB])
                bt = fw.bank()
                btv = bt.t.bitcast(BF16)
                for jj in range(2):
                    self.tp(btv[:, (0 * 2 + jj) * 128:(0 * 2 + jj + 1) * 128], LR[:, 4 + jj, ts_], self.ident_bf[:, :], [LR, self.ident_bf], [bt])
                    self.tp(btv[:, (1 * 2 + jj) * 128:(1 * 2 + jj + 1) * 128], KB[:, jj, :], self.ident_bf[:, :], [KB, self.ident_bf], [bt])
                    self.tp(btv[:, (2 * 2 + jj) * 128:(2 * 2 + jj + 1) * 128], PB[:, jj, :], self.ident_bf[:, :], [PB, self.ident_bf], [bt])
                self.cp("act", TOK[:, :, :], btv[:, 0:768].rearrange("p (a b) -> p a b", a=3), [bt], [TOK])
                shared = dict(RK=RK, KB=KB, PB=PB, TOK=TOK, expc=exc, expcb=exc, eplast=ep[:, :, lastc], eplastb=ep, YACC=YACC, HST=HST)
                gens = []
                for k, (jj, hp) in enumerate(((0, 0), (0, 1), (1, 0), (1, 1))):
                    Wk = dict(W4[k])
                    Wk.update(shared)
                    Wk["hst"] = HST[hp * 64:hp * 64 + 64, d, si, jj, :]
                    gens.append(self.lin_head(d, jj, hp, t0, Wk, True, [fw.banks[2 * k], fw.banks[2 * k + 1]]))
                self.run_gens(gens)
        if G == "P":
            self.lin_state_io(HST, None, self.rwst, False)
        self.dump("yacc_%s%d" % (G, l), YACC, YACC[:, :, :], [128, 2, T])
        self.dump("bacc_%s%d" % (G, l), BACC, BACC[:, :, :], [128, 2, T])
        blk = self.cst("blk")
        for jj in range(2):
            for th in range(2):
                sl = slice(th * 512, (th + 1) * 512)
                bk = fw.bank()
                self.mm(bk[:, :], blk, YACC[:, jj, sl], True, True, [self.cstb, YACC], [bk])
                yc = DIFF
                self.stt("dve", yc[:, sl], bk[:, :], -1.0 / 64, YACC[:, jj, sl], ALU.mult, ALU.add, [bk, YACC], [yc])
                self.tt("pool", WA[:, 0, sl], yc[:, sl], yc[:, sl], ALU.mult, [yc], [WA])
                bk2 = fw.bank()
                self.mm(bk2[:, :], blk, WA[:, 0, sl], True, True, [self.cstb, WA], [bk2])
                self.rsqrt(WA[:, 0, sl], bk2[:, :], 64 * LN_EPS, [bk2], WA)
                self.tt("pool", yc[:, sl], yc[:, sl], WA[:, 0, sl], ALU.mult, [yc, WA], [yc])
                self.ts("dve", yc[:, sl], yc[:, sl], cc[:, 2 + jj:3 + jj], self.col(l, "lnx_b", jj, 1), ALU.mult, ALU.add, [yc, cc, self.cols], [yc])
                self.tt("pool", yc[:, sl], yc[:, sl], BACC[:, jj, sl], ALU.add, [yc, BACC], [yc])
                bk3 = fw.bank()
                self.mm(bk3[:, :], G2[:, jj * 128:(jj + 1) * 128], SG[:, sl], True, True, [G2, SG], [bk3])
                self.tt("dve", self.MIX[jj][:, sl], yc[:, sl], bk3[:, :], ALU.mult, [yc, bk3], [self.MIX[jj]])

    def mixer_C(self):
        fw, l, G = self.fw, self.l, self.G
        ar = self.arena
        ar.reset()
        ns = len(self.seqs)
        Q = ar.get("Q", [128, 2, T], BF16)
        V = ar.get("V", [128, 2, T], BF16)
        GATE = ar.get("GATE", [128, 2, T], BF16)
        FR = ar.get("FR", [128, 2, 2, T], F32)
        YACC = ar.get("YACC", [128, 2, T], F32)
        HST = ar.get("HST", [128, 2, ns, 2, 64], F32)
        LFd = [ar.get("LF%d" % d, [128, 2, T], F32) for d in range(2)]
        KKd = [ar.get("KK%d" % d, [128, 2, T], BF16) for d in range(2)]
        LF = LFd[0]
        W8 = self.lin_common_bufs(ar, False, 8)
        fw.op("pool", lambda e: e.memset(YACC[:, :, :], 0.0), [], [YACC])
        hn8 = ar.get("hn8", [128, 2], F32)
        self.ts("dve", hn8[:, :], self.col(l, "hg_norm", 0, 2), 8.0, None, ALU.mult, None, [self.cols], [hn8])
        C = 32
        wk = lambda nm, shape, dt=F32: [ar.pool("%s_%d" % (nm, d), 2, shape, dt) for d in range(2)]
        GSp, GCp, EPp, EMp, EXCp = wk("gs", [128, 2, C]), wk("gc", [128, 2, C]), wk("ep", [128, 2, C]), wk("em", [128, 2, C]), wk("exc", [128, 2])
        RKp, KBp, TOKp = wk("RK", [128, 2, 2, C], BF16), wk("KB", [128, 2, C], BF16), wk("TOK", [128, 2, 256], BF16)
        if G == "P":
            fw.op("pool", lambda e: e.memset(HST[:, :, :, :, :], 0.0), [], [HST])
        else:
            self.lin_state_io(HST, self.shg, None, True)

        def evC(c0):
            def f(jj, th, bk):
                ci = c0 + jj
                sl = slice(th * 512, (th + 1) * 512)
                if ci < 2:
                    self.act(Q[:, ci, sl], bk[:, :], AF.Silu, [bk], [Q])
                elif ci < 4:
                    self.cp(self.ev(), V[:, ci - 2, sl], bk[:, :], [bk], [V])
                elif ci < 6:
                    self.act(GATE[:, ci - 4, sl], bk[:, :], AF.Sigmoid, [bk], [GATE])
                else:
                    self.cp(self.ev(), FR[:, (ci - 6) // 2, (ci - 6) % 2, sl], bk[:, :], [bk], [FR])
            return f
        for b0 in range(0, 10, 2):
            wb, wv = self.win_block(l, 1920 + b0 * 128, 256)
            self.proj_fm(wb, wv, 2, evC(b0))

        for d in range(2):
            for jj in range(2):
                self.act(LFd[d][:, jj, :], FR[:, d, jj, :], AF.Sigmoid, [FR], [LFd[d]])
                self.ts("dve", LFd[d][:, jj, :], LFd[d][:, jj, :], self.oml[:, d, l, jj:jj + 1], self.lb[:, d, l, jj:jj + 1], ALU.mult, ALU.add, [LFd[d], self.oml, self.lb], [LFd[d]])
                self.ts("pool", KKd[d][:, jj, :], LFd[d][:, jj, :], -1.0, 1.0, ALU.mult, ALU.add, [LFd[d]], [KKd[d]])
        for d in range(2):
            for jj in range(2):
                self.act(LFd[d][:, jj, :], LFd[d][:, jj, :], AF.Ln, [LFd[d]], [LFd[d]])
        orders = []
        for d in range(2):
            o = []
            for (si, ti, last) in self.tile_order(d):
                for sub in (range(128 // C) if d == 0 else range(128 // C - 1, -1, -1)):
                    o.append((si, ti * 128 + sub * C))
            orders.append(o)
        for step in range(len(orders[0])):
            gens = []
            for d in range(2):
                si, t0 = orders[d][step]
                ts_ = slice(t0, t0 + C)
                gs, gc, ep, em, exc = GSp[d].next(), GCp[d].next(), EPp[d].next(), EMp[d].next(), EXCp[d].next()
                RK, KB, TOK = RKp[d].next(), KBp[d].next(), TOKp[d].next()
                mid = C // 2 - 1 if d == 0 else C // 2
                lastc = C - 1 if d == 0 else 0
                for jj in range(2):
                    self.scan_tile(d, gs[:, jj, 0:C], LFd[d][:, jj, ts_], [LFd[d]], gs, C)
                self.cp("dve", exc[:, 0:2], gs[:, :, mid], [gs], [exc])
                for jj in range(2):
                    self.ts("dve", gc[:, jj, 0:C], gs[:, jj, 0:C], exc[:, jj:jj + 1], None, ALU.subtract, None, [gs, exc], [gc])
                self.act(ep[:, :, 0:C], gc[:, :, 0:C], AF.Exp, [gc], [ep])
                self.act(em[:, :, 0:C], gc[:, :, 0:C], AF.Exp, [gc], [em], scale=-1.0)
                self.act(exc[:, 0:2], exc[:, 0:2], AF.Exp, [exc], [exc])
                for jj in range(2):
                    e = self.el()
                    self.tt(e, RK[:, jj, 1, 0:C], Q[:, jj, ts_], ep[:, jj, 0:C], ALU.mult, [Q, ep], [RK])
                    self.tt(e, KB[:, jj, 0:C], KKd[d][:, jj, ts_], em[:, jj, 0:C], ALU.mult, [KKd[d], em], [KB])
                bt = fw.bank()
                btv = bt.t.bitcast(BF16)
                for jj in range(2):
                    self.tp(btv[0:C, jj * 128:(jj + 1) * 128], V[:, jj, ts_], self.ident_bf[:, :], [V, self.ident_bf], [bt])
                    self.tp(btv[0:C, (2 + jj) * 128:(3 + jj) * 128], KB[:, jj, 0:C], self.ident_bf[:, :], [KB, self.ident_bf], [bt])
                self.cp("act", TOK[0:C, :, :], btv[0:C, 0:512].rearrange("p (a b) -> p a b", a=2), [bt], [TOK])
                shared = dict(RK=RK, KB=KB, TOK=TOK, expc=exc, expcb=exc, eplast=ep[:, :, lastc], eplastb=ep, YACC=YACC, HST=HST)
                for k, (jj, hp) in enumerate(((0, 0), (0, 1), (1, 0), (1, 1))):
                    Wk = dict(W8[d * 4 + k])
                    Wk.update(shared)
                    Wk["hst"] = HST[hp * 64:hp * 64 + 64, d, si, jj, :]
                    gens.append(self.lin_head(d, jj, hp, t0, Wk, False, [fw.banks[d * 4 + k]], C))
            self.run_gens(gens)
        if G == "P":
            self.lin_state_io(HST, None, self.hgst, False)
        self.dump("yaccC_%s%d" % (G, l), YACC, YACC[:, :, :], [128, 2, T])
        self.dump("lfC_%s%d" % (G, l), LF, LF[:, :, :], [128, 2, T])
        blk = self.cst("blk")
        for jj in range(2):
            for th in range(2):
                sl = slice(th * 512, (th + 1) * 512)
                self.tt("pool", LF[:, 0, sl], YACC[:, jj, sl], YACC[:, jj, sl], ALU.mult, [YACC], [LF])
                bk = fw.bank()
                self.mm(bk[:, :], blk, LF[:, 0, sl], True, True, [self.cstb, LF], [bk])
                self.rsqrt(LF[:, 0, sl], bk[:, :], 64 * EPS, [bk], LF)
                self.tt("pool", LF[:, 0, sl], LF[:, 0, sl], YACC[:, jj, sl], ALU.mult, [LF, YACC], [LF])
                self.stt("dve", self.MIX[4 + jj][:, sl], LF[:, 0, sl], hn8[:, jj:jj + 1], GATE[:, jj, sl], ALU.mult, ALU.mult, [LF, hn8, GATE], [self.MIX[4 + jj]])

    def out_and_ffn(self):
        fw, l, G = self.fw, self.l, self.G
        ar = self.arena
        ar.reset()
        mc = self.mc
        X, H, MIX = self.X, self.H, self.MIX
        O = [ar.get("O%d" % c, [128, T], F32) for c in range(8)]
        WBIG = ar.get("WBIG", [128, 8, 1024], BF16)
        HID = [ar.get("HID%d" % c, [128, T], BF16) for c in range(8)]
        RL = ar.get("RL", [128, T], BF16)

        def add_residual(gco):
            rstd = self.rstd_of(O, lambda c: O[c][:, :], self.sq, self.rstd)
            for c in range(8):
                e = self.el()
                self.stt(e, O[c][:, :], O[c][:, :], gco[:, c:c + 1], rstd[:, :], ALU.mult, ALU.mult, [O[c], rstd, self.modc], [O[c]])
                self.tt(e, X[c][:, :], X[c][:, :], O[c][:, :], ALU.add, [O[c], X[c]], [X[c]])

        for q in range(4):
            st_ap = self.w_out[l, q * 256:(q + 1) * 256, :].rearrange("(k p) n -> p k n", p=128)
            stb, sv = self.load_block(st_ap, 2, 1024, cast=False)
            self.cp(self.fw.pick("cast", ("pool", "dve", "pool")), WBIG[:, 2 * q:2 * q + 2, :], sv, [stb], [WBIG])
        for oc in range(8):
            for th in range(2):
                bk = fw.bank()
                for kc in range(8):
                    self.mm(bk[:, :], WBIG[:, kc, oc * 128:(oc + 1) * 128], MIX[kc][:, th * 512:(th + 1) * 512], kc == 0, kc == 7, [WBIG, MIX[kc]], [bk])
                self.cp(self.ev(), O[oc][:, th * 512:(th + 1) * 512], bk[:, :], [bk], [O[oc]])
        add_residual(mc["gco1"])
        rstd = self.rstd_of(X, lambda c: X[c][:, :], self.sq, self.rstd)
        for c in range(8):
            tmp = self.sq.next()
            self.stt(self.el(), tmp[:, :], X[c][:, :], mc["coefA2"][:, c:c + 1], rstd[:, :], ALU.mult, ALU.mult, [X[c], rstd, self.modc], [tmp])
            self.act(H[c][:, :], tmp[:, :], AF.Identity, [tmp, self.modc], [H[c]], bias=mc["shift2"][:, c:c + 1])
        for q in range(4):
            for fb in range(4):
                wb, wv = self.load_block(self.ffn_w1[l, :, (q * 8 + fb * 2) * 128:(q * 8 + fb * 2 + 2) * 128].rearrange("(k p) n -> p k n", p=128), 8, 256)

                def evh(jj, th, bk, fb=fb):
                    sl = slice(th * 512, (th + 1) * 512)
                    self.act(RL[:, sl], bk[:, :], AF.Relu, [bk], [RL])
                    self.tt(self.el(), HID[fb * 2 + jj][:, sl], RL[:, sl], RL[:, sl], ALU.mult, [RL], [HID[fb * 2 + jj]])
                self.proj_fm(wb, wv, 2, evh)
            for hb in range(4):
                st_ap = self.ffn_w2[l, (q * 8 + hb * 2) * 128:(q * 8 + hb * 2 + 2) * 128, :].rearrange("(k p) n -> p k n", p=128)
                stb, sv = self.load_block(st_ap, 2, 1024, cast=False)
                self.cp(self.fw.pick("cast", ("pool", "dve", "pool")), WBIG[:, 2 * hb:2 * hb + 2, :], sv, [stb], [WBIG])
            for oc in range(8):
                for th in range(2):
                    sl = slice(th * 512, (th + 1) * 512)
                    bk = fw.bank()
                    for kc in range(8):
                        self.mm(bk[:, :], WBIG[:, kc, oc * 128:(oc + 1) * 128], HID[kc][:, sl], kc == 0, kc == 7, [WBIG, HID[kc]], [bk])
                    if q == 0:
                        self.cp(self.ev(), O[oc][:, sl], bk[:, :], [bk], [O[oc]])
                    else:
                        self.tt("dve", O[oc][:, sl], bk[:, :], O[oc][:, sl], ALU.add, [bk, O[oc]], [O[oc]])
        add_residual(mc["gco2"])

    def nat_latent(self, QT, KT):
        fw, l = self.fw, self.l
        ar = self.arena
        Vr = ar.get("Vr", [128, 16, 4, 128], BF16)
        CKf = ar.get("CKf", [128, 2, 256], F32)
        CK = ar.get("CK", [128, 2, 256], BF16)
        CVf = ar.get("CVf", [128, 2, 256], F32)
        CV = ar.get("CV", [128, 2, 4, 128], BF16)
        ET = ar.get("ET", [128, 4, 15, 64], F32)
        PTp = ar.pool("PT", 2, [128, 8, 64], BF16)
        PCp = ar.pool("PC", 2, [128, 2, 64], BF16)
        RDp = ar.pool("RD", 2, [128, 64], F32)
        fw.op("pool", lambda e: e.memset(Vr[0:64, :, :, 64:128], 1.0), [], [Vr])
        fw.op("pool", lambda e: e.memset(CV[:, :, :, 64:128], 1.0), [], [CV])
        fw.dma("sp", CKf[:, :, :], self.cnk[l].rearrange("j p t -> p j t"), writes=[CKf])
        self.cp("pool", CK[:, :, :], CKf[:, :, :], [CKf], [CK])
        fw.dma("sp", CVf[:, :, :], self.cnv[l].rearrange("(c p) f -> p c f", p=128), writes=[CVf])
        self.cp("pool", CV[:, :, :, 0:64], CVf[:, :, :].rearrange("p c (h d) -> p c h d", h=4), [CVf], [CV])
        ETr = ar.get("ETr", [128, 4, 15, 64], F32)
        src = bass.AP(tensor=self.rpbpad.t, offset=l * 4 * 15 * 127, ap=[[1, 64], [15 * 127, 4], [127, 15], [1, 64]])
        fw.dma("sp", ETr[0:64, :, :, :], src, writes=[ETr])
        flip = self.cst("flip", rows=64)
        etr2 = ETr[0:64, :, :, :].rearrange("p h r q -> p (h r q)")
        et2 = ET[0:64, :, :, :].rearrange("p h r q -> p (h r q)")
        for cb in range(8):
            bk = fw.bank()
            self.mm(bk[0:64, 0:480], flip, etr2[:, cb * 480:(cb + 1) * 480], True, True, [self.cstb, ETr], [bk])
            self.act(et2[:, cb * 480:(cb + 1) * 480], bk[0:64, 0:480], AF.Exp, [bk], [ET])
        nm = self.cst("natmask", rows=64)
        for h in range(4):
            self.tt("dve", ET[0:64, h, :, :], ET[0:64, h, :, :], nm.unsqueeze(1).to_broadcast([64, 15, 64]), ALU.mult, [ET, self.cstb], [ET])
        wb, wv = self.win_block(l, 1152, 256)
        self.proj_fm(wb, wv, 2, lambda jj, th, bk: self.cp(self.ev(), QT[:, jj, th * 512:(th + 1) * 512], bk[:, :], [bk], [QT]))
        wb, wv = self.win_block(l, 1408, 256)
        self.proj_fm(wb, wv, 2, lambda jj, th, bk: self.cp(self.ev(), KT[:, jj, th * 512:(th + 1) * 512], bk[:, :], [bk], [KT]))
        wb, wv = self.win_block(l, 1664, 256)
        self.proj_tm(wb, wv, 256, lambda ti, bk: self.cp(self.ev(), Vr[0:64, ti, :, 0:64], bk[0:64, 0:256].rearrange("p (h d) -> p h d", h=4), [bk], [Vr]), tile_tokens=64)
        def unit(slot, r, h):
            rs = min(max(r - 4, 0), 8)
            roff0 = rs - r + 7
            qs = slice(r * 64, (r + 1) * 64)
            j, hp = h // 2, h % 2
            P = slice(hp * 64, hp * 64 + 64)
            PT, PC, rd = PTp.bufs[slot], PCp.bufs[slot], RDp.bufs[slot]
            bl, bc, bo = fw.banks[slot * 4], fw.banks[slot * 4 + 1], fw.banks[slot * 4 + 2]
            for a in range(8):
                kr = rs + a
                self.mm(bl[0:64, a * 64:(a + 1) * 64], KT[P, j, kr * 64:(kr + 1) * 64], QT[P, j, qs], True, True, [KT, QT], [bl])
            for c in range(2):
                self.mm(bc[:, c * 64:(c + 1) * 64], CK[P, j, c * 128:(c + 1) * 128], QT[P, j, qs], True, True, [CK, QT], [bc])
            yield
            self.act(PT[0:64, :, :], bl[0:64, :].rearrange("p (a q) -> p a q", a=8), AF.Exp, [bl], [PT], scale=0.125)
            self.act(PC[:, :, :], bc[:, 0:128].rearrange("p (c q) -> p c q", c=2), AF.Exp, [bc], [PC], scale=0.125)
            yield
            self.tt("dve", PT[0:64, :, :], PT[0:64, :, :], ET[0:64, h, roff0:roff0 + 8, :], ALU.mult, [PT, ET], [PT])
            yield
            for a in range(8):
                self.mm(bo[:, 0:64], Vr[0:64, rs + a, h, :], PT[0:64, a, :], a == 0, False, [Vr, PT], [bo])
            for c in range(2):
                self.mm(bo[:, 0:64], CV[:, c, h, :], PC[:, c, :], False, c == 1, [CV, PC], [bo])
            yield
            self.fw.op("dve", lambda e: e.reciprocal(rd[0:64, :], bo[64:128, 0:64]), [bo], [rd])
            self.tt("dve", self.MIX[2 + j][P, qs], bo[0:64, 0:64], rd[0:64, :], ALU.mult, [bo, rd], [self.MIX[2 + j]])
        self.run_slots([(r_, h_) for r_ in range(16) for h_ in range(4)], unit, 2)

    def swa_latent(self, QT, KT, esink):
        fw, l = self.fw, self.l
        ar = self.arena
        ROPE = ar.get("ROPE", [128, 2, T], F32)
        TQ = ar.get("TQ", [128, 2, T], F32)
        KTf = ar.get("KTf", [128, T], F32)
        Vt = ar.get("Vt", [128, NT, 2, 128], BF16)
        CKf = ar.get("CKf", [128, 2, 256], F32)
        CK = ar.get("CK", [128, 2, 256], BF16)
        CVf = ar.get("CVf", [128, 2, 128], F32)
        CV = ar.get("CV", [128, 2, 2, 128], BF16)
        TMPp = ar.pool("TMP", 2, [128, 512], F32)
        PTp = ar.pool("PT", 2, [128, 3, 128], BF16)
        PCp = ar.pool("PC", 2, [128, 2, 128], BF16)
        RDp = ar.pool("RD", 2, [128, 128], F32)
        fw.dma("sp", ROPE[:, :, :], self.rope[:, :, :], writes=[ROPE])
        fw.op("pool", lambda e: e.memset(Vt[:, :, :, 64:128], 1.0), [], [Vt])
        fw.op("pool", lambda e: e.memset(CV[:, :, :, 64:128], 1.0), [], [CV])
        fw.dma("sp", CKf[:, :, :], self.csk[l].rearrange("g p t -> p g t"), writes=[CKf])
        self.cp("pool", CK[:, :, :], CKf[:, :, :], [CKf], [CK])
        fw.dma("sp", CVf[:, :, :], self.csv[l].rearrange("(c p) f -> p c f", p=128), writes=[CVf])
        self.cp("pool", CV[:, :, :, 0:64], CVf[:, :, :].rearrange("p c (h d) -> p c h d", h=2), [CVf], [CV])
        wb, wv = self.win_block(l, 3200, 256)
        self.proj_fm(wb, wv, 2, lambda jj, th, bk: self.tt("dve", TQ[:, jj, th * 512:(th + 1) * 512], bk[:, :], ROPE[:, 0, th * 512:(th + 1) * 512], ALU.mult, [bk, ROPE], [TQ]))
        wb, wv = self.win_block(l, 0, 256, src=self.w_in_sw)

        def evq(jj, th, bk):
            sl = slice(th * 512, (th + 1) * 512)
            tmp = TMPp.next()
            self.tt("dve", tmp[:, :], bk[:, :], ROPE[:, 1, sl], ALU.mult, [bk, ROPE], [tmp])
            self.tt("pool", QT[:, jj, sl], tmp[:, :], TQ[:, jj, sl], ALU.add, [tmp, TQ], [QT])
        self.proj_fm(wb, wv, 2, evq)
        wb, wv = self.win_block(l, 3456, 256)
        self.proj_fm(wb, wv, 1, lambda jj, th, bk: self.tt("dve", KTf[:, th * 512:(th + 1) * 512], bk[:, :], ROPE[:, 0, th * 512:(th + 1) * 512], ALU.mult, [bk, ROPE], [KTf]))
        self.proj_tm(wb, wv[:, :, 128:256], 128, lambda ti, bk: self.cp(self.ev(), Vt[:, ti, :, 0:64], bk[:, 0:128].rearrange("p (h d) -> p h d", h=2), [bk], [Vt]))
        wb, wv = self.win_block(l, 256, 128, src=self.w_in_sw)

        def evk(jj, th, bk):
            sl = slice(th * 512, (th + 1) * 512)
            tmp = TMPp.next()
            self.tt("dve", tmp[:, :], bk[:, :], ROPE[:, 1, sl], ALU.mult, [bk, ROPE], [tmp])
            self.tt("pool", KTf[:, sl], tmp[:, :], KTf[:, sl], ALU.add, [tmp, KTf], [KTf])
            for gg in range(2):
                for hp in range(2):
                    self.cp(self.ev(), KT[hp * 64:hp * 64 + 64, gg, sl], KTf[gg * 64:gg * 64 + 64, sl], [KTf], [KT])
        self.proj_fm(wb, wv, 1, evk)
        def unit(slot, n, h):
            qs = slice(n * 128, (n + 1) * 128)
            kbl = [kb for kb in (n - 1, n, n + 1) if 0 <= kb < NT]
            nk = len(kbl)
            j, hp = h // 2, h % 2
            g = j
            P = slice(hp * 64, hp * 64 + 64)
            PT, PC, rd = PTp.bufs[slot], PCp.bufs[slot], RDp.bufs[slot]
            bl, bc, bo = fw.banks[slot * 4], fw.banks[slot * 4 + 1], fw.banks[slot * 4 + 2]
            for i, kb in enumerate(kbl):
                self.mm(bl[:, i * 128:(i + 1) * 128], KT[P, g, kb * 128:(kb + 1) * 128], QT[P, j, qs], True, True, [KT, QT], [bl])
            for c in range(2):
                self.mm(bc[:, c * 128:(c + 1) * 128], CK[P, g, c * 128:(c + 1) * 128], QT[P, j, qs], True, True, [CK, QT], [bc])
            yield
            self.act(PT[:, 0:nk, :], bl[:, 0:nk * 128].rearrange("p (a q) -> p a q", a=nk), AF.Exp, [bl], [PT], scale=0.125)
            self.act(PC[:, :, :], bc[:, 0:256].rearrange("p (c q) -> p c q", c=2), AF.Exp, [bc], [PC], scale=0.125)
            yield
            for i, kb in enumerate(kbl):
                if kb == n - 1:
                    self.tt("dve", PT[:, i, :], PT[:, i, :], self.cst("mprev"), ALU.mult, [PT, self.cstb], [PT])
                elif kb == n + 1:
                    self.tt("pool", PT[:, i, :], PT[:, i, :], self.cst("mnext"), ALU.mult, [PT, self.cstb], [PT])
            yield
            for i, kb in enumerate(kbl):
                self.mm(bo[:, 0:128], Vt[:, kb, g, :], PT[:, i, :], i == 0, False, [Vt, PT], [bo])
            for c in range(2):
                self.mm(bo[:, 0:128], CV[:, c, g, :], PC[:, c, :], False, c == 1, [CV, PC], [bo])
            yield
            self.cp("dve", rd[0:64, :], bo[64:128, 0:128], [bo], [rd])
            self.ts("dve", rd[0:64, :], rd[0:64, :], esink[0:64, h:h + 1], None, ALU.add, None, [rd, esink], [rd])
            self.fw.op("dve", lambda e: e.reciprocal(rd[0:64, :], rd[0:64, :]), [rd], [rd])
            self.tt("dve", self.MIX[6 + j][P, qs], bo[0:64, 0:128], rd[0:64, :], ALU.mult, [bo, rd], [self.MIX[6 + j]])
        self.run_slots([(n_, h_) for n_ in range(NT) for h_ in range(4)], unit, 2)


_PROG_CACHE = {}


def _get_prog(debug=(), layers=DEPTH, groups=("P", "S")):
    key = (tuple(sorted(debug)), layers, tuple(groups))
    if key not in _PROG_CACHE:
        _PROG_CACHE[key] = Prog(debug, layers, groups)
    return _PROG_CACHE[key]


def _prep_inputs(inp, layers=DEPTH):
    f = lambda k: np.ascontiguousarray(np.asarray(inp[k], dtype=np.float32))
    cst, rope = _build_consts()
    cols = _build_cols(inp)
    w_in = f("w_in")
    idx = np.concatenate([3200 + h * 64 + ROPE_PERM for h in range(4)] + [3456 + h * 64 + ROPE_PERM for h in range(2)])
    w_in_sw = np.ascontiguousarray(w_in[:, :, idx])
    lora2 = np.ascontiguousarray(np.concatenate([f("rw_w2").transpose(0, 2, 1, 3), f("rw_a2").transpose(0, 2, 1, 3)], axis=1))
    rpb = f("nat_rpb")
    rpbpad = np.zeros((DEPTH, 4, 15, 127), np.float32)
    rpbpad[..., 48:79] = rpb[..., ::-1]
    NL = layers
    shared = dict(cols=cols, cst=cst, rope=rope, mod_w=f("mod_w")[:NL], w_in=w_in[:NL], w_in_sw=w_in_sw[:NL], w_out=f("w_out")[:NL],
                  ffn_w1=f("ffn_w1")[:NL], ffn_w2=f("ffn_w2")[:NL], lora2=lora2, rw_g2=f("rw_g2"), rpbpad=rpbpad)
    xp, xs = f("x_prompt"), f("x_sample")
    c, c_ctx = f("c"), f("c_ctx")
    cn, cs = f("cache_nat_kv"), f("cache_swa_kv")
    srw, shg = f("state_rwkv"), f("state_hgrn")
    maps = []
    for core in range(8):
        s = core % 2
        m = dict(shared)
        xpc = xp[core * 4:(core + 1) * 4].reshape(T, D)
        m["xT_p"] = np.ascontiguousarray(xpc.T.reshape(8, 128, T).transpose(1, 0, 2))
        m["xT_s"] = np.ascontiguousarray(xs[s].T.reshape(8, 128, T).transpose(1, 0, 2))
        cond = np.stack([c_ctx, c[s]], axis=1)
        m["condT"] = np.ascontiguousarray(cond.reshape(8, 128, 2).transpose(1, 0, 2))
        nk = cn[s, :, 0].reshape(DEPTH, 256, 256)
        m["cnk"] = np.ascontiguousarray(nk.transpose(0, 2, 1).reshape(DEPTH, 2, 128, 256))
        m["cnv"] = np.ascontiguousarray(cn[s, :, 1].reshape(DEPTH, 256, 256))
        sk = cs[s, :, 0].reshape(DEPTH, 256, 2, 64).transpose(0, 2, 3, 1)
        m["csk"] = np.ascontiguousarray(np.concatenate([sk, sk], axis=2))
        m["csv"] = np.ascontiguousarray(cs[s, :, 1].reshape(DEPTH, 256, 128))
        st = srw[s].transpose(0, 1, 2, 4, 3)
        m["srw"] = np.ascontiguousarray(st.reshape(DEPTH, 2, 2, 2, 64, 64).transpose(0, 1, 3, 4, 2, 5).reshape(DEPTH, 2, 128, 2, 64))
        sh = shg[s]
        m["shg"] = np.ascontiguousarray(sh.reshape(DEPTH, 2, 2, 2, 64, 64).transpose(0, 1, 3, 4, 2, 5).reshape(DEPTH, 2, 128, 2, 64))
        maps.append(m)
    return maps


def _assemble(results):
    B = 32
    y_p = np.zeros((B, 256, D), np.float32)
    y_s = np.zeros((2, 1024, D), np.float32)
    natkv = np.zeros((B, DEPTH, 2, 256, 4, 64), np.float32)
    swakv = np.zeros((B, DEPTH, 2, 256, 2, 64), np.float32)
    rwst = np.zeros((B, DEPTH, 2, 4, 64, 64), np.float32)
    hgst = np.zeros((B, DEPTH, 2, 4, 64, 64), np.float32)
    for core in range(8):
        r = results[core]
        bs = slice(core * 4, core * 4 + 4)
        yT = r["yT_p"].transpose(1, 0, 2).reshape(D, T)
        y_p[bs] = yT.T.reshape(4, 256, D)
        if core < 2:
            y_s[core] = r["yT_s"].transpose(1, 0, 2).reshape(D, T).T
        nk = r["natk"].reshape(DEPTH, 256, 4, 256)
        natkv[bs, :, 0] = nk.transpose(2, 0, 3, 1).reshape(4, DEPTH, 256, 4, 64)
        nv = r["natv"].reshape(DEPTH, 4, 256, 256)
        natkv[bs, :, 1] = nv.transpose(1, 0, 2, 3).reshape(4, DEPTH, 256, 4, 64)
        sk = r["swak"].reshape(DEPTH, 128, 4, 256)
        swakv[bs, :, 0] = sk.transpose(2, 0, 3, 1).reshape(4, DEPTH, 256, 2, 64)
        sv = r["swav"].reshape(DEPTH, 4, 256, 128)
        swakv[bs, :, 1] = sv.transpose(1, 0, 2, 3).reshape(4, DEPTH, 256, 2, 64)
        rs = r["rwst"].reshape(DEPTH, 2, 4, 2, 64, 2, 64)
        rwst[bs] = rs.transpose(2, 0, 1, 5, 3, 6, 4).reshape(4, DEPTH, 2, 4, 64, 64)
        hs = r["hgst"].reshape(DEPTH, 2, 4, 2, 64, 2, 64)
        hgst[bs] = hs.transpose(2, 0, 1, 5, 3, 4, 6).reshape(4, DEPTH, 2, 4, 64, 64)
    return (y_p, y_s, natkv, swakv, rwst, hgst)


def kernel(**inputs):
    prog = _get_prog()
    maps = _prep_inputs(inputs)
    res = run_bass_kernel_spmd(prog.nc, maps, core_ids=list(range(8)))
    return _assemble(res.results)
```

```python
import numpy as np
import concourse.bass as bass
import concourse.mybir as mybir
from concourse.bass_utils import run_bass_kernel_spmd

F32 = mybir.dt.float32
BF16 = mybir.dt.bfloat16
AF = mybir.ActivationFunctionType
ALU = mybir.AluOpType

DEPTH = 4
D = 1024
T = 1024
NT = 8
D_IN = 3712
DECAY = 0.6065306597126334
EPS = 1e-6
LN_EPS = 64e-5


class Buf:
    __slots__ = ("name", "t", "lw", "rd", "wsem", "wcnt", "rsem", "rcnt")

    def __init__(self, name, t):
        self.name = name
        self.t = t
        self.lw = None
        self.rd = []
        self.wsem = None
        self.wcnt = 0
        self.rsem = None
        self.rcnt = 0

    def __getitem__(self, idx):
        return self.t[idx]


class FW:
    ENG = ("pe", "dve", "act", "pool", "sp")

    def __init__(self, nc):
        self.nc = nc
        self.eng = {"pe": nc.tensor, "dve": nc.vector, "act": nc.scalar, "pool": nc.gpsimd, "sp": nc.sync}
        self.sem = {}
        self.cnt = {}
        for e in ("pe", "dve", "act", "pool"):
            self.sem[e] = nc.alloc_semaphore(name="s_" + e)
            self.cnt[e] = 0
        self.seen = {e: {} for e in self.ENG}
        self.ninst = 0
        self.out_dma = []
        self.dma_keys = {}
        self.rr = {}
        self.banks = None
        self.bank_i = 0
        self.dcnt = {}
        self.free_dsems = []

    def sb(self, name, shape, dtype=F32):
        return Buf(name, self.nc.alloc_sbuf_tensor(name, list(shape), dtype))

    def ps(self, name, shape, dtype=F32):
        return Buf(name, self.nc.alloc_psum_tensor(name, list(shape), dtype))

    def dram(self, name, shape, dtype=F32, kind="Internal"):
        return Buf(name, self.nc.dram_tensor(name, list(shape), dtype, kind=kind))

    def bank(self):
        if self.banks is None:
            self.banks = [self.ps("bank%d" % i, [128, 512]) for i in range(8)]
        b = self.banks[self.bank_i % 8]
        self.bank_i += 1
        return b

    def pick(self, key, engs):
        i = self.rr.get(key, 0)
        self.rr[key] = i + 1
        return engs[i % len(engs)]

    def _need(self, e, deps):
        eng = self.eng[e]
        seen = self.seen[e]
        best = {}
        for d in deps:
            if d is None:
                continue
            k, v = d
            if best.get(k, 0) < v:
                best[k] = v
        for k, v in best.items():
            if seen.get(k, 0) >= v:
                continue
            if e == "pe" and k == "pe":
                continue
            seen[k] = v
            eng.wait_ge(self.sem[k], v)

    @staticmethod
    def _deps(reads, writes, skip_dma_waw=False):
        deps = []
        for b in reads:
            deps.append(b.lw)
        for b in writes:
            if not (skip_dma_waw and b.lw is not None and isinstance(b.lw[0], tuple)):
                deps.append(b.lw)
            deps.extend(b.rd)
        return deps

    def op(self, e, fn, reads=(), writes=()):
        self._need(e, self._deps(reads, writes))
        ins = fn(self.eng[e])
        self.cnt[e] += 1
        v = self.cnt[e]
        ins.then_inc(self.sem[e], 1)
        for b in writes:
            b.lw = (e, v)
            b.rd = []
        for b in reads:
            if b not in writes:
                b.rd.append((e, v))
        self.ninst += 1
        return ins

    def dma(self, q, out, in_, reads=(), writes=(), part=False, is_output=False):
        self._need(q, self._deps(reads, writes, skip_dma_waw=part))
        ins = self.eng[q].dma_start(out=out, in_=in_)
        if writes:
            b = writes[0]
            if b.wsem is None:
                b.wsem = self._dsem()
            key = b.wsem
        else:
            b = reads[0]
            if b.rsem is None:
                b.rsem = self._dsem()
            key = b.rsem
        self.dcnt[key] += 16
        v = self.dcnt[key]
        ins.then_inc(self.sem[key], 16)
        self.dma_keys[key] = v
        for w in writes:
            w.lw = (key, v)
            w.rd = []
        for r in reads:
            r.rd.append((key, v))
        if is_output:
            self.out_dma.append((key, v))
        self.ninst += 1
        return ins

    def _dsem(self):
        if self.free_dsems:
            return self.free_dsems.pop()
        key = ("d", len(self.dcnt))
        self.sem[key] = self.nc.alloc_semaphore(name="dsem%d" % len(self.dcnt))
        self.dcnt[key] = 0
        return key

    def barrier(self):
        deps = [(e, self.cnt[e]) for e in ("pe", "dve", "act", "pool") if self.cnt[e]]
        deps += list(self.dma_keys.items())
        for e in self.ENG:
            self._need(e, deps)

    def finish(self):
        self.barrier()


class Arena:
    def __init__(self, fw, nwords):
        self.fw = fw
        self.nwords = nwords
        self.t32 = fw.nc.alloc_sbuf_tensor("arena", [128, nwords], F32)
        self.t16 = self.t32.bitcast(BF16)
        self.off = 0
        self.peak = 0
        self.made = []

    def reset(self):
        self.fw.barrier()
        self.off = 0
        for b in self.made:
            for k in (b.wsem, b.rsem):
                if k is not None:
                    self.fw.free_dsems.append(k)
            b.wsem = b.rsem = None
        self.made = []

    def get(self, name, shape, dtype=F32):
        n = int(np.prod(shape[1:]))
        words = n if dtype == F32 else (n + 1) // 2
        assert self.off + words <= self.nwords, (name, self.off, words, self.nwords)
        if dtype == F32:
            ap = self.t32[:, self.off:self.off + n]
        else:
            ap = self.t16[:, 2 * self.off:2 * self.off + n]
        if len(shape) > 2:
            names = "abcdefg"[:len(shape) - 1]
            kw = {names[i]: shape[1 + i] for i in range(len(shape) - 2)}
            ap = ap.rearrange("p (%s) -> p %s" % (" ".join(names), " ".join(names)), **kw)
        if shape[0] < 128:
            ap = ap[0:shape[0]]
        self.off += words
        self.peak = max(self.peak, self.off)
        b = Buf(name, ap)
        self.made.append(b)
        return b

    def pool(self, name, n, shape, dtype=F32):
        return Rot([self.get("%s%d" % (name, i), shape, dtype) for i in range(n)])


class Rot:
    def __init__(self, bufs):
        self.bufs = bufs
        self.i = 0

    def next(self):
        b = self.bufs[self.i % len(self.bufs)]
        self.i += 1
        return b


class ColTable:
    def __init__(self):
        self.idx = {}
        self.cols = []
        self.n = 0

    def add(self, name, per_layer_vecs):
        k = len(per_layer_vecs[0]) // 128
        self.idx[name] = self.n
        self.cols.append(np.stack([np.asarray(v, np.float32).reshape(k, 128) for v in per_layer_vecs]))
        self.n += k

    def build(self):
        a = np.concatenate(self.cols, axis=1)
        return np.ascontiguousarray(a.transpose(2, 0, 1))


def _col_names():
    ct = ColTable()
    z = lambda n: [np.zeros(n, np.float32)] * DEPTH
    for nm, n in COL_SPEC:
        ct.add(nm, z(n))
    return ct.idx, ct.n


COL_SPEC = [("norm_g", 4096), ("mod_b", 6144),
            ("mu_rkv0", 768), ("mu_rkv1", 768), ("mu_lora0", 128), ("mu_lora1", 128),
            ("w0_0", 256), ("w0_1", 256), ("a0_0", 256), ("a0_1", 256),
            ("kk", 256), ("ka", 256), ("rk", 256), ("lnx_w", 256), ("lnx_b", 256),
            ("lbl0", 1024), ("lbl1", 1024), ("hg_norm", 256), ("sink", 512)]
COLI, NCOL = _col_names()


def _build_cols(inp):
    ct = ColTable()
    L = range(DEPTH)
    g = lambda k: np.asarray(inp[k], np.float32)
    vals = {
        "norm_g": [g("norm_g")[l].reshape(-1) for l in L],
        "mod_b": [g("mod_b")[l] for l in L],
        "mu_rkv0": [g("rw_mu_rkv")[l, 0].reshape(-1) for l in L],
        "mu_rkv1": [g("rw_mu_rkv")[l, 1].reshape(-1) for l in L],
        "mu_lora0": [g("rw_mu_lora")[l, 0].reshape(-1) for l in L],
        "mu_lora1": [g("rw_mu_lora")[l, 1].reshape(-1) for l in L],
        "w0_0": [g("rw_w0")[l, 0] for l in L], "w0_1": [g("rw_w0")[l, 1] for l in L],
        "a0_0": [g("rw_a0")[l, 0] for l in L], "a0_1": [g("rw_a0")[l, 1] for l in L],
        "kk": [g("rw_kk")[l] for l in L], "ka": [g("rw_ka")[l] for l in L],
        "rk": [g("rw_rk")[l].reshape(-1) for l in L],
        "lnx_w": [g("rw_lnx_w")[l] for l in L], "lnx_b": [g("rw_lnx_b")[l] for l in L],
        "lbl0": [g("hg_lb_logits")[0].reshape(-1) for l in L],
        "lbl1": [g("hg_lb_logits")[1].reshape(-1) for l in L],
        "hg_norm": [g("hg_norm")[l] for l in L],
        "sink": [np.repeat(g("swa_sink")[l], 128) for l in L],
    }
    for nm, n in COL_SPEC:
        assert len(vals[nm][0]) == n, nm
        ct.add(nm, vals[nm])
    return ct.build()


CST_SPEC = [("ident", 128), ("ones", 128), ("blk", 128),
            ("g4_0", 512), ("g4_1", 512), ("mb_0", 128), ("mb_1", 128),
            ("reset", 256), ("natmask", 64),
            ("mprev", 128), ("mnext", 128), ("flip", 64)]
CSTI = {}
_o = 0
for _n, _w in CST_SPEC:
    CSTI[_n] = (_o, _w)
    _o += _w
NCST = _o


def _build_consts():
    c = np.zeros((128, NCST), np.float32)

    def put(nm, a):
        o, w = CSTI[nm]
        c[:a.shape[0], o:o + w] = a
    i = np.arange(128)
    put("ident", np.eye(128))
    put("ones", np.ones((128, 128)))
    put("blk", ((i[:, None] // 64) == (i[None, :] // 64)).astype(np.float32))
    for d in (0, 1):
        if d == 0:
            strictT = (i[:, None] < i[None, :]).astype(np.float32)
            inclT = (i[:, None] <= i[None, :]).astype(np.float32)
        else:
            strictT = (i[:, None] > i[None, :]).astype(np.float32)
            inclT = (i[:, None] >= i[None, :]).astype(np.float32)
        put("g4_%d" % d, np.concatenate([-strictT, inclT, strictT, inclT], axis=1))
        put("mb_%d" % d, -strictT.T)
    r = np.ones((128, 256), np.float32)
    r[:, 0] = 0
    r[:, 128] = 0
    put("reset", r)
    t = np.arange(1024)
    row = (t // 64).astype(np.float32)
    col = (t % 64).astype(np.float32)
    inv = (1.0 / (10000.0 ** (np.arange(16, dtype=np.float32) / 16))).astype(np.float32)
    C = np.zeros((64, 1024), np.float32)
    S = np.zeros((64, 1024), np.float32)
    for base, pos in ((0, row), (32, col)):
        ang = (pos[None, :] * inv[:, None]).astype(np.float32)
        C[base:base + 16] = np.cos(ang)
        C[base + 16:base + 32] = np.cos(ang)
        S[base:base + 16] = -np.sin(ang)
        S[base + 16:base + 32] = np.sin(ang)
    rope = np.stack([np.concatenate([C, C], axis=0), np.concatenate([S, S], axis=0)], axis=1)
    kc = np.arange(64)[:, None]
    qc = np.arange(64)[None, :]
    ws = np.clip(qc - 8, 0, 48)
    put("natmask", ((kc >= ws) & (kc < ws + 16)).astype(np.float32))
    put("mprev", (i[:, None] >= i[None, :]).astype(np.float32))
    put("mnext", (i[:, None] <= i[None, :]).astype(np.float32))
    put("flip", np.eye(64)[::-1].copy())
    return c, np.ascontiguousarray(rope.astype(np.float32))


ROPE_PERM = np.concatenate([np.arange(16, 32), np.arange(0, 16), np.arange(48, 64), np.arange(32, 48)])


class Prog:
    def __init__(self, debug=(), layers=DEPTH, groups=("P", "S")):
        self.debug = set(debug)
        self.layers = layers
        self.groups = groups
        self.nc = bass.Bass("TRN2", target_bir_lowering=False)
        self.fw = FW(self.nc)
        self.dbg_outs = {}
        self.build()

    def mm(self, out, lhsT, rhs, start, stop, reads, writes):
        return self.fw.op("pe", lambda e: e.matmul(out, lhsT, rhs, start=start, stop=stop), reads, writes)

    def tp(self, out, in_, ident, reads, writes):
        return self.fw.op("pe", lambda e: e.transpose(out, in_, ident), reads, writes)

    def act(self, out, in_, func, reads, writes, bias=None, scale=1.0):
        if bias is None:
            return self.fw.op("act", lambda e: e.activation(out, in_, func, scale=scale), reads, writes)
        return self.fw.op("act", lambda e: e.activation(out, in_, func, bias=bias, scale=scale), reads, writes)

    def tt(self, eng, out, in0, in1, op, reads, writes):
        return self.fw.op(eng, lambda e: e.tensor_tensor(out, in0, in1, op), reads, writes)

    def ts(self, eng, out, in0, s1, s2, op0, op1, reads, writes):
        if s2 is None:
            return self.fw.op(eng, lambda e: e.tensor_scalar(out, in0, s1, None, op0), reads, writes)
        return self.fw.op(eng, lambda e: e.tensor_scalar(out, in0, s1, s2, op0, op1), reads, writes)

    def stt(self, eng, out, in0, scalar, in1, op0, op1, reads, writes):
        eng = "dve"
        return self.fw.op(eng, lambda e: e.scalar_tensor_tensor(out, in0, scalar, in1, op0, op1), reads, writes)

    def rsqrt(self, out, in_, addc, reads, wbuf):
        self.act(out, in_, AF.Ln, reads + [self.epsb], [wbuf], bias=self.epsc(addc))
        self.act(out, out, AF.Exp, [wbuf], [wbuf], scale=-0.5)

    def epsc(self, v):
        i = self.eps_vals.index(v)
        return self.epsb[:, i:i + 1]

    def cp(self, eng, out, in_, reads, writes):
        if eng == "act":
            return self.fw.op("act", lambda e: e.copy(out, in_), reads, writes)
        return self.fw.op(eng, lambda e: e.tensor_copy(out, in_), reads, writes)

    def ev(self):
        return self.fw.pick("ev", ("dve", "act"))

    def el(self):
        return self.fw.pick("el", ("dve", "pool"))

    def dump(self, name, buf, ap, shape, dtype=F32):
        if name not in self.debug:
            return
        o = self.fw.dram("dbg_" + name, list(shape), dtype, kind="ExternalOutput")
        self.dbg_outs[name] = o
        self.fw.dma("sp", o[:], ap, reads=[buf], is_output=True)

    def col(self, l, name, j=0, n=1):
        i = COLI[name] + j
        return self.cols[:, l, i:i + n]

    def cst(self, name, lo=0, hi=None, rows=128):
        o, w = CSTI[name]
        hi = w if hi is None else hi
        return self.cstb[0:rows, o + lo:o + hi]

    def load_block(self, src_ap, a, b, cast=True):
        fw = self.fw
        n = a * b
        st = self.WST.next()
        sview = st[:, 0:n].rearrange("p (a b) -> p a b", a=a)
        fw.dma("sp", sview, src_ap, writes=[st])
        self.mod_tick()
        if not cast:
            return st, sview
        wb = self.WBF.next()
        wview = wb[:, 0:n].rearrange("p (a b) -> p a b", a=a)
        self.cp(self.fw.pick("cast", self.cast_engs), wb[:, 0:n], st[:, 0:n], [st], [wb])
        return wb, wview

    def win_block(self, l, c0, ncol, src=None):
        src = self.w_in if src is None else src
        ap = src[l, :, c0:c0 + ncol].rearrange("(k p) n -> p k n", p=128)
        return self.load_block(ap, 8, ncol)

    def mod_tick(self, force=False):
        if not self.mod_pending:
            return
        if self.in_mod:
            return
        self.in_mod = True
        l, kc, third = self.mod_pending.pop(0)
        fw = self.fw
        st = self.WST.next()
        fw.dma("sp", st[:, 0:2048], self.mod_w[l, kc * 128:(kc + 1) * 128, third * 2048:(third + 1) * 2048], writes=[st])
        bk = fw.bank()
        for j in range(16):
            self.mm(bk[:, 2 * j:2 * j + 2], st[:, j * 128:(j + 1) * 128], self.silu[:, kc, :], True, True, [st, self.silu], [bk])
        dst = self.modacc[:, l, third * 16:(third + 1) * 16, :]
        src = bk[:, 0:32].rearrange("p (a b) -> p a b", b=2)
        if kc == 0:
            self.cp("dve", dst, src, [bk], [self.modacc])
        else:
            self.tt("dve", dst, dst, src, ALU.add, [bk, self.modacc], [self.modacc])
        self.in_mod = False

    def mod_flush(self, l):
        while self.mod_pending and self.mod_pending[0][0] <= l:
            self.mod_tick()

    def mod_finalize(self, l, g):
        fw = self.fw
        mc = self.modc
        m = lambda i: self.modacc[:, l, i * 8:(i + 1) * 8, g]
        mb = lambda i: self.col(l, "mod_b", i * 8, 8)
        ng = lambda i: self.col(l, "norm_g", i * 8, 8)
        R = [self.modacc, self.cols, mc]
        for i in range(6):
            self.tt("dve", mc[:, i, :], m(i), mb(i), ALU.add, R, [mc])
        self.ts("dve", mc[:, 6, :], mc[:, 1, :], 1.0, 32.0, ALU.add, ALU.mult, R, [mc])
        self.tt("dve", mc[:, 6, :], mc[:, 6, :], ng(0), ALU.mult, R, [mc])
        self.ts("dve", mc[:, 7, :], mc[:, 2, :], 32.0, None, ALU.mult, None, R, [mc])
        self.tt("dve", mc[:, 7, :], mc[:, 7, :], ng(1), ALU.mult, R, [mc])
        self.ts("dve", mc[:, 8, :], mc[:, 4, :], 1.0, 32.0, ALU.add, ALU.mult, R, [mc])
        self.tt("dve", mc[:, 8, :], mc[:, 8, :], ng(2), ALU.mult, R, [mc])
        self.ts("dve", mc[:, 9, :], mc[:, 5, :], 32.0, None, ALU.mult, None, R, [mc])
        self.tt("dve", mc[:, 9, :], mc[:, 9, :], ng(3), ALU.mult, R, [mc])
        return dict(shift1=mc[:, 0, :], coefA1=mc[:, 6, :], gco1=mc[:, 7, :], shift2=mc[:, 3, :], coefA2=mc[:, 8, :], gco2=mc[:, 9, :])

    def rstd_of(self, src_bufs, src_ap, sq, rstd):
        fw = self.fw
        bks = [fw.bank(), fw.bank()]
        for c in range(8):
            s = sq.next()
            self.act(s[:, :], src_ap(c), AF.Square, [src_bufs[c]], [s])
            for th in range(2):
                self.mm(bks[th][:, :], self.ones_bf[:, :], s[:, th * 512:(th + 1) * 512], c == 0, c == 7, [self.ones_bf, s], [bks[th]])
        for th in range(2):
            self.rsqrt(rstd[:, th * 512:(th + 1) * 512], bks[th][:, :], D * EPS, [bks[th]], rstd)
        return rstd

    def build(self):
        fw = self.fw
        nc = self.nc
        di = lambda n, s: fw.dram(n, s, F32, kind="ExternalInput")
        do = lambda n, s: fw.dram(n, s, F32, kind="ExternalOutput")
        self.xT = {"P": di("xT_p", [128, 8, T]), "S": di("xT_s", [128, 8, T])}
        self.yT = {"P": do("yT_p", [128, 8, T]), "S": do("yT_s", [128, 8, T])}
        condT = di("condT", [128, 8, 2])
        cols_d = di("cols", [128, DEPTH, NCOL])
        cst_d = di("cst", [128, NCST])
        NL = self.layers
        self.mod_w = di("mod_w", [NL, D, 6 * D])
        self.w_in = di("w_in", [NL, D, D_IN])
        self.w_in_sw = di("w_in_sw", [NL, D, 384])
        self.w_out = di("w_out", [NL, D, D])
        self.ffn_w1 = di("ffn_w1", [NL, D, 4 * D])
        self.ffn_w2 = di("ffn_w2", [NL, 4 * D, D])
        self.lora2 = di("lora2", [DEPTH, 128, 2, 256])
        self.rw_g2 = di("rw_g2", [DEPTH, 128, 256])
        self.rpbpad = di("rpbpad", [DEPTH, 4, 15, 127])
        self.rope = di("rope", [128, 2, T])
        self.cnk = di("cnk", [DEPTH, 2, 128, 256])
        self.cnv = di("cnv", [DEPTH, 256, 256])
        self.csk = di("csk", [DEPTH, 2, 128, 256])
        self.csv = di("csv", [DEPTH, 256, 128])
        self.srw = di("srw", [DEPTH, 2, 128, 2, 64])
        self.shg = di("shg", [DEPTH, 2, 128, 2, 64])
        self.natk = do("natk", [DEPTH, 2, 128, T])
        self.natv = do("natv", [DEPTH, NT, 128, 256])
        self.swak = do("swak", [DEPTH, 128, T])
        self.swav = do("swav", [DEPTH, NT, 128, 128])
        self.rwst = do("rwst", [DEPTH, 2, 4, 128, 2, 64])
        self.hgst = do("hgst", [DEPTH, 2, 4, 128, 2, 64])

        self.cstb = fw.sb("cstb", [128, NCST])
        self.cols = fw.sb("colsb", [128, DEPTH, NCOL])
        self.ident_bf = fw.sb("ident_bf", [128, 128], BF16)
        self.ones_bf = fw.sb("ones_bf", [128, 128], BF16)
        self.blk_bf = fw.sb("blk_bf", [128, 128], BF16)
        self.silu = fw.sb("silu", [128, 8, 2])
        self.modacc = fw.sb("modacc", [128, DEPTH, 48, 2])
        self.modc = fw.sb("modc", [128, 10, 8])
        self.lb = fw.sb("lb", [128, 2, DEPTH, 2])
        self.oml = fw.sb("oml", [128, 2, DEPTH, 2])
        self.X = [fw.sb("X%d" % c, [128, T]) for c in range(8)]
        self.H = [fw.sb("H%d" % c, [128, T], BF16) for c in range(8)]
        self.MIX = [fw.sb("MIX%d" % c, [128, T], BF16) for c in range(8)]
        self.WST = Rot([fw.sb("wst%d" % i, [128, 2048]) for i in range(2)])
        self.WBF = Rot([fw.sb("wbf%d" % i, [128, 2048], BF16) for i in range(2)])
        self.rstd = fw.sb("rstd", [128, T])
        self.sq = Rot([fw.sb("sq%d" % i, [128, T], BF16) for i in range(2)])
        self.arena = Arena(fw, (nc.sbuf_bytes_remaining - 2048) // 4)

        self.eps_vals = [D * EPS, 1e-24, 64 * LN_EPS, 64 * EPS]
        self.epsb = fw.sb("epsb", [128, 4])
        for i, v in enumerate(self.eps_vals):
            fw.op("pool", lambda e: e.memset(self.epsb[:, i:i + 1], v), [], [self.epsb])
        fw.dma("sp", self.cstb[:, :], cst_d[:, :], writes=[self.cstb])
        fw.dma("sp", self.cols[:, :, :], cols_d[:, :, :], writes=[self.cols])
        fw.dma("sp", self.silu[:, :, :], condT[:, :, :], writes=[self.silu])
        self.cp("dve", self.ident_bf[:, :], self.cst("ident"), [self.cstb], [self.ident_bf])
        self.cp("dve", self.ones_bf[:, :], self.cst("ones"), [self.cstb], [self.ones_bf])
        self.cp("dve", self.blk_bf[:, :], self.cst("blk"), [self.cstb], [self.blk_bf])
        self.act(self.silu[:, :, :], self.silu[:, :, :], AF.Silu, [self.silu], [self.silu])
        self.setup_lb()

        self.cast_engs = ("pool", "dve", "pool")
        self.mod_pending = [(l, kc, th) for l in range(self.layers) for kc in range(8) for th in range(3)]
        self.in_mod = False

        for G in self.groups:
            for c in range(8):
                fw.dma("sp", self.X[c][:, :], self.xT[G][:, c, :], writes=[self.X[c]])
            for l in range(self.layers):
                self.block(G, l)
            for c in range(8):
                fw.dma("sp", self.yT[G][:, c, :], self.X[c][:, :], reads=[self.X[c]], is_output=True)
        fw.finish()

    def setup_lb(self):
        E = self.fw.sb("lbE", [128, 2, DEPTH, 2])
        S = self.fw.sb("lbS", [128, 2, 2])
        R = [self.cols, E, S, self.lb, self.oml]
        for d in range(2):
            i = COLI["lbl%d" % d]
            self.act(E[:, d, :, :], self.cols[:, 0, i:i + 8].rearrange("p (a b) -> p a b", a=DEPTH), AF.Exp, R, [E])
            self.tt("dve", S[:, d, :], E[:, d, 0, :], E[:, d, 1, :], ALU.add, R, [S])
            self.tt("dve", S[:, d, :], S[:, d, :], E[:, d, 2, :], ALU.add, R, [S])
            self.tt("dve", S[:, d, :], S[:, d, :], E[:, d, 3, :], ALU.add, R, [S])
            self.fw.op("dve", lambda e: e.reciprocal(S[:, d, :], S[:, d, :]), R, [S])
            for l in range(DEPTH):
                self.tt("dve", E[:, d, l, :], E[:, d, l, :], S[:, d, :], ALU.mult, R, [E])
            self.fw.op("dve", lambda e: e.memset(self.lb[:, d, 0, :], 0.0), R, [self.lb])
            for l in range(1, DEPTH):
                self.tt("dve", self.lb[:, d, l, :], self.lb[:, d, l - 1, :], E[:, d, l, :], ALU.add, R, [self.lb])
        self.ts("dve", self.oml[:, :, :, :], self.lb[:, :, :, :], -1.0, 1.0, ALU.mult, ALU.add, R, [self.oml])

    def block(self, G, l):
        fw = self.fw
        g = 0 if G == "P" else 1
        self.G, self.l = G, l
        self.seqs = [(s * 2, 2) for s in range(4)] if G == "P" else [(0, 8)]
        self.mod_flush(l)
        mc = self.mod_finalize(l, g)
        self.mc = mc
        X, H = self.X, self.H
        rstd = self.rstd_of(X, lambda c: X[c][:, :], self.sq, self.rstd)
        for c in range(8):
            tmp = self.sq.next()
            e = self.el()
            self.stt(e, tmp[:, :], X[c][:, :], mc["coefA1"][:, c:c + 1], rstd[:, :], ALU.mult, ALU.mult, [X[c], rstd, self.modc], [tmp])
            self.act(H[c][:, :], tmp[:, :], AF.Identity, [tmp, self.modc], [H[c]], bias=mc["shift1"][:, c:c + 1])
        self.dump("h_%s%d" % (G, l), H[0], H[0][:, :], [128, T], BF16)
        self.mixer_A()
        self.mixer_B()
        self.mixer_C()
        self.mixer_D()
        for c in range(8):
            self.dump("mix%d_%s%d" % (c, G, l), self.MIX[c], self.MIX[c][:, :], [128, T], BF16)
        self.out_and_ffn()

    def proj_fm(self, wb, wv, nchunks, evac):
        for jj in range(nchunks):
            for th in range(2):
                bk = self.fw.bank()
                for kc in range(8):
                    self.mm(bk[:, :], wv[:, kc, jj * 128:(jj + 1) * 128], self.H[kc][:, th * 512:(th + 1) * 512],
                            kc == 0, kc == 7, [wb, self.H[kc]], [bk])
                evac(jj, th, bk)

    def proj_tm(self, wb, wv, ncol, evac, tile_tokens=128):
        nt = T // tile_tokens
        for ti in range(nt):
            bk = self.fw.bank()
            for kc in range(8):
                self.mm(bk[0:tile_tokens, 0:ncol], self.H[kc][:, ti * tile_tokens:(ti + 1) * tile_tokens], wv[:, kc, 0:ncol],
                        kc == 0, kc == 7, [wb, self.H[kc]], [bk])
            evac(ti, bk)

    def ctx_attn(self, QT, KT, ksel, Vt, vsel, mix0, PTp, RDp, esink=None):
        fw = self.fw

        def unit(slot, s, h):
            t0 = s * 256
            j, hp = h // 2, h % 2
            P = slice(hp * 64, hp * 64 + 64)
            PT = PTp.bufs[slot]
            rd = RDp.bufs[slot]
            bks = [fw.banks[slot * 4 + kc] for kc in range(2)]
            for kc in range(2):
                self.mm(bks[kc][:, 0:256], KT[P, ksel(h), t0 + kc * 128:t0 + (kc + 1) * 128], QT[P, j, t0:t0 + 256], True, True, [KT, QT], [bks[kc]])
            yield
            for kc in range(2):
                self.act(PT[:, kc, :], bks[kc][:, 0:256], AF.Exp, [bks[kc]], [PT], scale=0.125)
            yield
            bo = fw.banks[slot * 4 + 2]
            for kc in range(2):
                self.mm(bo[:, 0:256], Vt[:, s * 2 + kc, vsel(h), :], PT[:, kc, :], kc == 0, kc == 1, [Vt, PT], [bo])
            yield
            self.cp("dve", rd[0:64, :], bo[64:128, 0:256], [bo], [rd])
            if esink is not None:
                self.ts("dve", rd[0:64, :], rd[0:64, :], esink[0:64, h:h + 1], None, ALU.add, None, [rd, esink], [rd])
            self.fw.op("dve", lambda e: e.reciprocal(rd[0:64, :], rd[0:64, :]), [rd], [rd])
            self.tt("dve", self.MIX[mix0 + j][P, t0:t0 + 256], bo[0:64, 0:256], rd[0:64, :], ALU.mult, [bo, rd], [self.MIX[mix0 + j]])
        self.run_slots([(s_, h_) for s_ in range(4) for h_ in range(4)], unit, 2)

    def run_slots(self, items, unit, width):
        pend = list(items)
        active = {}
        while pend or active:
            for slot in range(width):
                if slot not in active and pend:
                    active[slot] = unit(slot, *pend.pop(0))
            for slot in list(active.keys()):
                try:
                    next(active[slot])
                except StopIteration:
                    del active[slot]

    def mixer_B(self):
        fw, l, G = self.fw, self.l, self.G
        ar = self.arena
        ar.reset()
        QT = ar.get("QT", [128, 2, T], BF16)
        KT = ar.get("KT", [128, 2, T], BF16)
        if G == "P":
            KTf = ar.get("KTf", [128, 2, T], F32)
            Vf = ar.get("Vf", [128, NT, 256], F32)
            Vt = ar.get("Vt", [128, NT, 4, 128], BF16)
            PTp = ar.pool("PT", 2, [128, 2, 256], BF16)
            RDp = ar.pool("RD", 2, [128, 256], F32)
            fw.op("pool", lambda e: e.memset(Vt[:, :, :, 64:128], 1.0), [], [Vt])
            wb, wv = self.win_block(l, 1152, 256)
            self.proj_fm(wb, wv, 2, lambda jj, th, bk: self.cp(self.ev(), QT[:, jj, th * 512:(th + 1) * 512], bk[:, :], [bk], [QT]))
            wb, wv = self.win_block(l, 1408, 256)

            def evk(jj, th, bk):
                self.cp(self.ev(), KTf[:, jj, th * 512:(th + 1) * 512], bk[:, :], [bk], [KTf])
                self.cp("pool", KT[:, jj, th * 512:(th + 1) * 512], KTf[:, jj, th * 512:(th + 1) * 512], [KTf], [KT])
            self.proj_fm(wb, wv, 2, evk)
            fw.dma("sp", self.natk[l].rearrange("j p t -> p j t"), KTf[:, :, :], reads=[KTf], is_output=True)
            wb, wv = self.win_block(l, 1664, 256)

            def evv(ti, bk):
                self.cp(self.ev(), Vf[:, ti, :], bk[:, 0:256], [bk], [Vf])
                self.cp("pool", Vt[:, ti, :, 0:64], Vf[:, ti, :].rearrange("p (h d) -> p h d", h=4), [Vf], [Vt])
            self.proj_tm(wb, wv, 256, evv)
            fw.dma("sp", self.natv[l].rearrange("n p f -> p n f"), Vf[:, :, :], reads=[Vf], is_output=True)
            self.ctx_attn(QT, KT, lambda h: h // 2, Vt, lambda h: h, 2, PTp, RDp)
        else:
            self.nat_latent(QT, KT)

    def mixer_D(self):
        fw, l, G = self.fw, self.l, self.G
        ar = self.arena
        ar.reset()
        QT = ar.get("QT", [128, 2, T], BF16)
        KT = ar.get("KTd", [128, 2, T], BF16)
        esink = ar.get("esink", [128, 4], F32)
        self.act(esink[:, :], self.col(l, "sink", 0, 4), AF.Exp, [self.cols], [esink])
        if G == "P":
            KTf = ar.get("KTf", [128, T], F32)
            Vf = ar.get("Vf", [128, NT, 128], F32)
            Vt = ar.get("Vt", [128, NT, 2, 128], BF16)
            PTp = ar.pool("PT", 2, [128, 2, 256], BF16)
            RDp = ar.pool("RD", 2, [128, 256], F32)
            fw.op("pool", lambda e: e.memset(Vt[:, :, :, 64:128], 1.0), [], [Vt])
            wb, wv = self.win_block(l, 3200, 256)
            self.proj_fm(wb, wv, 2, lambda jj, th, bk: self.cp(self.ev(), QT[:, jj, th * 512:(th + 1) * 512], bk[:, :], [bk], [QT]))
            wb, wv = self.win_block(l, 3456, 256)

            def evk(jj, th, bk):
                sl = slice(th * 512, (th + 1) * 512)
                self.cp(self.ev(), KTf[:, sl], bk[:, :], [bk], [KTf])
                for gg in range(2):
                    for hp in range(2):
                        self.cp(self.ev(), KT[hp * 64:hp * 64 + 64, gg, sl], KTf[gg * 64:gg * 64 + 64, sl], [KTf], [KT])
            self.proj_fm(wb, wv, 1, evk)
            fw.dma("sp", self.swak[l], KTf[:, :], reads=[KTf], is_output=True)

            def evv(ti, bk):
                self.cp(self.ev(), Vf[:, ti, :], bk[:, 0:128], [bk], [Vf])
                self.cp("pool", Vt[:, ti, :, 0:64], Vf[:, ti, :].rearrange("p (h d) -> p h d", h=2), [Vf], [Vt])
            self.proj_tm(wb, wv[:, :, 128:256], 128, evv)
            fw.dma("sp", self.swav[l].rearrange("n p f -> p n f"), Vf[:, :, :], reads=[Vf], is_output=True)
            self.ctx_attn(QT, KT, lambda h: h // 2, Vt, lambda h: h // 2, 6, PTp, RDp, esink=esink)
        else:
            self.swa_latent(QT, KT, esink)

    def lerp(self, d, out_ap, src_buf, src_ap, mu_ap, DIFF, out_buf):
        ns = len(self.seqs)
        L = T // ns
        s3 = src_ap.rearrange("p (s t) -> p s t", s=ns)
        d3 = DIFF[:, :].rearrange("p (s t) -> p s t", s=ns)
        e = self.el()
        if d == 0:
            self.tt(e, d3[:, :, 1:L], s3[:, :, 0:L - 1], s3[:, :, 1:L], ALU.subtract, [src_buf], [DIFF])
            self.ts(e, d3[:, :, 0:1], s3[:, :, 0:1], -1.0, None, ALU.mult, None, [src_buf], [DIFF])
        else:
            self.tt(e, d3[:, :, 0:L - 1], s3[:, :, 1:L], s3[:, :, 0:L - 1], ALU.subtract, [src_buf], [DIFF])
            self.ts(e, d3[:, :, L - 1:L], s3[:, :, L - 1:L], -1.0, None, ALU.mult, None, [src_buf], [DIFF])
        self.stt(e, out_ap, DIFF[:, :], mu_ap, src_ap, ALU.mult, ALU.add, [DIFF, src_buf, self.cols], [out_buf])

    def tile_order(self, d):
        out = []
        for si, (t0, n) in enumerate(self.seqs):
            tl = list(range(t0, t0 + n))
            if d == 1:
                tl = tl[::-1]
            for i, ti in enumerate(tl):
                out.append((si, ti, i == n - 1))
        return out

    def scan_tile(self, d, gs_ap, src_ap, reads, wbuf, C=128):
        rs = self.cst("reset", 0, C)
        if d == 0:
            self.fw.op("dve", lambda e: e.tensor_tensor_scan(gs_ap, rs, src_ap, 0.0, ALU.mult, ALU.add), reads + [self.cstb], [wbuf])
        else:
            self.fw.op("dve", lambda e: e.tensor_tensor_scan(gs_ap[:, ::-1], rs, src_ap[:, ::-1], 0.0, ALU.mult, ALU.add), reads + [self.cstb], [wbuf])

    def run_gens(self, gens, width=None):
        pend = list(gens)
        width = width or len(pend)
        active = []
        while pend or active:
            while pend and len(active) < width:
                active.append(pend.pop(0))
            nxt = []
            for g in active:
                try:
                    next(g)
                    nxt.append(g)
                except StopIteration:
                    pass
            active = nxt

    def lin_head(self, d, j, hp, t0, W, dplr, banks, C=128):
        fw = self.fw
        bi = [0]

        def nb():
            b = banks[bi[0] % len(banks)]
            bi[0] += 1
            return b
        P = slice(hp * 64, hp * 64 + 64)
        h = 2 * j + hp
        RK, KB, TOK, YACC = W["RK"], W["KB"], W["TOK"], W["YACC"]
        hst = W["hst"]
        HST = W["HST"]
        rt = RK[P, j, 1, 0:C]
        kb = KB[P, j, 0:C]
        Vh = TOK[0:C, 0, h * 64:(h + 1) * 64]
        Kbh = TOK[0:C, 1, h * 64:(h + 1) * 64]
        G4 = W["G4p"].next()
        g4c = lambda a, b: self.cst("g4_%d" % d, a, b)
        if dplr:
            PB = W["PB"]
            kt = RK[P, j, 0, :]
            rk2 = RK[P, j, :, :]
            pb = PB[P, j, :]
            Pbh = TOK[:, 2, h * 64:(h + 1) * 64]
            bk = nb()
            self.mm(bk[:, 0:256], pb, rk2, True, True, [PB, RK], [bk])
            self.mm(bk[:, 256:512], kb, rk2, True, True, [KB, RK], [bk])
            bk2 = nb()
            self.mm(bk2[:, 0:128], kt, pb, True, True, [RK, PB], [bk2])
            yield
            self.tt("dve", G4[:, :], bk[:, :], g4c(0, 512), ALU.mult, [bk, self.cstb], [G4])
            Bm = W["Bp"].next()
            self.tt("dve", Bm[:, :], bk2[:, 0:128], self.cst("mb_%d" % d), ALU.mult, [bk2, self.cstb], [Bm])
            CN = W["CNp"].next()
            self.tt("pool", CN[:, 128:256], G4[:, 0:128], self.cst("ident"), ALU.add, [G4, self.cstb], [CN])
            yield
            bk = nb()
            self.mm(bk[:, 0:128], Bm[:, :], G4[:, 0:128], True, True, [Bm, G4], [bk])
            yield
            self.cp("act", CN[:, 0:128], bk[:, 0:128], [bk], [CN])
            Cprev_buf, Cprev = G4, G4[:, 0:128]
            Nfin = None
            for lev in range(1, 7):
                bk = nb()
                self.mm(bk[:, 0:128], Cprev, Bm[:, :], True, True, [Cprev_buf, Bm], [bk])
                yield
                Bn = W["Bp"].next()
                self.cp("act", Bn[:, :], bk[:, 0:128], [bk], [Bn])
                Bm = Bn
                yield
                bk = nb()
                if lev < 6:
                    self.mm(bk[:, 0:256], Bm[:, :], CN[:, 0:256], True, True, [Bm, CN], [bk])
                    yield
                    CN2 = W["CNp"].next()
                    self.cp("act", CN2[:, 0:128], bk[:, 0:128], [bk], [CN2])
                    self.tt("dve", CN2[:, 128:256], bk[:, 128:256], CN[:, 128:256], ALU.add, [bk, CN], [CN2])
                    Cprev_buf, Cprev = CN, CN[:, 0:128]
                    CN = CN2
                    yield
                else:
                    self.mm(bk[:, 0:128], Bm[:, :], CN[:, 128:256], True, True, [Bm, CN], [bk])
                    yield
                    Nfin = W["Bp"].next()
                    self.tt("dve", Nfin[:, :], bk[:, 0:128], CN[:, 128:256], ALU.add, [bk, CN], [Nfin])
                    yield
            mqk = G4[:, 384:512]
        else:
            bk = nb()
            self.mm(bk[0:C, 0:C], kb, rt, True, True, [KB, RK], [bk])
            yield
            self.tt("dve", G4[0:C, 0:C], bk[0:C, 0:C], self.cst("g4_%d" % d, 384, 384 + C, rows=C), ALU.mult, [bk, self.cstb], [G4])
            mqk = G4[0:C, 0:C]
        Hp = W["Hpp"].next()
        Hb = W["Hbp"].next()
        self.ts("dve", Hp[P, :], hst, W["expc"][P, j:j + 1], None, ALU.mult, None, [HST, W["expcb"]], [Hp])
        yield
        self.cp("act", Hb[P, :], Hp[P, :], [Hp], [Hb])
        yield
        if dplr:
            bx = nb()
            self.mm(bx[:, 0:64], G4[:, 256:384], Vh, True, False, [G4, TOK], [bx])
            self.mm(bx[:, 0:64], kt, Hb[P, :], False, True, [RK, Hb], [bx])
            yield
            Xs = W["Xp"].next()
            self.cp("act", Xs[:, :], bx[:, 0:64], [bx], [Xs])
            yield
            bu = nb()
            self.mm(bu[:, 0:64], Nfin[:, :], Xs[:, :], True, True, [Nfin, Xs], [bu])
            yield
            Un = W["Xp"].next()
            self.ts("dve", Un[:, :], bu[:, 0:64], -1.0, None, ALU.mult, None, [bu], [Un])
            yield
        by = nb()
        self.mm(by[P, 0:C], Hb[P, :], rt, True, False, [Hb, RK], [by])
        self.mm(by[P, 0:C], Vh, mqk, False, not dplr, [TOK, G4], [by])
        if dplr:
            self.mm(by[P, 0:C], Un[:, :], G4[:, 128:256], False, True, [Un, G4], [by])
        self.mm(by[P, 256:320], Kbh, Vh, True, not dplr, [TOK], [by])
        if dplr:
            self.mm(by[P, 256:320], Pbh, Un[:, :], False, True, [TOK, Un], [by])
        yield
        ydst = YACC[P, j, t0:t0 + C]
        self.tt("dve", ydst, by[P, 0:C], ydst, ALU.add, [by, YACC], [YACC])
        self.tt("dve", Hp[P, :], by[P, 256:320], Hp[P, :], ALU.add, [by, Hp], [Hp])
        yield
        self.ts("dve", hst, Hp[P, :], W["eplast"][P, j:j + 1], None, ALU.mult, None, [Hp, W["eplastb"]], [HST])

    def lin_common_bufs(self, ar, dplr, njobs):
        Ws = []
        for k in range(njobs):
            W = {}
            if dplr:
                W["G4p"] = ar.pool("G4_%d" % k, 1, [128, 512], BF16)
                W["Bp"] = ar.pool("Bm_%d" % k, 3, [128, 128], BF16)
                W["CNp"] = ar.pool("CN_%d" % k, 3, [128, 256], BF16)
                W["Xp"] = ar.pool("Xs_%d" % k, 2, [128, 64], BF16)
            else:
                W["G4p"] = ar.pool("G4_%d" % k, 1, [128, 32], BF16)
            W["Hpp"] = ar.pool("Hp_%d" % k, 1, [128, 64], F32)
            W["Hbp"] = ar.pool("Hb_%d" % k, 1, [128, 64], BF16)
            Ws.append(W)
        return Ws

    def lin_state_io(self, HST, src_dram, dst_dram, load):
        l = self.l
        if load:
            self.fw.dma("sp", HST[:, :, 0, :, :], src_dram[l].rearrange("d p j v -> p d j v"), writes=[HST])
        else:
            self.fw.dma("sp", dst_dram[l].rearrange("d s p j v -> p d s j v"), HST[:, :, :, :, :], reads=[HST], is_output=True)

    def mixer_A(self):
        fw, l, G = self.fw, self.l, self.G
        ar = self.arena
        ar.reset()
        ns = len(self.seqs)
        RAW = ar.get("RAW", [128, 6, T], BF16)
        WA = ar.get("WA", [128, 2, T], F32)
        SG = ar.get("SG", [128, T], BF16)
        YACC = ar.get("YACC", [128, 2, T], F32)
        BACC = ar.get("BACC", [128, 2, T], BF16)
        HST = ar.get("HST", [128, 2, ns, 2, 64], F32)
        L2 = ar.get("L2", [128, 2, 256], F32)
        G2 = ar.get("G2", [128, 256], BF16)
        LR = ar.get("LR", [128, 6, T], BF16)
        DIFF = ar.get("DIFF", [128, T], F32)
        cc = ar.get("cc", [128, 8], F32)
        W4 = self.lin_common_bufs(ar, True, 4)
        fw.op("pool", lambda e: e.memset(YACC[:, :, :], 0.0), [], [YACC])
        wk = lambda nm, shape, dt=F32: ar.pool(nm, 2, shape, dt)
        THp, SIGp, Ap = wk("th", [128, 128]), wk("sig", [128, 2, 128]), wk("a", [128, 2, 128])
        T1p, T2p, T3p = wk("t1", [128, 128]), wk("t2", [128, 128]), ar.pool("t3", 1, [128, 128], F32)
        wk1 = lambda nm, shape, dt=F32: ar.pool(nm, 1, shape, dt)
        KAPp, KMp, PPp = wk1("kap", [128, 2, 128]), wk1("km", [128, 2, 128]), wk1("pp", [128, 2, 128])
        SQp = wk("sqb", [128, 128], BF16)
        GSp, EPp, EMp, EPPp = wk1("gs", [128, 2, 128]), wk1("ep", [128, 2, 128]), wk1("em", [128, 2, 128]), wk("epp", [128, 2, 128])
        EXCp = wk("exc", [128, 2])
        RKp, KBp, PBp = wk("RK", [128, 2, 2, 128], BF16), wk("KB", [128, 2, 128], BF16), wk("PB", [128, 2, 128], BF16)
        TOKp = wk("TOK", [128, 3, 256], BF16)

        fw.dma("sp", L2[:, :, :], self.lora2[l], writes=[L2])
        g2st = self.WST.next()
        fw.dma("sp", g2st[:, 0:256], self.rw_g2[l], writes=[g2st])
        self.cp("pool", G2[:, :], g2st[:, 0:256], [g2st], [G2])
        self.ts("dve", cc[:, 0:2], self.col(l, "ka", 0, 2), -1.0, 1.0, ALU.mult, ALU.add, [self.cols], [cc])
        self.ts("dve", cc[:, 2:4], self.col(l, "lnx_w", 0, 2), 8.0, None, ALU.mult, None, [self.cols], [cc])
        if G == "P":
            fw.op("pool", lambda e: e.memset(HST[:, :, :, :, :], 0.0), [], [HST])
        else:
            self.lin_state_io(HST, self.srw, None, True)

        def evA(c0):
            def f(jj, th, bk):
                ci = c0 + jj
                sl = slice(th * 512, (th + 1) * 512)
                if ci < 6:
                    self.cp(self.ev(), RAW[:, ci, sl], bk[:, :], [bk], [RAW])
                elif ci == 6:
                    self.act(SG[:, sl], bk[:, :], AF.Sigmoid, [bk], [SG])
                else:
                    self.cp(self.ev(), WA[:, ci - 7, sl], bk[:, :], [bk], [WA])
            return f
        for b0 in range(0, 9, 2):
            nch = min(2, 9 - b0)
            wb, wv = self.win_block(l, b0 * 128, nch * 128)
            self.proj_fm(wb, wv, nch, evA(b0))

        for d in range(2):
            for a in range(3):
                for jj in range(2):
                    ci = a * 2 + jj
                    self.lerp(d, LR[:, ci, :], RAW, RAW[:, ci, :], self.col(l, "mu_rkv%d" % d, ci, 1), DIFF, LR)
            self.lerp(d, WA[:, d, :], WA, WA[:, d, :], self.col(l, "mu_lora%d" % d, 0, 1), DIFF, WA)
            LW = WA
            for (si, ti, last) in self.tile_order(d):
                t0 = ti * 128
                ts_ = slice(t0, t0 + 128)
                th = THp.next()
                self.act(th[0:64, :], WA[0:64, d, ts_], AF.Tanh, [WA], [th])
                sig, av = SIGp.next(), Ap.next()
                for jj in range(2):
                    bk = fw.bank()
                    self.mm(bk[:, 0:128], L2[0:64, d, jj * 128:(jj + 1) * 128], th[0:64, :], True, True, [L2, th], [bk])
                    self.act(sig[:, jj, :], bk[:, 0:128], AF.Sigmoid, [bk, self.cols], [sig], bias=self.col(l, "w0_%d" % d, jj, 1))
                    bk = fw.bank()
                    self.mm(bk[:, 0:128], L2[64:128, d, jj * 128:(jj + 1) * 128], WA[64:128, d, ts_], True, True, [L2, WA], [bk])
                    self.act(av[:, jj, :], bk[:, 0:128], AF.Sigmoid, [bk, self.cols], [av], bias=self.col(l, "a0_%d" % d, jj, 1))
                kap, km, pp = KAPp.next(), KMp.next(), PPp.next()
                gs, ep, em, epp, exc = GSp.next(), EPp.next(), EMp.next(), EPPp.next(), EXCp.next()
                RK, KB, PB, TOK = RKp.next(), KBp.next(), PBp.next(), TOKp.next()
                mid = 63 if d == 0 else 64
                lastc = 127 if d == 0 else 0
                for jj in range(2):
                    rl, kl, vl = LR[:, jj, ts_], LR[:, 2 + jj, ts_], LR[:, 4 + jj, ts_]
                    e = self.el()
                    t1, t2 = T1p.next(), T2p.next()
                    self.ts(e, t1[:, :], kl, self.col(l, "kk", jj, 1), None, ALU.mult, None, [LR, self.cols], [t1])
                    sqb = SQp.next()
                    self.tt(e, sqb[:, :], t1[:, :], t1[:, :], ALU.mult, [t1], [sqb])
                    bk = fw.bank()
                    self.mm(bk[:, 0:128], self.blk_bf[:, :], sqb[:, :], True, True, [self.blk_bf, sqb], [bk])
                    self.rsqrt(t2[:, :], bk[:, 0:128], 1e-24, [bk], t2)
                    self.tt(e, kap[:, jj, :], t1[:, :], t2[:, :], ALU.mult, [t1, t2], [kap])
                    t3 = T3p.next()
                    self.ts(e, t3[:, :], av[:, jj, :], self.col(l, "ka", jj, 1), cc[:, jj:jj + 1], ALU.mult, ALU.add, [av, self.cols, cc], [t3])
                    self.tt(e, km[:, jj, :], kl, t3[:, :], ALU.mult, [LR, t3], [km])
                    self.tt(e, pp[:, jj, :], kap[:, jj, :], av[:, jj, :], ALU.mult, [kap, av], [pp])
                    t1b = T1p.next()
                    self.stt(e, t1b[:, :], rl, self.col(l, "rk", jj, 1), km[:, jj, :], ALU.mult, ALU.mult, [LR, self.cols, km], [t1b])
                    sqb2 = SQp.next()
                    self.cp(e, sqb2[:, :], t1b[:, :], [t1b], [sqb2])
                    bk = fw.bank()
                    self.mm(bk[:, 0:128], self.blk_bf[:, :], sqb2[:, :], True, True, [self.blk_bf, sqb2], [bk])
                    if d == 0:
                        self.tt("dve", BACC[:, jj, ts_], bk[:, 0:128], vl, ALU.mult, [bk, LR], [BACC])
                    else:
                        t2b = T2p.next()
                        self.tt("dve", t2b[:, :], bk[:, 0:128], vl, ALU.mult, [bk, LR], [t2b])
                        self.tt(e, BACC[:, jj, ts_], BACC[:, jj, ts_], t2b[:, :], ALU.add, [t2b, BACC], [BACC])
                    self.scan_tile(d, gs[:, jj, :], sig[:, jj, :], [sig], gs)
                self.cp("dve", exc[:, 0:2], gs[:, :, mid], [gs], [exc])
                gc = EPPp.next()
                for jj in range(2):
                    self.ts("dve", gc[:, jj, :], gs[:, jj, :], exc[:, jj:jj + 1], None, ALU.subtract, None, [gs, exc], [gc])
                self.act(ep[:, :, :], gc[:, :, :], AF.Exp, [gc], [ep], scale=-DECAY)
                self.act(em[:, :, :], gc[:, :, :], AF.Exp, [gc], [em], scale=DECAY)
                self.tt(self.el(), gs[:, :, :], gc[:, :, :], sig[:, :, :], ALU.subtract, [gc, sig], [gs])
                self.act(epp[:, :, :], gs[:, :, :], AF.Exp, [gs], [epp], scale=-DECAY)
                self.act(exc[:, 0:2], exc[:, 0:2], AF.Exp, [exc], [exc], scale=-DECAY)
                for jj in range(2):
                    e = self.el()
                    self.tt(e, RK[:, jj, 0, :], kap[:, jj, :], epp[:, jj, :], ALU.mult, [kap, epp], [RK])
                    self.tt(e, RK[:, jj, 1, :], LR[:, jj, ts_], ep[:, jj, :], ALU.mult, [LR, ep], [RK])
                    self.tt(e, KB[:, jj, :], km[:, jj, :], em[:, jj, :], ALU.mult, [km, em], [KB])
                    self.tt(e, PB[:, jj, :], pp[:, jj, :], em[:, jj, :], ALU.mult, [pp, em], [PB])
                bt = fw.bank()
                btv = bt.t.bitcast(BF16)
                for jj in range(2):
                    self.tp(btv[:, (0 * 2 + jj) * 128:(0 * 2 + jj + 1) * 128], LR[:, 4 + jj, ts_], self.ident_bf[:, :], [LR, self.ident_bf], [bt])
                    self.tp(btv[:, (1 * 2 + jj) * 128:(1 * 2 + jj + 1) * 128], KB[:, jj, :], self.ident_bf[:, :], [KB, self.ident_bf], [bt])
                    self.tp(btv[:, (2 * 2 + jj) * 128:(2 * 2 + jj + 1) * 128], PB[:, jj, :], self.ident_bf[:, :], [PB, self.ident_bf], [bt])
                self.cp("act", TOK[:, :, :], btv[:, 0:768].rearrange("p (a b) -> p a b", a=3), [bt], [TOK])
                shared = dict(RK=RK, KB=KB, PB=PB, TOK=TOK, expc=exc, expcb=exc, eplast=ep[:, :, lastc], eplastb=ep, YACC=YACC, HST=HST)
                gens = []
                for k, (jj, hp) in enumerate(((0, 0), (0, 1), (1, 0), (1, 1))):
                    Wk = dict(W4[k])
                    Wk.update(shared)
                    Wk["hst"] = HST[hp * 64:hp * 64 + 64, d, si, jj, :]
                    gens.append(self.lin_head(d, jj, hp, t0, Wk, True, [fw.banks[2 * k], fw.banks[2 * k + 1]]))
                self.run_gens(gens)
        if G == "P":
            self.lin_state_io(HST, None, self.rwst, False)
        self.dump("yacc_%s%d" % (G, l), YACC, YACC[:, :, :], [128, 2, T])
        self.dump("bacc_%s%d" % (G, l), BACC, BACC[:, :, :], [128, 2, T])
        blk = self.cst("blk")
        for jj in range(2):
            for th in range(2):
                sl = slice(th * 512, (th + 1) * 512)
                bk = fw.bank()
                self.mm(bk[:, :], blk, YACC[:, jj, sl], True, True, [self.cstb, YACC], [bk])
                yc = DIFF
                self.stt("dve", yc[:, sl], bk[:, :], -1.0 / 64, YACC[:, jj, sl], ALU.mult, ALU.add, [bk, YACC], [yc])
                self.tt("pool", WA[:, 0, sl], yc[:, sl], yc[:, sl], ALU.mult, [yc], [WA])
                bk2 = fw.bank()
                self.mm(bk2[:, :], blk, WA[:, 0, sl], True, True, [self.cstb, WA], [bk2])
                self.rsqrt(WA[:, 0, sl], bk2[:, :], 64 * LN_EPS, [bk2], WA)
                self.tt("pool", yc[:, sl], yc[:, sl], WA[:, 0, sl], ALU.mult, [yc, WA], [yc])
                self.ts("dve", yc[:, sl], yc[:, sl], cc[:, 2 + jj:3 + jj], self.col(l, "lnx_b", jj, 1), ALU.mult, ALU.add, [yc, cc, self.cols], [yc])
                self.tt("pool", yc[:, sl], yc[:, sl], BACC[:, jj, sl], ALU.add, [yc, BACC], [yc])
                bk3 = fw.bank()
                self.mm(bk3[:, :], G2[:, jj * 128:(jj + 1) * 128], SG[:, sl], True, True, [G2, SG], [bk3])
                self.tt("dve", self.MIX[jj][:, sl], yc[:, sl], bk3[:, :], ALU.mult, [yc, bk3], [self.MIX[jj]])

    def mixer_C(self):
        fw, l, G = self.fw, self.l, self.G
        ar = self.arena
        ar.reset()
        ns = len(self.seqs)
        Q = ar.get("Q", [128, 2, T], BF16)
        V = ar.get("V", [128, 2, T], BF16)
        GATE = ar.get("GATE", [128, 2, T], BF16)
        FR = ar.get("FR", [128, 2, 2, T], F32)
        YACC = ar.get("YACC", [128, 2, T], F32)
        HST = ar.get("HST", [128, 2, ns, 2, 64], F32)
        LFd = [ar.get("LF%d" % d, [128, 2, T], F32) for d in range(2)]
        KKd = [ar.get("KK%d" % d, [128, 2, T], BF16) for d in range(2)]
        LF = LFd[0]
        W8 = self.lin_common_bufs(ar, False, 8)
        fw.op("pool", lambda e: e.memset(YACC[:, :, :], 0.0), [], [YACC])
        hn8 = ar.get("hn8", [128, 2], F32)
        self.ts("dve", hn8[:, :], self.col(l, "hg_norm", 0, 2), 8.0, None, ALU.mult, None, [self.cols], [hn8])
        C = 32
        wk = lambda nm, shape, dt=F32: [ar.pool("%s_%d" % (nm, d), 2, shape, dt) for d in range(2)]
        GSp, GCp, EPp, EMp, EXCp = wk("gs", [128, 2, C]), wk("gc", [128, 2, C]), wk("ep", [128, 2, C]), wk("em", [128, 2, C]), wk("exc", [128, 2])
        RKp, KBp, TOKp = wk("RK", [128, 2, 2, C], BF16), wk("KB", [128, 2, C], BF16), wk("TOK", [128, 2, 256], BF16)
        if G == "P":
            fw.op("pool", lambda e: e.memset(HST[:, :, :, :, :], 0.0), [], [HST])
        else:
            self.lin_state_io(HST, self.shg, None, True)

        def evC(c0):
            def f(jj, th, bk):
                ci = c0 + jj
                sl = slice(th * 512, (th + 1) * 512)
                if ci < 2:
                    self.act(Q[:, ci, sl], bk[:, :], AF.Silu, [bk], [Q])
                elif ci < 4:
                    self.cp(self.ev(), V[:, ci - 2, sl], bk[:, :], [bk], [V])
                elif ci < 6:
                    self.act(GATE[:, ci - 4, sl], bk[:, :], AF.Sigmoid, [bk], [GATE])
                else:
                    self.cp(self.ev(), FR[:, (ci - 6) // 2, (ci - 6) % 2, sl], bk[:, :], [bk], [FR])
            return f
        for b0 in range(0, 10, 2):
            wb, wv = self.win_block(l, 1920 + b0 * 128, 256)
            self.proj_fm(wb, wv, 2, evC(b0))

        for d in range(2):
            for jj in range(2):
                self.act(LFd[d][:, jj, :], FR[:, d, jj, :], AF.Sigmoid, [FR], [LFd[d]])
                self.ts("dve", LFd[d][:, jj, :], LFd[d][:, jj, :], self.oml[:, d, l, jj:jj + 1], self.lb[:, d, l, jj:jj + 1], ALU.mult, ALU.add, [LFd[d], self.oml, self.lb], [LFd[d]])
                self.ts("pool", KKd[d][:, jj, :], LFd[d][:, jj, :], -1.0, 1.0, ALU.mult, ALU.add, [LFd[d]], [KKd[d]])
        for d in range(2):
            for jj in range(2):
                self.act(LFd[d][:, jj, :], LFd[d][:, jj, :], AF.Ln, [LFd[d]], [LFd[d]])
        orders = []
        for d in range(2):
            o = []
            for (si, ti, last) in self.tile_order(d):
                for sub in (range(128 // C) if d == 0 else range(128 // C - 1, -1, -1)):
                    o.append((si, ti * 128 + sub * C))
            orders.append(o)
        for step in range(len(orders[0])):
            gens = []
            for d in range(2):
                si, t0 = orders[d][step]
                ts_ = slice(t0, t0 + C)
                gs, gc, ep, em, exc = GSp[d].next(), GCp[d].next(), EPp[d].next(), EMp[d].next(), EXCp[d].next()
                RK, KB, TOK = RKp[d].next(), KBp[d].next(), TOKp[d].next()
                mid = C // 2 - 1 if d == 0 else C // 2
                lastc = C - 1 if d == 0 else 0
                for jj in range(2):
                    self.scan_tile(d, gs[:, jj, 0:C], LFd[d][:, jj, ts_], [LFd[d]], gs, C)
                self.cp("dve", exc[:, 0:2], gs[:, :, mid], [gs], [exc])
                for jj in range(2):
                    self.ts("dve", gc[:, jj, 0:C], gs[:, jj, 0:C], exc[:, jj:jj + 1], None, ALU.subtract, None, [gs, exc], [gc])
                self.act(ep[:, :, 0:C], gc[:, :, 0:C], AF.Exp, [gc], [ep])
                self.act(em[:, :, 0:C], gc[:, :, 0:C], AF.Exp, [gc], [em], scale=-1.0)
                self.act(exc[:, 0:2], exc[:, 0:2], AF.Exp, [exc], [exc])
                for jj in range(2):
                    e = self.el()
                    self.tt(e, RK[:, jj, 1, 0:C], Q[:, jj, ts_], ep[:, jj, 0:C], ALU.mult, [Q, ep], [RK])
                    self.tt(e, KB[:, jj, 0:C], KKd[d][:, jj, ts_], em[:, jj, 0:C], ALU.mult, [KKd[d], em], [KB])
                bt = fw.bank()
                btv = bt.t.bitcast(BF16)
                for jj in range(2):
                    self.tp(btv[0:C, jj * 128:(jj + 1) * 128], V[:, jj, ts_], self.ident_bf[:, :], [V, self.ident_bf], [bt])
                    self.tp(btv[0:C, (2 + jj) * 128:(3 + jj) * 128], KB[:, jj, 0:C], self.ident_bf[:, :], [KB, self.ident_bf], [bt])
                self.cp("act", TOK[0:C, :, :], btv[0:C, 0:512].rearrange("p (a b) -> p a b", a=2), [bt], [TOK])
                shared = dict(RK=RK, KB=KB, TOK=TOK, expc=exc, expcb=exc, eplast=ep[:, :, lastc], eplastb=ep, YACC=YACC, HST=HST)
                for k, (jj, hp) in enumerate(((0, 0), (0, 1), (1, 0), (1, 1))):
                    Wk = dict(W8[d * 4 + k])
                    Wk.update(shared)
                    Wk["hst"] = HST[hp * 64:hp * 64 + 64, d, si, jj, :]
                    gens.append(self.lin_head(d, jj, hp, t0, Wk, False, [fw.banks[d * 4 + k]], C))
            self.run_gens(gens)
        if G == "P":
            self.lin_state_io(HST, None, self.hgst, False)
        self.dump("yaccC_%s%d" % (G, l), YACC, YACC[:, :, :], [128, 2, T])
        self.dump("lfC_%s%d" % (G, l), LF, LF[:, :, :], [128, 2, T])
        blk = self.cst("blk")
        for jj in range(2):
            for th in range(2):
                sl = slice(th * 512, (th + 1) * 512)
                self.tt("pool", LF[:, 0, sl], YACC[:, jj, sl], YACC[:, jj, sl], ALU.mult, [YACC], [LF])
                bk = fw.bank()
                self.mm(bk[:, :], blk, LF[:, 0, sl], True, True, [self.cstb, LF], [bk])
                self.rsqrt(LF[:, 0, sl], bk[:, :], 64 * EPS, [bk], LF)
                self.tt("pool", LF[:, 0, sl], LF[:, 0, sl], YACC[:, jj, sl], ALU.mult, [LF, YACC], [LF])
                self.stt("dve", self.MIX[4 + jj][:, sl], LF[:, 0, sl], hn8[:, jj:jj + 1], GATE[:, jj, sl], ALU.mult, ALU.mult, [LF, hn8, GATE], [self.MIX[4 + jj]])

    def out_and_ffn(self):
        fw, l, G = self.fw, self.l, self.G
        ar = self.arena
        ar.reset()
        mc = self.mc
        X, H, MIX = self.X, self.H, self.MIX
        O = [ar.get("O%d" % c, [128, T], F32) for c in range(8)]
        WBIG = ar.get("WBIG", [128, 8, 1024], BF16)
        HID = [ar.get("HID%d" % c, [128, T], BF16) for c in range(8)]
        RL = ar.get("RL", [128, T], BF16)
        WST0, WBF0 = self.WST, self.WBF
        self.WST = Rot(WST0.bufs + [ar.get("wstx%d" % i, [128, 2048], F32) for i in range(2)])
        self.WBF = Rot(WBF0.bufs + [ar.get("wbfx%d" % i, [128, 2048], BF16) for i in range(2)])
        self.cast_engs = ("pool", "dve", "act")

        def add_residual(gco):
            rstd = self.rstd_of(O, lambda c: O[c][:, :], self.sq, self.rstd)
            for c in range(8):
                e = self.el()
                self.stt(e, O[c][:, :], O[c][:, :], gco[:, c:c + 1], rstd[:, :], ALU.mult, ALU.mult, [O[c], rstd, self.modc], [O[c]])
                self.tt(e, X[c][:, :], X[c][:, :], O[c][:, :], ALU.add, [O[c], X[c]], [X[c]])

        for q in range(4):
            st_ap = self.w_out[l, q * 256:(q + 1) * 256, :].rearrange("(k p) n -> p k n", p=128)
            stb, sv = self.load_block(st_ap, 2, 1024, cast=False)
            self.cp(self.fw.pick("cast", self.cast_engs), WBIG[:, 2 * q:2 * q + 2, :], sv, [stb], [WBIG])
        for oc in range(8):
            for th in range(2):
                bk = fw.bank()
                for kc in range(8):
                    self.mm(bk[:, :], WBIG[:, kc, oc * 128:(oc + 1) * 128], MIX[kc][:, th * 512:(th + 1) * 512], kc == 0, kc == 7, [WBIG, MIX[kc]], [bk])
                self.cp(self.ev(), O[oc][:, th * 512:(th + 1) * 512], bk[:, :], [bk], [O[oc]])
        add_residual(mc["gco1"])
        rstd = self.rstd_of(X, lambda c: X[c][:, :], self.sq, self.rstd)
        for c in range(8):
            tmp = self.sq.next()
            self.stt(self.el(), tmp[:, :], X[c][:, :], mc["coefA2"][:, c:c + 1], rstd[:, :], ALU.mult, ALU.mult, [X[c], rstd, self.modc], [tmp])
            self.act(H[c][:, :], tmp[:, :], AF.Identity, [tmp, self.modc], [H[c]], bias=mc["shift2"][:, c:c + 1])
        for q in range(4):
            for fb in range(4):
                wb, wv = self.load_block(self.ffn_w1[l, :, (q * 8 + fb * 2) * 128:(q * 8 + fb * 2 + 2) * 128].rearrange("(k p) n -> p k n", p=128), 8, 256)

                def evh(jj, th, bk, fb=fb):
                    sl = slice(th * 512, (th + 1) * 512)
                    self.act(RL[:, sl], bk[:, :], AF.Relu, [bk], [RL])
                    self.tt(self.el(), HID[fb * 2 + jj][:, sl], RL[:, sl], RL[:, sl], ALU.mult, [RL], [HID[fb * 2 + jj]])
                self.proj_fm(wb, wv, 2, evh)
                hb = fb
                st_ap = self.ffn_w2[l, (q * 8 + hb * 2) * 128:(q * 8 + hb * 2 + 2) * 128, :].rearrange("(k p) n -> p k n", p=128)
                stb, sv = self.load_block(st_ap, 2, 1024, cast=False)
                self.cp(self.fw.pick("cast", self.cast_engs), WBIG[:, 2 * hb:2 * hb + 2, :], sv, [stb], [WBIG])
            for oc in range(8):
                for th in range(2):
                    sl = slice(th * 512, (th + 1) * 512)
                    bk = fw.bank()
                    for kc in range(8):
                        self.mm(bk[:, :], WBIG[:, kc, oc * 128:(oc + 1) * 128], HID[kc][:, sl], kc == 0, kc == 7, [WBIG, HID[kc]], [bk])
                    if q == 0:
                        self.cp(self.ev(), O[oc][:, sl], bk[:, :], [bk], [O[oc]])
                    else:
                        self.tt("dve", O[oc][:, sl], bk[:, :], O[oc][:, sl], ALU.add, [bk, O[oc]], [O[oc]])
        add_residual(mc["gco2"])
        self.WST, self.WBF = WST0, WBF0
        self.cast_engs = ("pool", "dve", "pool")

    def nat_latent(self, QT, KT):
        fw, l = self.fw, self.l
        ar = self.arena
        Vr = ar.get("Vr", [128, 16, 4, 128], BF16)
        CKf = ar.get("CKf", [128, 2, 256], F32)
        CK = ar.get("CK", [128, 2, 256], BF16)
        CVf = ar.get("CVf", [128, 2, 256], F32)
        CV = ar.get("CV", [128, 2, 4, 128], BF16)
        ET = ar.get("ET", [128, 4, 15, 64], F32)
        PTp = ar.pool("PT", 2, [128, 8, 64], BF16)
        PCp = ar.pool("PC", 2, [128, 2, 64], BF16)
        RDp = ar.pool("RD", 2, [128, 64], F32)
        fw.op("pool", lambda e: e.memset(Vr[0:64, :, :, 64:128], 1.0), [], [Vr])
        fw.op("pool", lambda e: e.memset(CV[:, :, :, 64:128], 1.0), [], [CV])
        fw.dma("sp", CKf[:, :, :], self.cnk[l].rearrange("j p t -> p j t"), writes=[CKf])
        self.cp("pool", CK[:, :, :], CKf[:, :, :], [CKf], [CK])
        fw.dma("sp", CVf[:, :, :], self.cnv[l].rearrange("(c p) f -> p c f", p=128), writes=[CVf])
        self.cp("pool", CV[:, :, :, 0:64], CVf[:, :, :].rearrange("p c (h d) -> p c h d", h=4), [CVf], [CV])
        ETr = ar.get("ETr", [128, 4, 15, 64], F32)
        src = bass.AP(tensor=self.rpbpad.t, offset=l * 4 * 15 * 127, ap=[[1, 64], [15 * 127, 4], [127, 15], [1, 64]])
        fw.dma("sp", ETr[0:64, :, :, :], src, writes=[ETr])
        flip = self.cst("flip", rows=64)
        etr2 = ETr[0:64, :, :, :].rearrange("p h r q -> p (h r q)")
        et2 = ET[0:64, :, :, :].rearrange("p h r q -> p (h r q)")
        for cb in range(8):
            bk = fw.bank()
            self.mm(bk[0:64, 0:480], flip, etr2[:, cb * 480:(cb + 1) * 480], True, True, [self.cstb, ETr], [bk])
            self.act(et2[:, cb * 480:(cb + 1) * 480], bk[0:64, 0:480], AF.Exp, [bk], [ET])
        nm = self.cst("natmask", rows=64)
        for h in range(4):
            self.tt("dve", ET[0:64, h, :, :], ET[0:64, h, :, :], nm.unsqueeze(1).to_broadcast([64, 15, 64]), ALU.mult, [ET, self.cstb], [ET])
        wb, wv = self.win_block(l, 1152, 256)
        self.proj_fm(wb, wv, 2, lambda jj, th, bk: self.cp(self.ev(), QT[:, jj, th * 512:(th + 1) * 512], bk[:, :], [bk], [QT]))
        wb, wv = self.win_block(l, 1408, 256)
        self.proj_fm(wb, wv, 2, lambda jj, th, bk: self.cp(self.ev(), KT[:, jj, th * 512:(th + 1) * 512], bk[:, :], [bk], [KT]))
        wb, wv = self.win_block(l, 1664, 256)
        self.proj_tm(wb, wv, 256, lambda ti, bk: self.cp(self.ev(), Vr[0:64, ti, :, 0:64], bk[0:64, 0:256].rearrange("p (h d) -> p h d", h=4), [bk], [Vr]), tile_tokens=64)
        def unit(slot, r, h):
            rs = min(max(r - 4, 0), 8)
            roff0 = rs - r + 7
            qs = slice(r * 64, (r + 1) * 64)
            j, hp = h // 2, h % 2
            P = slice(hp * 64, hp * 64 + 64)
            PT, PC, rd = PTp.bufs[slot], PCp.bufs[slot], RDp.bufs[slot]
            bl, bc, bo = fw.banks[slot * 4], fw.banks[slot * 4 + 1], fw.banks[slot * 4 + 2]
            for a in range(8):
                kr = rs + a
                self.mm(bl[0:64, a * 64:(a + 1) * 64], KT[P, j, kr * 64:(kr + 1) * 64], QT[P, j, qs], True, True, [KT, QT], [bl])
            for c in range(2):
                self.mm(bc[:, c * 64:(c + 1) * 64], CK[P, j, c * 128:(c + 1) * 128], QT[P, j, qs], True, True, [CK, QT], [bc])
            yield
            self.act(PT[0:64, :, :], bl[0:64, :].rearrange("p (a q) -> p a q", a=8), AF.Exp, [bl], [PT], scale=0.125)
            self.act(PC[:, :, :], bc[:, 0:128].rearrange("p (c q) -> p c q", c=2), AF.Exp, [bc], [PC], scale=0.125)
            yield
            self.tt("dve", PT[0:64, :, :], PT[0:64, :, :], ET[0:64, h, roff0:roff0 + 8, :], ALU.mult, [PT, ET], [PT])
            yield
            for a in range(8):
                self.mm(bo[:, 0:64], Vr[0:64, rs + a, h, :], PT[0:64, a, :], a == 0, False, [Vr, PT], [bo])
            for c in range(2):
                self.mm(bo[:, 0:64], CV[:, c, h, :], PC[:, c, :], False, c == 1, [CV, PC], [bo])
            yield
            self.fw.op("dve", lambda e: e.reciprocal(rd[0:64, :], bo[64:128, 0:64]), [bo], [rd])
            self.tt("dve", self.MIX[2 + j][P, qs], bo[0:64, 0:64], rd[0:64, :], ALU.mult, [bo, rd], [self.MIX[2 + j]])
        self.run_slots([(r_, h_) for r_ in range(16) for h_ in range(4)], unit, 2)

    def swa_latent(self, QT, KT, esink):
        fw, l = self.fw, self.l
        ar = self.arena
        ROPE = ar.get("ROPE", [128, 2, T], F32)
        TQ = ar.get("TQ", [128, 2, T], F32)
        KTf = ar.get("KTf", [128, T], F32)
        Vt = ar.get("Vt", [128, NT, 2, 128], BF16)
        CKf = ar.get("CKf", [128, 2, 256], F32)
        CK = ar.get("CK", [128, 2, 256], BF16)
        CVf = ar.get("CVf", [128, 2, 128], F32)
        CV = ar.get("CV", [128, 2, 2, 128], BF16)
        TMPp = ar.pool("TMP", 2, [128, 512], F32)
        PTp = ar.pool("PT", 2, [128, 3, 128], BF16)
        PCp = ar.pool("PC", 2, [128, 2, 128], BF16)
        RDp = ar.pool("RD", 2, [128, 128], F32)
        fw.dma("sp", ROPE[:, :, :], self.rope[:, :, :], writes=[ROPE])
        fw.op("pool", lambda e: e.memset(Vt[:, :, :, 64:128], 1.0), [], [Vt])
        fw.op("pool", lambda e: e.memset(CV[:, :, :, 64:128], 1.0), [], [CV])
        fw.dma("sp", CKf[:, :, :], self.csk[l].rearrange("g p t -> p g t"), writes=[CKf])
        self.cp("pool", CK[:, :, :], CKf[:, :, :], [CKf], [CK])
        fw.dma("sp", CVf[:, :, :], self.csv[l].rearrange("(c p) f -> p c f", p=128), writes=[CVf])
        self.cp("pool", CV[:, :, :, 0:64], CVf[:, :, :].rearrange("p c (h d) -> p c h d", h=2), [CVf], [CV])
        wb, wv = self.win_block(l, 3200, 256)
        self.proj_fm(wb, wv, 2, lambda jj, th, bk: self.tt("dve", TQ[:, jj, th * 512:(th + 1) * 512], bk[:, :], ROPE[:, 0, th * 512:(th + 1) * 512], ALU.mult, [bk, ROPE], [TQ]))
        wb, wv = self.win_block(l, 0, 256, src=self.w_in_sw)

        def evq(jj, th, bk):
            sl = slice(th * 512, (th + 1) * 512)
            tmp = TMPp.next()
            self.tt("dve", tmp[:, :], bk[:, :], ROPE[:, 1, sl], ALU.mult, [bk, ROPE], [tmp])
            self.tt("pool", QT[:, jj, sl], tmp[:, :], TQ[:, jj, sl], ALU.add, [tmp, TQ], [QT])
        self.proj_fm(wb, wv, 2, evq)
        wb, wv = self.win_block(l, 3456, 256)
        self.proj_fm(wb, wv, 1, lambda jj, th, bk: self.tt("dve", KTf[:, th * 512:(th + 1) * 512], bk[:, :], ROPE[:, 0, th * 512:(th + 1) * 512], ALU.mult, [bk, ROPE], [KTf]))
        self.proj_tm(wb, wv[:, :, 128:256], 128, lambda ti, bk: self.cp(self.ev(), Vt[:, ti, :, 0:64], bk[:, 0:128].rearrange("p (h d) -> p h d", h=2), [bk], [Vt]))
        wb, wv = self.win_block(l, 256, 128, src=self.w_in_sw)

        def evk(jj, th, bk):
            sl = slice(th * 512, (th + 1) * 512)
            tmp = TMPp.next()
            self.tt("dve", tmp[:, :], bk[:, :], ROPE[:, 1, sl], ALU.mult, [bk, ROPE], [tmp])
            self.tt("pool", KTf[:, sl], tmp[:, :], KTf[:, sl], ALU.add, [tmp, KTf], [KTf])
            for gg in range(2):
                for hp in range(2):
                    self.cp(self.ev(), KT[hp * 64:hp * 64 + 64, gg, sl], KTf[gg * 64:gg * 64 + 64, sl], [KTf], [KT])
        self.proj_fm(wb, wv, 1, evk)
        def unit(slot, n, h):
            qs = slice(n * 128, (n + 1) * 128)
            kbl = [kb for kb in (n - 1, n, n + 1) if 0 <= kb < NT]
            nk = len(kbl)
            j, hp = h // 2, h % 2
            g = j
            P = slice(hp * 64, hp * 64 + 64)
            PT, PC, rd = PTp.bufs[slot], PCp.bufs[slot], RDp.bufs[slot]
            bl, bc, bo = fw.banks[slot * 4], fw.banks[slot * 4 + 1], fw.banks[slot * 4 + 2]
            for i, kb in enumerate(kbl):
                self.mm(bl[:, i * 128:(i + 1) * 128], KT[P, g, kb * 128:(kb + 1) * 128], QT[P, j, qs], True, True, [KT, QT], [bl])
            for c in range(2):
                self.mm(bc[:, c * 128:(c + 1) * 128], CK[P, g, c * 128:(c + 1) * 128], QT[P, j, qs], True, True, [CK, QT], [bc])
            yield
            self.act(PT[:, 0:nk, :], bl[:, 0:nk * 128].rearrange("p (a q) -> p a q", a=nk), AF.Exp, [bl], [PT], scale=0.125)
            self.act(PC[:, :, :], bc[:, 0:256].rearrange("p (c q) -> p c q", c=2), AF.Exp, [bc], [PC], scale=0.125)
            yield
            for i, kb in enumerate(kbl):
                if kb == n - 1:
                    self.tt("dve", PT[:, i, :], PT[:, i, :], self.cst("mprev"), ALU.mult, [PT, self.cstb], [PT])
                elif kb == n + 1:
                    self.tt("pool", PT[:, i, :], PT[:, i, :], self.cst("mnext"), ALU.mult, [PT, self.cstb], [PT])
            yield
            for i, kb in enumerate(kbl):
                self.mm(bo[:, 0:128], Vt[:, kb, g, :], PT[:, i, :], i == 0, False, [Vt, PT], [bo])
            for c in range(2):
                self.mm(bo[:, 0:128], CV[:, c, g, :], PC[:, c, :], False, c == 1, [CV, PC], [bo])
            yield
            self.cp("dve", rd[0:64, :], bo[64:128, 0:128], [bo], [rd])
            self.ts("dve", rd[0:64, :], rd[0:64, :], esink[0:64, h:h + 1], None, ALU.add, None, [rd, esink], [rd])
            self.fw.op("dve", lambda e: e.reciprocal(rd[0:64, :], rd[0:64, :]), [rd], [rd])
            self.tt("dve", self.MIX[6 + j][P, qs], bo[0:64, 0:128], rd[0:64, :], ALU.mult, [bo, rd], [self.MIX[6 + j]])
        self.run_slots([(n_, h_) for n_ in range(NT) for h_ in range(4)], unit, 2)


_PROG_CACHE = {}


def _get_prog(debug=(), layers=DEPTH, groups=("P", "S")):
    key = (tuple(sorted(debug)), layers, tuple(groups))
    if key not in _PROG_CACHE:
        _PROG_CACHE[key] = Prog(debug, layers, groups)
    return _PROG_CACHE[key]


def _prep_inputs(inp, layers=DEPTH):
    f = lambda k: np.ascontiguousarray(np.asarray(inp[k], dtype=np.float32))
    cst, rope = _build_consts()
    cols = _build_cols(inp)
    w_in = f("w_in")
    idx = np.concatenate([3200 + h * 64 + ROPE_PERM for h in range(4)] + [3456 + h * 64 + ROPE_PERM for h in range(2)])
    w_in_sw = np.ascontiguousarray(w_in[:, :, idx])
    lora2 = np.ascontiguousarray(np.concatenate([f("rw_w2").transpose(0, 2, 1, 3), f("rw_a2").transpose(0, 2, 1, 3)], axis=1))
    rpb = f("nat_rpb")
    rpbpad = np.zeros((DEPTH, 4, 15, 127), np.float32)
    rpbpad[..., 48:79] = rpb[..., ::-1]
    NL = layers
    shared = dict(cols=cols, cst=cst, rope=rope, mod_w=f("mod_w")[:NL], w_in=w_in[:NL], w_in_sw=w_in_sw[:NL], w_out=f("w_out")[:NL],
                  ffn_w1=f("ffn_w1")[:NL], ffn_w2=f("ffn_w2")[:NL], lora2=lora2, rw_g2=f("rw_g2"), rpbpad=rpbpad)
    xp, xs = f("x_prompt"), f("x_sample")
    c, c_ctx = f("c"), f("c_ctx")
    cn, cs = f("cache_nat_kv"), f("cache_swa_kv")
    srw, shg = f("state_rwkv"), f("state_hgrn")
    maps = []
    for core in range(8):
        s = core % 2
        m = dict(shared)
        xpc = xp[core * 4:(core + 1) * 4].reshape(T, D)
        m["xT_p"] = np.ascontiguousarray(xpc.T.reshape(8, 128, T).transpose(1, 0, 2))
        m["xT_s"] = np.ascontiguousarray(xs[s].T.reshape(8, 128, T).transpose(1, 0, 2))
        cond = np.stack([c_ctx, c[s]], axis=1)
        m["condT"] = np.ascontiguousarray(cond.reshape(8, 128, 2).transpose(1, 0, 2))
        nk = cn[s, :, 0].reshape(DEPTH, 256, 256)
        m["cnk"] = np.ascontiguousarray(nk.transpose(0, 2, 1).reshape(DEPTH, 2, 128, 256))
        m["cnv"] = np.ascontiguousarray(cn[s, :, 1].reshape(DEPTH, 256, 256))
        sk = cs[s, :, 0].reshape(DEPTH, 256, 2, 64).transpose(0, 2, 3, 1)
        m["csk"] = np.ascontiguousarray(np.concatenate([sk, sk], axis=2))
        m["csv"] = np.ascontiguousarray(cs[s, :, 1].reshape(DEPTH, 256, 128))
        st = srw[s].transpose(0, 1, 2, 4, 3)
        m["srw"] = np.ascontiguousarray(st.reshape(DEPTH, 2, 2, 2, 64, 64).transpose(0, 1, 3, 4, 2, 5).reshape(DEPTH, 2, 128, 2, 64))
        sh = shg[s]
        m["shg"] = np.ascontiguousarray(sh.reshape(DEPTH, 2, 2, 2, 64, 64).transpose(0, 1, 3, 4, 2, 5).reshape(DEPTH, 2, 128, 2, 64))
        maps.append(m)
    return maps


def _assemble(results):
    B = 32
    y_p = np.zeros((B, 256, D), np.float32)
    y_s = np.zeros((2, 1024, D), np.float32)
    natkv = np.zeros((B, DEPTH, 2, 256, 4, 64), np.float32)
    swakv = np.zeros((B, DEPTH, 2, 256, 2, 64), np.float32)
    rwst = np.zeros((B, DEPTH, 2, 4, 64, 64), np.float32)
    hgst = np.zeros((B, DEPTH, 2, 4, 64, 64), np.float32)
    for core in range(8):
        r = results[core]
        bs = slice(core * 4, core * 4 + 4)
        yT = r["yT_p"].transpose(1, 0, 2).reshape(D, T)
        y_p[bs] = yT.T.reshape(4, 256, D)
        if core < 2:
            y_s[core] = r["yT_s"].transpose(1, 0, 2).reshape(D, T).T
        nk = r["natk"].reshape(DEPTH, 256, 4, 256)
        natkv[bs, :, 0] = nk.transpose(2, 0, 3, 1).reshape(4, DEPTH, 256, 4, 64)
        nv = r["natv"].reshape(DEPTH, 4, 256, 256)
        natkv[bs, :, 1] = nv.transpose(1, 0, 2, 3).reshape(4, DEPTH, 256, 4, 64)
        sk = r["swak"].reshape(DEPTH, 128, 4, 256)
        swakv[bs, :, 0] = sk.transpose(2, 0, 3, 1).reshape(4, DEPTH, 256, 2, 64)
        sv = r["swav"].reshape(DEPTH, 4, 256, 128)
        swakv[bs, :, 1] = sv.transpose(1, 0, 2, 3).reshape(4, DEPTH, 256, 2, 64)
        rs = r["rwst"].reshape(DEPTH, 2, 4, 2, 64, 2, 64)
        rwst[bs] = rs.transpose(2, 0, 1, 5, 3, 6, 4).reshape(4, DEPTH, 2, 4, 64, 64)
        hs = r["hgst"].reshape(DEPTH, 2, 4, 2, 64, 2, 64)
        hgst[bs] = hs.transpose(2, 0, 1, 5, 3, 4, 6).reshape(4, DEPTH, 2, 4, 64, 64)
    return (y_p, y_s, natkv, swakv, rwst, hgst)


def kernel(**inputs):
    prog = _get_prog()
    maps = _prep_inputs(inputs)
    res = run_bass_kernel_spmd(prog.nc, maps, core_ids=list(range(8)))
    return _assemble(res.results)
```

```python
import numpy as np
import concourse.bass as bass
import concourse.mybir as mybir
from concourse.bass_utils import run_bass_kernel_spmd

F32 = mybir.dt.float32
BF16 = mybir.dt.bfloat16
AF = mybir.ActivationFunctionType
ALU = mybir.AluOpType

DEPTH = 4
D = 1024
T = 1024
NT = 8
D_IN = 3712
DECAY = 0.6065306597126334
EPS = 1e-6
LN_EPS = 64e-5


class Buf:
    __slots__ = ("name", "t", "lw", "rd", "wsem", "wcnt", "rsem", "rcnt")

    def __init__(self, name, t):
        self.name = name
        self.t = t
        self.lw = None
        self.rd = []
        self.wsem = None
        self.wcnt = 0
        self.rsem = None
        self.rcnt = 0

    def __getitem__(self, idx):
        return self.t[idx]


class FW:
    ENG = ("pe", "dve", "act", "pool", "sp")

    def __init__(self, nc):
        self.nc = nc
        self.eng = {"pe": nc.tensor, "dve": nc.vector, "act": nc.scalar, "pool": nc.gpsimd, "sp": nc.sync}
        self.sem = {}
        self.cnt = {}
        for e in ("pe", "dve", "act", "pool"):
            self.sem[e] = nc.alloc_semaphore(name="s_" + e)
            self.cnt[e] = 0
        self.seen = {e: {} for e in self.ENG}
        self.ninst = 0
        self.out_dma = []
        self.dma_keys = {}
        self.rr = {}
        self.banks = None
        self.bank_i = 0
        self.dcnt = {}
        self.free_dsems = []

    def sb(self, name, shape, dtype=F32):
        return Buf(name, self.nc.alloc_sbuf_tensor(name, list(shape), dtype))

    def ps(self, name, shape, dtype=F32):
        return Buf(name, self.nc.alloc_psum_tensor(name, list(shape), dtype))

    def dram(self, name, shape, dtype=F32, kind="Internal"):
        return Buf(name, self.nc.dram_tensor(name, list(shape), dtype, kind=kind))

    def bank(self):
        if self.banks is None:
            self.banks = [self.ps("bank%d" % i, [128, 512]) for i in range(8)]
        b = self.banks[self.bank_i % 8]
        self.bank_i += 1
        return b

    def pick(self, key, engs):
        i = self.rr.get(key, 0)
        self.rr[key] = i + 1
        return engs[i % len(engs)]

    def _need(self, e, deps):
        eng = self.eng[e]
        seen = self.seen[e]
        best = {}
        for d in deps:
            if d is None:
                continue
            k, v = d
            if best.get(k, 0) < v:
                best[k] = v
        for k, v in best.items():
            if seen.get(k, 0) >= v:
                continue
            if e == "pe" and k == "pe":
                continue
            seen[k] = v
            eng.wait_ge(self.sem[k], v)

    @staticmethod
    def _deps(reads, writes, skip_dma_waw=False):
        deps = []
        for b in reads:
            deps.append(b.lw)
        for b in writes:
            if not (skip_dma_waw and b.lw is not None and isinstance(b.lw[0], tuple)):
                deps.append(b.lw)
            deps.extend(b.rd)
        return deps

    def op(self, e, fn, reads=(), writes=()):
        self._need(e, self._deps(reads, writes))
        ins = fn(self.eng[e])
        self.cnt[e] += 1
        v = self.cnt[e]
        ins.then_inc(self.sem[e], 1)
        for b in writes:
            b.lw = (e, v)
            b.rd = []
        for b in reads:
            if b not in writes:
                b.rd.append((e, v))
        self.ninst += 1
        return ins

    def dma(self, q, out, in_, reads=(), writes=(), part=False, is_output=False):
        self._need(q, self._deps(reads, writes, skip_dma_waw=part))
        ins = self.eng[q].dma_start(out=out, in_=in_)
        if writes:
            b = writes[0]
            if b.wsem is None:
                b.wsem = self._dsem()
            key = b.wsem
        else:
            b = reads[0]
            if b.rsem is None:
                b.rsem = self._dsem()
            key = b.rsem
        self.dcnt[key] += 16
        v = self.dcnt[key]
        ins.then_inc(self.sem[key], 16)
        self.dma_keys[key] = v
        for w in writes:
            w.lw = (key, v)
            w.rd = []
        for r in reads:
            r.rd.append((key, v))
        if is_output:
            self.out_dma.append((key, v))
        self.ninst += 1
        return ins

    def _dsem(self):
        if self.free_dsems:
            return self.free_dsems.pop()
        key = ("d", len(self.dcnt))
        self.sem[key] = self.nc.alloc_semaphore(name="dsem%d" % len(self.dcnt))
        self.dcnt[key] = 0
        return key

    def barrier(self):
        deps = [(e, self.cnt[e]) for e in ("pe", "dve", "act", "pool") if self.cnt[e]]
        deps += list(self.dma_keys.items())
        for e in self.ENG:
            self._need(e, deps)

    def finish(self):
        self.barrier()


class Arena:
    def __init__(self, fw, nwords):
        self.fw = fw
        self.nwords = nwords
        self.t32 = fw.nc.alloc_sbuf_tensor("arena", [128, nwords], F32)
        self.t16 = self.t32.bitcast(BF16)
        self.off = 0
        self.peak = 0
        self.made = []

    def reset(self):
        self.fw.barrier()
        self.off = 0
        for b in self.made:
            for k in (b.wsem, b.rsem):
                if k is not None:
                    self.fw.free_dsems.append(k)
            b.wsem = b.rsem = None
        self.made = []

    def get(self, name, shape, dtype=F32):
        n = int(np.prod(shape[1:]))
        words = n if dtype == F32 else (n + 1) // 2
        assert self.off + words <= self.nwords, (name, self.off, words, self.nwords)
        if dtype == F32:
            ap = self.t32[:, self.off:self.off + n]
        else:
            ap = self.t16[:, 2 * self.off:2 * self.off + n]
        if len(shape) > 2:
            names = "abcdefg"[:len(shape) - 1]
            kw = {names[i]: shape[1 + i] for i in range(len(shape) - 2)}
            ap = ap.rearrange("p (%s) -> p %s" % (" ".join(names), " ".join(names)), **kw)
        if shape[0] < 128:
            ap = ap[0:shape[0]]
        self.off += words
        self.peak = max(self.peak, self.off)
        b = Buf(name, ap)
        self.made.append(b)
        return b

    def pool(self, name, n, shape, dtype=F32):
        return Rot([self.get("%s%d" % (name, i), shape, dtype) for i in range(n)])


class Rot:
    def __init__(self, bufs):
        self.bufs = bufs
        self.i = 0

    def next(self):
        b = self.bufs[self.i % len(self.bufs)]
        self.i += 1
        return b


class ColTable:
    def __init__(self):
        self.idx = {}
        self.cols = []
        self.n = 0

    def add(self, name, per_layer_vecs):
        k = len(per_layer_vecs[0]) // 128
        self.idx[name] = self.n
        self.cols.append(np.stack([np.asarray(v, np.float32).reshape(k, 128) for v in per_layer_vecs]))
        self.n += k

    def build(self):
        a = np.concatenate(self.cols, axis=1)
        return np.ascontiguousarray(a.transpose(2, 0, 1))


def _col_names():
    ct = ColTable()
    z = lambda n: [np.zeros(n, np.float32)] * DEPTH
    for nm, n in COL_SPEC:
        ct.add(nm, z(n))
    return ct.idx, ct.n


COL_SPEC = [("norm_g", 4096), ("mod_b", 6144),
            ("mu_rkv0", 768), ("mu_rkv1", 768), ("mu_lora0", 128), ("mu_lora1", 128),
            ("w0_0", 256), ("w0_1", 256), ("a0_0", 256), ("a0_1", 256),
            ("kk", 256), ("ka", 256), ("rk", 256), ("lnx_w", 256), ("lnx_b", 256),
            ("lbl0", 1024), ("lbl1", 1024), ("hg_norm", 256), ("sink", 512)]
COLI, NCOL = _col_names()


def _build_cols(inp):
    ct = ColTable()
    L = range(DEPTH)
    g = lambda k: np.asarray(inp[k], np.float32)
    vals = {
        "norm_g": [g("norm_g")[l].reshape(-1) for l in L],
        "mod_b": [g("mod_b")[l] for l in L],
        "mu_rkv0": [g("rw_mu_rkv")[l, 0].reshape(-1) for l in L],
        "mu_rkv1": [g("rw_mu_rkv")[l, 1].reshape(-1) for l in L],
        "mu_lora0": [g("rw_mu_lora")[l, 0].reshape(-1) for l in L],
        "mu_lora1": [g("rw_mu_lora")[l, 1].reshape(-1) for l in L],
        "w0_0": [g("rw_w0")[l, 0] for l in L], "w0_1": [g("rw_w0")[l, 1] for l in L],
        "a0_0": [g("rw_a0")[l, 0] for l in L], "a0_1": [g("rw_a0")[l, 1] for l in L],
        "kk": [g("rw_kk")[l] for l in L], "ka": [g("rw_ka")[l] for l in L],
        "rk": [g("rw_rk")[l].reshape(-1) for l in L],
        "lnx_w": [g("rw_lnx_w")[l] for l in L], "lnx_b": [g("rw_lnx_b")[l] for l in L],
        "lbl0": [g("hg_lb_logits")[0].reshape(-1) for l in L],
        "lbl1": [g("hg_lb_logits")[1].reshape(-1) for l in L],
        "hg_norm": [g("hg_norm")[l] for l in L],
        "sink": [np.repeat(g("swa_sink")[l], 128) for l in L],
    }
    for nm, n in COL_SPEC:
        assert len(vals[nm][0]) == n, nm
        ct.add(nm, vals[nm])
    return ct.build()


CST_SPEC = [("ident", 128), ("ones", 128), ("blk", 128),
            ("g4_0", 512), ("g4_1", 512), ("mb_0", 128), ("mb_1", 128),
            ("reset", 256), ("natmask", 64),
            ("mprev", 128), ("mnext", 128), ("flip", 64)]
CSTI = {}
_o = 0
for _n, _w in CST_SPEC:
    CSTI[_n] = (_o, _w)
    _o += _w
NCST = _o


def _build_consts():
    c = np.zeros((128, NCST), np.float32)

    def put(nm, a):
        o, w = CSTI[nm]
        c[:a.shape[0], o:o + w] = a
    i = np.arange(128)
    put("ident", np.eye(128))
    put("ones", np.ones((128, 128)))
    put("blk", ((i[:, None] // 64) == (i[None, :] // 64)).astype(np.float32))
    for d in (0, 1):
        if d == 0:
            strictT = (i[:, None] < i[None, :]).astype(np.float32)
            inclT = (i[:, None] <= i[None, :]).astype(np.float32)
        else:
            strictT = (i[:, None] > i[None, :]).astype(np.float32)
            inclT = (i[:, None] >= i[None, :]).astype(np.float32)
        put("g4_%d" % d, np.concatenate([-strictT, inclT, strictT, inclT], axis=1))
        put("mb_%d" % d, -strictT.T)
    r = np.ones((128, 256), np.float32)
    r[:, 0] = 0
    r[:, 128] = 0
    put("reset", r)
    t = np.arange(1024)
    row = (t // 64).astype(np.float32)
    col = (t % 64).astype(np.float32)
    inv = (1.0 / (10000.0 ** (np.arange(16, dtype=np.float32) / 16))).astype(np.float32)
    C = np.zeros((64, 1024), np.float32)
    S = np.zeros((64, 1024), np.float32)
    for base, pos in ((0, row), (32, col)):
        ang = (pos[None, :] * inv[:, None]).astype(np.float32)
        C[base:base + 16] = np.cos(ang)
        C[base + 16:base + 32] = np.cos(ang)
        S[base:base + 16] = -np.sin(ang)
        S[base + 16:base + 32] = np.sin(ang)
    rope = np.stack([np.concatenate([C, C], axis=0), np.concatenate([S, S], axis=0)], axis=1)
    kc = np.arange(64)[:, None]
    qc = np.arange(64)[None, :]
    ws = np.clip(qc - 8, 0, 48)
    put("natmask", ((kc >= ws) & (kc < ws + 16)).astype(np.float32))
    put("mprev", (i[:, None] >= i[None, :]).astype(np.float32))
    put("mnext", (i[:, None] <= i[None, :]).astype(np.float32))
    put("flip", np.eye(64)[::-1].copy())
    return c, np.ascontiguousarray(rope.astype(np.float32))


ROPE_PERM = np.concatenate([np.arange(16, 32), np.arange(0, 16), np.arange(48, 64), np.arange(32, 48)])


class Prog:
    def __init__(self, debug=(), layers=DEPTH, groups=("P", "S")):
        self.debug = set(debug)
        self.layers = layers
        self.groups = groups
        self.nc = bass.Bass("TRN2", target_bir_lowering=False)
        self.fw = FW(self.nc)
        self.dbg_outs = {}
        self.build()

    def mm(self, out, lhsT, rhs, start, stop, reads, writes):
        return self.fw.op("pe", lambda e: e.matmul(out, lhsT, rhs, start=start, stop=stop), reads, writes)

    def tp(self, out, in_, ident, reads, writes):
        return self.fw.op("pe", lambda e: e.transpose(out, in_, ident), reads, writes)

    def act(self, out, in_, func, reads, writes, bias=None, scale=1.0):
        if bias is None:
            return self.fw.op("act", lambda e: e.activation(out, in_, func, scale=scale), reads, writes)
        return self.fw.op("act", lambda e: e.activation(out, in_, func, bias=bias, scale=scale), reads, writes)

    def tt(self, eng, out, in0, in1, op, reads, writes):
        return self.fw.op(eng, lambda e: e.tensor_tensor(out, in0, in1, op), reads, writes)

    def ts(self, eng, out, in0, s1, s2, op0, op1, reads, writes):
        if s2 is None:
            return self.fw.op(eng, lambda e: e.tensor_scalar(out, in0, s1, None, op0), reads, writes)
        return self.fw.op(eng, lambda e: e.tensor_scalar(out, in0, s1, s2, op0, op1), reads, writes)

    def stt(self, eng, out, in0, scalar, in1, op0, op1, reads, writes):
        eng = "dve"
        return self.fw.op(eng, lambda e: e.scalar_tensor_tensor(out, in0, scalar, in1, op0, op1), reads, writes)

    def rsqrt(self, out, in_, addc, reads, wbuf):
        self.act(out, in_, AF.Ln, reads + [self.epsb], [wbuf], bias=self.epsc(addc))
        self.act(out, out, AF.Exp, [wbuf], [wbuf], scale=-0.5)

    def epsc(self, v):
        i = self.eps_vals.index(v)
        return self.epsb[:, i:i + 1]

    def cp(self, eng, out, in_, reads, writes):
        if eng == "act":
            return self.fw.op("act", lambda e: e.copy(out, in_), reads, writes)
        return self.fw.op(eng, lambda e: e.tensor_copy(out, in_), reads, writes)

    def ev(self):
        return self.fw.pick("ev", ("dve", "act"))

    def el(self):
        return self.fw.pick("el", ("dve", "pool"))

    def dump(self, name, buf, ap, shape, dtype=F32):
        if name not in self.debug:
            return
        o = self.fw.dram("dbg_" + name, list(shape), dtype, kind="ExternalOutput")
        self.dbg_outs[name] = o
        self.fw.dma("sp", o[:], ap, reads=[buf], is_output=True)

    def col(self, l, name, j=0, n=1):
        i = COLI[name] + j
        return self.cols[:, l, i:i + n]

    def cst(self, name, lo=0, hi=None, rows=128):
        o, w = CSTI[name]
        hi = w if hi is None else hi
        return self.cstb[0:rows, o + lo:o + hi]

    def load_block(self, src_ap, a, b, cast=True):
        fw = self.fw
        n = a * b
        st = self.WST.next()
        sview = st[:, 0:n].rearrange("p (a b) -> p a b", a=a)
        fw.dma("sp", sview, src_ap, writes=[st])
        self.mod_tick()
        if not cast:
            return st, sview
        wb = self.WBF.next()
        wview = wb[:, 0:n].rearrange("p (a b) -> p a b", a=a)
        self.cp(self.fw.pick("cast", self.cast_engs), wb[:, 0:n], st[:, 0:n], [st], [wb])
        return wb, wview

    def win_block(self, l, c0, ncol, src=None):
        src = self.w_in if src is None else src
        ap = src[l, :, c0:c0 + ncol].rearrange("(k p) n -> p k n", p=128)
        return self.load_block(ap, 8, ncol)

    def mod_tick(self, force=False):
        if not self.mod_pending:
            return
        if self.in_mod:
            return
        self.in_mod = True
        l, kc, third = self.mod_pending.pop(0)
        fw = self.fw
        st = self.WST.next()
        fw.dma("sp", st[:, 0:2048], self.mod_w[l, kc * 128:(kc + 1) * 128, third * 2048:(third + 1) * 2048], writes=[st])
        bk = fw.bank()
        for j in range(16):
            self.mm(bk[:, 2 * j:2 * j + 2], st[:, j * 128:(j + 1) * 128], self.silu[:, kc, :], True, True, [st, self.silu], [bk])
        dst = self.modacc[:, l, third * 16:(third + 1) * 16, :]
        src = bk[:, 0:32].rearrange("p (a b) -> p a b", b=2)
        if kc == 0:
            self.cp("dve", dst, src, [bk], [self.modacc])
        else:
            self.tt("dve", dst, dst, src, ALU.add, [bk, self.modacc], [self.modacc])
        self.in_mod = False

    def mod_flush(self, l):
        while self.mod_pending and self.mod_pending[0][0] <= l:
            self.mod_tick()

    def mod_finalize(self, l, g):
        fw = self.fw
        mc = self.modc
        m = lambda i: self.modacc[:, l, i * 8:(i + 1) * 8, g]
        mb = lambda i: self.col(l, "mod_b", i * 8, 8)
        ng = lambda i: self.col(l, "norm_g", i * 8, 8)
        R = [self.modacc, self.cols, mc]
        for i in range(6):
            self.tt("dve", mc[:, i, :], m(i), mb(i), ALU.add, R, [mc])
        self.ts("dve", mc[:, 6, :], mc[:, 1, :], 1.0, 32.0, ALU.add, ALU.mult, R, [mc])
        self.tt("dve", mc[:, 6, :], mc[:, 6, :], ng(0), ALU.mult, R, [mc])
        self.ts("dve", mc[:, 7, :], mc[:, 2, :], 32.0, None, ALU.mult, None, R, [mc])
        self.tt("dve", mc[:, 7, :], mc[:, 7, :], ng(1), ALU.mult, R, [mc])
        self.ts("dve", mc[:, 8, :], mc[:, 4, :], 1.0, 32.0, ALU.add, ALU.mult, R, [mc])
        self.tt("dve", mc[:, 8, :], mc[:, 8, :], ng(2), ALU.mult, R, [mc])
        self.ts("dve", mc[:, 9, :], mc[:, 5, :], 32.0, None, ALU.mult, None, R, [mc])
        self.tt("dve", mc[:, 9, :], mc[:, 9, :], ng(3), ALU.mult, R, [mc])
        return dict(shift1=mc[:, 0, :], coefA1=mc[:, 6, :], gco1=mc[:, 7, :], shift2=mc[:, 3, :], coefA2=mc[:, 8, :], gco2=mc[:, 9, :])

    def rstd_of(self, src_bufs, src_ap, sq, rstd):
        fw = self.fw
        bks = [fw.bank(), fw.bank()]
        for c in range(8):
            s = sq.next()
            self.act(s[:, :], src_ap(c), AF.Square, [src_bufs[c]], [s])
            for th in range(2):
                self.mm(bks[th][:, :], self.ones_bf[:, :], s[:, th * 512:(th + 1) * 512], c == 0, c == 7, [self.ones_bf, s], [bks[th]])
        for th in range(2):
            self.rsqrt(rstd[:, th * 512:(th + 1) * 512], bks[th][:, :], D * EPS, [bks[th]], rstd)
        return rstd

    def build(self):
        fw = self.fw
        nc = self.nc
        di = lambda n, s: fw.dram(n, s, F32, kind="ExternalInput")
        do = lambda n, s: fw.dram(n, s, F32, kind="ExternalOutput")
        self.xT = {"P": di("xT_p", [128, 8, T]), "S": di("xT_s", [128, 8, T])}
        self.yT = {"P": do("yT_p", [128, 8, T]), "S": do("yT_s", [128, 8, T])}
        condT = di("condT", [128, 8, 2])
        cols_d = di("cols", [128, DEPTH, NCOL])
        cst_d = di("cst", [128, NCST])
        NL = self.layers
        self.mod_w = di("mod_w", [NL, D, 6 * D])
        self.w_in = di("w_in", [NL, D, D_IN])
        self.w_in_sw = di("w_in_sw", [NL, D, 384])
        self.w_out = di("w_out", [NL, D, D])
        self.ffn_w1 = di("ffn_w1", [NL, D, 4 * D])
        self.ffn_w2 = di("ffn_w2", [NL, 4 * D, D])
        self.lora2 = di("lora2", [DEPTH, 128, 2, 256])
        self.rw_g2 = di("rw_g2", [DEPTH, 128, 256])
        self.rpbpad = di("rpbpad", [DEPTH, 4, 15, 127])
        self.rope = di("rope", [128, 2, T])
        self.cnk = di("cnk", [DEPTH, 2, 128, 256])
        self.cnv = di("cnv", [DEPTH, 256, 256])
        self.csk = di("csk", [DEPTH, 2, 128, 256])
        self.csv = di("csv", [DEPTH, 256, 128])
        self.srw = di("srw", [DEPTH, 2, 128, 2, 64])
        self.shg = di("shg", [DEPTH, 2, 128, 2, 64])
        self.natk = do("natk", [DEPTH, 2, 128, T])
        self.natv = do("natv", [DEPTH, NT, 128, 256])
        self.swak = do("swak", [DEPTH, 128, T])
        self.swav = do("swav", [DEPTH, NT, 128, 128])
        self.rwst = do("rwst", [DEPTH, 2, 4, 128, 2, 64])
        self.hgst = do("hgst", [DEPTH, 2, 4, 128, 2, 64])

        self.cstb = fw.sb("cstb", [128, NCST])
        self.cols = fw.sb("colsb", [128, DEPTH, NCOL])
        self.ident_bf = fw.sb("ident_bf", [128, 128], BF16)
        self.ones_bf = fw.sb("ones_bf", [128, 128], BF16)
        self.blk_bf = fw.sb("blk_bf", [128, 128], BF16)
        self.silu = fw.sb("silu", [128, 8, 2])
        self.modacc = fw.sb("modacc", [128, DEPTH, 48, 2])
        self.modc = fw.sb("modc", [128, 10, 8])
        self.lb = fw.sb("lb", [128, 2, DEPTH, 2])
        self.oml = fw.sb("oml", [128, 2, DEPTH, 2])
        self.X = [fw.sb("X%d" % c, [128, T]) for c in range(8)]
        self.H = [fw.sb("H%d" % c, [128, T], BF16) for c in range(8)]
        self.MIX = [fw.sb("MIX%d" % c, [128, T], BF16) for c in range(8)]
        self.WST = Rot([fw.sb("wst%d" % i, [128, 2048]) for i in range(2)])
        self.WBF = Rot([fw.sb("wbf%d" % i, [128, 2048], BF16) for i in range(2)])
        self.rstd = fw.sb("rstd", [128, T])
        self.sq = Rot([fw.sb("sq%d" % i, [128, T], BF16) for i in range(2)])
        self.arena = Arena(fw, (nc.sbuf_bytes_remaining - 2048) // 4)

        self.eps_vals = [D * EPS, 1e-24, 64 * LN_EPS, 64 * EPS]
        self.epsb = fw.sb("epsb", [128, 4])
        for i, v in enumerate(self.eps_vals):
            fw.op("pool", lambda e: e.memset(self.epsb[:, i:i + 1], v), [], [self.epsb])
        fw.dma("sp", self.cstb[:, :], cst_d[:, :], writes=[self.cstb])
        fw.dma("sp", self.cols[:, :, :], cols_d[:, :, :], writes=[self.cols])
        fw.dma("sp", self.silu[:, :, :], condT[:, :, :], writes=[self.silu])
        self.cp("dve", self.ident_bf[:, :], self.cst("ident"), [self.cstb], [self.ident_bf])
        self.cp("dve", self.ones_bf[:, :], self.cst("ones"), [self.cstb], [self.ones_bf])
        self.cp("dve", self.blk_bf[:, :], self.cst("blk"), [self.cstb], [self.blk_bf])
        self.act(self.silu[:, :, :], self.silu[:, :, :], AF.Silu, [self.silu], [self.silu])
        self.setup_lb()

        self.cast_engs = ("pool", "dve", "pool")
        self.mod_pending = [(l, kc, th) for l in range(self.layers) for kc in range(8) for th in range(3)]
        self.in_mod = False

        for G in self.groups:
            for c in range(8):
                fw.dma("sp", self.X[c][:, :], self.xT[G][:, c, :], writes=[self.X[c]])
            for l in range(self.layers):
                self.block(G, l)
            for c in range(8):
                fw.dma("sp", self.yT[G][:, c, :], self.X[c][:, :], reads=[self.X[c]], is_output=True)
        fw.finish()

    def setup_lb(self):
        E = self.fw.sb("lbE", [128, 2, DEPTH, 2])
        S = self.fw.sb("lbS", [128, 2, 2])
        R = [self.cols, E, S, self.lb, self.oml]
        for d in range(2):
            i = COLI["lbl%d" % d]
            self.act(E[:, d, :, :], self.cols[:, 0, i:i + 8].rearrange("p (a b) -> p a b", a=DEPTH), AF.Exp, R, [E])
            self.tt("dve", S[:, d, :], E[:, d, 0, :], E[:, d, 1, :], ALU.add, R, [S])
            self.tt("dve", S[:, d, :], S[:, d, :], E[:, d, 2, :], ALU.add, R, [S])
            self.tt("dve", S[:, d, :], S[:, d, :], E[:, d, 3, :], ALU.add, R, [S])
            self.fw.op("dve", lambda e: e.reciprocal(S[:, d, :], S[:, d, :]), R, [S])
            for l in range(DEPTH):
                self.tt("dve", E[:, d, l, :], E[:, d, l, :], S[:, d, :], ALU.mult, R, [E])
            self.fw.op("dve", lambda e: e.memset(self.lb[:, d, 0, :], 0.0), R, [self.lb])
            for l in range(1, DEPTH):
                self.tt("dve", self.lb[:, d, l, :], self.lb[:, d, l - 1, :], E[:, d, l, :], ALU.add, R, [self.lb])
        self.ts("dve", self.oml[:, :, :, :], self.lb[:, :, :, :], -1.0, 1.0, ALU.mult, ALU.add, R, [self.oml])

    def block(self, G, l):
        fw = self.fw
        g = 0 if G == "P" else 1
        self.G, self.l = G, l
        self.seqs = [(s * 2, 2) for s in range(4)] if G == "P" else [(0, 8)]
        self.mod_flush(l)
        mc = self.mod_finalize(l, g)
        self.mc = mc
        X, H = self.X, self.H
        rstd = self.rstd_of(X, lambda c: X[c][:, :], self.sq, self.rstd)
        for c in range(8):
            tmp = self.sq.next()
            e = self.el()
            self.stt(e, tmp[:, :], X[c][:, :], mc["coefA1"][:, c:c + 1], rstd[:, :], ALU.mult, ALU.mult, [X[c], rstd, self.modc], [tmp])
            self.act(H[c][:, :], tmp[:, :], AF.Identity, [tmp, self.modc], [H[c]], bias=mc["shift1"][:, c:c + 1])
        self.dump("h_%s%d" % (G, l), H[0], H[0][:, :], [128, T], BF16)
        self.mixer_A()
        self.mixer_B()
        self.mixer_C()
        self.mixer_D()
        for c in range(8):
            self.dump("mix%d_%s%d" % (c, G, l), self.MIX[c], self.MIX[c][:, :], [128, T], BF16)
        self.out_and_ffn()

    def proj_fm(self, wb, wv, nchunks, evac):
        for jj in range(nchunks):
            for th in range(2):
                bk = self.fw.bank()
                for kc in range(8):
                    self.mm(bk[:, :], wv[:, kc, jj * 128:(jj + 1) * 128], self.H[kc][:, th * 512:(th + 1) * 512],
                            kc == 0, kc == 7, [wb, self.H[kc]], [bk])
                evac(jj, th, bk)

    def proj_tm(self, wb, wv, ncol, evac, tile_tokens=128):
        nt = T // tile_tokens
        for ti in range(nt):
            bk = self.fw.bank()
            for kc in range(8):
                self.mm(bk[0:tile_tokens, 0:ncol], self.H[kc][:, ti * tile_tokens:(ti + 1) * tile_tokens], wv[:, kc, 0:ncol],
                        kc == 0, kc == 7, [wb, self.H[kc]], [bk])
            evac(ti, bk)

    def ctx_attn(self, QT, KT, ksel, Vt, vsel, mix0, PTp, RDp, esink=None):
        fw = self.fw

        def unit(slot, s, h):
            t0 = s * 256
            j, hp = h // 2, h % 2
            P = slice(hp * 64, hp * 64 + 64)
            PT = PTp.bufs[slot]
            rd = RDp.bufs[slot]
            bks = [fw.banks[slot * 4 + kc] for kc in range(2)]
            for kc in range(2):
                self.mm(bks[kc][:, 0:256], KT[P, ksel(h), t0 + kc * 128:t0 + (kc + 1) * 128], QT[P, j, t0:t0 + 256], True, True, [KT, QT], [bks[kc]])
            yield
            for kc in range(2):
                self.act(PT[:, kc, :], bks[kc][:, 0:256], AF.Exp, [bks[kc]], [PT], scale=0.125)
            yield
            bo = fw.banks[slot * 4 + 2]
            for kc in range(2):
                self.mm(bo[:, 0:256], Vt[:, s * 2 + kc, vsel(h), :], PT[:, kc, :], kc == 0, kc == 1, [Vt, PT], [bo])
            yield
            self.cp("dve", rd[0:64, :], bo[64:128, 0:256], [bo], [rd])
            if esink is not None:
                self.ts("dve", rd[0:64, :], rd[0:64, :], esink[0:64, h:h + 1], None, ALU.add, None, [rd, esink], [rd])
            self.fw.op("dve", lambda e: e.reciprocal(rd[0:64, :], rd[0:64, :]), [rd], [rd])
            self.tt("dve", self.MIX[mix0 + j][P, t0:t0 + 256], bo[0:64, 0:256], rd[0:64, :], ALU.mult, [bo, rd], [self.MIX[mix0 + j]])
        self.run_slots([(s_, h_) for s_ in range(4) for h_ in range(4)], unit, 2)

    def run_slots(self, items, unit, width):
        pend = list(items)
        active = {}
        while pend or active:
            for slot in range(width):
                if slot not in active and pend:
                    active[slot] = unit(slot, *pend.pop(0))
            for slot in list(active.keys()):
                try:
                    next(active[slot])
                except StopIteration:
                    del active[slot]

    def mixer_B(self):
        fw, l, G = self.fw, self.l, self.G
        ar = self.arena
        ar.reset()
        QT = ar.get("QT", [128, 2, T], BF16)
        KT = ar.get("KT", [128, 2, T], BF16)
        if G == "P":
            KTf = ar.get("KTf", [128, 2, T], F32)
            Vf = ar.get("Vf", [128, NT, 256], F32)
            Vt = ar.get("Vt", [128, NT, 4, 128], BF16)
            PTp = ar.pool("PT", 2, [128, 2, 256], BF16)
            RDp = ar.pool("RD", 2, [128, 256], F32)
            fw.op("pool", lambda e: e.memset(Vt[:, :, :, 64:128], 1.0), [], [Vt])
            wb, wv = self.win_block(l, 1152, 256)
            self.proj_fm(wb, wv, 2, lambda jj, th, bk: self.cp(self.ev(), QT[:, jj, th * 512:(th + 1) * 512], bk[:, :], [bk], [QT]))
            wb, wv = self.win_block(l, 1408, 256)

            def evk(jj, th, bk):
                self.cp(self.ev(), KTf[:, jj, th * 512:(th + 1) * 512], bk[:, :], [bk], [KTf])
                self.cp("pool", KT[:, jj, th * 512:(th + 1) * 512], KTf[:, jj, th * 512:(th + 1) * 512], [KTf], [KT])
            self.proj_fm(wb, wv, 2, evk)
            fw.dma("sp", self.natk[l].rearrange("j p t -> p j t"), KTf[:, :, :], reads=[KTf], is_output=True)
            wb, wv = self.win_block(l, 1664, 256)

            def evv(ti, bk):
                self.cp(self.ev(), Vf[:, ti, :], bk[:, 0:256], [bk], [Vf])
                self.cp("pool", Vt[:, ti, :, 0:64], Vf[:, ti, :].rearrange("p (h d) -> p h d", h=4), [Vf], [Vt])
            self.proj_tm(wb, wv, 256, evv)
            fw.dma("sp", self.natv[l].rearrange("n p f -> p n f"), Vf[:, :, :], reads=[Vf], is_output=True)
            self.ctx_attn(QT, KT, lambda h: h // 2, Vt, lambda h: h, 2, PTp, RDp)
        else:
            self.nat_latent(QT, KT)

    def mixer_D(self):
        fw, l, G = self.fw, self.l, self.G
        ar = self.arena
        ar.reset()
        QT = ar.get("QT", [128, 2, T], BF16)
        KT = ar.get("KTd", [128, 2, T], BF16)
        esink = ar.get("esink", [128, 4], F32)
        self.act(esink[:, :], self.col(l, "sink", 0, 4), AF.Exp, [self.cols], [esink])
        if G == "P":
            KTf = ar.get("KTf", [128, T], F32)
            Vf = ar.get("Vf", [128, NT, 128], F32)
            Vt = ar.get("Vt", [128, NT, 2, 128], BF16)
            PTp = ar.pool("PT", 2, [128, 2, 256], BF16)
            RDp = ar.pool("RD", 2, [128, 256], F32)
            fw.op("pool", lambda e: e.memset(Vt[:, :, :, 64:128], 1.0), [], [Vt])
            wb, wv = self.win_block(l, 3200, 256)
            self.proj_fm(wb, wv, 2, lambda jj, th, bk: self.cp(self.ev(), QT[:, jj, th * 512:(th + 1) * 512], bk[:, :], [bk], [QT]))
            wb, wv = self.win_block(l, 3456, 256)

            def evk(jj, th, bk):
                sl = slice(th * 512, (th + 1) * 512)
                self.cp(self.ev(), KTf[:, sl], bk[:, :], [bk], [KTf])
                for gg in range(2):
                    for hp in range(2):
                        self.cp(self.ev(), KT[hp * 64:hp * 64 + 64, gg, sl], KTf[gg * 64:gg * 64 + 64, sl], [KTf], [KT])
            self.proj_fm(wb, wv, 1, evk)
            fw.dma("sp", self.swak[l], KTf[:, :], reads=[KTf], is_output=True)

            def evv(ti, bk):
                self.cp(self.ev(), Vf[:, ti, :], bk[:, 0:128], [bk], [Vf])
                self.cp("pool", Vt[:, ti, :, 0:64], Vf[:, ti, :].rearrange("p (h d) -> p h d", h=2), [Vf], [Vt])
            self.proj_tm(wb, wv[:, :, 128:256], 128, evv)
            fw.dma("sp", self.swav[l].rearrange("n p f -> p n f"), Vf[:, :, :], reads=[Vf], is_output=True)
            self.ctx_attn(QT, KT, lambda h: h // 2, Vt, lambda h: h // 2, 6, PTp, RDp, esink=esink)
        else:
            self.swa_latent(QT, KT, esink)

    def lerp(self, d, out_ap, src_buf, src_ap, mu_ap, DIFF, out_buf):
        ns = len(self.seqs)
        L = T // ns
        s3 = src_ap.rearrange("p (s t) -> p s t", s=ns)
        d3 = DIFF[:, :].rearrange("p (s t) -> p s t", s=ns)
        e = self.el()
        if d == 0:
            self.tt(e, d3[:, :, 1:L], s3[:, :, 0:L - 1], s3[:, :, 1:L], ALU.subtract, [src_buf], [DIFF])
            self.ts(e, d3[:, :, 0:1], s3[:, :, 0:1], -1.0, None, ALU.mult, None, [src_buf], [DIFF])
        else:
            self.tt(e, d3[:, :, 0:L - 1], s3[:, :, 1:L], s3[:, :, 0:L - 1], ALU.subtract, [src_buf], [DIFF])
            self.ts(e, d3[:, :, L - 1:L], s3[:, :, L - 1:L], -1.0, None, ALU.mult, None, [src_buf], [DIFF])
        self.stt(e, out_ap, DIFF[:, :], mu_ap, src_ap, ALU.mult, ALU.add, [DIFF, src_buf, self.cols], [out_buf])

    def tile_order(self, d):
        out = []
        for si, (t0, n) in enumerate(self.seqs):
            tl = list(range(t0, t0 + n))
            if d == 1:
                tl = tl[::-1]
            for i, ti in enumerate(tl):
                out.append((si, ti, i == n - 1))
        return out

    def scan_tile(self, d, gs_ap, src_ap, reads, wbuf, C=128):
        rs = self.cst("reset", 0, C)
        if d == 0:
            self.fw.op("dve", lambda e: e.tensor_tensor_scan(gs_ap, rs, src_ap, 0.0, ALU.mult, ALU.add), reads + [self.cstb], [wbuf])
        else:
            self.fw.op("dve", lambda e: e.tensor_tensor_scan(gs_ap[:, ::-1], rs, src_ap[:, ::-1], 0.0, ALU.mult, ALU.add), reads + [self.cstb], [wbuf])

    def run_gens(self, gens, width=None):
        pend = list(gens)
        width = width or len(pend)
        active = []
        while pend or active:
            while pend and len(active) < width:
                active.append(pend.pop(0))
            nxt = []
            for g in active:
                try:
                    next(g)
                    nxt.append(g)
                except StopIteration:
                    pass
            active = nxt

    def lin_head(self, d, j, hp, t0, W, dplr, banks, C=128):
        fw = self.fw
        bi = [0]

        def nb():
            b = banks[bi[0] % len(banks)]
            bi[0] += 1
            return b
        P = slice(hp * 64, hp * 64 + 64)
        h = 2 * j + hp
        RK, KB, TOK, YACC = W["RK"], W["KB"], W["TOK"], W["YACC"]
        hst = W["hst"]
        HST = W["HST"]
        rt = RK[P, j, 1, 0:C]
        kb = KB[P, j, 0:C]
        Vh = TOK[0:C, 0, h * 64:(h + 1) * 64]
        Kbh = TOK[0:C, 1, h * 64:(h + 1) * 64]
        G4 = W["G4p"].next()
        g4c = lambda a, b: self.cst("g4_%d" % d, a, b)
        if dplr:
            PB = W["PB"]
            kt = RK[P, j, 0, :]
            rk2 = RK[P, j, :, :]
            pb = PB[P, j, :]
            Pbh = TOK[:, 2, h * 64:(h + 1) * 64]
            bk = nb()
            self.mm(bk[:, 0:256], pb, rk2, True, True, [PB, RK], [bk])
            self.mm(bk[:, 256:512], kb, rk2, True, True, [KB, RK], [bk])
            bk2 = nb()
            self.mm(bk2[:, 0:128], kt, pb, True, True, [RK, PB], [bk2])
            yield
            self.tt("dve", G4[:, :], bk[:, :], g4c(0, 512), ALU.mult, [bk, self.cstb], [G4])
            Bm = W["Bp"].next()
            self.tt("dve", Bm[:, :], bk2[:, 0:128], self.cst("mb_%d" % d), ALU.mult, [bk2, self.cstb], [Bm])
            CN = W["CNp"].next()
            self.tt("pool", CN[:, 128:256], G4[:, 0:128], self.cst("ident"), ALU.add, [G4, self.cstb], [CN])
            yield
            bk = nb()
            self.mm(bk[:, 0:128], Bm[:, :], G4[:, 0:128], True, True, [Bm, G4], [bk])
            yield
            self.cp("act", CN[:, 0:128], bk[:, 0:128], [bk], [CN])
            Cprev_buf, Cprev = G4, G4[:, 0:128]
            Nfin = None
            for lev in range(1, 7):
                bk = nb()
                self.mm(bk[:, 0:128], Cprev, Bm[:, :], True, True, [Cprev_buf, Bm], [bk])
                yield
                Bn = W["Bp"].next()
                self.cp("act", Bn[:, :], bk[:, 0:128], [bk], [Bn])
                Bm = Bn
                yield
                bk = nb()
                if lev < 6:
                    self.mm(bk[:, 0:256], Bm[:, :], CN[:, 0:256], True, True, [Bm, CN], [bk])
                    yield
                    CN2 = W["CNp"].next()
                    self.cp("act", CN2[:, 0:128], bk[:, 0:128], [bk], [CN2])
                    self.tt("dve", CN2[:, 128:256], bk[:, 128:256], CN[:, 128:256], ALU.add, [bk, CN], [CN2])
                    Cprev_buf, Cprev = CN, CN[:, 0:128]
                    CN = CN2
                    yield
                else:
                    self.mm(bk[:, 0:128], Bm[:, :], CN[:, 128:256], True, True, [Bm, CN], [bk])
                    yield
                    Nfin = W["Bp"].next()
                    self.tt("dve", Nfin[:, :], bk[:, 0:128], CN[:, 128:256], ALU.add, [bk, CN], [Nfin])
                    yield
            mqk = G4[:, 384:512]
        else:
            bk = nb()
            self.mm(bk[0:C, 0:C], kb, rt, True, True, [KB, RK], [bk])
            yield
            self.tt("dve", G4[0:C, 0:C], bk[0:C, 0:C], self.cst("g4_%d" % d, 384, 384 + C, rows=C), ALU.mult, [bk, self.cstb], [G4])
            mqk = G4[0:C, 0:C]
        Hp = W["Hpp"].next()
        Hb = W["Hbp"].next()
        self.ts("dve", Hp[P, :], hst, W["expc"][P, j:j + 1], None, ALU.mult, None, [HST, W["expcb"]], [Hp])
        yield
        self.cp("act", Hb[P, :], Hp[P, :], [Hp], [Hb])
        yield
        if dplr:
            bx = nb()
            self.mm(bx[:, 0:64], G4[:, 256:384], Vh, True, False, [G4, TOK], [bx])
            self.mm(bx[:, 0:64], kt, Hb[P, :], False, True, [RK, Hb], [bx])
            yield
            Xs = W["Xp"].next()
            self.cp("act", Xs[:, :], bx[:, 0:64], [bx], [Xs])
            yield
            bu = nb()
            self.mm(bu[:, 0:64], Nfin[:, :], Xs[:, :], True, True, [Nfin, Xs], [bu])
            yield
            Un = W["Xp"].next()
            self.ts("dve", Un[:, :], bu[:, 0:64], -1.0, None, ALU.mult, None, [bu], [Un])
            yield
        by = nb()
        self.mm(by[P, 0:C], Hb[P, :], rt, True, False, [Hb, RK], [by])
        self.mm(by[P, 0:C], Vh, mqk, False, not dplr, [TOK, G4], [by])
        if dplr:
            self.mm(by[P, 0:C], Un[:, :], G4[:, 128:256], False, True, [Un, G4], [by])
        self.mm(by[P, 128:192], Kbh, Vh, True, not dplr, [TOK], [by])
        if dplr:
            self.mm(by[P, 128:192], Pbh, Un[:, :], False, True, [TOK, Un], [by])
        yield
        ydst = YACC[P, j, t0:t0 + C]
        self.tt("dve", ydst, by[P, 0:C], ydst, ALU.add, [by, YACC], [YACC])
        self.tt("dve", Hp[P, :], by[P, 128:192], Hp[P, :], ALU.add, [by, Hp], [Hp])
        yield
        self.ts("dve", hst, Hp[P, :], W["eplast"][P, j:j + 1], None, ALU.mult, None, [Hp, W["eplastb"]], [HST])

    def lin_common_bufs(self, ar, dplr, njobs):
        Ws = []
        for k in range(njobs):
            W = {}
            if dplr:
                W["G4p"] = ar.pool("G4_%d" % k, 1, [128, 512], BF16)
                W["Bp"] = ar.pool("Bm_%d" % k, 3, [128, 128], BF16)
                W["CNp"] = ar.pool("CN_%d" % k, 3, [128, 256], BF16)
                W["Xp"] = ar.pool("Xs_%d" % k, 2, [128, 64], BF16)
            else:
                W["G4p"] = ar.pool("G4_%d" % k, 1, [128, 32], BF16)
            W["Hpp"] = ar.pool("Hp_%d" % k, 1, [128, 64], F32)
            W["Hbp"] = ar.pool("Hb_%d" % k, 1, [128, 64], BF16)
            Ws.append(W)
        return Ws

    def lin_state_io(self, HST, src_dram, dst_dram, load):
        l = self.l
        if load:
            self.fw.dma("sp", HST[:, :, 0, :, :], src_dram[l].rearrange("d p j v -> p d j v"), writes=[HST])
        else:
            self.fw.dma("sp", dst_dram[l].rearrange("d s p j v -> p d s j v"), HST[:, :, :, :, :], reads=[HST], is_output=True)

    def mixer_A(self):
        fw, l, G = self.fw, self.l, self.G
        ar = self.arena
        ar.reset()
        ns = len(self.seqs)
        RAW = ar.get("RAW", [128, 6, T], BF16)
        WA = ar.get("WA", [128, 2, T], F32)
        SG = ar.get("SG", [128, T], BF16)
        YACC = ar.get("YACC", [128, 2, T], F32)
        BACC = ar.get("BACC", [128, 2, T], BF16)
        HST = ar.get("HST", [128, 2, ns, 2, 64], F32)
        L2 = ar.get("L2", [128, 2, 256], F32)
        G2 = ar.get("G2", [128, 256], BF16)
        LR = ar.get("LR", [128, 6, T], BF16)
        DIFF = ar.get("DIFF", [128, T], F32)
        cc = ar.get("cc", [128, 8], F32)
        W4 = self.lin_common_bufs(ar, True, 4)
        fw.op("pool", lambda e: e.memset(YACC[:, :, :], 0.0), [], [YACC])
        wk = lambda nm, shape, dt=F32: ar.pool(nm, 2, shape, dt)
        THp, SIGp, Ap = wk("th", [128, 128]), wk("sig", [128, 2, 128]), wk("a", [128, 2, 128])
        T1p, T2p, T3p = wk("t1", [128, 128]), wk("t2", [128, 128]), ar.pool("t3", 1, [128, 128], F32)
        wk1 = lambda nm, shape, dt=F32: ar.pool(nm, 1, shape, dt)
        KAPp, KMp, PPp = wk1("kap", [128, 2, 128]), wk1("km", [128, 2, 128]), wk1("pp", [128, 2, 128])
        SQp = wk("sqb", [128, 128], BF16)
        GSp, EPp, EMp, EPPp = wk1("gs", [128, 2, 128]), wk1("ep", [128, 2, 128]), wk1("em", [128, 2, 128]), wk("epp", [128, 2, 128])
        EXCp = wk("exc", [128, 2])
        RKp, KBp, PBp = wk("RK", [128, 2, 2, 128], BF16), wk("KB", [128, 2, 128], BF16), wk("PB", [128, 2, 128], BF16)
        TOKp = wk("TOK", [128, 3, 256], BF16)

        fw.dma("sp", L2[:, :, :], self.lora2[l], writes=[L2])
        g2st = self.WST.next()
        fw.dma("sp", g2st[:, 0:256], self.rw_g2[l], writes=[g2st])
        self.cp("pool", G2[:, :], g2st[:, 0:256], [g2st], [G2])
        self.ts("dve", cc[:, 0:2], self.col(l, "ka", 0, 2), -1.0, 1.0, ALU.mult, ALU.add, [self.cols], [cc])
        self.ts("dve", cc[:, 2:4], self.col(l, "lnx_w", 0, 2), 8.0, None, ALU.mult, None, [self.cols], [cc])
        if G == "P":
            fw.op("pool", lambda e: e.memset(HST[:, :, :, :, :], 0.0), [], [HST])
        else:
            self.lin_state_io(HST, self.srw, None, True)

        def evA(c0):
            def f(jj, th, bk):
                ci = c0 + jj
                sl = slice(th * 512, (th + 1) * 512)
                if ci < 6:
                    self.cp(self.ev(), RAW[:, ci, sl], bk[:, :], [bk], [RAW])
                elif ci == 6:
                    self.act(SG[:, sl], bk[:, :], AF.Sigmoid, [bk], [SG])
                else:
                    self.cp(self.ev(), WA[:, ci - 7, sl], bk[:, :], [bk], [WA])
            return f
        for b0 in range(0, 9, 2):
            nch = min(2, 9 - b0)
            wb, wv = self.win_block(l, b0 * 128, nch * 128)
            self.proj_fm(wb, wv, nch, evA(b0))

        for d in range(2):
            for a in range(3):
                for jj in range(2):
                    ci = a * 2 + jj
                    self.lerp(d, LR[:, ci, :], RAW, RAW[:, ci, :], self.col(l, "mu_rkv%d" % d, ci, 1), DIFF, LR)
            self.lerp(d, WA[:, d, :], WA, WA[:, d, :], self.col(l, "mu_lora%d" % d, 0, 1), DIFF, WA)
            LW = WA
            for (si, ti, last) in self.tile_order(d):
                t0 = ti * 128
                ts_ = slice(t0, t0 + 128)
                th = THp.next()
                self.act(th[0:64, :], WA[0:64, d, ts_], AF.Tanh, [WA], [th])
                sig, av = SIGp.next(), Ap.next()
                for jj in range(2):
                    bk = fw.bank()
                    self.mm(bk[:, 0:128], L2[0:64, d, jj * 128:(jj + 1) * 128], th[0:64, :], True, True, [L2, th], [bk])
                    self.act(sig[:, jj, :], bk[:, 0:128], AF.Sigmoid, [bk, self.cols], [sig], bias=self.col(l, "w0_%d" % d, jj, 1))
                    bk = fw.bank()
                    self.mm(bk[:, 0:128], L2[64:128, d, jj * 128:(jj + 1) * 128], WA[64:128, d, ts_], True, True, [L2, WA], [bk])
                    self.act(av[:, jj, :], bk[:, 0:128], AF.Sigmoid, [bk, self.cols], [av], bias=self.col(l, "a0_%d" % d, jj, 1))
                kap, km, pp = KAPp.next(), KMp.next(), PPp.next()
                gs, ep, em, epp, exc = GSp.next(), EPp.next(), EMp.next(), EPPp.next(), EXCp.next()
                RK, KB, PB, TOK = RKp.next(), KBp.next(), PBp.next(), TOKp.next()
                mid = 63 if d == 0 else 64
                lastc = 127 if d == 0 else 0
                for jj in range(2):
                    rl, kl, vl = LR[:, jj, ts_], LR[:, 2 + jj, ts_], LR[:, 4 + jj, ts_]
                    e = self.el()
                    t1, t2 = T1p.next(), T2p.next()
                    self.ts(e, t1[:, :], kl, self.col(l, "kk", jj, 1), None, ALU.mult, None, [LR, self.cols], [t1])
                    sqb = SQp.next()
                    self.tt(e, sqb[:, :], t1[:, :], t1[:, :], ALU.mult, [t1], [sqb])
                    bk = fw.bank()
                    self.mm(bk[:, 0:128], self.blk_bf[:, :], sqb[:, :], True, True, [self.blk_bf, sqb], [bk])
                    self.rsqrt(t2[:, :], bk[:, 0:128], 1e-24, [bk], t2)
                    self.tt(e, kap[:, jj, :], t1[:, :], t2[:, :], ALU.mult, [t1, t2], [kap])
                    t3 = T3p.next()
                    self.ts(e, t3[:, :], av[:, jj, :], self.col(l, "ka", jj, 1), cc[:, jj:jj + 1], ALU.mult, ALU.add, [av, self.cols, cc], [t3])
                    self.tt(e, km[:, jj, :], kl, t3[:, :], ALU.mult, [LR, t3], [km])
                    self.tt(e, pp[:, jj, :], kap[:, jj, :], av[:, jj, :], ALU.mult, [kap, av], [pp])
                    t1b = T1p.next()
                    self.stt(e, t1b[:, :], rl, self.col(l, "rk", jj, 1), km[:, jj, :], ALU.mult, ALU.mult, [LR, self.cols, km], [t1b])
                    sqb2 = SQp.next()
                    self.cp(e, sqb2[:, :], t1b[:, :], [t1b], [sqb2])
                    bk = fw.bank()
                    self.mm(bk[:, 0:128], self.blk_bf[:, :], sqb2[:, :], True, True, [self.blk_bf, sqb2], [bk])
                    if d == 0:
                        self.tt("dve", BACC[:, jj, ts_], bk[:, 0:128], vl, ALU.mult, [bk, LR], [BACC])
                    else:
                        t2b = T2p.next()
                        self.tt("dve", t2b[:, :], bk[:, 0:128], vl, ALU.mult, [bk, LR], [t2b])
                        self.tt(e, BACC[:, jj, ts_], BACC[:, jj, ts_], t2b[:, :], ALU.add, [t2b, BACC], [BACC])
                    self.scan_tile(d, gs[:, jj, :], sig[:, jj, :], [sig], gs)
                self.cp("dve", exc[:, 0:2], gs[:, :, mid], [gs], [exc])
                gc = EPPp.next()
                for jj in range(2):
                    self.ts("dve", gc[:, jj, :], gs[:, jj, :], exc[:, jj:jj + 1], None, ALU.subtract, None, [gs, exc], [gc])
                self.act(ep[:, :, :], gc[:, :, :], AF.Exp, [gc], [ep], scale=-DECAY)
                self.act(em[:, :, :], gc[:, :, :], AF.Exp, [gc], [em], scale=DECAY)
                self.tt(self.el(), gs[:, :, :], gc[:, :, :], sig[:, :, :], ALU.subtract, [gc, sig], [gs])
                self.act(epp[:, :, :], gs[:, :, :], AF.Exp, [gs], [epp], scale=-DECAY)
                self.act(exc[:, 0:2], exc[:, 0:2], AF.Exp, [exc], [exc], scale=-DECAY)
                for jj in range(2):
                    e = self.el()
                    self.tt(e, RK[:, jj, 0, :], kap[:, jj, :], epp[:, jj, :], ALU.mult, [kap, epp], [RK])
                    self.tt(e, RK[:, jj, 1, :], LR[:, jj, ts_], ep[:, jj, :], ALU.mult, [LR, ep], [RK])
                    self.tt(e, KB[:, jj, :], km[:, jj, :], em[:, jj, :], ALU.mult, [km, em], [KB])
                    self.tt(e, PB[:, jj, :], pp[:, jj, :], em[:, jj, :], ALU.mult, [pp, em], [PB])
                bt = fw.bank()
                btv = bt.t.bitcast(BF16)
                for jj in range(2):
                    self.tp(btv[:, (0 * 2 + jj) * 128:(0 * 2 + jj + 1) * 128], LR[:, 4 + jj, ts_], self.ident_bf[:, :], [LR, self.ident_bf], [bt])
                    self.tp(btv[:, (1 * 2 + jj) * 128:(1 * 2 + jj + 1) * 128], KB[:, jj, :], self.ident_bf[:, :], [KB, self.ident_bf], [bt])
                    self.tp(btv[:, (2 * 2 + jj) * 128:(2 * 2 + jj + 1) * 128], PB[:, jj, :], self.ident_bf[:, :], [PB, self.ident_bf], [bt])
                self.cp("act", TOK[:, :, :], btv[:, 0:768].rearrange("p (a b) -> p a b", a=3), [bt], [TOK])
                shared = dict(RK=RK, KB=KB, PB=PB, TOK=TOK, expc=exc, expcb=exc, eplast=ep[:, :, lastc], eplastb=ep, YACC=YACC, HST=HST)
                gens = []
                for k, (jj, hp) in enumerate(((0, 0), (0, 1), (1, 0), (1, 1))):
                    Wk = dict(W4[k])
                    Wk.update(shared)
                    Wk["hst"] = HST[hp * 64:hp * 64 + 64, d, si, jj, :]
                    gens.append(self.lin_head(d, jj, hp, t0, Wk, True, [fw.banks[2 * k], fw.banks[2 * k + 1]]))
                self.run_gens(gens)
        if G == "P":
            self.lin_state_io(HST, None, self.rwst, False)
        self.dump("yacc_%s%d" % (G, l), YACC, YACC[:, :, :], [128, 2, T])
        self.dump("bacc_%s%d" % (G, l), BACC, BACC[:, :, :], [128, 2, T])
        blk = self.cst("blk")
        for jj in range(2):
            for th in range(2):
                sl = slice(th * 512, (th + 1) * 512)
                bk = fw.bank()
                self.mm(bk[:, :], blk, YACC[:, jj, sl], True, True, [self.cstb, YACC], [bk])
                yc = DIFF
                self.stt("dve", yc[:, sl], bk[:, :], -1.0 / 64, YACC[:, jj, sl], ALU.mult, ALU.add, [bk, YACC], [yc])
                self.tt("pool", WA[:, 0, sl], yc[:, sl], yc[:, sl], ALU.mult, [yc], [WA])
                bk2 = fw.bank()
                self.mm(bk2[:, :], blk, WA[:, 0, sl], True, True, [self.cstb, WA], [bk2])
                self.rsqrt(WA[:, 0, sl], bk2[:, :], 64 * LN_EPS, [bk2], WA)
                self.tt("pool", yc[:, sl], yc[:, sl], WA[:, 0, sl], ALU.mult, [yc, WA], [yc])
                self.ts("dve", yc[:, sl], yc[:, sl], cc[:, 2 + jj:3 + jj], self.col(l, "lnx_b", jj, 1), ALU.mult, ALU.add, [yc, cc, self.cols], [yc])
                self.tt("pool", yc[:, sl], yc[:, sl], BACC[:, jj, sl], ALU.add, [yc, BACC], [yc])
                bk3 = fw.bank()
                self.mm(bk3[:, :], G2[:, jj * 128:(jj + 1) * 128], SG[:, sl], True, True, [G2, SG], [bk3])
                self.tt("dve", self.MIX[jj][:, sl], yc[:, sl], bk3[:, :], ALU.mult, [yc, bk3], [self.MIX[jj]])

    def mixer_C(self):
        fw, l, G = self.fw, self.l, self.G
        ar = self.arena
        ar.reset()
        ns = len(self.seqs)
        Q = ar.get("Q", [128, 2, T], BF16)
        V = ar.get("V", [128, 2, T], BF16)
        GATE = ar.get("GATE", [128, 2, T], BF16)
        FR = ar.get("FR", [128, 2, 2, T], F32)
        YACC = ar.get("YACC", [128, 2, T], F32)
        HST = ar.get("HST", [128, 2, ns, 2, 64], F32)
        LFd = [ar.get("LF%d" % d, [128, 2, T], F32) for d in range(2)]
        KKd = [ar.get("KK%d" % d, [128, 2, T], BF16) for d in range(2)]
        LF = LFd[0]
        W8 = self.lin_common_bufs(ar, False, 8)
        fw.op("pool", lambda e: e.memset(YACC[:, :, :], 0.0), [], [YACC])
        hn8 = ar.get("hn8", [128, 2], F32)
        self.ts("dve", hn8[:, :], self.col(l, "hg_norm", 0, 2), 8.0, None, ALU.mult, None, [self.cols], [hn8])
        C = 32
        wk = lambda nm, shape, dt=F32: [ar.pool("%s_%d" % (nm, d), 2, shape, dt) for d in range(2)]
        GSp, GCp, EPp, EMp, EXCp = wk("gs", [128, 2, C]), wk("gc", [128, 2, C]), wk("ep", [128, 2, C]), wk("em", [128, 2, C]), wk("exc", [128, 2])
        RKp, KBp, TOKp = wk("RK", [128, 2, 2, C], BF16), wk("KB", [128, 2, C], BF16), wk("TOK", [128, 2, 256], BF16)
        if G == "P":
            fw.op("pool", lambda e: e.memset(HST[:, :, :, :, :], 0.0), [], [HST])
        else:
            self.lin_state_io(HST, self.shg, None, True)

        def evC(c0):
            def f(jj, th, bk):
                ci = c0 + jj
                sl = slice(th * 512, (th + 1) * 512)
                if ci < 2:
                    self.act(Q[:, ci, sl], bk[:, :], AF.Silu, [bk], [Q])
                elif ci < 4:
                    self.cp(self.ev(), V[:, ci - 2, sl], bk[:, :], [bk], [V])
                elif ci < 6:
                    self.act(GATE[:, ci - 4, sl], bk[:, :], AF.Sigmoid, [bk], [GATE])
                else:
                    self.cp(self.ev(), FR[:, (ci - 6) // 2, (ci - 6) % 2, sl], bk[:, :], [bk], [FR])
            return f
        for b0 in range(0, 10, 2):
            wb, wv = self.win_block(l, 1920 + b0 * 128, 256)
            self.proj_fm(wb, wv, 2, evC(b0))

        for d in range(2):
            for jj in range(2):
                self.act(LFd[d][:, jj, :], FR[:, d, jj, :], AF.Sigmoid, [FR], [LFd[d]])
                self.ts("dve", LFd[d][:, jj, :], LFd[d][:, jj, :], self.oml[:, d, l, jj:jj + 1], self.lb[:, d, l, jj:jj + 1], ALU.mult, ALU.add, [LFd[d], self.oml, self.lb], [LFd[d]])
                self.ts("pool", KKd[d][:, jj, :], LFd[d][:, jj, :], -1.0, 1.0, ALU.mult, ALU.add, [LFd[d]], [KKd[d]])
        for d in range(2):
            for jj in range(2):
                self.act(LFd[d][:, jj, :], LFd[d][:, jj, :], AF.Ln, [LFd[d]], [LFd[d]])
        orders = []
        for d in range(2):
            o = []
            for (si, ti, last) in self.tile_order(d):
                for sub in (range(128 // C) if d == 0 else range(128 // C - 1, -1, -1)):
                    o.append((si, ti * 128 + sub * C))
            orders.append(o)
        import os
        fw.barrier()
        hbanks = [Buf("hbank%d" % k, fw.banks[k // 2].t[:, (k % 2) * 256:(k % 2 + 1) * 256]) for k in range(8)]
        nsteps = len(orders[0])
        work = {}

        def prep(i):
            par = i % 2
            wl = []
            for d in range(2):
                si, t0 = orders[d][i]
                ts_ = slice(t0, t0 + C)
                gs, gc, ep, em, exc = GSp[d].bufs[par], GCp[d].bufs[par], EPp[d].bufs[par], EMp[d].bufs[par], EXCp[d].bufs[par]
                RK, KB, TOK = RKp[d].bufs[par], KBp[d].bufs[par], TOKp[d].bufs[par]
                mid = C // 2 - 1 if d == 0 else C // 2
                lastc = C - 1 if d == 0 else 0
                for jj in range(2):
                    self.scan_tile(d, gs[:, jj, 0:C], LFd[d][:, jj, ts_], [LFd[d]], gs, C)
                yield
                self.cp("dve", exc[:, 0:2], gs[:, :, mid], [gs], [exc])
                yield
                for jj in range(2):
                    self.ts("dve", gc[:, jj, 0:C], gs[:, jj, 0:C], exc[:, jj:jj + 1], None, ALU.subtract, None, [gs, exc], [gc])
                yield
                self.act(ep[:, :, 0:C], gc[:, :, 0:C], AF.Exp, [gc], [ep])
                self.act(em[:, :, 0:C], gc[:, :, 0:C], AF.Exp, [gc], [em], scale=-1.0)
                self.act(exc[:, 0:2], exc[:, 0:2], AF.Exp, [exc], [exc])
                yield
                for jj in range(2):
                    e = self.el()
                    self.tt(e, RK[:, jj, 1, 0:C], Q[:, jj, ts_], ep[:, jj, 0:C], ALU.mult, [Q, ep], [RK])
                    self.tt(e, KB[:, jj, 0:C], KKd[d][:, jj, ts_], em[:, jj, 0:C], ALU.mult, [KKd[d], em], [KB])
                yield
                wl.append((d, si, t0, ts_, dict(RK=RK, KB=KB, TOK=TOK, expc=exc, expcb=exc, eplast=ep[:, :, lastc], eplastb=ep, YACC=YACC, HST=HST)))
            work[i] = wl

        def prepB(i):
            for (d, si, t0, ts_, sh) in work[i]:
                KB, TOK = sh["KB"], sh["TOK"]
                bt = fw.bank()
                btv = bt.t.bitcast(BF16)
                for jj in range(2):
                    self.tp(btv[0:C, jj * 128:(jj + 1) * 128], V[:, jj, ts_], self.ident_bf[:, :], [V, self.ident_bf], [bt])
                    self.tp(btv[0:C, (2 + jj) * 128:(3 + jj) * 128], KB[:, jj, 0:C], self.ident_bf[:, :], [KB, self.ident_bf], [bt])
                self.cp("act", TOK[0:C, :, :], btv[0:C, 0:512].rearrange("p (a b) -> p a b", a=2), [bt], [TOK])

        def heads(i):
            gens = []
            for (d, si, t0, ts_, shared) in work[i]:
                for k, (jj, hp) in enumerate(((0, 0), (0, 1), (1, 0), (1, 1))):
                    Wk = dict(W8[d * 4 + k])
                    Wk.update(shared)
                    Wk["hst"] = HST[hp * 64:hp * 64 + 64, d, si, jj, :]
                    gens.append(self.lin_head(d, jj, hp, t0, Wk, False, [fw.banks[d * 4 + k]], C))
            return gens

        import os
        self.run_gens([prep(0)])
        prepB(0)
        for i in range(nsteps):
            gens = heads(i)
            if i + 1 < nsteps:
                gens.append(prep(i + 1))
            self.run_gens(gens)
            if i + 1 < nsteps:
                prepB(i + 1)
        fw.barrier()
        if G == "P":
            self.lin_state_io(HST, None, self.hgst, False)
        self.dump("yaccC_%s%d" % (G, l), YACC, YACC[:, :, :], [128, 2, T])
        self.dump("lfC_%s%d" % (G, l), LF, LF[:, :, :], [128, 2, T])
        blk = self.cst("blk")
        for jj in range(2):
            for th in range(2):
                sl = slice(th * 512, (th + 1) * 512)
                self.tt("pool", LF[:, 0, sl], YACC[:, jj, sl], YACC[:, jj, sl], ALU.mult, [YACC], [LF])
                bk = fw.bank()
                self.mm(bk[:, :], blk, LF[:, 0, sl], True, True, [self.cstb, LF], [bk])
                self.rsqrt(LF[:, 0, sl], bk[:, :], 64 * EPS, [bk], LF)
                self.tt("pool", LF[:, 0, sl], LF[:, 0, sl], YACC[:, jj, sl], ALU.mult, [LF, YACC], [LF])
                self.stt("dve", self.MIX[4 + jj][:, sl], LF[:, 0, sl], hn8[:, jj:jj + 1], GATE[:, jj, sl], ALU.mult, ALU.mult, [LF, hn8, GATE], [self.MIX[4 + jj]])

    def out_and_ffn(self):
        fw, l, G = self.fw, self.l, self.G
        ar = self.arena
        ar.reset()
        mc = self.mc
        X, H, MIX = self.X, self.H, self.MIX
        O = [ar.get("O%d" % c, [128, T], F32) for c in range(8)]
        WBIG = ar.get("WBIG", [128, 8, 1024], BF16)
        HID = [ar.get("HID%d" % c, [128, T], BF16) for c in range(8)]
        RL = ar.get("RL", [128, T], BF16)
        WST0, WBF0 = self.WST, self.WBF
        self.WST = Rot(WST0.bufs + [ar.get("wstx%d" % i, [128, 2048], F32) for i in range(2)])
        self.WBF = Rot(WBF0.bufs + [ar.get("wbfx%d" % i, [128, 2048], BF16) for i in range(2)])
        self.cast_engs = ("pool", "dve", "act")

        def add_residual(gco):
            rstd = self.rstd_of(O, lambda c: O[c][:, :], self.sq, self.rstd)
            for c in range(8):
                e = self.el()
                self.stt(e, O[c][:, :], O[c][:, :], gco[:, c:c + 1], rstd[:, :], ALU.mult, ALU.mult, [O[c], rstd, self.modc], [O[c]])
                self.tt(e, X[c][:, :], X[c][:, :], O[c][:, :], ALU.add, [O[c], X[c]], [X[c]])

        for q in range(4):
            st_ap = self.w_out[l, q * 256:(q + 1) * 256, :].rearrange("(k p) n -> p k n", p=128)
            stb, sv = self.load_block(st_ap, 2, 1024, cast=False)
            self.cp(self.fw.pick("cast", self.cast_engs), WBIG[:, 2 * q:2 * q + 2, :], sv, [stb], [WBIG])
        for oc in range(8):
            for th in range(2):
                bk = fw.bank()
                for kc in range(8):
                    self.mm(bk[:, :], WBIG[:, kc, oc * 128:(oc + 1) * 128], MIX[kc][:, th * 512:(th + 1) * 512], kc == 0, kc == 7, [WBIG, MIX[kc]], [bk])
                self.cp(self.ev(), O[oc][:, th * 512:(th + 1) * 512], bk[:, :], [bk], [O[oc]])
        add_residual(mc["gco1"])
        rstd = self.rstd_of(X, lambda c: X[c][:, :], self.sq, self.rstd)
        for c in range(8):
            tmp = self.sq.next()
            self.stt(self.el(), tmp[:, :], X[c][:, :], mc["coefA2"][:, c:c + 1], rstd[:, :], ALU.mult, ALU.mult, [X[c], rstd, self.modc], [tmp])
            self.act(H[c][:, :], tmp[:, :], AF.Identity, [tmp, self.modc], [H[c]], bias=mc["shift2"][:, c:c + 1])
        for q in range(4):
            for fb in range(4):
                wb, wv = self.load_block(self.ffn_w1[l, :, (q * 8 + fb * 2) * 128:(q * 8 + fb * 2 + 2) * 128].rearrange("(k p) n -> p k n", p=128), 8, 256)

                def evh(jj, th, bk, fb=fb):
                    sl = slice(th * 512, (th + 1) * 512)
                    self.act(RL[:, sl], bk[:, :], AF.Relu, [bk], [RL])
                    self.tt(self.el(), HID[fb * 2 + jj][:, sl], RL[:, sl], RL[:, sl], ALU.mult, [RL], [HID[fb * 2 + jj]])
                self.proj_fm(wb, wv, 2, evh)
                hb = fb
                st_ap = self.ffn_w2[l, (q * 8 + hb * 2) * 128:(q * 8 + hb * 2 + 2) * 128, :].rearrange("(k p) n -> p k n", p=128)
                stb, sv = self.load_block(st_ap, 2, 1024, cast=False)
                self.cp(self.fw.pick("cast", self.cast_engs), WBIG[:, 2 * hb:2 * hb + 2, :], sv, [stb], [WBIG])
            for oc in range(8):
                for th in range(2):
                    sl = slice(th * 512, (th + 1) * 512)
                    bk = fw.bank()
                    for kc in range(8):
                        self.mm(bk[:, :], WBIG[:, kc, oc * 128:(oc + 1) * 128], HID[kc][:, sl], kc == 0, kc == 7, [WBIG, HID[kc]], [bk])
                    if q == 0:
                        self.cp(self.ev(), O[oc][:, sl], bk[:, :], [bk], [O[oc]])
                    else:
                        self.tt("dve", O[oc][:, sl], bk[:, :], O[oc][:, sl], ALU.add, [bk, O[oc]], [O[oc]])
        add_residual(mc["gco2"])
        self.WST, self.WBF = WST0, WBF0
        self.cast_engs = ("pool", "dve", "pool")

    def nat_latent(self, QT, KT):
        fw, l = self.fw, self.l
        ar = self.arena
        Vr = ar.get("Vr", [128, 16, 4, 128], BF16)
        CKf = ar.get("CKf", [128, 2, 256], F32)
        CK = ar.get("CK", [128, 2, 256], BF16)
        CVf = ar.get("CVf", [128, 2, 256], F32)
        CV = ar.get("CV", [128, 2, 4, 128], BF16)
        ET = ar.get("ET", [128, 4, 15, 64], F32)
        PTp = ar.pool("PT", 2, [128, 8, 64], BF16)
        PCp = ar.pool("PC", 2, [128, 2, 64], BF16)
        RDp = ar.pool("RD", 2, [128, 64], F32)
        fw.op("pool", lambda e: e.memset(Vr[0:64, :, :, 64:128], 1.0), [], [Vr])
        fw.op("pool", lambda e: e.memset(CV[:, :, :, 64:128], 1.0), [], [CV])
        fw.dma("sp", CKf[:, :, :], self.cnk[l].rearrange("j p t -> p j t"), writes=[CKf])
        self.cp("pool", CK[:, :, :], CKf[:, :, :], [CKf], [CK])
        fw.dma("sp", CVf[:, :, :], self.cnv[l].rearrange("(c p) f -> p c f", p=128), writes=[CVf])
        self.cp("pool", CV[:, :, :, 0:64], CVf[:, :, :].rearrange("p c (h d) -> p c h d", h=4), [CVf], [CV])
        ETr = ar.get("ETr", [128, 4, 15, 64], F32)
        src = bass.AP(tensor=self.rpbpad.t, offset=l * 4 * 15 * 127, ap=[[1, 64], [15 * 127, 4], [127, 15], [1, 64]])
        fw.dma("sp", ETr[0:64, :, :, :], src, writes=[ETr])
        flip = self.cst("flip", rows=64)
        etr2 = ETr[0:64, :, :, :].rearrange("p h r q -> p (h r q)")
        et2 = ET[0:64, :, :, :].rearrange("p h r q -> p (h r q)")
        for cb in range(8):
            bk = fw.bank()
            self.mm(bk[0:64, 0:480], flip, etr2[:, cb * 480:(cb + 1) * 480], True, True, [self.cstb, ETr], [bk])
            self.act(et2[:, cb * 480:(cb + 1) * 480], bk[0:64, 0:480], AF.Exp, [bk], [ET])
        nm = self.cst("natmask", rows=64)
        for h in range(4):
            self.tt("dve", ET[0:64, h, :, :], ET[0:64, h, :, :], nm.unsqueeze(1).to_broadcast([64, 15, 64]), ALU.mult, [ET, self.cstb], [ET])
        wb, wv = self.win_block(l, 1152, 256)
        self.proj_fm(wb, wv, 2, lambda jj, th, bk: self.cp(self.ev(), QT[:, jj, th * 512:(th + 1) * 512], bk[:, :], [bk], [QT]))
        wb, wv = self.win_block(l, 1408, 256)
        self.proj_fm(wb, wv, 2, lambda jj, th, bk: self.cp(self.ev(), KT[:, jj, th * 512:(th + 1) * 512], bk[:, :], [bk], [KT]))
        wb, wv = self.win_block(l, 1664, 256)
        self.proj_tm(wb, wv, 256, lambda ti, bk: self.cp(self.ev(), Vr[0:64, ti, :, 0:64], bk[0:64, 0:256].rearrange("p (h d) -> p h d", h=4), [bk], [Vr]), tile_tokens=64)
        def unit(slot, r, h):
            rs = min(max(r - 4, 0), 8)
            roff0 = rs - r + 7
            qs = slice(r * 64, (r + 1) * 64)
            j, hp = h // 2, h % 2
            P = slice(hp * 64, hp * 64 + 64)
            PT, PC, rd = PTp.bufs[slot], PCp.bufs[slot], RDp.bufs[slot]
            bl, bc, bo = fw.banks[slot * 4], fw.banks[slot * 4 + 1], fw.banks[slot * 4 + 2]
            for a in range(8):
                kr = rs + a
                self.mm(bl[0:64, a * 64:(a + 1) * 64], KT[P, j, kr * 64:(kr + 1) * 64], QT[P, j, qs], True, True, [KT, QT], [bl])
            for c in range(2):
                self.mm(bc[:, c * 64:(c + 1) * 64], CK[P, j, c * 128:(c + 1) * 128], QT[P, j, qs], True, True, [CK, QT], [bc])
            yield
            self.act(PT[0:64, :, :], bl[0:64, :].rearrange("p (a q) -> p a q", a=8), AF.Exp, [bl], [PT], scale=0.125)
            self.act(PC[:, :, :], bc[:, 0:128].rearrange("p (c q) -> p c q", c=2), AF.Exp, [bc], [PC], scale=0.125)
            yield
            self.tt("dve", PT[0:64, :, :], PT[0:64, :, :], ET[0:64, h, roff0:roff0 + 8, :], ALU.mult, [PT, ET], [PT])
            yield
            for a in range(8):
                self.mm(bo[:, 0:64], Vr[0:64, rs + a, h, :], PT[0:64, a, :], a == 0, False, [Vr, PT], [bo])
            for c in range(2):
                self.mm(bo[:, 0:64], CV[:, c, h, :], PC[:, c, :], False, c == 1, [CV, PC], [bo])
            yield
            self.fw.op("dve", lambda e: e.reciprocal(rd[0:64, :], bo[64:128, 0:64]), [bo], [rd])
            self.tt("dve", self.MIX[2 + j][P, qs], bo[0:64, 0:64], rd[0:64, :], ALU.mult, [bo, rd], [self.MIX[2 + j]])
        self.run_slots([(r_, h_) for r_ in range(16) for h_ in range(4)], unit, 2)

    def swa_latent(self, QT, KT, esink):
        fw, l = self.fw, self.l
        ar = self.arena
        ROPE = ar.get("ROPE", [128, 2, T], F32)
        TQ = ar.get("TQ", [128, 2, T], F32)
        KTf = ar.get("KTf", [128, T], F32)
        Vt = ar.get("Vt", [128, NT, 2, 128], BF16)
        CKf = ar.get("CKf", [128, 2, 256], F32)
        CK = ar.get("CK", [128, 2, 256], BF16)
        CVf = ar.get("CVf", [128, 2, 128], F32)
        CV = ar.get("CV", [128, 2, 2, 128], BF16)
        TMPp = ar.pool("TMP", 2, [128, 512], F32)
        PTp = ar.pool("PT", 2, [128, 3, 128], BF16)
        PCp = ar.pool("PC", 2, [128, 2, 128], BF16)
        RDp = ar.pool("RD", 2, [128, 128], F32)
        fw.dma("sp", ROPE[:, :, :], self.rope[:, :, :], writes=[ROPE])
        fw.op("pool", lambda e: e.memset(Vt[:, :, :, 64:128], 1.0), [], [Vt])
        fw.op("pool", lambda e: e.memset(CV[:, :, :, 64:128], 1.0), [], [CV])
        fw.dma("sp", CKf[:, :, :], self.csk[l].rearrange("g p t -> p g t"), writes=[CKf])
        self.cp("pool", CK[:, :, :], CKf[:, :, :], [CKf], [CK])
        fw.dma("sp", CVf[:, :, :], self.csv[l].rearrange("(c p) f -> p c f", p=128), writes=[CVf])
        self.cp("pool", CV[:, :, :, 0:64], CVf[:, :, :].rearrange("p c (h d) -> p c h d", h=2), [CVf], [CV])
        wb, wv = self.win_block(l, 3200, 256)
        self.proj_fm(wb, wv, 2, lambda jj, th, bk: self.tt("dve", TQ[:, jj, th * 512:(th + 1) * 512], bk[:, :], ROPE[:, 0, th * 512:(th + 1) * 512], ALU.mult, [bk, ROPE], [TQ]))
        wb, wv = self.win_block(l, 0, 256, src=self.w_in_sw)

        def evq(jj, th, bk):
            sl = slice(th * 512, (th + 1) * 512)
            tmp = TMPp.next()
            self.tt("dve", tmp[:, :], bk[:, :], ROPE[:, 1, sl], ALU.mult, [bk, ROPE], [tmp])
            self.tt("pool", QT[:, jj, sl], tmp[:, :], TQ[:, jj, sl], ALU.add, [tmp, TQ], [QT])
        self.proj_fm(wb, wv, 2, evq)
        wb, wv = self.win_block(l, 3456, 256)
        self.proj_fm(wb, wv, 1, lambda jj, th, bk: self.tt("dve", KTf[:, th * 512:(th + 1) * 512], bk[:, :], ROPE[:, 0, th * 512:(th + 1) * 512], ALU.mult, [bk, ROPE], [KTf]))
        self.proj_tm(wb, wv[:, :, 128:256], 128, lambda ti, bk: self.cp(self.ev(), Vt[:, ti, :, 0:64], bk[:, 0:128].rearrange("p (h d) -> p h d", h=2), [bk], [Vt]))
        wb, wv = self.win_block(l, 256, 128, src=self.w_in_sw)

        def evk(jj, th, bk):
            sl = slice(th * 512, (th + 1) * 512)
            tmp = TMPp.next()
            self.tt("dve", tmp[:, :], bk[:, :], ROPE[:, 1, sl], ALU.mult, [bk, ROPE], [tmp])
            self.tt("pool", KTf[:, sl], tmp[:, :], KTf[:, sl], ALU.add, [tmp, KTf], [KTf])
            for gg in range(2):
                for hp in range(2):
                    self.cp(self.ev(), KT[hp * 64:hp * 64 + 64, gg, sl], KTf[gg * 64:gg * 64 + 64, sl], [KTf], [KT])
        self.proj_fm(wb, wv, 1, evk)
        def unit(slot, n, h):
            qs = slice(n * 128, (n + 1) * 128)
            kbl = [kb for kb in (n - 1, n, n + 1) if 0 <= kb < NT]
            nk = len(kbl)
            j, hp = h // 2, h % 2
            g = j
            P = slice(hp * 64, hp * 64 + 64)
            PT, PC, rd = PTp.bufs[slot], PCp.bufs[slot], RDp.bufs[slot]
            bl, bc, bo = fw.banks[slot * 4], fw.banks[slot * 4 + 1], fw.banks[slot * 4 + 2]
            for i, kb in enumerate(kbl):
                self.mm(bl[:, i * 128:(i + 1) * 128], KT[P, g, kb * 128:(kb + 1) * 128], QT[P, j, qs], True, True, [KT, QT], [bl])
            for c in range(2):
                self.mm(bc[:, c * 128:(c + 1) * 128], CK[P, g, c * 128:(c + 1) * 128], QT[P, j, qs], True, True, [CK, QT], [bc])
            yield
            self.act(PT[:, 0:nk, :], bl[:, 0:nk * 128].rearrange("p (a q) -> p a q", a=nk), AF.Exp, [bl], [PT], scale=0.125)
            self.act(PC[:, :, :], bc[:, 0:256].rearrange("p (c q) -> p c q", c=2), AF.Exp, [bc], [PC], scale=0.125)
            yield
            for i, kb in enumerate(kbl):
                if kb == n - 1:
                    self.tt("dve", PT[:, i, :], PT[:, i, :], self.cst("mprev"), ALU.mult, [PT, self.cstb], [PT])
                elif kb == n + 1:
                    self.tt("pool", PT[:, i, :], PT[:, i, :], self.cst("mnext"), ALU.mult, [PT, self.cstb], [PT])
            yield
            for i, kb in enumerate(kbl):
                self.mm(bo[:, 0:128], Vt[:, kb, g, :], PT[:, i, :], i == 0, False, [Vt, PT], [bo])
            for c in range(2):
                self.mm(bo[:, 0:128], CV[:, c, g, :], PC[:, c, :], False, c == 1, [CV, PC], [bo])
            yield
            self.cp("dve", rd[0:64, :], bo[64:128, 0:128], [bo], [rd])
            self.ts("dve", rd[0:64, :], rd[0:64, :], esink[0:64, h:h + 1], None, ALU.add, None, [rd, esink], [rd])
            self.fw.op("dve", lambda e: e.reciprocal(rd[0:64, :], rd[0:64, :]), [rd], [rd])
            self.tt("dve", self.MIX[6 + j][P, qs], bo[0:64, 0:128], rd[0:64, :], ALU.mult, [bo, rd], [self.MIX[6 + j]])
        self.run_slots([(n_, h_) for n_ in range(NT) for h_ in range(4)], unit, 2)


_PROG_CACHE = {}


def _get_prog(debug=(), layers=DEPTH, groups=("P", "S")):
    key = (tuple(sorted(debug)), layers, tuple(groups))
    if key not in _PROG_CACHE:
        _PROG_CACHE[key] = Prog(debug, layers, groups)
    return _PROG_CACHE[key]


def _prep_inputs(inp, layers=DEPTH):
    f = lambda k: np.ascontiguousarray(np.asarray(inp[k], dtype=np.float32))
    cst, rope = _build_consts()
    cols = _build_cols(inp)
    w_in = f("w_in")
    idx = np.concatenate([3200 + h * 64 + ROPE_PERM for h in range(4)] + [3456 + h * 64 + ROPE_PERM for h in range(2)])
    w_in_sw = np.ascontiguousarray(w_in[:, :, idx])
    lora2 = np.ascontiguousarray(np.concatenate([f("rw_w2").transpose(0, 2, 1, 3), f("rw_a2").transpose(0, 2, 1, 3)], axis=1))
    rpb = f("nat_rpb")
    rpbpad = np.zeros((DEPTH, 4, 15, 127), np.float32)
    rpbpad[..., 48:79] = rpb[..., ::-1]
    NL = layers
    shared = dict(cols=cols, cst=cst, rope=rope, mod_w=f("mod_w")[:NL], w_in=w_in[:NL], w_in_sw=w_in_sw[:NL], w_out=f("w_out")[:NL],
                  ffn_w1=f("ffn_w1")[:NL], ffn_w2=f("ffn_w2")[:NL], lora2=lora2, rw_g2=f("rw_g2"), rpbpad=rpbpad)
    xp, xs = f("x_prompt"), f("x_sample")
    c, c_ctx = f("c"), f("c_ctx")
    cn, cs = f("cache_nat_kv"), f("cache_swa_kv")
    srw, shg = f("state_rwkv"), f("state_hgrn")
    maps = []
    for core in range(8):
        s = core % 2
        m = dict(shared)
        xpc = xp[core * 4:(core + 1) * 4].reshape(T, D)
        m["xT_p"] = np.ascontiguousarray(xpc.T.reshape(8, 128, T).transpose(1, 0, 2))
        m["xT_s"] = np.ascontiguousarray(xs[s].T.reshape(8, 128, T).transpose(1, 0, 2))
        cond = np.stack([c_ctx, c[s]], axis=1)
        m["condT"] = np.ascontiguousarray(cond.reshape(8, 128, 2).transpose(1, 0, 2))
        nk = cn[s, :, 0].reshape(DEPTH, 256, 256)
        m["cnk"] = np.ascontiguousarray(nk.transpose(0, 2, 1).reshape(DEPTH, 2, 128, 256))
        m["cnv"] = np.ascontiguousarray(cn[s, :, 1].reshape(DEPTH, 256, 256))
        sk = cs[s, :, 0].reshape(DEPTH, 256, 2, 64).transpose(0, 2, 3, 1)
        m["csk"] = np.ascontiguousarray(np.concatenate([sk, sk], axis=2))
        m["csv"] = np.ascontiguousarray(cs[s, :, 1].reshape(DEPTH, 256, 128))
        st = srw[s].transpose(0, 1, 2, 4, 3)
        m["srw"] = np.ascontiguousarray(st.reshape(DEPTH, 2, 2, 2, 64, 64).transpose(0, 1, 3, 4, 2, 5).reshape(DEPTH, 2, 128, 2, 64))
        sh = shg[s]
        m["shg"] = np.ascontiguousarray(sh.reshape(DEPTH, 2, 2, 2, 64, 64).transpose(0, 1, 3, 4, 2, 5).reshape(DEPTH, 2, 128, 2, 64))
        maps.append(m)
    return maps


def _assemble(results):
    B = 32
    y_p = np.zeros((B, 256, D), np.float32)
    y_s = np.zeros((2, 1024, D), np.float32)
    natkv = np.zeros((B, DEPTH, 2, 256, 4, 64), np.float32)
    swakv = np.zeros((B, DEPTH, 2, 256, 2, 64), np.float32)
    rwst = np.zeros((B, DEPTH, 2, 4, 64, 64), np.float32)
    hgst = np.zeros((B, DEPTH, 2, 4, 64, 64), np.float32)
    for core in range(8):
        r = results[core]
        bs = slice(core * 4, core * 4 + 4)
        yT = r["yT_p"].transpose(1, 0, 2).reshape(D, T)
        y_p[bs] = yT.T.reshape(4, 256, D)
        if core < 2:
            y_s[core] = r["yT_s"].transpose(1, 0, 2).reshape(D, T).T
        nk = r["natk"].reshape(DEPTH, 256, 4, 256)
        natkv[bs, :, 0] = nk.transpose(2, 0, 3, 1).reshape(4, DEPTH, 256, 4, 64)
        nv = r["natv"].reshape(DEPTH, 4, 256, 256)
        natkv[bs, :, 1] = nv.transpose(1, 0, 2, 3).reshape(4, DEPTH, 256, 4, 64)
        sk = r["swak"].reshape(DEPTH, 128, 4, 256)
        swakv[bs, :, 0] = sk.transpose(2, 0, 3, 1).reshape(4, DEPTH, 256, 2, 64)
        sv = r["swav"].reshape(DEPTH, 4, 256, 128)
        swakv[bs, :, 1] = sv.transpose(1, 0, 2, 3).reshape(4, DEPTH, 256, 2, 64)
        rs = r["rwst"].reshape(DEPTH, 2, 4, 2, 64, 2, 64)
        rwst[bs] = rs.transpose(2, 0, 1, 5, 3, 6, 4).reshape(4, DEPTH, 2, 4, 64, 64)
        hs = r["hgst"].reshape(DEPTH, 2, 4, 2, 64, 2, 64)
        hgst[bs] = hs.transpose(2, 0, 1, 5, 3, 4, 6).reshape(4, DEPTH, 2, 4, 64, 64)
    return (y_p, y_s, natkv, swakv, rwst, hgst)


def kernel(**inputs):
    prog = _get_prog()
    maps = _prep_inputs(inputs)
    res = run_bass_kernel_spmd(prog.nc, maps, core_ids=list(range(8)))
    return _assemble(res.results)
```

```python
import numpy as np
import concourse.bass as bass
import concourse.mybir as mybir
from concourse.bass_utils import run_bass_kernel_spmd

F32 = mybir.dt.float32
BF16 = mybir.dt.bfloat16
AF = mybir.ActivationFunctionType
ALU = mybir.AluOpType

DEPTH = 4
D = 1024
T = 1024
NT = 8
D_IN = 3712
DECAY = 0.6065306597126334
EPS = 1e-6
LN_EPS = 64e-5


class Buf:
    __slots__ = ("name", "t", "lw", "rd", "wsem", "wcnt", "rsem", "rcnt")

    def __init__(self, name, t):
        self.name = name
        self.t = t
        self.lw = None
        self.rd = []
        self.wsem = None
        self.wcnt = 0
        self.rsem = None
        self.rcnt = 0

    def __getitem__(self, idx):
        return self.t[idx]


class FW:
    ENG = ("pe", "dve", "act", "pool", "sp")

    def __init__(self, nc):
        self.nc = nc
        self.eng = {"pe": nc.tensor, "dve": nc.vector, "act": nc.scalar, "pool": nc.gpsimd, "sp": nc.sync}
        self.sem = {}
        self.cnt = {}
        for e in ("pe", "dve", "act", "pool"):
            self.sem[e] = nc.alloc_semaphore(name="s_" + e)
            self.cnt[e] = 0
        self.seen = {e: {} for e in self.ENG}
        self.ninst = 0
        self.out_dma = []
        self.dma_keys = {}
        self.rr = {}
        self.banks = None
        self.bank_i = 0
        self.dcnt = {}
        self.free_dsems = []

    def sb(self, name, shape, dtype=F32):
        return Buf(name, self.nc.alloc_sbuf_tensor(name, list(shape), dtype))

    def ps(self, name, shape, dtype=F32):
        return Buf(name, self.nc.alloc_psum_tensor(name, list(shape), dtype))

    def dram(self, name, shape, dtype=F32, kind="Internal"):
        return Buf(name, self.nc.dram_tensor(name, list(shape), dtype, kind=kind))

    def bank(self):
        if self.banks is None:
            self.banks = [self.ps("bank%d" % i, [128, 512]) for i in range(8)]
        b = self.banks[self.bank_i % 8]
        self.bank_i += 1
        return b

    def pick(self, key, engs):
        i = self.rr.get(key, 0)
        self.rr[key] = i + 1
        return engs[i % len(engs)]

    def _need(self, e, deps):
        eng = self.eng[e]
        seen = self.seen[e]
        best = {}
        for d in deps:
            if d is None:
                continue
            k, v = d
            if best.get(k, 0) < v:
                best[k] = v
        for k, v in best.items():
            if seen.get(k, 0) >= v:
                continue
            if e == "pe" and k == "pe":
                continue
            seen[k] = v
            eng.wait_ge(self.sem[k], v)

    @staticmethod
    def _deps(reads, writes, skip_dma_waw=False):
        deps = []
        for b in reads:
            deps.append(b.lw)
        for b in writes:
            if not (skip_dma_waw and b.lw is not None and isinstance(b.lw[0], tuple)):
                deps.append(b.lw)
            deps.extend(b.rd)
        return deps

    def op(self, e, fn, reads=(), writes=()):
        self._need(e, self._deps(reads, writes))
        ins = fn(self.eng[e])
        self.cnt[e] += 1
        v = self.cnt[e]
        ins.then_inc(self.sem[e], 1)
        for b in writes:
            b.lw = (e, v)
            b.rd = []
        for b in reads:
            if b not in writes:
                b.rd.append((e, v))
        self.ninst += 1
        return ins

    def dma(self, q, out, in_, reads=(), writes=(), part=False, is_output=False):
        self._need(q, self._deps(reads, writes, skip_dma_waw=part))
        ins = self.eng[q].dma_start(out=out, in_=in_)
        if writes:
            b = writes[0]
            if b.wsem is None:
                b.wsem = self._dsem()
            key = b.wsem
        else:
            b = reads[0]
            if b.rsem is None:
                b.rsem = self._dsem()
            key = b.rsem
        self.dcnt[key] += 16
        v = self.dcnt[key]
        ins.then_inc(self.sem[key], 16)
        self.dma_keys[key] = v
        for w in writes:
            w.lw = (key, v)
            w.rd = []
        for r in reads:
            r.rd.append((key, v))
        if is_output:
            self.out_dma.append((key, v))
        self.ninst += 1
        return ins

    def _dsem(self):
        if self.free_dsems:
            return self.free_dsems.pop()
        key = ("d", len(self.dcnt))
        self.sem[key] = self.nc.alloc_semaphore(name="dsem%d" % len(self.dcnt))
        self.dcnt[key] = 0
        return key

    def barrier(self):
        deps = [(e, self.cnt[e]) for e in ("pe", "dve", "act", "pool") if self.cnt[e]]
        deps += list(self.dma_keys.items())
        for e in self.ENG:
            self._need(e, deps)

    def finish(self):
        self.barrier()


class Arena:
    def __init__(self, fw, nwords):
        self.fw = fw
        self.nwords = nwords
        self.t32 = fw.nc.alloc_sbuf_tensor("arena", [128, nwords], F32)
        self.t16 = self.t32.bitcast(BF16)
        self.off = 0
        self.peak = 0
        self.made = []

    def reset(self):
        self.fw.barrier()
        self.off = 0
        for b in self.made:
            for k in (b.wsem, b.rsem):
                if k is not None:
                    self.fw.free_dsems.append(k)
            b.wsem = b.rsem = None
        self.made = []

    def get(self, name, shape, dtype=F32):
        n = int(np.prod(shape[1:]))
        words = n if dtype == F32 else (n + 1) // 2
        assert self.off + words <= self.nwords, (name, self.off, words, self.nwords)
        if dtype == F32:
            ap = self.t32[:, self.off:self.off + n]
        else:
            ap = self.t16[:, 2 * self.off:2 * self.off + n]
        if len(shape) > 2:
            names = "abcdefg"[:len(shape) - 1]
            kw = {names[i]: shape[1 + i] for i in range(len(shape) - 2)}
            ap = ap.rearrange("p (%s) -> p %s" % (" ".join(names), " ".join(names)), **kw)
        if shape[0] < 128:
            ap = ap[0:shape[0]]
        self.off += words
        self.peak = max(self.peak, self.off)
        b = Buf(name, ap)
        self.made.append(b)
        return b

    def pool(self, name, n, shape, dtype=F32):
        return Rot([self.get("%s%d" % (name, i), shape, dtype) for i in range(n)])


class Rot:
    def __init__(self, bufs):
        self.bufs = bufs
        self.i = 0

    def next(self):
        b = self.bufs[self.i % len(self.bufs)]
        self.i += 1
        return b


class ColTable:
    def __init__(self):
        self.idx = {}
        self.cols = []
        self.n = 0

    def add(self, name, per_layer_vecs):
        k = len(per_layer_vecs[0]) // 128
        self.idx[name] = self.n
        self.cols.append(np.stack([np.asarray(v, np.float32).reshape(k, 128) for v in per_layer_vecs]))
        self.n += k

    def build(self):
        a = np.concatenate(self.cols, axis=1)
        return np.ascontiguousarray(a.transpose(2, 0, 1))


def _col_names():
    ct = ColTable()
    z = lambda n: [np.zeros(n, np.float32)] * DEPTH
    for nm, n in COL_SPEC:
        ct.add(nm, z(n))
    return ct.idx, ct.n


COL_SPEC = [("norm_g", 4096), ("mod_b", 6144),
            ("mu_rkv0", 768), ("mu_rkv1", 768), ("mu_lora0", 128), ("mu_lora1", 128),
            ("w0_0", 256), ("w0_1", 256), ("a0_0", 256), ("a0_1", 256),
            ("kk", 256), ("ka", 256), ("rk", 256), ("lnx_w", 256), ("lnx_b", 256),
            ("lbl0", 1024), ("lbl1", 1024), ("hg_norm", 256), ("sink", 512)]
COLI, NCOL = _col_names()


def _build_cols(inp):
    ct = ColTable()
    L = range(DEPTH)
    g = lambda k: np.asarray(inp[k], np.float32)
    vals = {
        "norm_g": [g("norm_g")[l].reshape(-1) for l in L],
        "mod_b": [g("mod_b")[l] for l in L],
        "mu_rkv0": [g("rw_mu_rkv")[l, 0].reshape(-1) for l in L],
        "mu_rkv1": [g("rw_mu_rkv")[l, 1].reshape(-1) for l in L],
        "mu_lora0": [g("rw_mu_lora")[l, 0].reshape(-1) for l in L],
        "mu_lora1": [g("rw_mu_lora")[l, 1].reshape(-1) for l in L],
        "w0_0": [g("rw_w0")[l, 0] for l in L], "w0_1": [g("rw_w0")[l, 1] for l in L],
        "a0_0": [g("rw_a0")[l, 0] for l in L], "a0_1": [g("rw_a0")[l, 1] for l in L],
        "kk": [g("rw_kk")[l] for l in L], "ka": [g("rw_ka")[l] for l in L],
        "rk": [g("rw_rk")[l].reshape(-1) for l in L],
        "lnx_w": [g("rw_lnx_w")[l] for l in L], "lnx_b": [g("rw_lnx_b")[l] for l in L],
        "lbl0": [g("hg_lb_logits")[0].reshape(-1) for l in L],
        "lbl1": [g("hg_lb_logits")[1].reshape(-1) for l in L],
        "hg_norm": [g("hg_norm")[l] for l in L],
        "sink": [np.repeat(g("swa_sink")[l], 128) for l in L],
    }
    for nm, n in COL_SPEC:
        assert len(vals[nm][0]) == n, nm
        ct.add(nm, vals[nm])
    return ct.build()


CST_SPEC = [("ident", 128), ("ones", 128), ("blk", 128),
            ("g4_0", 512), ("g4_1", 512), ("mb_0", 128), ("mb_1", 128),
            ("reset", 256), ("natmask", 64),
            ("mprev", 128), ("mnext", 128), ("flip", 64)]
CSTI = {}
_o = 0
for _n, _w in CST_SPEC:
    CSTI[_n] = (_o, _w)
    _o += _w
NCST = _o


def _build_consts():
    c = np.zeros((128, NCST), np.float32)

    def put(nm, a):
        o, w = CSTI[nm]
        c[:a.shape[0], o:o + w] = a
    i = np.arange(128)
    put("ident", np.eye(128))
    put("ones", np.ones((128, 128)))
    put("blk", ((i[:, None] // 64) == (i[None, :] // 64)).astype(np.float32))
    for d in (0, 1):
        if d == 0:
            strictT = (i[:, None] < i[None, :]).astype(np.float32)
            inclT = (i[:, None] <= i[None, :]).astype(np.float32)
        else:
            strictT = (i[:, None] > i[None, :]).astype(np.float32)
            inclT = (i[:, None] >= i[None, :]).astype(np.float32)
        put("g4_%d" % d, np.concatenate([-strictT, inclT, strictT, inclT], axis=1))
        put("mb_%d" % d, -strictT.T)
    r = np.ones((128, 256), np.float32)
    r[:, 0] = 0
    r[:, 128] = 0
    put("reset", r)
    t = np.arange(1024)
    row = (t // 64).astype(np.float32)
    col = (t % 64).astype(np.float32)
    inv = (1.0 / (10000.0 ** (np.arange(16, dtype=np.float32) / 16))).astype(np.float32)
    C = np.zeros((64, 1024), np.float32)
    S = np.zeros((64, 1024), np.float32)
    for base, pos in ((0, row), (32, col)):
        ang = (pos[None, :] * inv[:, None]).astype(np.float32)
        C[base:base + 16] = np.cos(ang)
        C[base + 16:base + 32] = np.cos(ang)
        S[base:base + 16] = -np.sin(ang)
        S[base + 16:base + 32] = np.sin(ang)
    rope = np.stack([np.concatenate([C, C], axis=0), np.concatenate([S, S], axis=0)], axis=1)
    kc = np.arange(64)[:, None]
    qc = np.arange(64)[None, :]
    ws = np.clip(qc - 8, 0, 48)
    put("natmask", ((kc >= ws) & (kc < ws + 16)).astype(np.float32))
    put("mprev", (i[:, None] >= i[None, :]).astype(np.float32))
    put("mnext", (i[:, None] <= i[None, :]).astype(np.float32))
    put("flip", np.eye(64)[::-1].copy())
    return c, np.ascontiguousarray(rope.astype(np.float32))


ROPE_PERM = np.concatenate([np.arange(16, 32), np.arange(0, 16), np.arange(48, 64), np.arange(32, 48)])


class Prog:
    def __init__(self, debug=(), layers=DEPTH, groups=("P", "S")):
        self.debug = set(debug)
        self.layers = layers
        self.groups = groups
        self.nc = bass.Bass("TRN2", target_bir_lowering=False)
        self.fw = FW(self.nc)
        self.dbg_outs = {}
        self.build()

    def mm(self, out, lhsT, rhs, start, stop, reads, writes):
        return self.fw.op("pe", lambda e: e.matmul(out, lhsT, rhs, start=start, stop=stop), reads, writes)

    def tp(self, out, in_, ident, reads, writes):
        return self.fw.op("pe", lambda e: e.transpose(out, in_, ident), reads, writes)

    def act(self, out, in_, func, reads, writes, bias=None, scale=1.0):
        if bias is None:
            return self.fw.op("act", lambda e: e.activation(out, in_, func, scale=scale), reads, writes)
        return self.fw.op("act", lambda e: e.activation(out, in_, func, bias=bias, scale=scale), reads, writes)

    def tt(self, eng, out, in0, in1, op, reads, writes):
        return self.fw.op(eng, lambda e: e.tensor_tensor(out, in0, in1, op), reads, writes)

    def ts(self, eng, out, in0, s1, s2, op0, op1, reads, writes):
        if s2 is None:
            return self.fw.op(eng, lambda e: e.tensor_scalar(out, in0, s1, None, op0), reads, writes)
        return self.fw.op(eng, lambda e: e.tensor_scalar(out, in0, s1, s2, op0, op1), reads, writes)

    def stt(self, eng, out, in0, scalar, in1, op0, op1, reads, writes):
        eng = "dve"
        return self.fw.op(eng, lambda e: e.scalar_tensor_tensor(out, in0, scalar, in1, op0, op1), reads, writes)

    def rsqrt(self, out, in_, addc, reads, wbuf):
        self.act(out, in_, AF.Ln, reads + [self.epsb], [wbuf], bias=self.epsc(addc))
        self.act(out, out, AF.Exp, [wbuf], [wbuf], scale=-0.5)

    def epsc(self, v):
        i = self.eps_vals.index(v)
        return self.epsb[:, i:i + 1]

    def cp(self, eng, out, in_, reads, writes):
        if eng == "act":
            return self.fw.op("act", lambda e: e.copy(out, in_), reads, writes)
        return self.fw.op(eng, lambda e: e.tensor_copy(out, in_), reads, writes)

    def ev(self):
        return self.fw.pick("ev", ("dve", "act"))

    def el(self):
        return self.fw.pick("el", ("dve", "pool"))

    def dump(self, name, buf, ap, shape, dtype=F32):
        if name not in self.debug:
            return
        o = self.fw.dram("dbg_" + name, list(shape), dtype, kind="ExternalOutput")
        self.dbg_outs[name] = o
        self.fw.dma("sp", o[:], ap, reads=[buf], is_output=True)

    def col(self, l, name, j=0, n=1):
        i = COLI[name] + j
        return self.cols[:, l, i:i + n]

    def cst(self, name, lo=0, hi=None, rows=128):
        o, w = CSTI[name]
        hi = w if hi is None else hi
        return self.cstb[0:rows, o + lo:o + hi]

    def load_block(self, src_ap, a, b, cast=True):
        fw = self.fw
        n = a * b
        st = self.WST.next()
        sview = st[:, 0:n].rearrange("p (a b) -> p a b", a=a)
        fw.dma("sp", sview, src_ap, writes=[st])
        self.mod_tick()
        if not cast:
            return st, sview
        wb = self.WBF.next()
        wview = wb[:, 0:n].rearrange("p (a b) -> p a b", a=a)
        self.cp(self.fw.pick("cast", self.cast_engs), wb[:, 0:n], st[:, 0:n], [st], [wb])
        return wb, wview

    def win_block(self, l, c0, ncol, src=None):
        src = self.w_in if src is None else src
        ap = src[l, :, c0:c0 + ncol].rearrange("(k p) n -> p k n", p=128)
        return self.load_block(ap, 8, ncol)

    def mod_tick(self, force=False):
        if not self.mod_pending:
            return
        if self.in_mod:
            return
        self.in_mod = True
        l, kc, third = self.mod_pending.pop(0)
        fw = self.fw
        st = self.WST.next()
        fw.dma("sp", st[:, 0:2048], self.mod_w[l, kc * 128:(kc + 1) * 128, third * 2048:(third + 1) * 2048], writes=[st])
        bk = fw.bank()
        for j in range(16):
            self.mm(bk[:, 2 * j:2 * j + 2], st[:, j * 128:(j + 1) * 128], self.silu[:, kc, :], True, True, [st, self.silu], [bk])
        dst = self.modacc[:, l, third * 16:(third + 1) * 16, :]
        src = bk[:, 0:32].rearrange("p (a b) -> p a b", b=2)
        if kc == 0:
            self.cp("dve", dst, src, [bk], [self.modacc])
        else:
            self.tt("dve", dst, dst, src, ALU.add, [bk, self.modacc], [self.modacc])
        self.in_mod = False

    def mod_flush(self, l):
        while self.mod_pending and self.mod_pending[0][0] <= l:
            self.mod_tick()

    def mod_finalize(self, l, g):
        fw = self.fw
        mc = self.modc
        m = lambda i: self.modacc[:, l, i * 8:(i + 1) * 8, g]
        mb = lambda i: self.col(l, "mod_b", i * 8, 8)
        ng = lambda i: self.col(l, "norm_g", i * 8, 8)
        R = [self.modacc, self.cols, mc]
        for i in range(6):
            self.tt("dve", mc[:, i, :], m(i), mb(i), ALU.add, R, [mc])
        self.ts("dve", mc[:, 6, :], mc[:, 1, :], 1.0, 32.0, ALU.add, ALU.mult, R, [mc])
        self.tt("dve", mc[:, 6, :], mc[:, 6, :], ng(0), ALU.mult, R, [mc])
        self.ts("dve", mc[:, 7, :], mc[:, 2, :], 32.0, None, ALU.mult, None, R, [mc])
        self.tt("dve", mc[:, 7, :], mc[:, 7, :], ng(1), ALU.mult, R, [mc])
        self.ts("dve", mc[:, 8, :], mc[:, 4, :], 1.0, 32.0, ALU.add, ALU.mult, R, [mc])
        self.tt("dve", mc[:, 8, :], mc[:, 8, :], ng(2), ALU.mult, R, [mc])
        self.ts("dve", mc[:, 9, :], mc[:, 5, :], 32.0, None, ALU.mult, None, R, [mc])
        self.tt("dve", mc[:, 9, :], mc[:, 9, :], ng(3), ALU.mult, R, [mc])
        return dict(shift1=mc[:, 0, :], coefA1=mc[:, 6, :], gco1=mc[:, 7, :], shift2=mc[:, 3, :], coefA2=mc[:, 8, :], gco2=mc[:, 9, :])

    def rstd_of(self, src_bufs, src_ap, sq, rstd):
        fw = self.fw
        bks = [fw.bank(), fw.bank()]
        for c in range(8):
            s = sq.next()
            self.act(s[:, :], src_ap(c), AF.Square, [src_bufs[c]], [s])
            for th in range(2):
                self.mm(bks[th][:, :], self.ones_bf[:, :], s[:, th * 512:(th + 1) * 512], c == 0, c == 7, [self.ones_bf, s], [bks[th]])
        for th in range(2):
            self.rsqrt(rstd[:, th * 512:(th + 1) * 512], bks[th][:, :], D * EPS, [bks[th]], rstd)
        return rstd

    def build(self):
        fw = self.fw
        nc = self.nc
        di = lambda n, s: fw.dram(n, s, F32, kind="ExternalInput")
        do = lambda n, s: fw.dram(n, s, F32, kind="ExternalOutput")
        self.xT = {"P": di("xT_p", [128, 8, T]), "S": di("xT_s", [128, 8, T])}
        self.yT = {"P": do("yT_p", [128, 8, T]), "S": do("yT_s", [128, 8, T])}
        condT = di("condT", [128, 8, 2])
        cols_d = di("cols", [128, DEPTH, NCOL])
        cst_d = di("cst", [128, NCST])
        NL = self.layers
        self.mod_w = di("mod_w", [NL, D, 6 * D])
        self.w_in = di("w_in", [NL, D, D_IN])
        self.w_in_sw = di("w_in_sw", [NL, D, 384])
        self.w_out = di("w_out", [NL, D, D])
        self.ffn_w1 = di("ffn_w1", [NL, D, 4 * D])
        self.ffn_w2 = di("ffn_w2", [NL, 4 * D, D])
        self.lora2 = di("lora2", [DEPTH, 128, 2, 256])
        self.rw_g2 = di("rw_g2", [DEPTH, 128, 256])
        self.rpbpad = di("rpbpad", [DEPTH, 4, 15, 127])
        self.rope = di("rope", [128, 2, T])
        self.cnk = di("cnk", [DEPTH, 2, 128, 256])
        self.cnv = di("cnv", [DEPTH, 256, 256])
        self.csk = di("csk", [DEPTH, 2, 128, 256])
        self.csv = di("csv", [DEPTH, 256, 128])
        self.srw = di("srw", [DEPTH, 2, 128, 2, 64])
        self.shg = di("shg", [DEPTH, 2, 128, 2, 64])
        self.natk = do("natk", [DEPTH, 2, 128, T])
        self.natv = do("natv", [DEPTH, NT, 128, 256])
        self.swak = do("swak", [DEPTH, 128, T])
        self.swav = do("swav", [DEPTH, NT, 128, 128])
        self.rwst = do("rwst", [DEPTH, 2, 4, 128, 2, 64])
        self.hgst = do("hgst", [DEPTH, 2, 4, 128, 2, 64])

        self.cstb = fw.sb("cstb", [128, NCST])
        self.cols = fw.sb("colsb", [128, DEPTH, NCOL])
        self.ident_bf = fw.sb("ident_bf", [128, 128], BF16)
        self.ones_bf = fw.sb("ones_bf", [128, 128], BF16)
        self.blk_bf = fw.sb("blk_bf", [128, 128], BF16)
        self.silu = fw.sb("silu", [128, 8, 2])
        self.modacc = fw.sb("modacc", [128, DEPTH, 48, 2])
        self.modc = fw.sb("modc", [128, 10, 8])
        self.lb = fw.sb("lb", [128, 2, DEPTH, 2])
        self.oml = fw.sb("oml", [128, 2, DEPTH, 2])
        self.X = [fw.sb("X%d" % c, [128, T]) for c in range(8)]
        self.H = [fw.sb("H%d" % c, [128, T], BF16) for c in range(8)]
        self.MIX = [fw.sb("MIX%d" % c, [128, T], BF16) for c in range(8)]
        self.WST = Rot([fw.sb("wst%d" % i, [128, 2048]) for i in range(2)])
        self.WBF = Rot([fw.sb("wbf%d" % i, [128, 2048], BF16) for i in range(2)])
        self.rstd = fw.sb("rstd", [128, T])
        self.sq = Rot([fw.sb("sq%d" % i, [128, T], BF16) for i in range(2)])
        self.arena = Arena(fw, (nc.sbuf_bytes_remaining - 2048) // 4)

        self.eps_vals = [D * EPS, 1e-24, 64 * LN_EPS, 64 * EPS]
        self.epsb = fw.sb("epsb", [128, 4])
        for i, v in enumerate(self.eps_vals):
            fw.op("pool", lambda e: e.memset(self.epsb[:, i:i + 1], v), [], [self.epsb])
        fw.dma("sp", self.cstb[:, :], cst_d[:, :], writes=[self.cstb])
        fw.dma("sp", self.cols[:, :, :], cols_d[:, :, :], writes=[self.cols])
        fw.dma("sp", self.silu[:, :, :], condT[:, :, :], writes=[self.silu])
        self.cp("dve", self.ident_bf[:, :], self.cst("ident"), [self.cstb], [self.ident_bf])
        self.cp("dve", self.ones_bf[:, :], self.cst("ones"), [self.cstb], [self.ones_bf])
        self.cp("dve", self.blk_bf[:, :], self.cst("blk"), [self.cstb], [self.blk_bf])
        self.act(self.silu[:, :, :], self.silu[:, :, :], AF.Silu, [self.silu], [self.silu])
        self.setup_lb()

        self.cast_engs = ("pool", "dve", "pool")
        self.mod_pending = [(l, kc, th) for l in range(self.layers) for kc in range(8) for th in range(3)]
        self.in_mod = False

        for G in self.groups:
            for c in range(8):
                fw.dma("sp", self.X[c][:, :], self.xT[G][:, c, :], writes=[self.X[c]])
            for l in range(self.layers):
                self.block(G, l)
            for c in range(8):
                fw.dma("sp", self.yT[G][:, c, :], self.X[c][:, :], reads=[self.X[c]], is_output=True)
        fw.finish()

    def setup_lb(self):
        E = self.fw.sb("lbE", [128, 2, DEPTH, 2])
        S = self.fw.sb("lbS", [128, 2, 2])
        R = [self.cols, E, S, self.lb, self.oml]
        for d in range(2):
            i = COLI["lbl%d" % d]
            self.act(E[:, d, :, :], self.cols[:, 0, i:i + 8].rearrange("p (a b) -> p a b", a=DEPTH), AF.Exp, R, [E])
            self.tt("dve", S[:, d, :], E[:, d, 0, :], E[:, d, 1, :], ALU.add, R, [S])
            self.tt("dve", S[:, d, :], S[:, d, :], E[:, d, 2, :], ALU.add, R, [S])
            self.tt("dve", S[:, d, :], S[:, d, :], E[:, d, 3, :], ALU.add, R, [S])
            self.fw.op("dve", lambda e: e.reciprocal(S[:, d, :], S[:, d, :]), R, [S])
            for l in range(DEPTH):
                self.tt("dve", E[:, d, l, :], E[:, d, l, :], S[:, d, :], ALU.mult, R, [E])
            self.fw.op("dve", lambda e: e.memset(self.lb[:, d, 0, :], 0.0), R, [self.lb])
            for l in range(1, DEPTH):
                self.tt("dve", self.lb[:, d, l, :], self.lb[:, d, l - 1, :], E[:, d, l, :], ALU.add, R, [self.lb])
        self.ts("dve", self.oml[:, :, :, :], self.lb[:, :, :, :], -1.0, 1.0, ALU.mult, ALU.add, R, [self.oml])

    def block(self, G, l):
        fw = self.fw
        g = 0 if G == "P" else 1
        self.G, self.l = G, l
        self.seqs = [(s * 2, 2) for s in range(4)] if G == "P" else [(0, 8)]
        self.mod_flush(l)
        mc = self.mod_finalize(l, g)
        self.mc = mc
        X, H = self.X, self.H
        rstd = self.rstd_of(X, lambda c: X[c][:, :], self.sq, self.rstd)
        for c in range(8):
            tmp = self.sq.next()
            e = self.el()
            self.stt(e, tmp[:, :], X[c][:, :], mc["coefA1"][:, c:c + 1], rstd[:, :], ALU.mult, ALU.mult, [X[c], rstd, self.modc], [tmp])
            self.act(H[c][:, :], tmp[:, :], AF.Identity, [tmp, self.modc], [H[c]], bias=mc["shift1"][:, c:c + 1])
        self.dump("h_%s%d" % (G, l), H[0], H[0][:, :], [128, T], BF16)
        self.mixer_A()
        self.mixer_B()
        self.mixer_C()
        self.mixer_D()
        for c in range(8):
            self.dump("mix%d_%s%d" % (c, G, l), self.MIX[c], self.MIX[c][:, :], [128, T], BF16)
        self.out_and_ffn()

    def proj_fm(self, wb, wv, nchunks, evac):
        for jj in range(nchunks):
            for th in range(2):
                bk = self.fw.bank()
                for kc in range(8):
                    self.mm(bk[:, :], wv[:, kc, jj * 128:(jj + 1) * 128], self.H[kc][:, th * 512:(th + 1) * 512],
                            kc == 0, kc == 7, [wb, self.H[kc]], [bk])
                evac(jj, th, bk)

    def proj_tm(self, wb, wv, ncol, evac, tile_tokens=128):
        nt = T // tile_tokens
        for ti in range(nt):
            bk = self.fw.bank()
            for kc in range(8):
                self.mm(bk[0:tile_tokens, 0:ncol], self.H[kc][:, ti * tile_tokens:(ti + 1) * tile_tokens], wv[:, kc, 0:ncol],
                        kc == 0, kc == 7, [wb, self.H[kc]], [bk])
            evac(ti, bk)

    def ctx_attn(self, QT, KT, ksel, Vt, vsel, mix0, PTp, RDp, esink=None):
        fw = self.fw

        def unit(slot, s, h):
            t0 = s * 256
            j, hp = h // 2, h % 2
            P = slice(hp * 64, hp * 64 + 64)
            PT = PTp.bufs[slot]
            rd = RDp.bufs[slot]
            bks = [fw.banks[slot * 4 + kc] for kc in range(2)]
            for kc in range(2):
                self.mm(bks[kc][:, 0:256], KT[P, ksel(h), t0 + kc * 128:t0 + (kc + 1) * 128], QT[P, j, t0:t0 + 256], True, True, [KT, QT], [bks[kc]])
            yield
            for kc in range(2):
                self.act(PT[:, kc, :], bks[kc][:, 0:256], AF.Exp, [bks[kc]], [PT], scale=0.125)
            yield
            bo = fw.banks[slot * 4 + 2]
            for kc in range(2):
                self.mm(bo[:, 0:256], Vt[:, s * 2 + kc, vsel(h), :], PT[:, kc, :], kc == 0, kc == 1, [Vt, PT], [bo])
            yield
            self.cp("dve", rd[0:64, :], bo[64:128, 0:256], [bo], [rd])
            if esink is not None:
                self.ts("dve", rd[0:64, :], rd[0:64, :], esink[0:64, h:h + 1], None, ALU.add, None, [rd, esink], [rd])
            self.fw.op("dve", lambda e: e.reciprocal(rd[0:64, :], rd[0:64, :]), [rd], [rd])
            self.tt("dve", self.MIX[mix0 + j][P, t0:t0 + 256], bo[0:64, 0:256], rd[0:64, :], ALU.mult, [bo, rd], [self.MIX[mix0 + j]])
        self.run_slots([(s_, h_) for s_ in range(4) for h_ in range(4)], unit, 2)

    def run_slots(self, items, unit, width):
        pend = list(items)
        active = {}
        while pend or active:
            for slot in range(width):
                if slot not in active and pend:
                    active[slot] = unit(slot, *pend.pop(0))
            for slot in list(active.keys()):
                try:
                    next(active[slot])
                except StopIteration:
                    del active[slot]

    def mixer_B(self):
        fw, l, G = self.fw, self.l, self.G
        ar = self.arena
        ar.reset()
        QT = ar.get("QT", [128, 2, T], BF16)
        KT = ar.get("KT", [128, 2, T], BF16)
        if G == "P":
            KTf = ar.get("KTf", [128, 2, T], F32)
            Vf = ar.get("Vf", [128, NT, 256], F32)
            Vt = ar.get("Vt", [128, NT, 4, 128], BF16)
            PTp = ar.pool("PT", 2, [128, 2, 256], BF16)
            RDp = ar.pool("RD", 2, [128, 256], F32)
            fw.op("pool", lambda e: e.memset(Vt[:, :, :, 64:128], 1.0), [], [Vt])
            wb, wv = self.win_block(l, 1152, 256)
            self.proj_fm(wb, wv, 2, lambda jj, th, bk: self.cp(self.ev(), QT[:, jj, th * 512:(th + 1) * 512], bk[:, :], [bk], [QT]))
            wb, wv = self.win_block(l, 1408, 256)

            def evk(jj, th, bk):
                self.cp(self.ev(), KTf[:, jj, th * 512:(th + 1) * 512], bk[:, :], [bk], [KTf])
                self.cp("pool", KT[:, jj, th * 512:(th + 1) * 512], KTf[:, jj, th * 512:(th + 1) * 512], [KTf], [KT])
            self.proj_fm(wb, wv, 2, evk)
            fw.dma("sp", self.natk[l].rearrange("j p t -> p j t"), KTf[:, :, :], reads=[KTf], is_output=True)
            wb, wv = self.win_block(l, 1664, 256)

            def evv(ti, bk):
                self.cp(self.ev(), Vf[:, ti, :], bk[:, 0:256], [bk], [Vf])
                self.cp("pool", Vt[:, ti, :, 0:64], Vf[:, ti, :].rearrange("p (h d) -> p h d", h=4), [Vf], [Vt])
            self.proj_tm(wb, wv, 256, evv)
            fw.dma("sp", self.natv[l].rearrange("n p f -> p n f"), Vf[:, :, :], reads=[Vf], is_output=True)
            self.ctx_attn(QT, KT, lambda h: h // 2, Vt, lambda h: h, 2, PTp, RDp)
        else:
            self.nat_latent(QT, KT)

    def mixer_D(self):
        fw, l, G = self.fw, self.l, self.G
        ar = self.arena
        ar.reset()
        QT = ar.get("QT", [128, 2, T], BF16)
        KT = ar.get("KTd", [128, 2, T], BF16)
        esink = ar.get("esink", [128, 4], F32)
        self.act(esink[:, :], self.col(l, "sink", 0, 4), AF.Exp, [self.cols], [esink])
        if G == "P":
            KTf = ar.get("KTf", [128, T], F32)
            Vf = ar.get("Vf", [128, NT, 128], F32)
            Vt = ar.get("Vt", [128, NT, 2, 128], BF16)
            PTp = ar.pool("PT", 2, [128, 2, 256], BF16)
            RDp = ar.pool("RD", 2, [128, 256], F32)
            fw.op("pool", lambda e: e.memset(Vt[:, :, :, 64:128], 1.0), [], [Vt])
            wb, wv = self.win_block(l, 3200, 256)
            self.proj_fm(wb, wv, 2, lambda jj, th, bk: self.cp(self.ev(), QT[:, jj, th * 512:(th + 1) * 512], bk[:, :], [bk], [QT]))
            wb, wv = self.win_block(l, 3456, 256)

            def evk(jj, th, bk):
                sl = slice(th * 512, (th + 1) * 512)
                self.cp(self.ev(), KTf[:, sl], bk[:, :], [bk], [KTf])
                for gg in range(2):
                    for hp in range(2):
                        self.cp(self.ev(), KT[hp * 64:hp * 64 + 64, gg, sl], KTf[gg * 64:gg * 64 + 64, sl], [KTf], [KT])
            self.proj_fm(wb, wv, 1, evk)
            fw.dma("sp", self.swak[l], KTf[:, :], reads=[KTf], is_output=True)

            def evv(ti, bk):
                self.cp(self.ev(), Vf[:, ti, :], bk[:, 0:128], [bk], [Vf])
                self.cp("pool", Vt[:, ti, :, 0:64], Vf[:, ti, :].rearrange("p (h d) -> p h d", h=2), [Vf], [Vt])
            self.proj_tm(wb, wv[:, :, 128:256], 128, evv)
            fw.dma("sp", self.swav[l].rearrange("n p f -> p n f"), Vf[:, :, :], reads=[Vf], is_output=True)
            self.ctx_attn(QT, KT, lambda h: h // 2, Vt, lambda h: h // 2, 6, PTp, RDp, esink=esink)
        else:
            self.swa_latent(QT, KT, esink)

    def lerp(self, d, out_ap, src_buf, src_ap, mu_ap, DIFF, out_buf):
        ns = len(self.seqs)
        L = T // ns
        s3 = src_ap.rearrange("p (s t) -> p s t", s=ns)
        d3 = DIFF[:, :].rearrange("p (s t) -> p s t", s=ns)
        e = self.el()
        if d == 0:
            self.tt(e, d3[:, :, 1:L], s3[:, :, 0:L - 1], s3[:, :, 1:L], ALU.subtract, [src_buf], [DIFF])
            self.ts(e, d3[:, :, 0:1], s3[:, :, 0:1], -1.0, None, ALU.mult, None, [src_buf], [DIFF])
        else:
            self.tt(e, d3[:, :, 0:L - 1], s3[:, :, 1:L], s3[:, :, 0:L - 1], ALU.subtract, [src_buf], [DIFF])
            self.ts(e, d3[:, :, L - 1:L], s3[:, :, L - 1:L], -1.0, None, ALU.mult, None, [src_buf], [DIFF])
        self.stt(e, out_ap, DIFF[:, :], mu_ap, src_ap, ALU.mult, ALU.add, [DIFF, src_buf, self.cols], [out_buf])

    def tile_order(self, d):
        out = []
        for si, (t0, n) in enumerate(self.seqs):
            tl = list(range(t0, t0 + n))
            if d == 1:
                tl = tl[::-1]
            for i, ti in enumerate(tl):
                out.append((si, ti, i == n - 1))
        return out

    def scan_tile(self, d, gs_ap, src_ap, reads, wbuf, C=128):
        rs = self.cst("reset", 0, C)
        if d == 0:
            self.fw.op("dve", lambda e: e.tensor_tensor_scan(gs_ap, rs, src_ap, 0.0, ALU.mult, ALU.add), reads + [self.cstb], [wbuf])
        else:
            self.fw.op("dve", lambda e: e.tensor_tensor_scan(gs_ap[:, ::-1], rs, src_ap[:, ::-1], 0.0, ALU.mult, ALU.add), reads + [self.cstb], [wbuf])

    def run_gens(self, gens, width=None):
        pend = list(gens)
        width = width or len(pend)
        active = []
        while pend or active:
            while pend and len(active) < width:
                active.append(pend.pop(0))
            nxt = []
            for g in active:
                try:
                    next(g)
                    nxt.append(g)
                except StopIteration:
                    pass
            active = nxt

    def lin_head(self, d, j, hp, t0, W, dplr, banks, C=128):
        fw = self.fw
        bi = [0]

        def nb():
            b = banks[bi[0] % len(banks)]
            bi[0] += 1
            return b
        P = slice(hp * 64, hp * 64 + 64)
        h = 2 * j + hp
        RK, KB, TOK, YACC = W["RK"], W["KB"], W["TOK"], W["YACC"]
        hst = W["hst"]
        HST = W["HST"]
        rt = RK[P, j, 1, 0:C]
        kb = KB[P, j, 0:C]
        Vh = TOK[0:C, 0, h * 64:(h + 1) * 64]
        Kbh = TOK[0:C, 1, h * 64:(h + 1) * 64]
        G4 = W["G4p"].next()
        g4c = lambda a, b: self.cst("g4_%d" % d, a, b)
        if dplr:
            PB = W["PB"]
            kt = RK[P, j, 0, :]
            rk2 = RK[P, j, :, :]
            pb = PB[P, j, :]
            Pbh = TOK[:, 2, h * 64:(h + 1) * 64]
            bk = nb()
            self.mm(bk[:, 0:256], pb, rk2, True, True, [PB, RK], [bk])
            self.mm(bk[:, 256:512], kb, rk2, True, True, [KB, RK], [bk])
            yield
            self.tt("dve", G4[:, :], bk[:, :], g4c(0, 512), ALU.mult, [bk, self.cstb], [G4])
            yield
            bk2 = nb()
            self.mm(bk2[:, 0:128], kt, pb, True, True, [RK, PB], [bk2])
            yield
            Bm = W["Bp"].next()
            self.tt("dve", Bm[:, :], bk2[:, 0:128], self.cst("mb_%d" % d), ALU.mult, [bk2, self.cstb], [Bm])
            CN = W["CNp"].next()
            self.tt("pool", CN[:, 128:256], G4[:, 0:128], self.cst("ident"), ALU.add, [G4, self.cstb], [CN])
            yield
            bk = nb()
            self.mm(bk[:, 0:128], Bm[:, :], G4[:, 0:128], True, True, [Bm, G4], [bk])
            yield
            self.cp("act", CN[:, 0:128], bk[:, 0:128], [bk], [CN])
            Cprev_buf, Cprev = G4, G4[:, 0:128]
            Nfin = None
            for lev in range(1, 7):
                bk = nb()
                self.mm(bk[:, 0:128], Cprev, Bm[:, :], True, True, [Cprev_buf, Bm], [bk])
                yield
                Bn = W["Bp"].next()
                self.cp("act", Bn[:, :], bk[:, 0:128], [bk], [Bn])
                Bm = Bn
                yield
                bk = nb()
                if lev < 6:
                    self.mm(bk[:, 0:256], Bm[:, :], CN[:, 0:256], True, True, [Bm, CN], [bk])
                    yield
                    CN2 = W["CNp"].next()
                    self.cp("act", CN2[:, 0:128], bk[:, 0:128], [bk], [CN2])
                    self.tt("dve", CN2[:, 128:256], bk[:, 128:256], CN[:, 128:256], ALU.add, [bk, CN], [CN2])
                    Cprev_buf, Cprev = CN, CN[:, 0:128]
                    CN = CN2
                    yield
                else:
                    self.mm(bk[:, 0:128], Bm[:, :], CN[:, 128:256], True, True, [Bm, CN], [bk])
                    yield
                    Nfin = W["Bp"].next()
                    self.tt("dve", Nfin[:, :], bk[:, 0:128], CN[:, 128:256], ALU.add, [bk, CN], [Nfin])
                    yield
            mqk = G4[:, 384:512]
        else:
            bk = nb()
            self.mm(bk[0:C, 0:C], kb, rt, True, True, [KB, RK], [bk])
            yield
            self.tt("dve", G4[0:C, 0:C], bk[0:C, 0:C], self.cst("g4_%d" % d, 384, 384 + C, rows=C), ALU.mult, [bk, self.cstb], [G4])
            mqk = G4[0:C, 0:C]
        Hp = W["Hpp"].next()
        Hb = W["Hbp"].next()
        self.ts("dve", Hp[P, :], hst, W["expc"][P, j:j + 1], None, ALU.mult, None, [HST, W["expcb"]], [Hp])
        yield
        self.cp("act", Hb[P, :], Hp[P, :], [Hp], [Hb])
        yield
        if dplr:
            bx = nb()
            self.mm(bx[:, 0:64], G4[:, 256:384], Vh, True, False, [G4, TOK], [bx])
            self.mm(bx[:, 0:64], kt, Hb[P, :], False, True, [RK, Hb], [bx])
            yield
            Xs = W["Xp"].next()
            self.cp("act", Xs[:, :], bx[:, 0:64], [bx], [Xs])
            yield
            bu = nb()
            self.mm(bu[:, 0:64], Nfin[:, :], Xs[:, :], True, True, [Nfin, Xs], [bu])
            yield
            Un = W["Xp"].next()
            self.ts("dve", Un[:, :], bu[:, 0:64], -1.0, None, ALU.mult, None, [bu], [Un])
            yield
        by = nb()
        self.mm(by[P, 0:C], Hb[P, :], rt, True, False, [Hb, RK], [by])
        self.mm(by[P, 0:C], Vh, mqk, False, not dplr, [TOK, G4], [by])
        if dplr:
            self.mm(by[P, 0:C], Un[:, :], G4[:, 128:256], False, True, [Un, G4], [by])
        self.mm(by[P, 128:192], Kbh, Vh, True, not dplr, [TOK], [by])
        if dplr:
            self.mm(by[P, 128:192], Pbh, Un[:, :], False, True, [TOK, Un], [by])
        yield
        ydst = YACC[P, j, t0:t0 + C]
        self.tt("dve", ydst, by[P, 0:C], ydst, ALU.add, [by, YACC], [YACC])
        self.tt("dve", Hp[P, :], by[P, 128:192], Hp[P, :], ALU.add, [by, Hp], [Hp])
        yield
        self.ts("dve", hst, Hp[P, :], W["eplast"][P, j:j + 1], None, ALU.mult, None, [Hp, W["eplastb"]], [HST])

    def lin_common_bufs(self, ar, dplr, njobs):
        Ws = []
        for k in range(njobs):
            W = {}
            if dplr:
                W["G4p"] = ar.pool("G4_%d" % k, 1, [128, 512], BF16)
                W["Bp"] = ar.pool("Bm_%d" % k, 3, [128, 128], BF16)
                W["CNp"] = ar.pool("CN_%d" % k, 3, [128, 256], BF16)
                W["Xp"] = ar.pool("Xs_%d" % k, 2, [128, 64], BF16)
            else:
                W["G4p"] = ar.pool("G4_%d" % k, 1, [128, 32], BF16)
            W["Hpp"] = ar.pool("Hp_%d" % k, 1, [128, 64], F32)
            W["Hbp"] = ar.pool("Hb_%d" % k, 1, [128, 64], BF16)
            Ws.append(W)
        return Ws

    def lin_state_io(self, HST, src_dram, dst_dram, load):
        l = self.l
        if load:
            self.fw.dma("sp", HST[:, :, 0, :, :], src_dram[l].rearrange("d p j v -> p d j v"), writes=[HST])
        else:
            self.fw.dma("sp", dst_dram[l].rearrange("d s p j v -> p d s j v"), HST[:, :, :, :, :], reads=[HST], is_output=True)

    def mixer_A(self):
        fw, l, G = self.fw, self.l, self.G
        ar = self.arena
        ar.reset()
        ns = len(self.seqs)
        RAW = ar.get("RAW", [128, 6, T], BF16)
        WA = ar.get("WA", [128, 2, T], F32)
        SG = ar.get("SG", [128, T], BF16)
        YACC = ar.get("YACC", [128, 2, T], F32)
        BACC = ar.get("BACC", [128, 2, T], BF16)
        HST = ar.get("HST", [128, 2, ns, 2, 64], F32)
        L2 = ar.get("L2", [128, 2, 256], F32)
        G2 = ar.get("G2", [128, 256], BF16)
        LR = ar.get("LR", [128, 6, T], BF16)
        DIFF = ar.get("DIFF", [128, T], F32)
        cc = ar.get("cc", [128, 8], F32)
        W4 = self.lin_common_bufs(ar, True, 4)
        fw.op("pool", lambda e: e.memset(YACC[:, :, :], 0.0), [], [YACC])
        wk = lambda nm, shape, dt=F32: ar.pool(nm, 2, shape, dt)
        THp, SIGp, Ap = wk("th", [128, 128]), wk("sig", [128, 2, 128]), wk("a", [128, 2, 128])
        T1p, T2p, T3p = wk("t1", [128, 128]), wk("t2", [128, 128]), ar.pool("t3", 1, [128, 128], F32)
        wk1 = lambda nm, shape, dt=F32: ar.pool(nm, 1, shape, dt)
        KAPp, KMp, PPp = wk1("kap", [128, 2, 128]), wk1("km", [128, 2, 128]), wk1("pp", [128, 2, 128])
        SQp = wk("sqb", [128, 128], BF16)
        GSp, EPp, EMp, EPPp = wk1("gs", [128, 2, 128]), wk1("ep", [128, 2, 128]), wk1("em", [128, 2, 128]), wk("epp", [128, 2, 128])
        EXCp = wk("exc", [128, 4])
        RKp, KBp, PBp = wk("RK", [128, 2, 2, 128], BF16), wk("KB", [128, 2, 128], BF16), wk("PB", [128, 2, 128], BF16)
        TOKp = wk("TOK", [128, 3, 256], BF16)

        fw.dma("sp", L2[:, :, :], self.lora2[l], writes=[L2])
        g2st = self.WST.next()
        fw.dma("sp", g2st[:, 0:256], self.rw_g2[l], writes=[g2st])
        self.cp("pool", G2[:, :], g2st[:, 0:256], [g2st], [G2])
        self.ts("dve", cc[:, 0:2], self.col(l, "ka", 0, 2), -1.0, 1.0, ALU.mult, ALU.add, [self.cols], [cc])
        self.ts("dve", cc[:, 2:4], self.col(l, "lnx_w", 0, 2), 8.0, None, ALU.mult, None, [self.cols], [cc])
        if G == "P":
            fw.op("pool", lambda e: e.memset(HST[:, :, :, :, :], 0.0), [], [HST])
        else:
            self.lin_state_io(HST, self.srw, None, True)

        def evA(c0):
            def f(jj, th, bk):
                ci = c0 + jj
                sl = slice(th * 512, (th + 1) * 512)
                if ci < 6:
                    self.cp(self.ev(), RAW[:, ci, sl], bk[:, :], [bk], [RAW])
                elif ci == 6:
                    self.act(SG[:, sl], bk[:, :], AF.Sigmoid, [bk], [SG])
                else:
                    self.cp(self.ev(), WA[:, ci - 7, sl], bk[:, :], [bk], [WA])
            return f
        for b0 in range(0, 9, 2):
            nch = min(2, 9 - b0)
            wb, wv = self.win_block(l, b0 * 128, nch * 128)
            self.proj_fm(wb, wv, nch, evA(b0))

        order = [(d, si, ti) for d in range(2) for (si, ti, last) in self.tile_order(d)]
        work = {}
        pbi = [0]

        def pbank():
            b = fw.banks[4 + pbi[0] % 4]
            pbi[0] += 1
            return b

        def lerps(d):
            for a in range(3):
                for jj in range(2):
                    ci = a * 2 + jj
                    self.lerp(d, LR[:, ci, :], RAW, RAW[:, ci, :], self.col(l, "mu_rkv%d" % d, ci, 1), DIFF, LR)
            self.lerp(d, WA[:, d, :], WA, WA[:, d, :], self.col(l, "mu_lora%d" % d, 0, 1), DIFF, WA)
            LW = WA

        def prepA(idx):
            d, si, ti = order[idx]
            if True:
                t0 = ti * 128
                ts_ = slice(t0, t0 + 128)
                th = THp.next()
                self.act(th[0:64, :], WA[0:64, d, ts_], AF.Tanh, [WA], [th])
                sig, av = SIGp.next(), Ap.next()
                for jj in range(2):
                    bk = pbank()
                    self.mm(bk[:, 0:128], L2[0:64, d, jj * 128:(jj + 1) * 128], th[0:64, :], True, True, [L2, th], [bk])
                    self.act(sig[:, jj, :], bk[:, 0:128], AF.Sigmoid, [bk, self.cols], [sig], bias=self.col(l, "w0_%d" % d, jj, 1))
                    bk = pbank()
                    self.mm(bk[:, 0:128], L2[64:128, d, jj * 128:(jj + 1) * 128], WA[64:128, d, ts_], True, True, [L2, WA], [bk])
                    self.act(av[:, jj, :], bk[:, 0:128], AF.Sigmoid, [bk, self.cols], [av], bias=self.col(l, "a0_%d" % d, jj, 1))
                yield
                kap, km, pp = KAPp.next(), KMp.next(), PPp.next()
                gs, ep, em, epp, exc = GSp.next(), EPp.next(), EMp.next(), EPPp.next(), EXCp.next()
                RK, KB, PB, TOK = RKp.next(), KBp.next(), PBp.next(), TOKp.next()
                mid = 63 if d == 0 else 64
                lastc = 127 if d == 0 else 0
                for jj in range(2):
                    rl, kl, vl = LR[:, jj, ts_], LR[:, 2 + jj, ts_], LR[:, 4 + jj, ts_]
                    e = self.el()
                    t1, t2 = T1p.next(), T2p.next()
                    self.ts(e, t1[:, :], kl, self.col(l, "kk", jj, 1), None, ALU.mult, None, [LR, self.cols], [t1])
                    sqb = SQp.next()
                    self.tt(e, sqb[:, :], t1[:, :], t1[:, :], ALU.mult, [t1], [sqb])
                    bk = pbank()
                    self.mm(bk[:, 0:128], self.blk_bf[:, :], sqb[:, :], True, True, [self.blk_bf, sqb], [bk])
                    self.rsqrt(t2[:, :], bk[:, 0:128], 1e-24, [bk], t2)
                    self.tt(e, kap[:, jj, :], t1[:, :], t2[:, :], ALU.mult, [t1, t2], [kap])
                    yield
                    t3 = T3p.next()
                    self.ts(e, t3[:, :], av[:, jj, :], self.col(l, "ka", jj, 1), cc[:, jj:jj + 1], ALU.mult, ALU.add, [av, self.cols, cc], [t3])
                    self.tt(e, km[:, jj, :], kl, t3[:, :], ALU.mult, [LR, t3], [km])
                    self.tt(e, pp[:, jj, :], kap[:, jj, :], av[:, jj, :], ALU.mult, [kap, av], [pp])
                    yield
                    t1b = T1p.next()
                    self.stt(e, t1b[:, :], rl, self.col(l, "rk", jj, 1), km[:, jj, :], ALU.mult, ALU.mult, [LR, self.cols, km], [t1b])
                    sqb2 = SQp.next()
                    self.cp(e, sqb2[:, :], t1b[:, :], [t1b], [sqb2])
                    bk = pbank()
                    self.mm(bk[:, 0:128], self.blk_bf[:, :], sqb2[:, :], True, True, [self.blk_bf, sqb2], [bk])
                    if d == 0:
                        self.tt("dve", BACC[:, jj, ts_], bk[:, 0:128], vl, ALU.mult, [bk, LR], [BACC])
                    else:
                        t2b = T2p.next()
                        self.tt("dve", t2b[:, :], bk[:, 0:128], vl, ALU.mult, [bk, LR], [t2b])
                        self.tt(e, BACC[:, jj, ts_], BACC[:, jj, ts_], t2b[:, :], ALU.add, [t2b, BACC], [BACC])
                    yield
                    self.scan_tile(d, gs[:, jj, :], sig[:, jj, :], [sig], gs)
                yield
                self.cp("dve", exc[:, 0:2], gs[:, :, mid], [gs], [exc])
                gc = EPPp.next()
                for jj in range(2):
                    self.ts("dve", gc[:, jj, :], gs[:, jj, :], exc[:, jj:jj + 1], None, ALU.subtract, None, [gs, exc], [gc])
                yield
                self.act(ep[:, :, :], gc[:, :, :], AF.Exp, [gc], [ep], scale=-DECAY)
                self.act(em[:, :, :], gc[:, :, :], AF.Exp, [gc], [em], scale=DECAY)
                self.tt(self.el(), gs[:, :, :], gc[:, :, :], sig[:, :, :], ALU.subtract, [gc, sig], [gs])
                self.act(epp[:, :, :], gs[:, :, :], AF.Exp, [gs], [epp], scale=-DECAY)
                self.act(exc[:, 0:2], exc[:, 0:2], AF.Exp, [exc], [exc], scale=-DECAY)
                yield
                self.cp("dve", exc[:, 2:4], ep[:, :, lastc], [ep], [exc])
                for jj in range(2):
                    e = self.el()
                    self.tt(e, RK[:, jj, 0, :], kap[:, jj, :], epp[:, jj, :], ALU.mult, [kap, epp], [RK])
                    self.tt(e, RK[:, jj, 1, :], LR[:, jj, ts_], ep[:, jj, :], ALU.mult, [LR, ep], [RK])
                    self.tt(e, KB[:, jj, :], km[:, jj, :], em[:, jj, :], ALU.mult, [km, em], [KB])
                    self.tt(e, PB[:, jj, :], pp[:, jj, :], em[:, jj, :], ALU.mult, [pp, em], [PB])
                yield
                bt = pbank()
                btv = bt.t.bitcast(BF16)
                for jj in range(2):
                    self.tp(btv[:, (0 * 2 + jj) * 128:(0 * 2 + jj + 1) * 128], LR[:, 4 + jj, ts_], self.ident_bf[:, :], [LR, self.ident_bf], [bt])
                    self.tp(btv[:, (1 * 2 + jj) * 128:(1 * 2 + jj + 1) * 128], KB[:, jj, :], self.ident_bf[:, :], [KB, self.ident_bf], [bt])
                    self.tp(btv[:, (2 * 2 + jj) * 128:(2 * 2 + jj + 1) * 128], PB[:, jj, :], self.ident_bf[:, :], [PB, self.ident_bf], [bt])
                yield
                self.cp("act", TOK[:, :, :], btv[:, 0:768].rearrange("p (a b) -> p a b", a=3), [bt], [TOK])
                yield
                work[idx] = (d, si, t0, dict(RK=RK, KB=KB, PB=PB, TOK=TOK, expc=exc, expcb=exc, eplast=exc[:, 2:4], eplastb=exc, YACC=YACC, HST=HST))

        def heads(idx):
            d, si, t0, shared = work[idx]
            gens = []
            for k, (jj, hp) in enumerate(((0, 0), (0, 1), (1, 0), (1, 1))):
                Wk = dict(W4[k])
                Wk.update(shared)
                Wk["hst"] = HST[hp * 64:hp * 64 + 64, d, si, jj, :]
                gens.append(self.lin_head(d, jj, hp, t0, Wk, True, [fw.banks[k]]))
            return gens

        def start_prep(idx):
            if idx == 0 or order[idx - 1][0] != order[idx][0]:
                lerps(order[idx][0])
            return prepA(idx)

        fw.barrier()
        self.run_gens([start_prep(0)])
        for idx in range(len(order)):
            gens = heads(idx)
            if idx + 1 < len(order):
                gens.append(start_prep(idx + 1))
            self.run_gens(gens)
        fw.barrier()
        if G == "P":
            self.lin_state_io(HST, None, self.rwst, False)
        self.dump("yacc_%s%d" % (G, l), YACC, YACC[:, :, :], [128, 2, T])
        self.dump("bacc_%s%d" % (G, l), BACC, BACC[:, :, :], [128, 2, T])
        blk = self.cst("blk")
        for jj in range(2):
            for th in range(2):
                sl = slice(th * 512, (th + 1) * 512)
                bk = fw.bank()
                self.mm(bk[:, :], blk, YACC[:, jj, sl], True, True, [self.cstb, YACC], [bk])
                yc = DIFF
                self.stt("dve", yc[:, sl], bk[:, :], -1.0 / 64, YACC[:, jj, sl], ALU.mult, ALU.add, [bk, YACC], [yc])
                self.tt("pool", WA[:, 0, sl], yc[:, sl], yc[:, sl], ALU.mult, [yc], [WA])
                bk2 = fw.bank()
                self.mm(bk2[:, :], blk, WA[:, 0, sl], True, True, [self.cstb, WA], [bk2])
                self.rsqrt(WA[:, 0, sl], bk2[:, :], 64 * LN_EPS, [bk2], WA)
                self.tt("pool", yc[:, sl], yc[:, sl], WA[:, 0, sl], ALU.mult, [yc, WA], [yc])
                self.ts("dve", yc[:, sl], yc[:, sl], cc[:, 2 + jj:3 + jj], self.col(l, "lnx_b", jj, 1), ALU.mult, ALU.add, [yc, cc, self.cols], [yc])
                self.tt("pool", yc[:, sl], yc[:, sl], BACC[:, jj, sl], ALU.add, [yc, BACC], [yc])
                bk3 = fw.bank()
                self.mm(bk3[:, :], G2[:, jj * 128:(jj + 1) * 128], SG[:, sl], True, True, [G2, SG], [bk3])
                self.tt("dve", self.MIX[jj][:, sl], yc[:, sl], bk3[:, :], ALU.mult, [yc, bk3], [self.MIX[jj]])

    def mixer_C(self):
        fw, l, G = self.fw, self.l, self.G
        ar = self.arena
        ar.reset()
        ns = len(self.seqs)
        Q = ar.get("Q", [128, 2, T], BF16)
        V = ar.get("V", [128, 2, T], BF16)
        GATE = ar.get("GATE", [128, 2, T], BF16)
        FR = ar.get("FR", [128, 2, 2, T], F32)
        YACC = ar.get("YACC", [128, 2, T], F32)
        HST = ar.get("HST", [128, 2, ns, 2, 64], F32)
        LFd = [ar.get("LF%d" % d, [128, 2, T], F32) for d in range(2)]
        KKd = [ar.get("KK%d" % d, [128, 2, T], BF16) for d in range(2)]
        LF = LFd[0]
        W8 = self.lin_common_bufs(ar, False, 8)
        fw.op("pool", lambda e: e.memset(YACC[:, :, :], 0.0), [], [YACC])
        hn8 = ar.get("hn8", [128, 2], F32)
        self.ts("dve", hn8[:, :], self.col(l, "hg_norm", 0, 2), 8.0, None, ALU.mult, None, [self.cols], [hn8])
        C = 32
        wk = lambda nm, shape, dt=F32: [ar.pool("%s_%d" % (nm, d), 2, shape, dt) for d in range(2)]
        GSp, GCp, EPp, EMp, EXCp = wk("gs", [128, 2, C]), wk("gc", [128, 2, C]), wk("ep", [128, 2, C]), wk("em", [128, 2, C]), wk("exc", [128, 2])
        RKp, KBp, TOKp = wk("RK", [128, 2, 2, C], BF16), wk("KB", [128, 2, C], BF16), wk("TOK", [128, 2, 256], BF16)
        if G == "P":
            fw.op("pool", lambda e: e.memset(HST[:, :, :, :, :], 0.0), [], [HST])
        else:
            self.lin_state_io(HST, self.shg, None, True)

        def evC(c0):
            def f(jj, th, bk):
                ci = c0 + jj
                sl = slice(th * 512, (th + 1) * 512)
                if ci < 2:
                    self.act(Q[:, ci, sl], bk[:, :], AF.Silu, [bk], [Q])
                elif ci < 4:
                    self.cp(self.ev(), V[:, ci - 2, sl], bk[:, :], [bk], [V])
                elif ci < 6:
                    self.act(GATE[:, ci - 4, sl], bk[:, :], AF.Sigmoid, [bk], [GATE])
                else:
                    self.cp(self.ev(), FR[:, (ci - 6) // 2, (ci - 6) % 2, sl], bk[:, :], [bk], [FR])
            return f
        for b0 in range(0, 10, 2):
            wb, wv = self.win_block(l, 1920 + b0 * 128, 256)
            self.proj_fm(wb, wv, 2, evC(b0))

        for d in range(2):
            for jj in range(2):
                self.act(LFd[d][:, jj, :], FR[:, d, jj, :], AF.Sigmoid, [FR], [LFd[d]])
                self.ts("dve", LFd[d][:, jj, :], LFd[d][:, jj, :], self.oml[:, d, l, jj:jj + 1], self.lb[:, d, l, jj:jj + 1], ALU.mult, ALU.add, [LFd[d], self.oml, self.lb], [LFd[d]])
                self.ts("pool", KKd[d][:, jj, :], LFd[d][:, jj, :], -1.0, 1.0, ALU.mult, ALU.add, [LFd[d]], [KKd[d]])
        for d in range(2):
            for jj in range(2):
                self.act(LFd[d][:, jj, :], LFd[d][:, jj, :], AF.Ln, [LFd[d]], [LFd[d]])
        orders = []
        for d in range(2):
            o = []
            for (si, ti, last) in self.tile_order(d):
                for sub in (range(128 // C) if d == 0 else range(128 // C - 1, -1, -1)):
                    o.append((si, ti * 128 + sub * C))
            orders.append(o)
        import os
        fw.barrier()
        hbanks = [Buf("hbank%d" % k, fw.banks[k // 2].t[:, (k % 2) * 256:(k % 2 + 1) * 256]) for k in range(8)]
        nsteps = len(orders[0])
        work = {}

        def prep(i):
            par = i % 2
            wl = []
            for d in range(2):
                si, t0 = orders[d][i]
                ts_ = slice(t0, t0 + C)
                gs, gc, ep, em, exc = GSp[d].bufs[par], GCp[d].bufs[par], EPp[d].bufs[par], EMp[d].bufs[par], EXCp[d].bufs[par]
                RK, KB, TOK = RKp[d].bufs[par], KBp[d].bufs[par], TOKp[d].bufs[par]
                mid = C // 2 - 1 if d == 0 else C // 2
                lastc = C - 1 if d == 0 else 0
                for jj in range(2):
                    self.scan_tile(d, gs[:, jj, 0:C], LFd[d][:, jj, ts_], [LFd[d]], gs, C)
                yield
                self.cp("dve", exc[:, 0:2], gs[:, :, mid], [gs], [exc])
                yield
                for jj in range(2):
                    self.ts("dve", gc[:, jj, 0:C], gs[:, jj, 0:C], exc[:, jj:jj + 1], None, ALU.subtract, None, [gs, exc], [gc])
                yield
                self.act(ep[:, :, 0:C], gc[:, :, 0:C], AF.Exp, [gc], [ep])
                self.act(em[:, :, 0:C], gc[:, :, 0:C], AF.Exp, [gc], [em], scale=-1.0)
                self.act(exc[:, 0:2], exc[:, 0:2], AF.Exp, [exc], [exc])
                yield
                for jj in range(2):
                    e = self.el()
                    self.tt(e, RK[:, jj, 1, 0:C], Q[:, jj, ts_], ep[:, jj, 0:C], ALU.mult, [Q, ep], [RK])
                    self.tt(e, KB[:, jj, 0:C], KKd[d][:, jj, ts_], em[:, jj, 0:C], ALU.mult, [KKd[d], em], [KB])
                yield
                wl.append((d, si, t0, ts_, dict(RK=RK, KB=KB, TOK=TOK, expc=exc, expcb=exc, eplast=ep[:, :, lastc], eplastb=ep, YACC=YACC, HST=HST)))
            work[i] = wl

        def prepB(i):
            for (d, si, t0, ts_, sh) in work[i]:
                KB, TOK = sh["KB"], sh["TOK"]
                bt = fw.bank()
                btv = bt.t.bitcast(BF16)
                for jj in range(2):
                    self.tp(btv[0:C, jj * 128:(jj + 1) * 128], V[:, jj, ts_], self.ident_bf[:, :], [V, self.ident_bf], [bt])
                    self.tp(btv[0:C, (2 + jj) * 128:(3 + jj) * 128], KB[:, jj, 0:C], self.ident_bf[:, :], [KB, self.ident_bf], [bt])
                self.cp("act", TOK[0:C, :, :], btv[0:C, 0:512].rearrange("p (a b) -> p a b", a=2), [bt], [TOK])

        def heads(i):
            gens = []
            for (d, si, t0, ts_, shared) in work[i]:
                for k, (jj, hp) in enumerate(((0, 0), (0, 1), (1, 0), (1, 1))):
                    Wk = dict(W8[d * 4 + k])
                    Wk.update(shared)
                    Wk["hst"] = HST[hp * 64:hp * 64 + 64, d, si, jj, :]
                    gens.append(self.lin_head(d, jj, hp, t0, Wk, False, [fw.banks[d * 4 + k]], C))
            return gens

        import os
        self.run_gens([prep(0)])
        prepB(0)
        for i in range(nsteps):
            gens = heads(i)
            if i + 1 < nsteps:
                gens.append(prep(i + 1))
            self.run_gens(gens)
            if i + 1 < nsteps:
                prepB(i + 1)
        fw.barrier()
        if G == "P":
            self.lin_state_io(HST, None, self.hgst, False)
        self.dump("yaccC_%s%d" % (G, l), YACC, YACC[:, :, :], [128, 2, T])
        self.dump("lfC_%s%d" % (G, l), LF, LF[:, :, :], [128, 2, T])
        blk = self.cst("blk")
        for jj in range(2):
            for th in range(2):
                sl = slice(th * 512, (th + 1) * 512)
                self.tt("pool", LF[:, 0, sl], YACC[:, jj, sl], YACC[:, jj, sl], ALU.mult, [YACC], [LF])
                bk = fw.bank()
                self.mm(bk[:, :], blk, LF[:, 0, sl], True, True, [self.cstb, LF], [bk])
                self.rsqrt(LF[:, 0, sl], bk[:, :], 64 * EPS, [bk], LF)
                self.tt("pool", LF[:, 0, sl], LF[:, 0, sl], YACC[:, jj, sl], ALU.mult, [LF, YACC], [LF])
                self.stt("dve", self.MIX[4 + jj][:, sl], LF[:, 0, sl], hn8[:, jj:jj + 1], GATE[:, jj, sl], ALU.mult, ALU.mult, [LF, hn8, GATE], [self.MIX[4 + jj]])

    def out_and_ffn(self):
        fw, l, G = self.fw, self.l, self.G
        ar = self.arena
        ar.reset()
        mc = self.mc
        X, H, MIX = self.X, self.H, self.MIX
        O = [ar.get("O%d" % c, [128, T], F32) for c in range(8)]
        WBIG = ar.get("WBIG", [128, 8, 1024], BF16)
        HID = [ar.get("HID%d" % c, [128, T], BF16) for c in range(8)]
        RL = ar.get("RL", [128, T], BF16)
        WST0, WBF0 = self.WST, self.WBF
        self.WST = Rot(WST0.bufs + [ar.get("wstx%d" % i, [128, 2048], F32) for i in range(2)])
        self.WBF = Rot(WBF0.bufs + [ar.get("wbfx%d" % i, [128, 2048], BF16) for i in range(2)])
        self.cast_engs = ("pool", "dve", "act")

        def add_residual(gco):
            rstd = self.rstd_of(O, lambda c: O[c][:, :], self.sq, self.rstd)
            for c in range(8):
                e = self.el()
                self.stt(e, O[c][:, :], O[c][:, :], gco[:, c:c + 1], rstd[:, :], ALU.mult, ALU.mult, [O[c], rstd, self.modc], [O[c]])
                self.tt(e, X[c][:, :], X[c][:, :], O[c][:, :], ALU.add, [O[c], X[c]], [X[c]])

        for q in range(4):
            st_ap = self.w_out[l, q * 256:(q + 1) * 256, :].rearrange("(k p) n -> p k n", p=128)
            stb, sv = self.load_block(st_ap, 2, 1024, cast=False)
            self.cp(self.fw.pick("cast", self.cast_engs), WBIG[:, 2 * q:2 * q + 2, :], sv, [stb], [WBIG])
        for oc in range(8):
            for th in range(2):
                bk = fw.bank()
                for kc in range(8):
                    self.mm(bk[:, :], WBIG[:, kc, oc * 128:(oc + 1) * 128], MIX[kc][:, th * 512:(th + 1) * 512], kc == 0, kc == 7, [WBIG, MIX[kc]], [bk])
                self.cp(self.ev(), O[oc][:, th * 512:(th + 1) * 512], bk[:, :], [bk], [O[oc]])
        add_residual(mc["gco1"])
        rstd = self.rstd_of(X, lambda c: X[c][:, :], self.sq, self.rstd)
        for c in range(8):
            tmp = self.sq.next()
            self.stt(self.el(), tmp[:, :], X[c][:, :], mc["coefA2"][:, c:c + 1], rstd[:, :], ALU.mult, ALU.mult, [X[c], rstd, self.modc], [tmp])
            self.act(H[c][:, :], tmp[:, :], AF.Identity, [tmp, self.modc], [H[c]], bias=mc["shift2"][:, c:c + 1])
        for q in range(4):
            for fb in range(4):
                wb, wv = self.load_block(self.ffn_w1[l, :, (q * 8 + fb * 2) * 128:(q * 8 + fb * 2 + 2) * 128].rearrange("(k p) n -> p k n", p=128), 8, 256)

                def evh(jj, th, bk, fb=fb):
                    sl = slice(th * 512, (th + 1) * 512)
                    self.act(RL[:, sl], bk[:, :], AF.Relu, [bk], [RL])
                    self.tt(self.el(), HID[fb * 2 + jj][:, sl], RL[:, sl], RL[:, sl], ALU.mult, [RL], [HID[fb * 2 + jj]])
                self.proj_fm(wb, wv, 2, evh)
                hb = fb
                st_ap = self.ffn_w2[l, (q * 8 + hb * 2) * 128:(q * 8 + hb * 2 + 2) * 128, :].rearrange("(k p) n -> p k n", p=128)
                stb, sv = self.load_block(st_ap, 2, 1024, cast=False)
                self.cp(self.fw.pick("cast", self.cast_engs), WBIG[:, 2 * hb:2 * hb + 2, :], sv, [stb], [WBIG])
            for oc in range(8):
                for th in range(2):
                    sl = slice(th * 512, (th + 1) * 512)
                    bk = fw.bank()
                    for kc in range(8):
                        self.mm(bk[:, :], WBIG[:, kc, oc * 128:(oc + 1) * 128], HID[kc][:, sl], kc == 0, kc == 7, [WBIG, HID[kc]], [bk])
                    if q == 0:
                        self.cp(self.ev(), O[oc][:, sl], bk[:, :], [bk], [O[oc]])
                    else:
                        self.tt("dve", O[oc][:, sl], bk[:, :], O[oc][:, sl], ALU.add, [bk, O[oc]], [O[oc]])
        add_residual(mc["gco2"])
        self.WST, self.WBF = WST0, WBF0
        self.cast_engs = ("pool", "dve", "pool")

    def nat_latent(self, QT, KT):
        fw, l = self.fw, self.l
        ar = self.arena
        Vr = ar.get("Vr", [128, 16, 4, 128], BF16)
        CKf = ar.get("CKf", [128, 2, 256], F32)
        CK = ar.get("CK", [128, 2, 256], BF16)
        CVf = ar.get("CVf", [128, 2, 256], F32)
        CV = ar.get("CV", [128, 2, 4, 128], BF16)
        ET = ar.get("ET", [128, 4, 15, 64], F32)
        PTp = ar.pool("PT", 2, [128, 8, 64], BF16)
        PCp = ar.pool("PC", 2, [128, 2, 64], BF16)
        RDp = ar.pool("RD", 2, [128, 64], F32)
        fw.op("pool", lambda e: e.memset(Vr[0:64, :, :, 64:128], 1.0), [], [Vr])
        fw.op("pool", lambda e: e.memset(CV[:, :, :, 64:128], 1.0), [], [CV])
        fw.dma("sp", CKf[:, :, :], self.cnk[l].rearrange("j p t -> p j t"), writes=[CKf])
        self.cp("pool", CK[:, :, :], CKf[:, :, :], [CKf], [CK])
        fw.dma("sp", CVf[:, :, :], self.cnv[l].rearrange("(c p) f -> p c f", p=128), writes=[CVf])
        self.cp("pool", CV[:, :, :, 0:64], CVf[:, :, :].rearrange("p c (h d) -> p c h d", h=4), [CVf], [CV])
        ETr = ar.get("ETr", [128, 4, 15, 64], F32)
        src = bass.AP(tensor=self.rpbpad.t, offset=l * 4 * 15 * 127, ap=[[1, 64], [15 * 127, 4], [127, 15], [1, 64]])
        fw.dma("sp", ETr[0:64, :, :, :], src, writes=[ETr])
        flip = self.cst("flip", rows=64)
        etr2 = ETr[0:64, :, :, :].rearrange("p h r q -> p (h r q)")
        et2 = ET[0:64, :, :, :].rearrange("p h r q -> p (h r q)")
        for cb in range(8):
            bk = fw.bank()
            self.mm(bk[0:64, 0:480], flip, etr2[:, cb * 480:(cb + 1) * 480], True, True, [self.cstb, ETr], [bk])
            self.act(et2[:, cb * 480:(cb + 1) * 480], bk[0:64, 0:480], AF.Exp, [bk], [ET])
        nm = self.cst("natmask", rows=64)
        for h in range(4):
            self.tt("dve", ET[0:64, h, :, :], ET[0:64, h, :, :], nm.unsqueeze(1).to_broadcast([64, 15, 64]), ALU.mult, [ET, self.cstb], [ET])
        wb, wv = self.win_block(l, 1152, 256)
        self.proj_fm(wb, wv, 2, lambda jj, th, bk: self.cp(self.ev(), QT[:, jj, th * 512:(th + 1) * 512], bk[:, :], [bk], [QT]))
        wb, wv = self.win_block(l, 1408, 256)
        self.proj_fm(wb, wv, 2, lambda jj, th, bk: self.cp(self.ev(), KT[:, jj, th * 512:(th + 1) * 512], bk[:, :], [bk], [KT]))
        wb, wv = self.win_block(l, 1664, 256)
        self.proj_tm(wb, wv, 256, lambda ti, bk: self.cp(self.ev(), Vr[0:64, ti, :, 0:64], bk[0:64, 0:256].rearrange("p (h d) -> p h d", h=4), [bk], [Vr]), tile_tokens=64)
        def unit(slot, r, h):
            rs = min(max(r - 4, 0), 8)
            roff0 = rs - r + 7
            qs = slice(r * 64, (r + 1) * 64)
            j, hp = h // 2, h % 2
            P = slice(hp * 64, hp * 64 + 64)
            PT, PC, rd = PTp.bufs[slot], PCp.bufs[slot], RDp.bufs[slot]
            bl, bc, bo = fw.banks[slot * 4], fw.banks[slot * 4 + 1], fw.banks[slot * 4 + 2]
            for a in range(8):
                kr = rs + a
                self.mm(bl[0:64, a * 64:(a + 1) * 64], KT[P, j, kr * 64:(kr + 1) * 64], QT[P, j, qs], True, True, [KT, QT], [bl])
            for c in range(2):
                self.mm(bc[:, c * 64:(c + 1) * 64], CK[P, j, c * 128:(c + 1) * 128], QT[P, j, qs], True, True, [CK, QT], [bc])
            yield
            self.act(PT[0:64, :, :], bl[0:64, :].rearrange("p (a q) -> p a q", a=8), AF.Exp, [bl], [PT], scale=0.125)
            self.act(PC[:, :, :], bc[:, 0:128].rearrange("p (c q) -> p c q", c=2), AF.Exp, [bc], [PC], scale=0.125)
            yield
            self.tt("dve", PT[0:64, :, :], PT[0:64, :, :], ET[0:64, h, roff0:roff0 + 8, :], ALU.mult, [PT, ET], [PT])
            yield
            for a in range(8):
                self.mm(bo[:, 0:64], Vr[0:64, rs + a, h, :], PT[0:64, a, :], a == 0, False, [Vr, PT], [bo])
            for c in range(2):
                self.mm(bo[:, 0:64], CV[:, c, h, :], PC[:, c, :], False, c == 1, [CV, PC], [bo])
            yield
            self.fw.op("dve", lambda e: e.reciprocal(rd[0:64, :], bo[64:128, 0:64]), [bo], [rd])
            self.tt("dve", self.MIX[2 + j][P, qs], bo[0:64, 0:64], rd[0:64, :], ALU.mult, [bo, rd], [self.MIX[2 + j]])
        self.run_slots([(r_, h_) for r_ in range(16) for h_ in range(4)], unit, 2)

    def swa_latent(self, QT, KT, esink):
        fw, l = self.fw, self.l
        ar = self.arena
        ROPE = ar.get("ROPE", [128, 2, T], F32)
        TQ = ar.get("TQ", [128, 2, T], F32)
        KTf = ar.get("KTf", [128, T], F32)
        Vt = ar.get("Vt", [128, NT, 2, 128], BF16)
        CKf = ar.get("CKf", [128, 2, 256], F32)
        CK = ar.get("CK", [128, 2, 256], BF16)
        CVf = ar.get("CVf", [128, 2, 128], F32)
        CV = ar.get("CV", [128, 2, 2, 128], BF16)
        TMPp = ar.pool("TMP", 2, [128, 512], F32)
        PTp = ar.pool("PT", 2, [128, 3, 128], BF16)
        PCp = ar.pool("PC", 2, [128, 2, 128], BF16)
        RDp = ar.pool("RD", 2, [128, 128], F32)
        fw.dma("sp", ROPE[:, :, :], self.rope[:, :, :], writes=[ROPE])
        fw.op("pool", lambda e: e.memset(Vt[:, :, :, 64:128], 1.0), [], [Vt])
        fw.op("pool", lambda e: e.memset(CV[:, :, :, 64:128], 1.0), [], [CV])
        fw.dma("sp", CKf[:, :, :], self.csk[l].rearrange("g p t -> p g t"), writes=[CKf])
        self.cp("pool", CK[:, :, :], CKf[:, :, :], [CKf], [CK])
        fw.dma("sp", CVf[:, :, :], self.csv[l].rearrange("(c p) f -> p c f", p=128), writes=[CVf])
        self.cp("pool", CV[:, :, :, 0:64], CVf[:, :, :].rearrange("p c (h d) -> p c h d", h=2), [CVf], [CV])
        wb, wv = self.win_block(l, 3200, 256)
        self.proj_fm(wb, wv, 2, lambda jj, th, bk: self.tt("dve", TQ[:, jj, th * 512:(th + 1) * 512], bk[:, :], ROPE[:, 0, th * 512:(th + 1) * 512], ALU.mult, [bk, ROPE], [TQ]))
        wb, wv = self.win_block(l, 0, 256, src=self.w_in_sw)

        def evq(jj, th, bk):
            sl = slice(th * 512, (th + 1) * 512)
            tmp = TMPp.next()
            self.tt("dve", tmp[:, :], bk[:, :], ROPE[:, 1, sl], ALU.mult, [bk, ROPE], [tmp])
            self.tt("pool", QT[:, jj, sl], tmp[:, :], TQ[:, jj, sl], ALU.add, [tmp, TQ], [QT])
        self.proj_fm(wb, wv, 2, evq)
        wb, wv = self.win_block(l, 3456, 256)
        self.proj_fm(wb, wv, 1, lambda jj, th, bk: self.tt("dve", KTf[:, th * 512:(th + 1) * 512], bk[:, :], ROPE[:, 0, th * 512:(th + 1) * 512], ALU.mult, [bk, ROPE], [KTf]))
        self.proj_tm(wb, wv[:, :, 128:256], 128, lambda ti, bk: self.cp(self.ev(), Vt[:, ti, :, 0:64], bk[:, 0:128].rearrange("p (h d) -> p h d", h=2), [bk], [Vt]))
        wb, wv = self.win_block(l, 256, 128, src=self.w_in_sw)

        def evk(jj, th, bk):
            sl = slice(th * 512, (th + 1) * 512)
            tmp = TMPp.next()
            self.tt("dve", tmp[:, :], bk[:, :], ROPE[:, 1, sl], ALU.mult, [bk, ROPE], [tmp])
            self.tt("pool", KTf[:, sl], tmp[:, :], KTf[:, sl], ALU.add, [tmp, KTf], [KTf])
            for gg in range(2):
                for hp in range(2):
                    self.cp(self.ev(), KT[hp * 64:hp * 64 + 64, gg, sl], KTf[gg * 64:gg * 64 + 64, sl], [KTf], [KT])
        self.proj_fm(wb, wv, 1, evk)
        def unit(slot, n, h):
            qs = slice(n * 128, (n + 1) * 128)
            kbl = [kb for kb in (n - 1, n, n + 1) if 0 <= kb < NT]
            nk = len(kbl)
            j, hp = h // 2, h % 2
            g = j
            P = slice(hp * 64, hp * 64 + 64)
            PT, PC, rd = PTp.bufs[slot], PCp.bufs[slot], RDp.bufs[slot]
            bl, bc, bo = fw.banks[slot * 4], fw.banks[slot * 4 + 1], fw.banks[slot * 4 + 2]
            for i, kb in enumerate(kbl):
                self.mm(bl[:, i * 128:(i + 1) * 128], KT[P, g, kb * 128:(kb + 1) * 128], QT[P, j, qs], True, True, [KT, QT], [bl])
            for c in range(2):
                self.mm(bc[:, c * 128:(c + 1) * 128], CK[P, g, c * 128:(c + 1) * 128], QT[P, j, qs], True, True, [CK, QT], [bc])
            yield
            self.act(PT[:, 0:nk, :], bl[:, 0:nk * 128].rearrange("p (a q) -> p a q", a=nk), AF.Exp, [bl], [PT], scale=0.125)
            self.act(PC[:, :, :], bc[:, 0:256].rearrange("p (c q) -> p c q", c=2), AF.Exp, [bc], [PC], scale=0.125)
            yield
            for i, kb in enumerate(kbl):
                if kb == n - 1:
                    self.tt("dve", PT[:, i, :], PT[:, i, :], self.cst("mprev"), ALU.mult, [PT, self.cstb], [PT])
                elif kb == n + 1:
                    self.tt("pool", PT[:, i, :], PT[:, i, :], self.cst("mnext"), ALU.mult, [PT, self.cstb], [PT])
            yield
            for i, kb in enumerate(kbl):
                self.mm(bo[:, 0:128], Vt[:, kb, g, :], PT[:, i, :], i == 0, False, [Vt, PT], [bo])
            for c in range(2):
                self.mm(bo[:, 0:128], CV[:, c, g, :], PC[:, c, :], False, c == 1, [CV, PC], [bo])
            yield
            self.cp("dve", rd[0:64, :], bo[64:128, 0:128], [bo], [rd])
            self.ts("dve", rd[0:64, :], rd[0:64, :], esink[0:64, h:h + 1], None, ALU.add, None, [rd, esink], [rd])
            self.fw.op("dve", lambda e: e.reciprocal(rd[0:64, :], rd[0:64, :]), [rd], [rd])
            self.tt("dve", self.MIX[6 + j][P, qs], bo[0:64, 0:128], rd[0:64, :], ALU.mult, [bo, rd], [self.MIX[6 + j]])
        self.run_slots([(n_, h_) for n_ in range(NT) for h_ in range(4)], unit, 2)


_PROG_CACHE = {}


def _get_prog(debug=(), layers=DEPTH, groups=("P", "S")):
    key = (tuple(sorted(debug)), layers, tuple(groups))
    if key not in _PROG_CACHE:
        _PROG_CACHE[key] = Prog(debug, layers, groups)
    return _PROG_CACHE[key]


def _prep_inputs(inp, layers=DEPTH):
    f = lambda k: np.ascontiguousarray(np.asarray(inp[k], dtype=np.float32))
    cst, rope = _build_consts()
    cols = _build_cols(inp)
    w_in = f("w_in")
    idx = np.concatenate([3200 + h * 64 + ROPE_PERM for h in range(4)] + [3456 + h * 64 + ROPE_PERM for h in range(2)])
    w_in_sw = np.ascontiguousarray(w_in[:, :, idx])
    lora2 = np.ascontiguousarray(np.concatenate([f("rw_w2").transpose(0, 2, 1, 3), f("rw_a2").transpose(0, 2, 1, 3)], axis=1))
    rpb = f("nat_rpb")
    rpbpad = np.zeros((DEPTH, 4, 15, 127), np.float32)
    rpbpad[..., 48:79] = rpb[..., ::-1]
    NL = layers
    shared = dict(cols=cols, cst=cst, rope=rope, mod_w=f("mod_w")[:NL], w_in=w_in[:NL], w_in_sw=w_in_sw[:NL], w_out=f("w_out")[:NL],
                  ffn_w1=f("ffn_w1")[:NL], ffn_w2=f("ffn_w2")[:NL], lora2=lora2, rw_g2=f("rw_g2"), rpbpad=rpbpad)
    xp, xs = f("x_prompt"), f("x_sample")
    c, c_ctx = f("c"), f("c_ctx")
    cn, cs = f("cache_nat_kv"), f("cache_swa_kv")
    srw, shg = f("state_rwkv"), f("state_hgrn")
    maps = []
    for core in range(8):
        s = core % 2
        m = dict(shared)
        xpc = xp[core * 4:(core + 1) * 4].reshape(T, D)
        m["xT_p"] = np.ascontiguousarray(xpc.T.reshape(8, 128, T).transpose(1, 0, 2))
        m["xT_s"] = np.ascontiguousarray(xs[s].T.reshape(8, 128, T).transpose(1, 0, 2))
        cond = np.stack([c_ctx, c[s]], axis=1)
        m["condT"] = np.ascontiguousarray(cond.reshape(8, 128, 2).transpose(1, 0, 2))
        nk = cn[s, :, 0].reshape(DEPTH, 256, 256)
        m["cnk"] = np.ascontiguousarray(nk.transpose(0, 2, 1).reshape(DEPTH, 2, 128, 256))
        m["cnv"] = np.ascontiguousarray(cn[s, :, 1].reshape(DEPTH, 256, 256))
        sk = cs[s, :, 0].reshape(DEPTH, 256, 2, 64).transpose(0, 2, 3, 1)
        m["csk"] = np.ascontiguousarray(np.concatenate([sk, sk], axis=2))
        m["csv"] = np.ascontiguousarray(cs[s, :, 1].reshape(DEPTH, 256, 128))
        st = srw[s].transpose(0, 1, 2, 4, 3)
        m["srw"] = np.ascontiguousarray(st.reshape(DEPTH, 2, 2, 2, 64, 64).transpose(0, 1, 3, 4, 2, 5).reshape(DEPTH, 2, 128, 2, 64))
        sh = shg[s]
        m["shg"] = np.ascontiguousarray(sh.reshape(DEPTH, 2, 2, 2, 64, 64).transpose(0, 1, 3, 4, 2, 5).reshape(DEPTH, 2, 128, 2, 64))
        maps.append(m)
    return maps


def _assemble(results):
    B = 32
    y_p = np.zeros((B, 256, D), np.float32)
    y_s = np.zeros((2, 1024, D), np.float32)
    natkv = np.zeros((B, DEPTH, 2, 256, 4, 64), np.float32)
    swakv = np.zeros((B, DEPTH, 2, 256, 2, 64), np.float32)
    rwst = np.zeros((B, DEPTH, 2, 4, 64, 64), np.float32)
    hgst = np.zeros((B, DEPTH, 2, 4, 64, 64), np.float32)
    for core in range(8):
        r = results[core]
        bs = slice(core * 4, core * 4 + 4)
        yT = r["yT_p"].transpose(1, 0, 2).reshape(D, T)
        y_p[bs] = yT.T.reshape(4, 256, D)
        if core < 2:
            y_s[core] = r["yT_s"].transpose(1, 0, 2).reshape(D, T).T
        nk = r["natk"].reshape(DEPTH, 256, 4, 256)
        natkv[bs, :, 0] = nk.transpose(2, 0, 3, 1).reshape(4, DEPTH, 256, 4, 64)
        nv = r["natv"].reshape(DEPTH, 4, 256, 256)
        natkv[bs, :, 1] = nv.transpose(1, 0, 2, 3).reshape(4, DEPTH, 256, 4, 64)
        sk = r["swak"].reshape(DEPTH, 128, 4, 256)
        swakv[bs, :, 0] = sk.transpose(2, 0, 3, 1).reshape(4, DEPTH, 256, 2, 64)
        sv = r["swav"].reshape(DEPTH, 4, 256, 128)
        swakv[bs, :, 1] = sv.transpose(1, 0, 2, 3).reshape(4, DEPTH, 256, 2, 64)
        rs = r["rwst"].reshape(DEPTH, 2, 4, 2, 64, 2, 64)
        rwst[bs] = rs.transpose(2, 0, 1, 5, 3, 6, 4).reshape(4, DEPTH, 2, 4, 64, 64)
        hs = r["hgst"].reshape(DEPTH, 2, 4, 2, 64, 2, 64)
        hgst[bs] = hs.transpose(2, 0, 1, 5, 3, 4, 6).reshape(4, DEPTH, 2, 4, 64, 64)
    return (y_p, y_s, natkv, swakv, rwst, hgst)


def kernel(**inputs):
    prog = _get_prog()
    maps = _prep_inputs(inputs)
    res = run_bass_kernel_spmd(prog.nc, maps, core_ids=list(range(8)))
    return _assemble(res.results)
```

```python
import numpy as np
import concourse.bass as bass
import concourse.mybir as mybir
from concourse.bass_utils import run_bass_kernel_spmd

F32 = mybir.dt.float32
BF16 = mybir.dt.bfloat16
AF = mybir.ActivationFunctionType
ALU = mybir.AluOpType

DEPTH = 4
D = 1024
T = 1024
NT = 8
D_IN = 3712
DECAY = 0.6065306597126334
EPS = 1e-6
LN_EPS = 64e-5


class Buf:
    __slots__ = ("name", "t", "lw", "rd", "wsem", "wcnt", "rsem", "rcnt")

    def __init__(self, name, t):
        self.name = name
        self.t = t
        self.lw = None
        self.rd = []
        self.wsem = None
        self.wcnt = 0
        self.rsem = None
        self.rcnt = 0

    def __getitem__(self, idx):
        return self.t[idx]


class FW:
    ENG = ("pe", "dve", "act", "pool", "sp")

    def __init__(self, nc):
        self.nc = nc
        self.eng = {"pe": nc.tensor, "dve": nc.vector, "act": nc.scalar, "pool": nc.gpsimd, "sp": nc.sync}
        self.sem = {}
        self.cnt = {}
        for e in ("pe", "dve", "act", "pool"):
            self.sem[e] = nc.alloc_semaphore(name="s_" + e)
            self.cnt[e] = 0
        self.seen = {e: {} for e in self.ENG}
        self.ninst = 0
        self.out_dma = []
        self.dma_keys = {}
        self.rr = {}
        self.banks = None
        self.bank_i = 0
        self.dcnt = {}
        self.free_dsems = []

    def sb(self, name, shape, dtype=F32):
        return Buf(name, self.nc.alloc_sbuf_tensor(name, list(shape), dtype))

    def ps(self, name, shape, dtype=F32):
        return Buf(name, self.nc.alloc_psum_tensor(name, list(shape), dtype))

    def dram(self, name, shape, dtype=F32, kind="Internal"):
        return Buf(name, self.nc.dram_tensor(name, list(shape), dtype, kind=kind))

    def bank(self):
        if self.banks is None:
            self.banks = [self.ps("bank%d" % i, [128, 512]) for i in range(8)]
        b = self.banks[self.bank_i % 8]
        self.bank_i += 1
        return b

    def pick(self, key, engs):
        i = self.rr.get(key, 0)
        self.rr[key] = i + 1
        return engs[i % len(engs)]

    def _need(self, e, deps):
        eng = self.eng[e]
        seen = self.seen[e]
        best = {}
        for d in deps:
            if d is None:
                continue
            k, v = d
            if best.get(k, 0) < v:
                best[k] = v
        for k, v in best.items():
            if seen.get(k, 0) >= v:
                continue
            if e == "pe" and k == "pe":
                continue
            seen[k] = v
            eng.wait_ge(self.sem[k], v)

    @staticmethod
    def _deps(reads, writes, skip_dma_waw=False):
        deps = []
        for b in reads:
            deps.append(b.lw)
        for b in writes:
            if not (skip_dma_waw and b.lw is not None and isinstance(b.lw[0], tuple)):
                deps.append(b.lw)
            deps.extend(b.rd)
        return deps

    def op(self, e, fn, reads=(), writes=()):
        self._need(e, self._deps(reads, writes))
        ins = fn(self.eng[e])
        self.cnt[e] += 1
        v = self.cnt[e]
        ins.then_inc(self.sem[e], 1)
        for b in writes:
            b.lw = (e, v)
            b.rd = []
        for b in reads:
            if b not in writes:
                b.rd.append((e, v))
        self.ninst += 1
        return ins

    def dma(self, q, out, in_, reads=(), writes=(), part=False, is_output=False):
        self._need(q, self._deps(reads, writes, skip_dma_waw=part))
        ins = self.eng[q].dma_start(out=out, in_=in_)
        if writes:
            b = writes[0]
            if b.wsem is None:
                b.wsem = self._dsem()
            key = b.wsem
        else:
            b = reads[0]
            if b.rsem is None:
                b.rsem = self._dsem()
            key = b.rsem
        self.dcnt[key] += 16
        v = self.dcnt[key]
        ins.then_inc(self.sem[key], 16)
        self.dma_keys[key] = v
        for w in writes:
            w.lw = (key, v)
            w.rd = []
        for r in reads:
            r.rd.append((key, v))
        if is_output:
            self.out_dma.append((key, v))
        self.ninst += 1
        return ins

    def _dsem(self):
        if self.free_dsems:
            return self.free_dsems.pop()
        key = ("d", len(self.dcnt))
        self.sem[key] = self.nc.alloc_semaphore(name="dsem%d" % len(self.dcnt))
        self.dcnt[key] = 0
        return key

    def barrier(self):
        deps = [(e, self.cnt[e]) for e in ("pe", "dve", "act", "pool") if self.cnt[e]]
        deps += list(self.dma_keys.items())
        for e in self.ENG:
            self._need(e, deps)

    def finish(self):
        self.barrier()


class Arena:
    def __init__(self, fw, nwords):
        self.fw = fw
        self.nwords = nwords
        self.t32 = fw.nc.alloc_sbuf_tensor("arena", [128, nwords], F32)
        self.t16 = self.t32.bitcast(BF16)
        self.off = 0
        self.peak = 0
        self.made = []

    def reset(self):
        self.fw.barrier()
        self.off = 0
        for b in self.made:
            for k in (b.wsem, b.rsem):
                if k is not None:
                    self.fw.free_dsems.append(k)
            b.wsem = b.rsem = None
        self.made = []

    def get(self, name, shape, dtype=F32):
        n = int(np.prod(shape[1:]))
        words = n if dtype == F32 else (n + 1) // 2
        assert self.off + words <= self.nwords, (name, self.off, words, self.nwords)
        if dtype == F32:
            ap = self.t32[:, self.off:self.off + n]
        else:
            ap = self.t16[:, 2 * self.off:2 * self.off + n]
        if len(shape) > 2:
            names = "abcdefg"[:len(shape) - 1]
            kw = {names[i]: shape[1 + i] for i in range(len(shape) - 2)}
            ap = ap.rearrange("p (%s) -> p %s" % (" ".join(names), " ".join(names)), **kw)
        if shape[0] < 128:
            ap = ap[0:shape[0]]
        self.off += words
        self.peak = max(self.peak, self.off)
        b = Buf(name, ap)
        self.made.append(b)
        return b

    def pool(self, name, n, shape, dtype=F32):
        return Rot([self.get("%s%d" % (name, i), shape, dtype) for i in range(n)])


class Rot:
    def __init__(self, bufs):
        self.bufs = bufs
        self.i = 0

    def next(self):
        b = self.bufs[self.i % len(self.bufs)]
        self.i += 1
        return b


class ColTable:
    def __init__(self):
        self.idx = {}
        self.cols = []
        self.n = 0

    def add(self, name, per_layer_vecs):
        k = len(per_layer_vecs[0]) // 128
        self.idx[name] = self.n
        self.cols.append(np.stack([np.asarray(v, np.float32).reshape(k, 128) for v in per_layer_vecs]))
        self.n += k

    def build(self):
        a = np.concatenate(self.cols, axis=1)
        return np.ascontiguousarray(a.transpose(2, 0, 1))


def _col_names():
    ct = ColTable()
    z = lambda n: [np.zeros(n, np.float32)] * DEPTH
    for nm, n in COL_SPEC:
        ct.add(nm, z(n))
    return ct.idx, ct.n


COL_SPEC = [("norm_g", 4096), ("mod_b", 6144),
            ("mu_rkv0", 768), ("mu_rkv1", 768), ("mu_lora0", 128), ("mu_lora1", 128),
            ("w0_0", 256), ("w0_1", 256), ("a0_0", 256), ("a0_1", 256),
            ("kk", 256), ("ka", 256), ("rk", 256), ("lnx_w", 256), ("lnx_b", 256),
            ("lbl0", 1024), ("lbl1", 1024), ("hg_norm", 256), ("sink", 512)]
COLI, NCOL = _col_names()


def _build_cols(inp):
    ct = ColTable()
    L = range(DEPTH)
    g = lambda k: np.asarray(inp[k], np.float32)
    vals = {
        "norm_g": [g("norm_g")[l].reshape(-1) for l in L],
        "mod_b": [g("mod_b")[l] for l in L],
        "mu_rkv0": [g("rw_mu_rkv")[l, 0].reshape(-1) for l in L],
        "mu_rkv1": [g("rw_mu_rkv")[l, 1].reshape(-1) for l in L],
        "mu_lora0": [g("rw_mu_lora")[l, 0].reshape(-1) for l in L],
        "mu_lora1": [g("rw_mu_lora")[l, 1].reshape(-1) for l in L],
        "w0_0": [g("rw_w0")[l, 0] for l in L], "w0_1": [g("rw_w0")[l, 1] for l in L],
        "a0_0": [g("rw_a0")[l, 0] for l in L], "a0_1": [g("rw_a0")[l, 1] for l in L],
        "kk": [g("rw_kk")[l] for l in L], "ka": [g("rw_ka")[l] for l in L],
        "rk": [g("rw_rk")[l].reshape(-1) for l in L],
        "lnx_w": [g("rw_lnx_w")[l] for l in L], "lnx_b": [g("rw_lnx_b")[l] for l in L],
        "lbl0": [g("hg_lb_logits")[0].reshape(-1) for l in L],
        "lbl1": [g("hg_lb_logits")[1].reshape(-1) for l in L],
        "hg_norm": [g("hg_norm")[l] for l in L],
        "sink": [np.repeat(g("swa_sink")[l], 128) for l in L],
    }
    for nm, n in COL_SPEC:
        assert len(vals[nm][0]) == n, nm
        ct.add(nm, vals[nm])
    return ct.build()


CST_SPEC = [("ident", 128), ("ones", 128), ("blk", 128),
            ("g4_0", 512), ("g4_1", 512), ("mb_0", 128), ("mb_1", 128),
            ("reset", 256), ("natmask", 64),
            ("mprev", 128), ("mnext", 128), ("flip", 64)]
CSTI = {}
_o = 0
for _n, _w in CST_SPEC:
    CSTI[_n] = (_o, _w)
    _o += _w
NCST = _o


def _build_consts():
    c = np.zeros((128, NCST), np.float32)

    def put(nm, a):
        o, w = CSTI[nm]
        c[:a.shape[0], o:o + w] = a
    i = np.arange(128)
    put("ident", np.eye(128))
    put("ones", np.ones((128, 128)))
    put("blk", ((i[:, None] // 64) == (i[None, :] // 64)).astype(np.float32))
    for d in (0, 1):
        if d == 0:
            strictT = (i[:, None] < i[None, :]).astype(np.float32)
            inclT = (i[:, None] <= i[None, :]).astype(np.float32)
        else:
            strictT = (i[:, None] > i[None, :]).astype(np.float32)
            inclT = (i[:, None] >= i[None, :]).astype(np.float32)
        put("g4_%d" % d, np.concatenate([-strictT, inclT, strictT, inclT], axis=1))
        put("mb_%d" % d, -strictT.T)
    r = np.ones((128, 256), np.float32)
    r[:, 0] = 0
    r[:, 128] = 0
    put("reset", r)
    t = np.arange(1024)
    row = (t // 64).astype(np.float32)
    col = (t % 64).astype(np.float32)
    inv = (1.0 / (10000.0 ** (np.arange(16, dtype=np.float32) / 16))).astype(np.float32)
    C = np.zeros((64, 1024), np.float32)
    S = np.zeros((64, 1024), np.float32)
    for base, pos in ((0, row), (32, col)):
        ang = (pos[None, :] * inv[:, None]).astype(np.float32)
        C[base:base + 16] = np.cos(ang)
        C[base + 16:base + 32] = np.cos(ang)
        S[base:base + 16] = -np.sin(ang)
        S[base + 16:base + 32] = np.sin(ang)
    rope = np.stack([np.concatenate([C, C], axis=0), np.concatenate([S, S], axis=0)], axis=1)
    kc = np.arange(64)[:, None]
    qc = np.arange(64)[None, :]
    ws = np.clip(qc - 8, 0, 48)
    put("natmask", ((kc >= ws) & (kc < ws + 16)).astype(np.float32))
    put("mprev", (i[:, None] >= i[None, :]).astype(np.float32))
    put("mnext", (i[:, None] <= i[None, :]).astype(np.float32))
    put("flip", np.eye(64)[::-1].copy())
    return c, np.ascontiguousarray(rope.astype(np.float32))


ROPE_PERM = np.concatenate([np.arange(16, 32), np.arange(0, 16), np.arange(48, 64), np.arange(32, 48)])


class Prog:
    def __init__(self, debug=(), layers=DEPTH, groups=("P", "S")):
        self.debug = set(debug)
        self.layers = layers
        self.groups = groups
        self.nc = bass.Bass("TRN2", target_bir_lowering=False)
        self.fw = FW(self.nc)
        self.dbg_outs = {}
        self.build()

    def mm(self, out, lhsT, rhs, start, stop, reads, writes):
        return self.fw.op("pe", lambda e: e.matmul(out, lhsT, rhs, start=start, stop=stop), reads, writes)

    def tp(self, out, in_, ident, reads, writes):
        return self.fw.op("pe", lambda e: e.transpose(out, in_, ident), reads, writes)

    def act(self, out, in_, func, reads, writes, bias=None, scale=1.0):
        if bias is None:
            return self.fw.op("act", lambda e: e.activation(out, in_, func, scale=scale), reads, writes)
        return self.fw.op("act", lambda e: e.activation(out, in_, func, bias=bias, scale=scale), reads, writes)

    def tt(self, eng, out, in0, in1, op, reads, writes):
        return self.fw.op(eng, lambda e: e.tensor_tensor(out, in0, in1, op), reads, writes)

    def ts(self, eng, out, in0, s1, s2, op0, op1, reads, writes):
        if s2 is None:
            return self.fw.op(eng, lambda e: e.tensor_scalar(out, in0, s1, None, op0), reads, writes)
        return self.fw.op(eng, lambda e: e.tensor_scalar(out, in0, s1, s2, op0, op1), reads, writes)

    def stt(self, eng, out, in0, scalar, in1, op0, op1, reads, writes):
        eng = "dve"
        return self.fw.op(eng, lambda e: e.scalar_tensor_tensor(out, in0, scalar, in1, op0, op1), reads, writes)

    def rsqrt(self, out, in_, addc, reads, wbuf):
        self.act(out, in_, AF.Ln, reads + [self.epsb], [wbuf], bias=self.epsc(addc))
        self.act(out, out, AF.Exp, [wbuf], [wbuf], scale=-0.5)

    def epsc(self, v):
        i = self.eps_vals.index(v)
        return self.epsb[:, i:i + 1]

    def cp(self, eng, out, in_, reads, writes):
        if eng == "act":
            return self.fw.op("act", lambda e: e.copy(out, in_), reads, writes)
        return self.fw.op(eng, lambda e: e.tensor_copy(out, in_), reads, writes)

    def ev(self):
        return self.fw.pick("ev", ("dve", "act"))

    def el(self):
        return self.fw.pick("el", ("dve", "pool"))

    def dump(self, name, buf, ap, shape, dtype=F32):
        if name not in self.debug:
            return
        o = self.fw.dram("dbg_" + name, list(shape), dtype, kind="ExternalOutput")
        self.dbg_outs[name] = o
        self.fw.dma("sp", o[:], ap, reads=[buf], is_output=True)

    def col(self, l, name, j=0, n=1):
        i = COLI[name] + j
        return self.cols[:, l, i:i + n]

    def cst(self, name, lo=0, hi=None, rows=128):
        o, w = CSTI[name]
        hi = w if hi is None else hi
        return self.cstb[0:rows, o + lo:o + hi]

    def load_block(self, src_ap, a, b, cast=True):
        fw = self.fw
        n = a * b
        st = self.WST.next()
        sview = st[:, 0:n].rearrange("p (a b) -> p a b", a=a)
        fw.dma("sp", sview, src_ap, writes=[st])
        self.mod_tick()
        if not cast:
            return st, sview
        wb = self.WBF.next()
        wview = wb[:, 0:n].rearrange("p (a b) -> p a b", a=a)
        self.cp(self.fw.pick("cast", self.cast_engs), wb[:, 0:n], st[:, 0:n], [st], [wb])
        return wb, wview

    def win_block(self, l, c0, ncol, src=None):
        src = self.w_in if src is None else src
        ap = src[l, :, c0:c0 + ncol].rearrange("(k p) n -> p k n", p=128)
        return self.load_block(ap, 8, ncol)

    def mod_tick(self, force=False):
        if not self.mod_pending:
            return
        if self.in_mod:
            return
        self.in_mod = True
        l, kc, third = self.mod_pending.pop(0)
        fw = self.fw
        st = self.WST.next()
        fw.dma("sp", st[:, 0:2048], self.mod_w[l, kc * 128:(kc + 1) * 128, third * 2048:(third + 1) * 2048], writes=[st])
        bk = fw.bank()
        for j in range(16):
            self.mm(bk[:, 2 * j:2 * j + 2], st[:, j * 128:(j + 1) * 128], self.silu[:, kc, :], True, True, [st, self.silu], [bk])
        dst = self.modacc[:, l, third * 16:(third + 1) * 16, :]
        src = bk[:, 0:32].rearrange("p (a b) -> p a b", b=2)
        if kc == 0:
            self.cp("dve", dst, src, [bk], [self.modacc])
        else:
            self.tt("dve", dst, dst, src, ALU.add, [bk, self.modacc], [self.modacc])
        self.in_mod = False

    def mod_flush(self, l):
        while self.mod_pending and self.mod_pending[0][0] <= l:
            self.mod_tick()

    def mod_finalize(self, l, g):
        fw = self.fw
        mc = self.modc
        m = lambda i: self.modacc[:, l, i * 8:(i + 1) * 8, g]
        mb = lambda i: self.col(l, "mod_b", i * 8, 8)
        ng = lambda i: self.col(l, "norm_g", i * 8, 8)
        R = [self.modacc, self.cols, mc]
        for i in range(6):
            self.tt("dve", mc[:, i, :], m(i), mb(i), ALU.add, R, [mc])
        self.ts("dve", mc[:, 6, :], mc[:, 1, :], 1.0, 32.0, ALU.add, ALU.mult, R, [mc])
        self.tt("dve", mc[:, 6, :], mc[:, 6, :], ng(0), ALU.mult, R, [mc])
        self.ts("dve", mc[:, 7, :], mc[:, 2, :], 32.0, None, ALU.mult, None, R, [mc])
        self.tt("dve", mc[:, 7, :], mc[:, 7, :], ng(1), ALU.mult, R, [mc])
        self.ts("dve", mc[:, 8, :], mc[:, 4, :], 1.0, 32.0, ALU.add, ALU.mult, R, [mc])
        self.tt("dve", mc[:, 8, :], mc[:, 8, :], ng(2), ALU.mult, R, [mc])
        self.ts("dve", mc[:, 9, :], mc[:, 5, :], 32.0, None, ALU.mult, None, R, [mc])
        self.tt("dve", mc[:, 9, :], mc[:, 9, :], ng(3), ALU.mult, R, [mc])
        return dict(shift1=mc[:, 0, :], coefA1=mc[:, 6, :], gco1=mc[:, 7, :], shift2=mc[:, 3, :], coefA2=mc[:, 8, :], gco2=mc[:, 9, :])

    def rstd_of(self, src_bufs, src_ap, sq, rstd):
        fw = self.fw
        bks = [fw.bank(), fw.bank()]
        for c in range(8):
            s = sq.next()
            self.act(s[:, :], src_ap(c), AF.Square, [src_bufs[c]], [s])
            for th in range(2):
                self.mm(bks[th][:, :], self.ones_bf[:, :], s[:, th * 512:(th + 1) * 512], c == 0, c == 7, [self.ones_bf, s], [bks[th]])
        for th in range(2):
            self.rsqrt(rstd[:, th * 512:(th + 1) * 512], bks[th][:, :], D * EPS, [bks[th]], rstd)
        return rstd

    def build(self):
        fw = self.fw
        nc = self.nc
        di = lambda n, s: fw.dram(n, s, F32, kind="ExternalInput")
        do = lambda n, s: fw.dram(n, s, F32, kind="ExternalOutput")
        self.xT = {"P": di("xT_p", [128, 8, T]), "S": di("xT_s", [128, 8, T])}
        self.yT = {"P": do("yT_p", [128, 8, T]), "S": do("yT_s", [128, 8, T])}
        condT = di("condT", [128, 8, 2])
        cols_d = di("cols", [128, DEPTH, NCOL])
        cst_d = di("cst", [128, NCST])
        NL = self.layers
        self.mod_w = di("mod_w", [NL, D, 6 * D])
        self.w_in = di("w_in", [NL, D, D_IN])
        self.w_in_sw = di("w_in_sw", [NL, D, 384])
        self.w_out = di("w_out", [NL, D, D])
        self.ffn_w1 = di("ffn_w1", [NL, D, 4 * D])
        self.ffn_w2 = di("ffn_w2", [NL, 4 * D, D])
        self.lora2 = di("lora2", [DEPTH, 128, 2, 256])
        self.rw_g2 = di("rw_g2", [DEPTH, 128, 256])
        self.rpbpad = di("rpbpad", [DEPTH, 4, 15, 127])
        self.rope = di("rope", [128, 2, T])
        self.cnk = di("cnk", [DEPTH, 2, 128, 256])
        self.cnv = di("cnv", [DEPTH, 256, 256])
        self.csk = di("csk", [DEPTH, 2, 128, 256])
        self.csv = di("csv", [DEPTH, 256, 128])
        self.srw = di("srw", [DEPTH, 2, 128, 2, 64])
        self.shg = di("shg", [DEPTH, 2, 128, 2, 64])
        self.natk = do("natk", [DEPTH, 2, 128, T])
        self.natv = do("natv", [DEPTH, NT, 128, 256])
        self.swak = do("swak", [DEPTH, 128, T])
        self.swav = do("swav", [DEPTH, NT, 128, 128])
        self.rwst = do("rwst", [DEPTH, 2, 4, 128, 2, 64])
        self.hgst = do("hgst", [DEPTH, 2, 4, 128, 2, 64])

        self.cstb = fw.sb("cstb", [128, NCST])
        self.cols = fw.sb("colsb", [128, DEPTH, NCOL])
        self.ident_bf = fw.sb("ident_bf", [128, 128], BF16)
        self.ones_bf = fw.sb("ones_bf", [128, 128], BF16)
        self.blk_bf = fw.sb("blk_bf", [128, 128], BF16)
        self.silu = fw.sb("silu", [128, 8, 2])
        self.modacc = fw.sb("modacc", [128, DEPTH, 48, 2])
        self.modc = fw.sb("modc", [128, 10, 8])
        self.lb = fw.sb("lb", [128, 2, DEPTH, 2])
        self.oml = fw.sb("oml", [128, 2, DEPTH, 2])
        self.X = [fw.sb("X%d" % c, [128, T]) for c in range(8)]
        self.H = [fw.sb("H%d" % c, [128, T], BF16) for c in range(8)]
        self.MIX = [fw.sb("MIX%d" % c, [128, T], BF16) for c in range(8)]
        self.WST = Rot([fw.sb("wst%d" % i, [128, 2048]) for i in range(2)])
        self.WBF = Rot([fw.sb("wbf%d" % i, [128, 2048], BF16) for i in range(2)])
        self.rstd = fw.sb("rstd", [128, T])
        self.sq = Rot([fw.sb("sq%d" % i, [128, T], BF16) for i in range(2)])
        self.arena = Arena(fw, (nc.sbuf_bytes_remaining - 2048) // 4)

        self.eps_vals = [D * EPS, 1e-24, 64 * LN_EPS, 64 * EPS]
        self.epsb = fw.sb("epsb", [128, 4])
        for i, v in enumerate(self.eps_vals):
            fw.op("pool", lambda e: e.memset(self.epsb[:, i:i + 1], v), [], [self.epsb])
        fw.dma("sp", self.cstb[:, :], cst_d[:, :], writes=[self.cstb])
        fw.dma("sp", self.cols[:, :, :], cols_d[:, :, :], writes=[self.cols])
        fw.dma("sp", self.silu[:, :, :], condT[:, :, :], writes=[self.silu])
        self.cp("dve", self.ident_bf[:, :], self.cst("ident"), [self.cstb], [self.ident_bf])
        self.cp("dve", self.ones_bf[:, :], self.cst("ones"), [self.cstb], [self.ones_bf])
        self.cp("dve", self.blk_bf[:, :], self.cst("blk"), [self.cstb], [self.blk_bf])
        self.act(self.silu[:, :, :], self.silu[:, :, :], AF.Silu, [self.silu], [self.silu])
        self.setup_lb()

        self.cast_engs = ("pool", "dve", "pool")
        self.mod_pending = [(l, kc, th) for l in range(self.layers) for kc in range(8) for th in range(3)]
        self.in_mod = False

        for G in self.groups:
            for c in range(8):
                fw.dma("sp", self.X[c][:, :], self.xT[G][:, c, :], writes=[self.X[c]])
            for l in range(self.layers):
                self.block(G, l)
            for c in range(8):
                fw.dma("sp", self.yT[G][:, c, :], self.X[c][:, :], reads=[self.X[c]], is_output=True)
        fw.finish()

    def setup_lb(self):
        E = self.fw.sb("lbE", [128, 2, DEPTH, 2])
        S = self.fw.sb("lbS", [128, 2, 2])
        R = [self.cols, E, S, self.lb, self.oml]
        for d in range(2):
            i = COLI["lbl%d" % d]
            self.act(E[:, d, :, :], self.cols[:, 0, i:i + 8].rearrange("p (a b) -> p a b", a=DEPTH), AF.Exp, R, [E])
            self.tt("dve", S[:, d, :], E[:, d, 0, :], E[:, d, 1, :], ALU.add, R, [S])
            self.tt("dve", S[:, d, :], S[:, d, :], E[:, d, 2, :], ALU.add, R, [S])
            self.tt("dve", S[:, d, :], S[:, d, :], E[:, d, 3, :], ALU.add, R, [S])
            self.fw.op("dve", lambda e: e.reciprocal(S[:, d, :], S[:, d, :]), R, [S])
            for l in range(DEPTH):
                self.tt("dve", E[:, d, l, :], E[:, d, l, :], S[:, d, :], ALU.mult, R, [E])
            self.fw.op("dve", lambda e: e.memset(self.lb[:, d, 0, :], 0.0), R, [self.lb])
            for l in range(1, DEPTH):
                self.tt("dve", self.lb[:, d, l, :], self.lb[:, d, l - 1, :], E[:, d, l, :], ALU.add, R, [self.lb])
        self.ts("dve", self.oml[:, :, :, :], self.lb[:, :, :, :], -1.0, 1.0, ALU.mult, ALU.add, R, [self.oml])

    def block(self, G, l):
        fw = self.fw
        g = 0 if G == "P" else 1
        self.G, self.l = G, l
        self.seqs = [(s * 2, 2) for s in range(4)] if G == "P" else [(0, 8)]
        self.mod_flush(l)
        mc = self.mod_finalize(l, g)
        self.mc = mc
        X, H = self.X, self.H
        rstd = self.rstd_of(X, lambda c: X[c][:, :], self.sq, self.rstd)
        for c in range(8):
            tmp = self.sq.next()
            e = self.el()
            self.stt(e, tmp[:, :], X[c][:, :], mc["coefA1"][:, c:c + 1], rstd[:, :], ALU.mult, ALU.mult, [X[c], rstd, self.modc], [tmp])
            self.act(H[c][:, :], tmp[:, :], AF.Identity, [tmp, self.modc], [H[c]], bias=mc["shift1"][:, c:c + 1])
        self.dump("h_%s%d" % (G, l), H[0], H[0][:, :], [128, T], BF16)
        self.mixer_A()
        self.mixer_B()
        self.mixer_C()
        self.mixer_D()
        for c in range(8):
            self.dump("mix%d_%s%d" % (c, G, l), self.MIX[c], self.MIX[c][:, :], [128, T], BF16)
        self.out_and_ffn()

    def proj_fm(self, wb, wv, nchunks, evac):
        for jj in range(nchunks):
            for th in range(2):
                bk = self.fw.bank()
                for kc in range(8):
                    self.mm(bk[:, :], wv[:, kc, jj * 128:(jj + 1) * 128], self.H[kc][:, th * 512:(th + 1) * 512],
                            kc == 0, kc == 7, [wb, self.H[kc]], [bk])
                evac(jj, th, bk)

    def proj_tm(self, wb, wv, ncol, evac, tile_tokens=128):
        nt = T // tile_tokens
        for ti in range(nt):
            bk = self.fw.bank()
            for kc in range(8):
                self.mm(bk[0:tile_tokens, 0:ncol], self.H[kc][:, ti * tile_tokens:(ti + 1) * tile_tokens], wv[:, kc, 0:ncol],
                        kc == 0, kc == 7, [wb, self.H[kc]], [bk])
            evac(ti, bk)

    def ctx_attn(self, QT, KT, ksel, Vt, vsel, mix0, PTp, RDp, esink=None):
        fw = self.fw

        def unit(slot, s, h):
            t0 = s * 256
            j, hp = h // 2, h % 2
            P = slice(hp * 64, hp * 64 + 64)
            PT = PTp.bufs[slot]
            rd = RDp.bufs[slot]
            bks = [fw.banks[slot * 4 + kc] for kc in range(2)]
            for kc in range(2):
                self.mm(bks[kc][:, 0:256], KT[P, ksel(h), t0 + kc * 128:t0 + (kc + 1) * 128], QT[P, j, t0:t0 + 256], True, True, [KT, QT], [bks[kc]])
            yield
            for kc in range(2):
                self.act(PT[:, kc, :], bks[kc][:, 0:256], AF.Exp, [bks[kc]], [PT], scale=0.125)
            yield
            bo = fw.banks[slot * 4 + 2]
            for kc in range(2):
                self.mm(bo[:, 0:256], Vt[:, s * 2 + kc, vsel(h), :], PT[:, kc, :], kc == 0, kc == 1, [Vt, PT], [bo])
            yield
            self.cp("dve", rd[0:64, :], bo[64:128, 0:256], [bo], [rd])
            if esink is not None:
                self.ts("dve", rd[0:64, :], rd[0:64, :], esink[0:64, h:h + 1], None, ALU.add, None, [rd, esink], [rd])
            self.fw.op("dve", lambda e: e.reciprocal(rd[0:64, :], rd[0:64, :]), [rd], [rd])
            self.tt("dve", self.MIX[mix0 + j][P, t0:t0 + 256], bo[0:64, 0:256], rd[0:64, :], ALU.mult, [bo, rd], [self.MIX[mix0 + j]])
        self.run_slots([(s_, h_) for s_ in range(4) for h_ in range(4)], unit, 2)

    def run_slots(self, items, unit, width):
        pend = list(items)
        active = {}
        while pend or active:
            for slot in range(width):
                if slot not in active and pend:
                    active[slot] = unit(slot, *pend.pop(0))
            for slot in list(active.keys()):
                try:
                    next(active[slot])
                except StopIteration:
                    del active[slot]

    def mixer_B(self):
        fw, l, G = self.fw, self.l, self.G
        ar = self.arena
        ar.reset()
        QT = ar.get("QT", [128, 2, T], BF16)
        KT = ar.get("KT", [128, 2, T], BF16)
        if G == "P":
            KTf = ar.get("KTf", [128, 2, T], F32)
            Vf = ar.get("Vf", [128, NT, 256], F32)
            Vt = ar.get("Vt", [128, NT, 4, 128], BF16)
            PTp = ar.pool("PT", 2, [128, 2, 256], BF16)
            RDp = ar.pool("RD", 2, [128, 256], F32)
            fw.op("pool", lambda e: e.memset(Vt[:, :, :, 64:128], 1.0), [], [Vt])
            wb, wv = self.win_block(l, 1152, 256)
            self.proj_fm(wb, wv, 2, lambda jj, th, bk: self.cp(self.ev(), QT[:, jj, th * 512:(th + 1) * 512], bk[:, :], [bk], [QT]))
            wb, wv = self.win_block(l, 1408, 256)

            def evk(jj, th, bk):
                self.cp(self.ev(), KTf[:, jj, th * 512:(th + 1) * 512], bk[:, :], [bk], [KTf])
                self.cp("pool", KT[:, jj, th * 512:(th + 1) * 512], KTf[:, jj, th * 512:(th + 1) * 512], [KTf], [KT])
            self.proj_fm(wb, wv, 2, evk)
            fw.dma("sp", self.natk[l].rearrange("j p t -> p j t"), KTf[:, :, :], reads=[KTf], is_output=True)
            wb, wv = self.win_block(l, 1664, 256)

            def evv(ti, bk):
                self.cp(self.ev(), Vf[:, ti, :], bk[:, 0:256], [bk], [Vf])
                self.cp("pool", Vt[:, ti, :, 0:64], Vf[:, ti, :].rearrange("p (h d) -> p h d", h=4), [Vf], [Vt])
            self.proj_tm(wb, wv, 256, evv)
            fw.dma("sp", self.natv[l].rearrange("n p f -> p n f"), Vf[:, :, :], reads=[Vf], is_output=True)
            self.ctx_attn(QT, KT, lambda h: h // 2, Vt, lambda h: h, 2, PTp, RDp)
        else:
            self.nat_latent(QT, KT)

    def mixer_D(self):
        fw, l, G = self.fw, self.l, self.G
        ar = self.arena
        ar.reset()
        QT = ar.get("QT", [128, 2, T], BF16)
        KT = ar.get("KTd", [128, 2, T], BF16)
        esink = ar.get("esink", [128, 4], F32)
        self.act(esink[:, :], self.col(l, "sink", 0, 4), AF.Exp, [self.cols], [esink])
        if G == "P":
            KTf = ar.get("KTf", [128, T], F32)
            Vf = ar.get("Vf", [128, NT, 128], F32)
            Vt = ar.get("Vt", [128, NT, 2, 128], BF16)
            PTp = ar.pool("PT", 2, [128, 2, 256], BF16)
            RDp = ar.pool("RD", 2, [128, 256], F32)
            fw.op("pool", lambda e: e.memset(Vt[:, :, :, 64:128], 1.0), [], [Vt])
            wb, wv = self.win_block(l, 3200, 256)
            self.proj_fm(wb, wv, 2, lambda jj, th, bk: self.cp(self.ev(), QT[:, jj, th * 512:(th + 1) * 512], bk[:, :], [bk], [QT]))
            wb, wv = self.win_block(l, 3456, 256)

            def evk(jj, th, bk):
                sl = slice(th * 512, (th + 1) * 512)
                self.cp(self.ev(), KTf[:, sl], bk[:, :], [bk], [KTf])
                for gg in range(2):
                    for hp in range(2):
                        self.cp(self.ev(), KT[hp * 64:hp * 64 + 64, gg, sl], KTf[gg * 64:gg * 64 + 64, sl], [KTf], [KT])
            self.proj_fm(wb, wv, 1, evk)
            fw.dma("sp", self.swak[l], KTf[:, :], reads=[KTf], is_output=True)

            def evv(ti, bk):
                self.cp(self.ev(), Vf[:, ti, :], bk[:, 0:128], [bk], [Vf])
                self.cp("pool", Vt[:, ti, :, 0:64], Vf[:, ti, :].rearrange("p (h d) -> p h d", h=2), [Vf], [Vt])
            self.proj_tm(wb, wv[:, :, 128:256], 128, evv)
            fw.dma("sp", self.swav[l].rearrange("n p f -> p n f"), Vf[:, :, :], reads=[Vf], is_output=True)
            self.ctx_attn(QT, KT, lambda h: h // 2, Vt, lambda h: h // 2, 6, PTp, RDp, esink=esink)
        else:
            self.swa_latent(QT, KT, esink)

    def lerp(self, d, out_ap, src_buf, src_ap, mu_ap, DIFF, out_buf):
        ns = len(self.seqs)
        L = T // ns
        s3 = src_ap.rearrange("p (s t) -> p s t", s=ns)
        d3 = DIFF[:, :].rearrange("p (s t) -> p s t", s=ns)
        e = self.el()
        if d == 0:
            self.tt(e, d3[:, :, 1:L], s3[:, :, 0:L - 1], s3[:, :, 1:L], ALU.subtract, [src_buf], [DIFF])
            self.ts(e, d3[:, :, 0:1], s3[:, :, 0:1], -1.0, None, ALU.mult, None, [src_buf], [DIFF])
        else:
            self.tt(e, d3[:, :, 0:L - 1], s3[:, :, 1:L], s3[:, :, 0:L - 1], ALU.subtract, [src_buf], [DIFF])
            self.ts(e, d3[:, :, L - 1:L], s3[:, :, L - 1:L], -1.0, None, ALU.mult, None, [src_buf], [DIFF])
        self.stt(e, out_ap, DIFF[:, :], mu_ap, src_ap, ALU.mult, ALU.add, [DIFF, src_buf, self.cols], [out_buf])

    def tile_order(self, d):
        out = []
        for si, (t0, n) in enumerate(self.seqs):
            tl = list(range(t0, t0 + n))
            if d == 1:
                tl = tl[::-1]
            for i, ti in enumerate(tl):
                out.append((si, ti, i == n - 1))
        return out

    def scan_tile(self, d, gs_ap, src_ap, reads, wbuf, C=128):
        rs = self.cst("reset", 0, C)
        if d == 0:
            self.fw.op("dve", lambda e: e.tensor_tensor_scan(gs_ap, rs, src_ap, 0.0, ALU.mult, ALU.add), reads + [self.cstb], [wbuf])
        else:
            self.fw.op("dve", lambda e: e.tensor_tensor_scan(gs_ap[:, ::-1], rs, src_ap[:, ::-1], 0.0, ALU.mult, ALU.add), reads + [self.cstb], [wbuf])

    def run_gens(self, gens, width=None):
        pend = list(gens)
        width = width or len(pend)
        active = []
        while pend or active:
            while pend and len(active) < width:
                active.append(pend.pop(0))
            nxt = []
            for g in active:
                try:
                    next(g)
                    nxt.append(g)
                except StopIteration:
                    pass
            active = nxt

    def lin_head(self, d, j, hp, t0, W, dplr, banks, C=128):
        fw = self.fw
        bi = [0]

        def nb():
            b = banks[bi[0] % len(banks)]
            bi[0] += 1
            return b
        P = slice(hp * 64, hp * 64 + 64)
        h = 2 * j + hp
        RK, KB, TOK, YACC = W["RK"], W["KB"], W["TOK"], W["YACC"]
        hst = W["hst"]
        HST = W["HST"]
        rt = RK[P, j, 1, 0:C]
        kb = KB[P, j, 0:C]
        Vh = TOK[0:C, 0, h * 64:(h + 1) * 64]
        Kbh = TOK[0:C, 1, h * 64:(h + 1) * 64]
        G4 = W["G4p"].next()
        g4c = lambda a, b: self.cst("g4_%d" % d, a, b)
        if dplr:
            PB = W["PB"]
            kt = RK[P, j, 0, :]
            rk2 = RK[P, j, :, :]
            pb = PB[P, j, :]
            Pbh = TOK[:, 2, h * 64:(h + 1) * 64]
            bk = nb()
            self.mm(bk[:, 0:256], pb, rk2, True, True, [PB, RK], [bk])
            self.mm(bk[:, 256:512], kb, rk2, True, True, [KB, RK], [bk])
            yield
            self.tt("dve", G4[:, :], bk[:, :], g4c(0, 512), ALU.mult, [bk, self.cstb], [G4])
            yield
            bk2 = nb()
            self.mm(bk2[:, 0:128], kt, pb, True, True, [RK, PB], [bk2])
            yield
            Bm = W["Bp"].next()
            self.tt("dve", Bm[:, :], bk2[:, 0:128], self.cst("mb_%d" % d), ALU.mult, [bk2, self.cstb], [Bm])
            CN = W["CNp"].next()
            self.tt("pool", CN[:, 128:256], G4[:, 0:128], self.cst("ident"), ALU.add, [G4, self.cstb], [CN])
            yield
            bk = nb()
            self.mm(bk[:, 0:128], Bm[:, :], G4[:, 0:128], True, True, [Bm, G4], [bk])
            yield
            self.cp("act", CN[:, 0:128], bk[:, 0:128], [bk], [CN])
            Cprev_buf, Cprev = G4, G4[:, 0:128]
            Nfin = None
            for lev in range(1, 7):
                bk = nb()
                self.mm(bk[:, 0:128], Cprev, Bm[:, :], True, True, [Cprev_buf, Bm], [bk])
                yield
                Bn = W["Bp"].next()
                self.cp("act", Bn[:, :], bk[:, 0:128], [bk], [Bn])
                Bm = Bn
                yield
                bk = nb()
                if lev < 6:
                    self.mm(bk[:, 0:256], Bm[:, :], CN[:, 0:256], True, True, [Bm, CN], [bk])
                    yield
                    CN2 = W["CNp"].next()
                    self.cp("act", CN2[:, 0:128], bk[:, 0:128], [bk], [CN2])
                    self.tt("dve", CN2[:, 128:256], bk[:, 128:256], CN[:, 128:256], ALU.add, [bk, CN], [CN2])
                    Cprev_buf, Cprev = CN, CN[:, 0:128]
                    CN = CN2
                    yield
                else:
                    self.mm(bk[:, 0:128], Bm[:, :], CN[:, 128:256], True, True, [Bm, CN], [bk])
                    yield
                    Nfin = W["Bp"].next()
                    self.tt("dve", Nfin[:, :], bk[:, 0:128], CN[:, 128:256], ALU.add, [bk, CN], [Nfin])
                    yield
            mqk = G4[:, 384:512]
        else:
            bk = nb()
            self.mm(bk[0:C, 0:C], kb, rt, True, True, [KB, RK], [bk])
            yield
            self.tt("dve", G4[0:C, 0:C], bk[0:C, 0:C], self.cst("g4_%d" % d, 384, 384 + C, rows=C), ALU.mult, [bk, self.cstb], [G4])
            mqk = G4[0:C, 0:C]
        Hp = W["Hpp"].next()
        Hb = W["Hbp"].next()
        self.act(Hb[P, :], hst, AF.Identity, [HST, W["expcb"]], [Hb], scale=W["expc"][P, j:j + 1])
        self.act(Hp[P, :], hst, AF.Identity, [HST, W["expcb"]], [Hp], scale=W["expc"][P, j:j + 1])
        yield
        if dplr:
            bx = nb()
            self.mm(bx[:, 0:64], G4[:, 256:384], Vh, True, False, [G4, TOK], [bx])
            self.mm(bx[:, 0:64], kt, Hb[P, :], False, True, [RK, Hb], [bx])
            yield
            Xs = W["Xp"].next()
            self.cp("act", Xs[:, :], bx[:, 0:64], [bx], [Xs])
            yield
            bu = nb()
            self.mm(bu[:, 0:64], Nfin[:, :], Xs[:, :], True, True, [Nfin, Xs], [bu])
            yield
            Un = W["Xp"].next()
            self.act(Un[:, :], bu[:, 0:64], AF.Identity, [bu], [Un], scale=-1.0)
            yield
        by = nb()
        self.mm(by[P, 0:C], Hb[P, :], rt, True, False, [Hb, RK], [by])
        self.mm(by[P, 0:C], Vh, mqk, False, not dplr, [TOK, G4], [by])
        if dplr:
            self.mm(by[P, 0:C], Un[:, :], G4[:, 128:256], False, True, [Un, G4], [by])
        self.mm(by[P, 128:192], Kbh, Vh, True, not dplr, [TOK], [by])
        if dplr:
            self.mm(by[P, 128:192], Pbh, Un[:, :], False, True, [TOK, Un], [by])
        yield
        ydst = YACC[P, j, t0:t0 + C]
        self.tt("dve", ydst, by[P, 0:C], ydst, ALU.add, [by, YACC], [YACC])
        self.tt("dve", Hp[P, :], by[P, 128:192], Hp[P, :], ALU.add, [by, Hp], [Hp])
        yield
        self.act(hst, Hp[P, :], AF.Identity, [Hp, W["eplastb"]], [HST], scale=W["eplast"][P, j:j + 1])

    def lin_common_bufs(self, ar, dplr, njobs):
        Ws = []
        for k in range(njobs):
            W = {}
            if dplr:
                W["G4p"] = ar.pool("G4_%d" % k, 1, [128, 512], BF16)
                W["Bp"] = ar.pool("Bm_%d" % k, 3, [128, 128], BF16)
                W["CNp"] = ar.pool("CN_%d" % k, 3, [128, 256], BF16)
                W["Xp"] = ar.pool("Xs_%d" % k, 2, [128, 64], BF16)
            else:
                W["G4p"] = ar.pool("G4_%d" % k, 1, [128, 32], BF16)
            W["Hpp"] = ar.pool("Hp_%d" % k, 1, [128, 64], F32)
            W["Hbp"] = ar.pool("Hb_%d" % k, 1, [128, 64], BF16)
            Ws.append(W)
        return Ws

    def lin_state_io(self, HST, src_dram, dst_dram, load):
        l = self.l
        if load:
            self.fw.dma("sp", HST[:, :, 0, :, :], src_dram[l].rearrange("d p j v -> p d j v"), writes=[HST])
        else:
            self.fw.dma("sp", dst_dram[l].rearrange("d s p j v -> p d s j v"), HST[:, :, :, :, :], reads=[HST], is_output=True)

    def mixer_A(self):
        fw, l, G = self.fw, self.l, self.G
        ar = self.arena
        ar.reset()
        ns = len(self.seqs)
        RAW = ar.get("RAW", [128, 6, T], BF16)
        WA = ar.get("WA", [128, 2, T], F32)
        SG = ar.get("SG", [128, T], BF16)
        YACC = ar.get("YACC", [128, 2, T], F32)
        BACC = ar.get("BACC", [128, 2, T], BF16)
        HST = ar.get("HST", [128, 2, ns, 2, 64], F32)
        L2 = ar.get("L2", [128, 2, 256], F32)
        G2 = ar.get("G2", [128, 256], BF16)
        LR = ar.get("LR", [128, 6, T], BF16)
        DIFF = ar.get("DIFF", [128, T], F32)
        cc = ar.get("cc", [128, 8], F32)
        W4 = self.lin_common_bufs(ar, True, 4)
        fw.op("pool", lambda e: e.memset(YACC[:, :, :], 0.0), [], [YACC])
        wk = lambda nm, shape, dt=F32: ar.pool(nm, 2, shape, dt)
        THp, SIGp, Ap = wk("th", [128, 128]), wk("sig", [128, 2, 128]), wk("a", [128, 2, 128])
        T1p, T2p, T3p = wk("t1", [128, 128]), wk("t2", [128, 128]), ar.pool("t3", 1, [128, 128], F32)
        wk1 = lambda nm, shape, dt=F32: ar.pool(nm, 1, shape, dt)
        KAPp, KMp, PPp = wk1("kap", [128, 2, 128]), wk1("km", [128, 2, 128]), wk1("pp", [128, 2, 128])
        SQp = wk("sqb", [128, 128], BF16)
        GSp, EPp, EMp, EPPp = wk1("gs", [128, 2, 128]), wk1("ep", [128, 2, 128]), wk1("em", [128, 2, 128]), wk("epp", [128, 2, 128])
        EXCp = wk("exc", [128, 4])
        RKp, KBp, PBp = wk("RK", [128, 2, 2, 128], BF16), wk("KB", [128, 2, 128], BF16), wk("PB", [128, 2, 128], BF16)
        TOKp = wk("TOK", [128, 3, 256], BF16)

        fw.dma("sp", L2[:, :, :], self.lora2[l], writes=[L2])
        g2st = self.WST.next()
        fw.dma("sp", g2st[:, 0:256], self.rw_g2[l], writes=[g2st])
        self.cp("pool", G2[:, :], g2st[:, 0:256], [g2st], [G2])
        self.ts("dve", cc[:, 0:2], self.col(l, "ka", 0, 2), -1.0, 1.0, ALU.mult, ALU.add, [self.cols], [cc])
        self.ts("dve", cc[:, 2:4], self.col(l, "lnx_w", 0, 2), 8.0, None, ALU.mult, None, [self.cols], [cc])
        if G == "P":
            fw.op("pool", lambda e: e.memset(HST[:, :, :, :, :], 0.0), [], [HST])
        else:
            self.lin_state_io(HST, self.srw, None, True)

        def evA(c0):
            def f(jj, th, bk):
                ci = c0 + jj
                sl = slice(th * 512, (th + 1) * 512)
                if ci < 6:
                    self.cp(self.ev(), RAW[:, ci, sl], bk[:, :], [bk], [RAW])
                elif ci == 6:
                    self.act(SG[:, sl], bk[:, :], AF.Sigmoid, [bk], [SG])
                else:
                    self.cp(self.ev(), WA[:, ci - 7, sl], bk[:, :], [bk], [WA])
            return f
        for b0 in range(0, 9, 2):
            nch = min(2, 9 - b0)
            wb, wv = self.win_block(l, b0 * 128, nch * 128)
            self.proj_fm(wb, wv, nch, evA(b0))

        order = [(d, si, ti) for d in range(2) for (si, ti, last) in self.tile_order(d)]
        work = {}
        pbi = [0]

        def pbank():
            b = fw.banks[4 + pbi[0] % 4]
            pbi[0] += 1
            return b

        def lerps(d):
            for a in range(3):
                for jj in range(2):
                    ci = a * 2 + jj
                    self.lerp(d, LR[:, ci, :], RAW, RAW[:, ci, :], self.col(l, "mu_rkv%d" % d, ci, 1), DIFF, LR)
            self.lerp(d, WA[:, d, :], WA, WA[:, d, :], self.col(l, "mu_lora%d" % d, 0, 1), DIFF, WA)
            LW = WA

        def prepA(idx):
            d, si, ti = order[idx]
            if True:
                t0 = ti * 128
                ts_ = slice(t0, t0 + 128)
                th = THp.next()
                self.act(th[0:64, :], WA[0:64, d, ts_], AF.Tanh, [WA], [th])
                sig, av = SIGp.next(), Ap.next()
                for jj in range(2):
                    bk = pbank()
                    self.mm(bk[:, 0:128], L2[0:64, d, jj * 128:(jj + 1) * 128], th[0:64, :], True, True, [L2, th], [bk])
                    self.act(sig[:, jj, :], bk[:, 0:128], AF.Sigmoid, [bk, self.cols], [sig], bias=self.col(l, "w0_%d" % d, jj, 1))
                    bk = pbank()
                    self.mm(bk[:, 0:128], L2[64:128, d, jj * 128:(jj + 1) * 128], WA[64:128, d, ts_], True, True, [L2, WA], [bk])
                    self.act(av[:, jj, :], bk[:, 0:128], AF.Sigmoid, [bk, self.cols], [av], bias=self.col(l, "a0_%d" % d, jj, 1))
                yield
                kap, km, pp = KAPp.next(), KMp.next(), PPp.next()
                gs, ep, em, epp, exc = GSp.next(), EPp.next(), EMp.next(), EPPp.next(), EXCp.next()
                RK, KB, PB, TOK = RKp.next(), KBp.next(), PBp.next(), TOKp.next()
                mid = 63 if d == 0 else 64
                lastc = 127 if d == 0 else 0
                for jj in range(2):
                    rl, kl, vl = LR[:, jj, ts_], LR[:, 2 + jj, ts_], LR[:, 4 + jj, ts_]
                    e = self.el()
                    t1, t2 = T1p.next(), T2p.next()
                    self.ts(e, t1[:, :], kl, self.col(l, "kk", jj, 1), None, ALU.mult, None, [LR, self.cols], [t1])
                    sqb = SQp.next()
                    self.tt(e, sqb[:, :], t1[:, :], t1[:, :], ALU.mult, [t1], [sqb])
                    bk = pbank()
                    self.mm(bk[:, 0:128], self.blk_bf[:, :], sqb[:, :], True, True, [self.blk_bf, sqb], [bk])
                    self.rsqrt(t2[:, :], bk[:, 0:128], 1e-24, [bk], t2)
                    self.tt(e, kap[:, jj, :], t1[:, :], t2[:, :], ALU.mult, [t1, t2], [kap])
                    yield
                    t3 = T3p.next()
                    self.ts(e, t3[:, :], av[:, jj, :], self.col(l, "ka", jj, 1), cc[:, jj:jj + 1], ALU.mult, ALU.add, [av, self.cols, cc], [t3])
                    self.tt(e, km[:, jj, :], kl, t3[:, :], ALU.mult, [LR, t3], [km])
                    self.tt(e, pp[:, jj, :], kap[:, jj, :], av[:, jj, :], ALU.mult, [kap, av], [pp])
                    yield
                    t1b = T1p.next()
                    self.stt(e, t1b[:, :], rl, self.col(l, "rk", jj, 1), km[:, jj, :], ALU.mult, ALU.mult, [LR, self.cols, km], [t1b])
                    sqb2 = SQp.next()
                    self.cp(e, sqb2[:, :], t1b[:, :], [t1b], [sqb2])
                    bk = pbank()
                    self.mm(bk[:, 0:128], self.blk_bf[:, :], sqb2[:, :], True, True, [self.blk_bf, sqb2], [bk])
                    if d == 0:
                        self.tt("dve", BACC[:, jj, ts_], bk[:, 0:128], vl, ALU.mult, [bk, LR], [BACC])
                    else:
                        t2b = T2p.next()
                        self.tt("dve", t2b[:, :], bk[:, 0:128], vl, ALU.mult, [bk, LR], [t2b])
                        self.tt(e, BACC[:, jj, ts_], BACC[:, jj, ts_], t2b[:, :], ALU.add, [t2b, BACC], [BACC])
                    yield
                    self.scan_tile(d, gs[:, jj, :], sig[:, jj, :], [sig], gs)
                yield
                self.cp("dve", exc[:, 0:2], gs[:, :, mid], [gs], [exc])
                gc = EPPp.next()
                for jj in range(2):
                    self.ts("dve", gc[:, jj, :], gs[:, jj, :], exc[:, jj:jj + 1], None, ALU.subtract, None, [gs, exc], [gc])
                yield
                self.act(ep[:, :, :], gc[:, :, :], AF.Exp, [gc], [ep], scale=-DECAY)
                self.act(em[:, :, :], gc[:, :, :], AF.Exp, [gc], [em], scale=DECAY)
                self.tt(self.el(), gs[:, :, :], gc[:, :, :], sig[:, :, :], ALU.subtract, [gc, sig], [gs])
                self.act(epp[:, :, :], gs[:, :, :], AF.Exp, [gs], [epp], scale=-DECAY)
                self.act(exc[:, 0:2], exc[:, 0:2], AF.Exp, [exc], [exc], scale=-DECAY)
                yield
                self.cp("dve", exc[:, 2:4], ep[:, :, lastc], [ep], [exc])
                for jj in range(2):
                    e = self.el()
                    self.tt(e, RK[:, jj, 0, :], kap[:, jj, :], epp[:, jj, :], ALU.mult, [kap, epp], [RK])
                    self.tt(e, RK[:, jj, 1, :], LR[:, jj, ts_], ep[:, jj, :], ALU.mult, [LR, ep], [RK])
                    self.tt(e, KB[:, jj, :], km[:, jj, :], em[:, jj, :], ALU.mult, [km, em], [KB])
                    self.tt(e, PB[:, jj, :], pp[:, jj, :], em[:, jj, :], ALU.mult, [pp, em], [PB])
                yield
                bt = pbank()
                btv = bt.t.bitcast(BF16)
                for jj in range(2):
                    self.tp(btv[:, (0 * 2 + jj) * 128:(0 * 2 + jj + 1) * 128], LR[:, 4 + jj, ts_], self.ident_bf[:, :], [LR, self.ident_bf], [bt])
                    self.tp(btv[:, (1 * 2 + jj) * 128:(1 * 2 + jj + 1) * 128], KB[:, jj, :], self.ident_bf[:, :], [KB, self.ident_bf], [bt])
                    self.tp(btv[:, (2 * 2 + jj) * 128:(2 * 2 + jj + 1) * 128], PB[:, jj, :], self.ident_bf[:, :], [PB, self.ident_bf], [bt])
                yield
                self.cp("act", TOK[:, :, :], btv[:, 0:768].rearrange("p (a b) -> p a b", a=3), [bt], [TOK])
                yield
                work[idx] = (d, si, t0, dict(RK=RK, KB=KB, PB=PB, TOK=TOK, expc=exc, expcb=exc, eplast=exc[:, 2:4], eplastb=exc, YACC=YACC, HST=HST))

        def heads(idx):
            d, si, t0, shared = work[idx]
            gens = []
            for k, (jj, hp) in enumerate(((0, 0), (0, 1), (1, 0), (1, 1))):
                Wk = dict(W4[k])
                Wk.update(shared)
                Wk["hst"] = HST[hp * 64:hp * 64 + 64, d, si, jj, :]
                gens.append(self.lin_head(d, jj, hp, t0, Wk, True, [fw.banks[k]]))
            return gens

        def start_prep(idx):
            if idx == 0 or order[idx - 1][0] != order[idx][0]:
                lerps(order[idx][0])
            return prepA(idx)

        fw.barrier()
        self.run_gens([start_prep(0)])
        for idx in range(len(order)):
            gens = heads(idx)
            if idx + 1 < len(order):
                gens.append(start_prep(idx + 1))
            self.run_gens(gens)
        fw.barrier()
        if G == "P":
            self.lin_state_io(HST, None, self.rwst, False)
        self.dump("yacc_%s%d" % (G, l), YACC, YACC[:, :, :], [128, 2, T])
        self.dump("bacc_%s%d" % (G, l), BACC, BACC[:, :, :], [128, 2, T])
        blk = self.cst("blk")
        for jj in range(2):
            for th in range(2):
                sl = slice(th * 512, (th + 1) * 512)
                bk = fw.bank()
                self.mm(bk[:, :], blk, YACC[:, jj, sl], True, True, [self.cstb, YACC], [bk])
                yc = DIFF
                self.stt("dve", yc[:, sl], bk[:, :], -1.0 / 64, YACC[:, jj, sl], ALU.mult, ALU.add, [bk, YACC], [yc])
                self.tt("pool", WA[:, 0, sl], yc[:, sl], yc[:, sl], ALU.mult, [yc], [WA])
                bk2 = fw.bank()
                self.mm(bk2[:, :], blk, WA[:, 0, sl], True, True, [self.cstb, WA], [bk2])
                self.rsqrt(WA[:, 0, sl], bk2[:, :], 64 * LN_EPS, [bk2], WA)
                self.tt("pool", yc[:, sl], yc[:, sl], WA[:, 0, sl], ALU.mult, [yc, WA], [yc])
                self.ts("dve", yc[:, sl], yc[:, sl], cc[:, 2 + jj:3 + jj], self.col(l, "lnx_b", jj, 1), ALU.mult, ALU.add, [yc, cc, self.cols], [yc])
                self.tt("pool", yc[:, sl], yc[:, sl], BACC[:, jj, sl], ALU.add, [yc, BACC], [yc])
                bk3 = fw.bank()
                self.mm(bk3[:, :], G2[:, jj * 128:(jj + 1) * 128], SG[:, sl], True, True, [G2, SG], [bk3])
                self.tt("dve", self.MIX[jj][:, sl], yc[:, sl], bk3[:, :], ALU.mult, [yc, bk3], [self.MIX[jj]])

    def mixer_C(self):
        fw, l, G = self.fw, self.l, self.G
        ar = self.arena
        ar.reset()
        ns = len(self.seqs)
        Q = ar.get("Q", [128, 2, T], BF16)
        V = ar.get("V", [128, 2, T], BF16)
        GATE = ar.get("GATE", [128, 2, T], BF16)
        FR = ar.get("FR", [128, 2, 2, T], F32)
        YACC = ar.get("YACC", [128, 2, T], F32)
        HST = ar.get("HST", [128, 2, ns, 2, 64], F32)
        LFd = [ar.get("LF%d" % d, [128, 2, T], F32) for d in range(2)]
        KKd = [ar.get("KK%d" % d, [128, 2, T], BF16) for d in range(2)]
        LF = LFd[0]
        W8 = self.lin_common_bufs(ar, False, 8)
        fw.op("pool", lambda e: e.memset(YACC[:, :, :], 0.0), [], [YACC])
        hn8 = ar.get("hn8", [128, 2], F32)
        self.ts("dve", hn8[:, :], self.col(l, "hg_norm", 0, 2), 8.0, None, ALU.mult, None, [self.cols], [hn8])
        C = 32
        wk = lambda nm, shape, dt=F32: [ar.pool("%s_%d" % (nm, d), 2, shape, dt) for d in range(2)]
        GSp, GCp, EPp, EMp, EXCp = wk("gs", [128, 2, C]), wk("gc", [128, 2, C]), wk("ep", [128, 2, C]), wk("em", [128, 2, C]), wk("exc", [128, 2])
        RKp, KBp, TOKp = wk("RK", [128, 2, 2, C], BF16), wk("KB", [128, 2, C], BF16), wk("TOK", [128, 2, 256], BF16)
        if G == "P":
            fw.op("pool", lambda e: e.memset(HST[:, :, :, :, :], 0.0), [], [HST])
        else:
            self.lin_state_io(HST, self.shg, None, True)

        def evC(c0):
            def f(jj, th, bk):
                ci = c0 + jj
                sl = slice(th * 512, (th + 1) * 512)
                if ci < 2:
                    self.act(Q[:, ci, sl], bk[:, :], AF.Silu, [bk], [Q])
                elif ci < 4:
                    self.cp(self.ev(), V[:, ci - 2, sl], bk[:, :], [bk], [V])
                elif ci < 6:
                    self.act(GATE[:, ci - 4, sl], bk[:, :], AF.Sigmoid, [bk], [GATE])
                else:
                    self.cp(self.ev(), FR[:, (ci - 6) // 2, (ci - 6) % 2, sl], bk[:, :], [bk], [FR])
            return f
        for b0 in range(0, 10, 2):
            wb, wv = self.win_block(l, 1920 + b0 * 128, 256)
            self.proj_fm(wb, wv, 2, evC(b0))

        for d in range(2):
            for jj in range(2):
                self.act(LFd[d][:, jj, :], FR[:, d, jj, :], AF.Sigmoid, [FR], [LFd[d]])
                self.ts("dve", LFd[d][:, jj, :], LFd[d][:, jj, :], self.oml[:, d, l, jj:jj + 1], self.lb[:, d, l, jj:jj + 1], ALU.mult, ALU.add, [LFd[d], self.oml, self.lb], [LFd[d]])
                self.ts("pool", KKd[d][:, jj, :], LFd[d][:, jj, :], -1.0, 1.0, ALU.mult, ALU.add, [LFd[d]], [KKd[d]])
        for d in range(2):
            for jj in range(2):
                self.act(LFd[d][:, jj, :], LFd[d][:, jj, :], AF.Ln, [LFd[d]], [LFd[d]])
        orders = []
        for d in range(2):
            o = []
            for (si, ti, last) in self.tile_order(d):
                for sub in (range(128 // C) if d == 0 else range(128 // C - 1, -1, -1)):
                    o.append((si, ti * 128 + sub * C))
            orders.append(o)
        import os
        fw.barrier()
        hbanks = [Buf("hbank%d" % k, fw.banks[k // 2].t[:, (k % 2) * 256:(k % 2 + 1) * 256]) for k in range(8)]
        nsteps = len(orders[0])
        work = {}

        def prep(i):
            par = i % 2
            wl = []
            for d in range(2):
                si, t0 = orders[d][i]
                ts_ = slice(t0, t0 + C)
                gs, gc, ep, em, exc = GSp[d].bufs[par], GCp[d].bufs[par], EPp[d].bufs[par], EMp[d].bufs[par], EXCp[d].bufs[par]
                RK, KB, TOK = RKp[d].bufs[par], KBp[d].bufs[par], TOKp[d].bufs[par]
                mid = C // 2 - 1 if d == 0 else C // 2
                lastc = C - 1 if d == 0 else 0
                for jj in range(2):
                    self.scan_tile(d, gs[:, jj, 0:C], LFd[d][:, jj, ts_], [LFd[d]], gs, C)
                yield
                self.cp("dve", exc[:, 0:2], gs[:, :, mid], [gs], [exc])
                yield
                for jj in range(2):
                    self.ts("dve", gc[:, jj, 0:C], gs[:, jj, 0:C], exc[:, jj:jj + 1], None, ALU.subtract, None, [gs, exc], [gc])
                yield
                self.act(ep[:, :, 0:C], gc[:, :, 0:C], AF.Exp, [gc], [ep])
                self.act(em[:, :, 0:C], gc[:, :, 0:C], AF.Exp, [gc], [em], scale=-1.0)
                self.act(exc[:, 0:2], exc[:, 0:2], AF.Exp, [exc], [exc])
                yield
                for jj in range(2):
                    e = self.el()
                    self.tt(e, RK[:, jj, 1, 0:C], Q[:, jj, ts_], ep[:, jj, 0:C], ALU.mult, [Q, ep], [RK])
                    self.tt(e, KB[:, jj, 0:C], KKd[d][:, jj, ts_], em[:, jj, 0:C], ALU.mult, [KKd[d], em], [KB])
                yield
                wl.append((d, si, t0, ts_, dict(RK=RK, KB=KB, TOK=TOK, expc=exc, expcb=exc, eplast=ep[:, :, lastc], eplastb=ep, YACC=YACC, HST=HST)))
            work[i] = wl

        def prepB(i):
            for (d, si, t0, ts_, sh) in work[i]:
                KB, TOK = sh["KB"], sh["TOK"]
                bt = fw.bank()
                btv = bt.t.bitcast(BF16)
                for jj in range(2):
                    self.tp(btv[0:C, jj * 128:(jj + 1) * 128], V[:, jj, ts_], self.ident_bf[:, :], [V, self.ident_bf], [bt])
                    self.tp(btv[0:C, (2 + jj) * 128:(3 + jj) * 128], KB[:, jj, 0:C], self.ident_bf[:, :], [KB, self.ident_bf], [bt])
                self.cp("act", TOK[0:C, :, :], btv[0:C, 0:512].rearrange("p (a b) -> p a b", a=2), [bt], [TOK])

        def heads(i):
            gens = []
            for (d, si, t0, ts_, shared) in work[i]:
                for k, (jj, hp) in enumerate(((0, 0), (0, 1), (1, 0), (1, 1))):
                    Wk = dict(W8[d * 4 + k])
                    Wk.update(shared)
                    Wk["hst"] = HST[hp * 64:hp * 64 + 64, d, si, jj, :]
                    gens.append(self.lin_head(d, jj, hp, t0, Wk, False, [fw.banks[d * 4 + k]], C))
            return gens

        import os
        self.run_gens([prep(0)])
        prepB(0)
        for i in range(nsteps):
            gens = heads(i)
            if i + 1 < nsteps:
                gens.append(prep(i + 1))
            self.run_gens(gens)
            if i + 1 < nsteps:
                prepB(i + 1)
        fw.barrier()
        if G == "P":
            self.lin_state_io(HST, None, self.hgst, False)
        self.dump("yaccC_%s%d" % (G, l), YACC, YACC[:, :, :], [128, 2, T])
        self.dump("lfC_%s%d" % (G, l), LF, LF[:, :, :], [128, 2, T])
        blk = self.cst("blk")
        for jj in range(2):
            for th in range(2):
                sl = slice(th * 512, (th + 1) * 512)
                self.tt("pool", LF[:, 0, sl], YACC[:, jj, sl], YACC[:, jj, sl], ALU.mult, [YACC], [LF])
                bk = fw.bank()
                self.mm(bk[:, :], blk, LF[:, 0, sl], True, True, [self.cstb, LF], [bk])
                self.rsqrt(LF[:, 0, sl], bk[:, :], 64 * EPS, [bk], LF)
                self.tt("pool", LF[:, 0, sl], LF[:, 0, sl], YACC[:, jj, sl], ALU.mult, [LF, YACC], [LF])
                self.stt("dve", self.MIX[4 + jj][:, sl], LF[:, 0, sl], hn8[:, jj:jj + 1], GATE[:, jj, sl], ALU.mult, ALU.mult, [LF, hn8, GATE], [self.MIX[4 + jj]])

    def out_and_ffn(self):
        fw, l, G = self.fw, self.l, self.G
        ar = self.arena
        ar.reset()
        mc = self.mc
        X, H, MIX = self.X, self.H, self.MIX
        O = [ar.get("O%d" % c, [128, T], F32) for c in range(8)]
        WBIG = ar.get("WBIG", [128, 8, 1024], BF16)
        HID = [ar.get("HID%d" % c, [128, T], BF16) for c in range(8)]
        RL = ar.get("RL", [128, T], BF16)
        WST0, WBF0 = self.WST, self.WBF
        self.WST = Rot(WST0.bufs + [ar.get("wstx%d" % i, [128, 2048], F32) for i in range(2)])
        self.WBF = Rot(WBF0.bufs + [ar.get("wbfx%d" % i, [128, 2048], BF16) for i in range(2)])
        self.cast_engs = ("pool", "dve", "act")

        def add_residual(gco):
            rstd = self.rstd_of(O, lambda c: O[c][:, :], self.sq, self.rstd)
            for c in range(8):
                e = self.el()
                self.stt(e, O[c][:, :], O[c][:, :], gco[:, c:c + 1], rstd[:, :], ALU.mult, ALU.mult, [O[c], rstd, self.modc], [O[c]])
                self.tt(e, X[c][:, :], X[c][:, :], O[c][:, :], ALU.add, [O[c], X[c]], [X[c]])

        for q in range(4):
            st_ap = self.w_out[l, q * 256:(q + 1) * 256, :].rearrange("(k p) n -> p k n", p=128)
            stb, sv = self.load_block(st_ap, 2, 1024, cast=False)
            self.cp(self.fw.pick("cast", self.cast_engs), WBIG[:, 2 * q:2 * q + 2, :], sv, [stb], [WBIG])
        for oc in range(8):
            for th in range(2):
                bk = fw.bank()
                for kc in range(8):
                    self.mm(bk[:, :], WBIG[:, kc, oc * 128:(oc + 1) * 128], MIX[kc][:, th * 512:(th + 1) * 512], kc == 0, kc == 7, [WBIG, MIX[kc]], [bk])
                self.cp(self.ev(), O[oc][:, th * 512:(th + 1) * 512], bk[:, :], [bk], [O[oc]])
        add_residual(mc["gco1"])
        rstd = self.rstd_of(X, lambda c: X[c][:, :], self.sq, self.rstd)
        for c in range(8):
            tmp = self.sq.next()
            self.stt(self.el(), tmp[:, :], X[c][:, :], mc["coefA2"][:, c:c + 1], rstd[:, :], ALU.mult, ALU.mult, [X[c], rstd, self.modc], [tmp])
            self.act(H[c][:, :], tmp[:, :], AF.Identity, [tmp, self.modc], [H[c]], bias=mc["shift2"][:, c:c + 1])
        for q in range(4):
            for fb in range(4):
                wb, wv = self.load_block(self.ffn_w1[l, :, (q * 8 + fb * 2) * 128:(q * 8 + fb * 2 + 2) * 128].rearrange("(k p) n -> p k n", p=128), 8, 256)

                def evh(jj, th, bk, fb=fb):
                    sl = slice(th * 512, (th + 1) * 512)
                    self.act(RL[:, sl], bk[:, :], AF.Relu, [bk], [RL])
                    self.tt(self.el(), HID[fb * 2 + jj][:, sl], RL[:, sl], RL[:, sl], ALU.mult, [RL], [HID[fb * 2 + jj]])
                self.proj_fm(wb, wv, 2, evh)
                hb = fb
                st_ap = self.ffn_w2[l, (q * 8 + hb * 2) * 128:(q * 8 + hb * 2 + 2) * 128, :].rearrange("(k p) n -> p k n", p=128)
                stb, sv = self.load_block(st_ap, 2, 1024, cast=False)
                self.cp(self.fw.pick("cast", self.cast_engs), WBIG[:, 2 * hb:2 * hb + 2, :], sv, [stb], [WBIG])
            for oc in range(8):
                for th in range(2):
                    sl = slice(th * 512, (th + 1) * 512)
                    bk = fw.bank()
                    for kc in range(8):
                        self.mm(bk[:, :], WBIG[:, kc, oc * 128:(oc + 1) * 128], HID[kc][:, sl], kc == 0, kc == 7, [WBIG, HID[kc]], [bk])
                    if q == 0:
                        self.cp(self.ev(), O[oc][:, sl], bk[:, :], [bk], [O[oc]])
                    else:
                        self.tt("dve", O[oc][:, sl], bk[:, :], O[oc][:, sl], ALU.add, [bk, O[oc]], [O[oc]])
        add_residual(mc["gco2"])
        self.WST, self.WBF = WST0, WBF0
        self.cast_engs = ("pool", "dve", "pool")

    def nat_latent(self, QT, KT):
        fw, l = self.fw, self.l
        ar = self.arena
        Vr = ar.get("Vr", [128, 16, 4, 128], BF16)
        CKf = ar.get("CKf", [128, 2, 256], F32)
        CK = ar.get("CK", [128, 2, 256], BF16)
        CVf = ar.get("CVf", [128, 2, 256], F32)
        CV = ar.get("CV", [128, 2, 4, 128], BF16)
        ET = ar.get("ET", [128, 4, 15, 64], F32)
        PTp = ar.pool("PT", 2, [128, 8, 64], BF16)
        PCp = ar.pool("PC", 2, [128, 2, 64], BF16)
        RDp = ar.pool("RD", 2, [128, 64], F32)
        fw.op("pool", lambda e: e.memset(Vr[0:64, :, :, 64:128], 1.0), [], [Vr])
        fw.op("pool", lambda e: e.memset(CV[:, :, :, 64:128], 1.0), [], [CV])
        fw.dma("sp", CKf[:, :, :], self.cnk[l].rearrange("j p t -> p j t"), writes=[CKf])
        self.cp("pool", CK[:, :, :], CKf[:, :, :], [CKf], [CK])
        fw.dma("sp", CVf[:, :, :], self.cnv[l].rearrange("(c p) f -> p c f", p=128), writes=[CVf])
        self.cp("pool", CV[:, :, :, 0:64], CVf[:, :, :].rearrange("p c (h d) -> p c h d", h=4), [CVf], [CV])
        ETr = ar.get("ETr", [128, 4, 15, 64], F32)
        src = bass.AP(tensor=self.rpbpad.t, offset=l * 4 * 15 * 127, ap=[[1, 64], [15 * 127, 4], [127, 15], [1, 64]])
        fw.dma("sp", ETr[0:64, :, :, :], src, writes=[ETr])
        flip = self.cst("flip", rows=64)
        etr2 = ETr[0:64, :, :, :].rearrange("p h r q -> p (h r q)")
        et2 = ET[0:64, :, :, :].rearrange("p h r q -> p (h r q)")
        for cb in range(8):
            bk = fw.bank()
            self.mm(bk[0:64, 0:480], flip, etr2[:, cb * 480:(cb + 1) * 480], True, True, [self.cstb, ETr], [bk])
            self.act(et2[:, cb * 480:(cb + 1) * 480], bk[0:64, 0:480], AF.Exp, [bk], [ET])
        nm = self.cst("natmask", rows=64)
        for h in range(4):
            self.tt("dve", ET[0:64, h, :, :], ET[0:64, h, :, :], nm.unsqueeze(1).to_broadcast([64, 15, 64]), ALU.mult, [ET, self.cstb], [ET])
        wb, wv = self.win_block(l, 1152, 256)
        self.proj_fm(wb, wv, 2, lambda jj, th, bk: self.cp(self.ev(), QT[:, jj, th * 512:(th + 1) * 512], bk[:, :], [bk], [QT]))
        wb, wv = self.win_block(l, 1408, 256)
        self.proj_fm(wb, wv, 2, lambda jj, th, bk: self.cp(self.ev(), KT[:, jj, th * 512:(th + 1) * 512], bk[:, :], [bk], [KT]))
        wb, wv = self.win_block(l, 1664, 256)
        self.proj_tm(wb, wv, 256, lambda ti, bk: self.cp(self.ev(), Vr[0:64, ti, :, 0:64], bk[0:64, 0:256].rearrange("p (h d) -> p h d", h=4), [bk], [Vr]), tile_tokens=64)
        def unit(slot, r, h):
            rs = min(max(r - 4, 0), 8)
            roff0 = rs - r + 7
            qs = slice(r * 64, (r + 1) * 64)
            j, hp = h // 2, h % 2
            P = slice(hp * 64, hp * 64 + 64)
            PT, PC, rd = PTp.bufs[slot], PCp.bufs[slot], RDp.bufs[slot]
            bl, bc, bo = fw.banks[slot * 4], fw.banks[slot * 4 + 1], fw.banks[slot * 4 + 2]
            for a in range(8):
                kr = rs + a
                self.mm(bl[0:64, a * 64:(a + 1) * 64], KT[P, j, kr * 64:(kr + 1) * 64], QT[P, j, qs], True, True, [KT, QT], [bl])
            for c in range(2):
                self.mm(bc[:, c * 64:(c + 1) * 64], CK[P, j, c * 128:(c + 1) * 128], QT[P, j, qs], True, True, [CK, QT], [bc])
            yield
            self.act(PT[0:64, :, :], bl[0:64, :].rearrange("p (a q) -> p a q", a=8), AF.Exp, [bl], [PT], scale=0.125)
            self.act(PC[:, :, :], bc[:, 0:128].rearrange("p (c q) -> p c q", c=2), AF.Exp, [bc], [PC], scale=0.125)
            yield
            self.tt("dve", PT[0:64, :, :], PT[0:64, :, :], ET[0:64, h, roff0:roff0 + 8, :], ALU.mult, [PT, ET], [PT])
            yield
            for a in range(8):
                self.mm(bo[:, 0:64], Vr[0:64, rs + a, h, :], PT[0:64, a, :], a == 0, False, [Vr, PT], [bo])
            for c in range(2):
                self.mm(bo[:, 0:64], CV[:, c, h, :], PC[:, c, :], False, c == 1, [CV, PC], [bo])
            yield
            self.fw.op("dve", lambda e: e.reciprocal(rd[0:64, :], bo[64:128, 0:64]), [bo], [rd])
            self.tt("dve", self.MIX[2 + j][P, qs], bo[0:64, 0:64], rd[0:64, :], ALU.mult, [bo, rd], [self.MIX[2 + j]])
        self.run_slots([(r_, h_) for r_ in range(16) for h_ in range(4)], unit, 2)

    def swa_latent(self, QT, KT, esink):
        fw, l = self.fw, self.l
        ar = self.arena
        ROPE = ar.get("ROPE", [128, 2, T], F32)
        TQ = ar.get("TQ", [128, 2, T], F32)
        KTf = ar.get("KTf", [128, T], F32)
        Vt = ar.get("Vt", [128, NT, 2, 128], BF16)
        CKf = ar.get("CKf", [128, 2, 256], F32)
        CK = ar.get("CK", [128, 2, 256], BF16)
        CVf = ar.get("CVf", [128, 2, 128], F32)
        CV = ar.get("CV", [128, 2, 2, 128], BF16)
        TMPp = ar.pool("TMP", 2, [128, 512], F32)
        PTp = ar.pool("PT", 2, [128, 3, 128], BF16)
        PCp = ar.pool("PC", 2, [128, 2, 128], BF16)
        RDp = ar.pool("RD", 2, [128, 128], F32)
        fw.dma("sp", ROPE[:, :, :], self.rope[:, :, :], writes=[ROPE])
        fw.op("pool", lambda e: e.memset(Vt[:, :, :, 64:128], 1.0), [], [Vt])
        fw.op("pool", lambda e: e.memset(CV[:, :, :, 64:128], 1.0), [], [CV])
        fw.dma("sp", CKf[:, :, :], self.csk[l].rearrange("g p t -> p g t"), writes=[CKf])
        self.cp("pool", CK[:, :, :], CKf[:, :, :], [CKf], [CK])
        fw.dma("sp", CVf[:, :, :], self.csv[l].rearrange("(c p) f -> p c f", p=128), writes=[CVf])
        self.cp("pool", CV[:, :, :, 0:64], CVf[:, :, :].rearrange("p c (h d) -> p c h d", h=2), [CVf], [CV])
        wb, wv = self.win_block(l, 3200, 256)
        self.proj_fm(wb, wv, 2, lambda jj, th, bk: self.tt("dve", TQ[:, jj, th * 512:(th + 1) * 512], bk[:, :], ROPE[:, 0, th * 512:(th + 1) * 512], ALU.mult, [bk, ROPE], [TQ]))
        wb, wv = self.win_block(l, 0, 256, src=self.w_in_sw)

        def evq(jj, th, bk):
            sl = slice(th * 512, (th + 1) * 512)
            tmp = TMPp.next()
            self.tt("dve", tmp[:, :], bk[:, :], ROPE[:, 1, sl], ALU.mult, [bk, ROPE], [tmp])
            self.tt("pool", QT[:, jj, sl], tmp[:, :], TQ[:, jj, sl], ALU.add, [tmp, TQ], [QT])
        self.proj_fm(wb, wv, 2, evq)
        wb, wv = self.win_block(l, 3456, 256)
        self.proj_fm(wb, wv, 1, lambda jj, th, bk: self.tt("dve", KTf[:, th * 512:(th + 1) * 512], bk[:, :], ROPE[:, 0, th * 512:(th + 1) * 512], ALU.mult, [bk, ROPE], [KTf]))
        self.proj_tm(wb, wv[:, :, 128:256], 128, lambda ti, bk: self.cp(self.ev(), Vt[:, ti, :, 0:64], bk[:, 0:128].rearrange("p (h d) -> p h d", h=2), [bk], [Vt]))
        wb, wv = self.win_block(l, 256, 128, src=self.w_in_sw)

        def evk(jj, th, bk):
            sl = slice(th * 512, (th + 1) * 512)
            tmp = TMPp.next()
            self.tt("dve", tmp[:, :], bk[:, :], ROPE[:, 1, sl], ALU.mult, [bk, ROPE], [tmp])
            self.tt("pool", KTf[:, sl], tmp[:, :], KTf[:, sl], ALU.add, [tmp, KTf], [KTf])
            for gg in range(2):
                for hp in range(2):
                    self.cp(self.ev(), KT[hp * 64:hp * 64 + 64, gg, sl], KTf[gg * 64:gg * 64 + 64, sl], [KTf], [KT])
        self.proj_fm(wb, wv, 1, evk)
        def unit(slot, n, h):
            qs = slice(n * 128, (n + 1) * 128)
            kbl = [kb for kb in (n - 1, n, n + 1) if 0 <= kb < NT]
            nk = len(kbl)
            j, hp = h // 2, h % 2
            g = j
            P = slice(hp * 64, hp * 64 + 64)
            PT, PC, rd = PTp.bufs[slot], PCp.bufs[slot], RDp.bufs[slot]
            bl, bc, bo = fw.banks[slot * 4], fw.banks[slot * 4 + 1], fw.banks[slot * 4 + 2]
            for i, kb in enumerate(kbl):
                self.mm(bl[:, i * 128:(i + 1) * 128], KT[P, g, kb * 128:(kb + 1) * 128], QT[P, j, qs], True, True, [KT, QT], [bl])
            for c in range(2):
                self.mm(bc[:, c * 128:(c + 1) * 128], CK[P, g, c * 128:(c + 1) * 128], QT[P, j, qs], True, True, [CK, QT], [bc])
            yield
            self.act(PT[:, 0:nk, :], bl[:, 0:nk * 128].rearrange("p (a q) -> p a q", a=nk), AF.Exp, [bl], [PT], scale=0.125)
            self.act(PC[:, :, :], bc[:, 0:256].rearrange("p (c q) -> p c q", c=2), AF.Exp, [bc], [PC], scale=0.125)
            yield
            for i, kb in enumerate(kbl):
                if kb == n - 1:
                    self.tt("dve", PT[:, i, :], PT[:, i, :], self.cst("mprev"), ALU.mult, [PT, self.cstb], [PT])
                elif kb == n + 1:
                    self.tt("pool", PT[:, i, :], PT[:, i, :], self.cst("mnext"), ALU.mult, [PT, self.cstb], [PT])
            yield
            for i, kb in enumerate(kbl):
                self.mm(bo[:, 0:128], Vt[:, kb, g, :], PT[:, i, :], i == 0, False, [Vt, PT], [bo])
            for c in range(2):
                self.mm(bo[:, 0:128], CV[:, c, g, :], PC[:, c, :], False, c == 1, [CV, PC], [bo])
            yield
            self.cp("dve", rd[0:64, :], bo[64:128, 0:128], [bo], [rd])
            self.ts("dve", rd[0:64, :], rd[0:64, :], esink[0:64, h:h + 1], None, ALU.add, None, [rd, esink], [rd])
            self.fw.op("dve", lambda e: e.reciprocal(rd[0:64, :], rd[0:64, :]), [rd], [rd])
            self.tt("dve", self.MIX[6 + j][P, qs], bo[0:64, 0:128], rd[0:64, :], ALU.mult, [bo, rd], [self.MIX[6 + j]])
        self.run_slots([(n_, h_) for n_ in range(NT) for h_ in range(4)], unit, 2)


_PROG_CACHE = {}


def _get_prog(debug=(), layers=DEPTH, groups=("P", "S")):
    key = (tuple(sorted(debug)), layers, tuple(groups))
    if key not in _PROG_CACHE:
        _PROG_CACHE[key] = Prog(debug, layers, groups)
    return _PROG_CACHE[key]


def _prep_inputs(inp, layers=DEPTH):
    f = lambda k: np.ascontiguousarray(np.asarray(inp[k], dtype=np.float32))
    cst, rope = _build_consts()
    cols = _build_cols(inp)
    w_in = f("w_in")
    idx = np.concatenate([3200 + h * 64 + ROPE_PERM for h in range(4)] + [3456 + h * 64 + ROPE_PERM for h in range(2)])
    w_in_sw = np.ascontiguousarray(w_in[:, :, idx])
    lora2 = np.ascontiguousarray(np.concatenate([f("rw_w2").transpose(0, 2, 1, 3), f("rw_a2").transpose(0, 2, 1, 3)], axis=1))
    rpb = f("nat_rpb")
    rpbpad = np.zeros((DEPTH, 4, 15, 127), np.float32)
    rpbpad[..., 48:79] = rpb[..., ::-1]
    NL = layers
    shared = dict(cols=cols, cst=cst, rope=rope, mod_w=f("mod_w")[:NL], w_in=w_in[:NL], w_in_sw=w_in_sw[:NL], w_out=f("w_out")[:NL],
                  ffn_w1=f("ffn_w1")[:NL], ffn_w2=f("ffn_w2")[:NL], lora2=lora2, rw_g2=f("rw_g2"), rpbpad=rpbpad)
    xp, xs = f("x_prompt"), f("x_sample")
    c, c_ctx = f("c"), f("c_ctx")
    cn, cs = f("cache_nat_kv"), f("cache_swa_kv")
    srw, shg = f("state_rwkv"), f("state_hgrn")
    maps = []
    for core in range(8):
        s = core % 2
        m = dict(shared)
        xpc = xp[core * 4:(core + 1) * 4].reshape(T, D)
        m["xT_p"] = np.ascontiguousarray(xpc.T.reshape(8, 128, T).transpose(1, 0, 2))
        m["xT_s"] = np.ascontiguousarray(xs[s].T.reshape(8, 128, T).transpose(1, 0, 2))
        cond = np.stack([c_ctx, c[s]], axis=1)
        m["condT"] = np.ascontiguousarray(cond.reshape(8, 128, 2).transpose(1, 0, 2))
        nk = cn[s, :, 0].reshape(DEPTH, 256, 256)
        m["cnk"] = np.ascontiguousarray(nk.transpose(0, 2, 1).reshape(DEPTH, 2, 128, 256))
        m["cnv"] = np.ascontiguousarray(cn[s, :, 1].reshape(DEPTH, 256, 256))
        sk = cs[s, :, 0].reshape(DEPTH, 256, 2, 64).transpose(0, 2, 3, 1)
        m["csk"] = np.ascontiguousarray(np.concatenate([sk, sk], axis=2))
        m["csv"] = np.ascontiguousarray(cs[s, :, 1].reshape(DEPTH, 256, 128))
        st = srw[s].transpose(0, 1, 2, 4, 3)
        m["srw"] = np.ascontiguousarray(st.reshape(DEPTH, 2, 2, 2, 64, 64).transpose(0, 1, 3, 4, 2, 5).reshape(DEPTH, 2, 128, 2, 64))
        sh = shg[s]
        m["shg"] = np.ascontiguousarray(sh.reshape(DEPTH, 2, 2, 2, 64, 64).transpose(0, 1, 3, 4, 2, 5).reshape(DEPTH, 2, 128, 2, 64))
        maps.append(m)
    return maps


def _assemble(results):
    B = 32
    y_p = np.zeros((B, 256, D), np.float32)
    y_s = np.zeros((2, 1024, D), np.float32)
    natkv = np.zeros((B, DEPTH, 2, 256, 4, 64), np.float32)
    swakv = np.zeros((B, DEPTH, 2, 256, 2, 64), np.float32)
    rwst = np.zeros((B, DEPTH, 2, 4, 64, 64), np.float32)
    hgst = np.zeros((B, DEPTH, 2, 4, 64, 64), np.float32)
    for core in range(8):
        r = results[core]
        bs = slice(core * 4, core * 4 + 4)
        yT = r["yT_p"].transpose(1, 0, 2).reshape(D, T)
        y_p[bs] = yT.T.reshape(4, 256, D)
        if core < 2:
            y_s[core] = r["yT_s"].transpose(1, 0, 2).reshape(D, T).T
        nk = r["natk"].reshape(DEPTH, 256, 4, 256)
        natkv[bs, :, 0] = nk.transpose(2, 0, 3, 1).reshape(4, DEPTH, 256, 4, 64)
        nv = r["natv"].reshape(DEPTH, 4, 256, 256)
        natkv[bs, :, 1] = nv.transpose(1, 0, 2, 3).reshape(4, DEPTH, 256, 4, 64)
        sk = r["swak"].reshape(DEPTH, 128, 4, 256)
        swakv[bs, :, 0] = sk.transpose(2, 0, 3, 1).reshape(4, DEPTH, 256, 2, 64)
        sv = r["swav"].reshape(DEPTH, 4, 256, 128)
        swakv[bs, :, 1] = sv.transpose(1, 0, 2, 3).reshape(4, DEPTH, 256, 2, 64)
        rs = r["rwst"].reshape(DEPTH, 2, 4, 2, 64, 2, 64)
        rwst[bs] = rs.transpose(2, 0, 1, 5, 3, 6, 4).reshape(4, DEPTH, 2, 4, 64, 64)
        hs = r["hgst"].reshape(DEPTH, 2, 4, 2, 64, 2, 64)
        hgst[bs] = hs.transpose(2, 0, 1, 5, 3, 4, 6).reshape(4, DEPTH, 2, 4, 64, 64)
    return (y_p, y_s, natkv, swakv, rwst, hgst)


def kernel(**inputs):
    prog = _get_prog()
    maps = _prep_inputs(inputs)
    res = run_bass_kernel_spmd(prog.nc, maps, core_ids=list(range(8)))
    return _assemble(res.results)
```
